# Optimizing a Trainium2 kernel written in Bass

```python
import math
import jax, jax.numpy as jnp
from jax import lax
import numpy as np

D_MODEL = 4096
BATCH = 4
SEQ = 4096
DEPTH = 1
DEC_BATCH = 16
DEC_SEQ = 64
PAST_LEN = 4096

CHUNK = 64
MIX_WIDTH = D_MODEL
CONV_CH = MIX_WIDTH // 2
CONV_WIDTH = 31
CONV_STATE = CONV_WIDTH - 1
HEAD_DIM = 128
N_HEADS = (MIX_WIDTH - CONV_CH) // HEAD_DIM
N_KV_HEADS = 4
GQA_GROUP = N_HEADS // N_KV_HEADS
ROPE_DIM = HEAD_DIM // 4
ROPE_THETA = 500000.0
IDX_HEADS = 32
IDX_DIM = 128
IDX_ROPE_DIM = IDX_DIM // 4
IDX_SCALE = (IDX_HEADS ** -0.5) * (IDX_DIM ** -0.5)
TOPK_MAX = 256
ATTN_BLOCK = 128
ATTN_SCALE = HEAD_DIM ** -0.5
MEM_TOKENS = 256
MEM_HEADS = 4
MEM_DIM = MEM_HEADS * HEAD_DIM
PEER_KEYS = 128
PEER_EXPERTS = PEER_KEYS * PEER_KEYS
PEER_HEADS = 8
PEER_QDIM = 256
PEER_HALF = PEER_QDIM // 2
PEER_TOPK = 16
PEER_BLOCK = 128
EPS = 1e-6

OFF_Q = 2 * CONV_CH
OFF_K = OFF_Q + N_HEADS * HEAD_DIM
OFF_V = OFF_K + N_KV_HEADS * HEAD_DIM
OFF_QI = OFF_V + N_KV_HEADS * HEAD_DIM
OFF_KI = OFF_QI + IDX_HEADS * IDX_DIM
OFF_WI = OFF_KI + IDX_DIM
IN_COLS = OFF_WI + IDX_HEADS
IN_SPLITS = (OFF_Q, OFF_K, OFF_V, OFF_QI, OFF_KI, OFF_WI)

kernel_name = "hybrid_conv_dsa_peer_stream_step"


def _rmsnorm(x, g):
    xf = x.astype(jnp.float32)
    y = xf * lax.rsqrt(jnp.mean(xf * xf, axis=-1, keepdims=True) + EPS)
    return (y * g.astype(jnp.float32)).astype(x.dtype)


def _layernorm(x, g, b):
    xf = x.astype(jnp.float32)
    mu = jnp.mean(xf, axis=-1, keepdims=True)
    var = jnp.mean(jnp.square(xf - mu), axis=-1, keepdims=True)
    y = (xf - mu) * lax.rsqrt(var + EPS)
    return (y * g.astype(jnp.float32) + b.astype(jnp.float32)).astype(x.dtype)


def _rope(x, pos, rot_dim):
    half = rot_dim // 2
    inv_freq = jnp.power(ROPE_THETA, -jnp.arange(half, dtype=jnp.float32) / half)
    ang = pos.astype(jnp.float32)[:, None] * inv_freq[None, :]
    cos = jnp.cos(ang)[:, None, :].astype(x.dtype)
    sin = jnp.sin(ang)[:, None, :].astype(x.dtype)
    x1 = x[..., :half]
    x2 = x[..., half:rot_dim]
    return jnp.concatenate([x1 * cos - x2 * sin, x2 * cos + x1 * sin, x[..., rot_dim:]], axis=-1)


def _conv_module(glu, conv_prev, dw_w, dw_b, ln_g, ln_b):
    a, g = jnp.split(glu, 2, axis=-1)
    u = a * jax.nn.sigmoid(g)
    if conv_prev is None:
        conv_prev = jnp.zeros((u.shape[0], CONV_STATE, CONV_CH), u.dtype)
    padded = jnp.concatenate([conv_prev.astype(u.dtype), u], axis=1)
    c = lax.conv_general_dilated(padded, dw_w[:, None, :].astype(u.dtype), (1,), 'VALID',
                                 dimension_numbers=('NWC', 'WIO', 'NWC'),
                                 feature_group_count=CONV_CH) + dw_b
    c = _layernorm(c, ln_g, ln_b)
    return jax.nn.silu(c), padded[:, -CONV_STATE:]


def _gather_rows(a, idx):
    return jax.vmap(lambda ab, ib: ab[ib])(a, idx)


def _dsa_attention(q, k, v, qi, ki, wi, q_pos, k_pos, topk):
    B, T = q.shape[0], q.shape[1]
    qb = min(ATTN_BLOCK, T)
    nb = T // qb
    k_chunk = k_pos // CHUNK

    def blockify(a):
        return jnp.moveaxis(a.reshape((B, nb, qb) + a.shape[2:]), 1, 0)

    def one_block(args):
        q_b, qi_b, wi_b, pos_b = args
        q_chunk = pos_b // CHUNK
        s = jnp.einsum('bthd,bsd->bths', qi_b, ki)
        score = jnp.einsum('bth,bths->bts', wi_b, jax.nn.relu(s)).astype(jnp.float32) * IDX_SCALE
        adm = k_chunk[None, :] <= q_chunk[:, None]
        score = jnp.where(adm[None], score, -jnp.inf)
        _, idx = lax.top_k(score, topk)
        valid = k_chunk[idx] <= q_chunk[None, :, None]
        ks = _gather_rows(k, idx)
        vs = _gather_rows(v, idx)
        qg = q_b.reshape(B, qb, N_KV_HEADS, GQA_GROUP, HEAD_DIM)
        logits = jnp.einsum('btgrd,btkgd->btgrk', qg, ks).astype(jnp.float32) * ATTN_SCALE
        logits = jnp.where(valid[:, :, None, None, :], logits, -jnp.inf)
        p = jax.nn.softmax(logits, axis=-1).astype(vs.dtype)
        o = jnp.einsum('btgrk,btkgd->btgrd', p, vs)
        return o.reshape(B, qb, N_HEADS * HEAD_DIM)

    out = lax.map(one_block, (blockify(q), blockify(qi), blockify(wi), q_pos.reshape(nb, qb)))
    return jnp.moveaxis(out, 0, 1).reshape(B, T, N_HEADS * HEAD_DIM)


def _mem_kv(mem, mem_norm_g, w_k_mem, w_v_mem, mem_k_norm_g):
    B, M, _ = mem.shape
    mn = _rmsnorm(mem, mem_norm_g)
    mk = _rmsnorm((mn @ w_k_mem).reshape(B, M, MEM_HEADS, HEAD_DIM), mem_k_norm_g)
    mv = (mn @ w_v_mem).reshape(B, M, MEM_HEADS, HEAD_DIM)
    return mk, mv


def _mem_attention(hn, mem_k, mem_v, w_q_mem, mem_q_norm_g, w_o_mem):
    B, T, _ = hn.shape
    q = _rmsnorm((hn @ w_q_mem).reshape(B, T, MEM_HEADS, HEAD_DIM), mem_q_norm_g)
    logits = jnp.einsum('bthd,bmhd->bhtm', q, mem_k).astype(jnp.float32) * ATTN_SCALE
    p = jax.nn.softmax(logits, axis=-1).astype(mem_v.dtype)
    o = jnp.einsum('bhtm,bmhd->bthd', p, mem_v).reshape(B, T, MEM_DIM)
    return o @ w_o_mem


def _peer(hn, peer_wq, sub_k1, sub_k2, u_tab, v_tab):
    B, T, D = hn.shape
    n = B * T
    nb = -(-n // PEER_BLOCK)
    xf = jnp.pad(hn.reshape(n, D), ((0, nb * PEER_BLOCK - n), (0, 0)))

    def one_block(xb):
        q = (xb @ peer_wq).reshape(PEER_BLOCK, PEER_HEADS, PEER_QDIM)
        s1 = jnp.einsum('thd,hkd->thk', q[..., :PEER_HALF], sub_k1).astype(jnp.float32)
        s2 = jnp.einsum('thd,hkd->thk', q[..., PEER_HALF:], sub_k2).astype(jnp.float32)
        v1, i1 = lax.top_k(s1, PEER_TOPK)
        v2, i2 = lax.top_k(s2, PEER_TOPK)
        cand = (v1[..., :, None] + v2[..., None, :]).reshape(PEER_BLOCK, PEER_HEADS, PEER_TOPK * PEER_TOPK)
        cand_id = (i1[..., :, None] * PEER_KEYS + i2[..., None, :]).reshape(PEER_BLOCK, PEER_HEADS, PEER_TOPK * PEER_TOPK)
        top_s, top_pos = lax.top_k(cand, PEER_TOPK)
        eid = jnp.take_along_axis(cand_id, top_pos, axis=-1)
        g = jax.nn.softmax(top_s, axis=-1)
        u_sel = u_tab[eid]
        act = jax.nn.gelu(jnp.einsum('thkd,td->thk', u_sel, xb), approximate=False)
        coef = (g * act.astype(jnp.float32)).astype(xb.dtype)
        return jnp.einsum('thk,thkd->td', coef, v_tab[eid])

    out = lax.map(one_block, xf.reshape(nb, PEER_BLOCK, D)).reshape(nb * PEER_BLOCK, D)
    return out[:n].reshape(B, T, D)


def _layer(x, pos, k_pos, topk, conv_prev, k_past, v_past, ki_past, mem_k, mem_v,
           norm_mix_g, w_in, dw_w, dw_b, conv_ln_g, conv_ln_b, q_norm_g, k_norm_g, w_out,
           norm_mem_g, w_q_mem, mem_q_norm_g, w_o_mem,
           norm_ffn_g, peer_wq, peer_sub_k1, peer_sub_k2, peer_u, peer_v):
    B, T, _ = x.shape
    hn = _rmsnorm(x, norm_mix_g)
    glu, q, k, v, qi, ki, wi = jnp.split(hn @ w_in, IN_SPLITS, axis=-1)
    conv_out, conv_new = _conv_module(glu, conv_prev, dw_w, dw_b, conv_ln_g, conv_ln_b)
    q = _rope(_rmsnorm(q.reshape(B, T, N_HEADS, HEAD_DIM), q_norm_g), pos, ROPE_DIM)
    k = _rope(_rmsnorm(k.reshape(B, T, N_KV_HEADS, HEAD_DIM), k_norm_g), pos, ROPE_DIM)
    v = v.reshape(B, T, N_KV_HEADS, HEAD_DIM)
    qi = _rope(qi.reshape(B, T, IDX_HEADS, IDX_DIM), pos, IDX_ROPE_DIM)
    ki = _rope(ki[:, :, None, :], pos, IDX_ROPE_DIM)[:, :, 0]
    if k_past is None:
        k_all, v_all, ki_all = k, v, ki
    else:
        k_all = jnp.concatenate([k_past.astype(k.dtype), k], axis=1)
        v_all = jnp.concatenate([v_past.astype(v.dtype), v], axis=1)
        ki_all = jnp.concatenate([ki_past.astype(ki.dtype), ki], axis=1)
    attn_out = _dsa_attention(q, k_all, v_all, qi, ki_all, wi, pos, k_pos, topk)
    h = x + jnp.concatenate([conv_out, attn_out], axis=-1) @ w_out
    h = h + _mem_attention(_rmsnorm(h, norm_mem_g), mem_k, mem_v, w_q_mem, mem_q_norm_g, w_o_mem)
    y = h + _peer(_rmsnorm(h, norm_ffn_g), peer_wq, peer_sub_k1, peer_sub_k2, peer_u, peer_v)
    return y, k, v, ki, conv_new


def setup_inputs(seed: int = 0) -> dict:
    key = jax.random.key(seed)
    ks = iter(jax.random.split(key, 40))

    def nrm(shape, scale):
        return jax.random.normal(next(ks), shape, jnp.float32) * scale

    def gain(shape):
        return 1.0 + nrm(shape, 0.02)

    L = DEPTH
    return {
        "x_prompt": nrm((BATCH, SEQ, D_MODEL), 1.0),
        "x_sample": nrm((DEC_BATCH, DEC_SEQ, D_MODEL), 1.0),
        "mem_prompt": nrm((BATCH, MEM_TOKENS, D_MODEL), 1.0),
        "cache_k": nrm((L, DEC_BATCH, PAST_LEN, N_KV_HEADS, HEAD_DIM), 1.0),
        "cache_v": nrm((L, DEC_BATCH, PAST_LEN, N_KV_HEADS, HEAD_DIM), 1.0),
        "cache_k_idx": nrm((L, DEC_BATCH, PAST_LEN, IDX_DIM), 1.0),
        "state_conv": nrm((L, DEC_BATCH, CONV_STATE, CONV_CH), 0.5),
        "cache_mem_k": nrm((L, DEC_BATCH, MEM_TOKENS, MEM_HEADS, HEAD_DIM), 1.0),
        "cache_mem_v": nrm((L, DEC_BATCH, MEM_TOKENS, MEM_HEADS, HEAD_DIM), 1.0),
        "norm_mix_g": gain((L, D_MODEL)),
        "w_in": nrm((L, D_MODEL, IN_COLS), D_MODEL ** -0.5),
        "dw_w": nrm((L, CONV_WIDTH, CONV_CH), CONV_WIDTH ** -0.5),
        "dw_b": nrm((L, CONV_CH), 0.02),
        "conv_ln_g": gain((L, CONV_CH)),
        "conv_ln_b": nrm((L, CONV_CH), 0.02),
        "q_norm_g": gain((L, HEAD_DIM)),
        "k_norm_g": gain((L, HEAD_DIM)),
        "w_out": nrm((L, MIX_WIDTH, D_MODEL), MIX_WIDTH ** -0.5),
        "norm_mem_g": gain((L, D_MODEL)),
        "mem_norm_g": gain((L, D_MODEL)),
        "w_q_mem": nrm((L, D_MODEL, MEM_DIM), D_MODEL ** -0.5),
        "w_k_mem": nrm((L, D_MODEL, MEM_DIM), D_MODEL ** -0.5),
        "w_v_mem": nrm((L, D_MODEL, MEM_DIM), D_MODEL ** -0.5),
        "mem_q_norm_g": gain((L, HEAD_DIM)),
        "mem_k_norm_g": gain((L, HEAD_DIM)),
        "w_o_mem": nrm((L, MEM_DIM, D_MODEL), MEM_DIM ** -0.5),
        "norm_ffn_g": gain((L, D_MODEL)),
        "peer_wq": nrm((L, D_MODEL, PEER_HEADS * PEER_QDIM), D_MODEL ** -0.5),
        "peer_sub_k1": nrm((L, PEER_HEADS, PEER_KEYS, PEER_HALF), PEER_HALF ** -0.5),
        "peer_sub_k2": nrm((L, PEER_HEADS, PEER_KEYS, PEER_HALF), PEER_HALF ** -0.5),
        "peer_u": nrm((L, PEER_EXPERTS, D_MODEL), D_MODEL ** -0.5),
        "peer_v": nrm((L, PEER_EXPERTS, D_MODEL), 0.2),
    }


def reference(x_prompt, x_sample, mem_prompt, cache_k, cache_v, cache_k_idx, state_conv,
              cache_mem_k, cache_mem_v,
              norm_mix_g, w_in, dw_w, dw_b, conv_ln_g, conv_ln_b, q_norm_g, k_norm_g, w_out,
              norm_mem_g, mem_norm_g, w_q_mem, w_k_mem, w_v_mem, mem_q_norm_g, mem_k_norm_g, w_o_mem,
              norm_ffn_g, peer_wq, peer_sub_k1, peer_sub_k2, peer_u, peer_v):
    T_p = x_prompt.shape[1]
    T_s = x_sample.shape[1]
    pos_p = jnp.arange(T_p, dtype=jnp.int32)
    pos_s = PAST_LEN + jnp.arange(T_s, dtype=jnp.int32)
    kpos_s = jnp.arange(PAST_LEN + T_s, dtype=jnp.int32)
    topk_p = min(TOPK_MAX, T_p // 4)
    topk_s = min(TOPK_MAX, (PAST_LEN + T_s) // 4)

    h_p, h_s = x_prompt, x_sample
    kp_l, vp_l, kip_l, cp_l, mkp_l, mvp_l = [], [], [], [], [], []
    ks_l, vs_l, kis_l, cs_l = [], [], [], []
    for l in range(DEPTH):
        lp = (norm_mix_g[l], w_in[l], dw_w[l], dw_b[l], conv_ln_g[l], conv_ln_b[l], q_norm_g[l], k_norm_g[l],
              w_out[l], norm_mem_g[l], w_q_mem[l], mem_q_norm_g[l], w_o_mem[l],
              norm_ffn_g[l], peer_wq[l], peer_sub_k1[l], peer_sub_k2[l], peer_u[l], peer_v[l])
        mk_p, mv_p = _mem_kv(mem_prompt, mem_norm_g[l], w_k_mem[l], w_v_mem[l], mem_k_norm_g[l])
        h_p, kp, vp, kip, cp = _layer(h_p, pos_p, pos_p, topk_p, None, None, None, None, mk_p, mv_p, *lp)
        h_s, ks_, vs_, kis, cs = _layer(h_s, pos_s, kpos_s, topk_s, state_conv[l], cache_k[l], cache_v[l],
                                        cache_k_idx[l], cache_mem_k[l].astype(h_s.dtype),
                                        cache_mem_v[l].astype(h_s.dtype), *lp)
        kp_l.append(kp); vp_l.append(vp); kip_l.append(kip); cp_l.append(cp)
        mkp_l.append(mk_p); mvp_l.append(mv_p)
        ks_l.append(ks_); vs_l.append(vs_); kis_l.append(kis); cs_l.append(cs)

    return (h_p, h_s,
            jnp.stack(kp_l), jnp.stack(vp_l), jnp.stack(kip_l), jnp.stack(cp_l),
            jnp.stack(mkp_l), jnp.stack(mvp_l),
            jnp.stack(ks_l), jnp.stack(vs_l), jnp.stack(kis_l), jnp.stack(cs_l))
```

```python
import contextlib
import math
import numpy as np
import ml_dtypes
import concourse.bass as bass
import concourse.mybir as mybir
from concourse.bass_utils import run_bass_kernel_spmd

F32 = mybir.dt.float32
BF16 = mybir.dt.bfloat16
ALU = mybir.AluOpType
AF = mybir.ActivationFunctionType
AX = mybir.AxisListType

EPS = 1e-6
ROPE_THETA = 500000.0
NEG = -1.0e30


class Prog:
    def __init__(self, nc, es):
        self.nc = nc
        self.es = es
        self.st = es
        self.ops = []
        self.engs = {"pe": nc.tensor, "act": nc.scalar, "dve": nc.vector, "pool": nc.gpsimd, "sp": nc.sync}
        self.n_t = 0
        self.eng_sem = {}
        self.eng_cnt = {}
        self.pool = {}
        self.npool = {}
        self.key_sem = {}
        self.fence_sem = None
        self.fence_cnt = 0
        self.tot_ops = 0
        self.tot_wait = 0
        self.free_sems = []
        self.n_dsem = 0

    def sb(self, shape, dt=F32, name=None):
        self.n_t += 1
        return self.st.enter_context(self.nc.sbuf_tensor(name or f"sb{self.n_t}", list(shape), dt))

    def ps(self, shape=(128, 512), dt=F32, name=None):
        self.n_t += 1
        return self.st.enter_context(self.nc.psum_tensor(name or f"ps{self.n_t}", list(shape), dt))

    @staticmethod
    def key(x):
        def nm(a):
            if isinstance(a, str):
                return a
            t = getattr(a, "tensor", None)
            return t.name if t is not None else a.name
        if isinstance(x, tuple):
            return (nm(x[0]), x[1])
        return (nm(x), None)

    def op(self, eng, fn, reads=(), writes=(), dma=False):
        rk = []
        for r in reads:
            if r is None or isinstance(r, (int, float)):
                continue
            k = self.key(r)
            if k not in rk:
                rk.append(k)
        wk = []
        for w in writes:
            k = self.key(w)
            if k not in wk:
                wk.append(k)
        self.ops.append(dict(eng=eng, fn=fn, reads=rk, writes=wk, dma=dma))

    def _esem(self, e):
        if e not in self.eng_sem:
            self.eng_sem[e] = self.es.enter_context(self.nc.semaphore(f"s_{e}"))
            self.eng_cnt[e] = 0
        return self.eng_sem[e]

    def flush(self):
        nc = self.nc
        ops = self.ops
        state = {}
        deps = [None] * len(ops)

        def confl(k):
            ent = state.get(k[0])
            if not ent:
                return []
            if k[1] is None:
                return list(ent.values())
            return [ent[s_] for s_ in (k[1], None) if s_ in ent]

        joined = [False] * len(ops)
        for i, o in enumerate(ops):
            d = set()
            for k in o["reads"]:
                for st in confl(k):
                    d.update(st[0])
            joins = {}
            for k in o["writes"]:
                own = state.get(k[0], {}).get(k[1])
                joinable = bool(o["dma"] and own and own[0] and all(ops[j]["dma"] for j in own[0]) and not own[1])
                joins[k] = joinable
                for st in confl(k):
                    d.update(st[1])
                    if not (joinable and st is own):
                        d.update(st[0])
            if o["dma"]:
                joined[i] = joins[o["writes"][0]]
            for k in o["reads"]:
                st = state.setdefault(k[0], {}).setdefault(k[1], [[], []])
                st[1].append(i)
            for k in o["writes"]:
                ent = state.setdefault(k[0], {})
                if joins[k]:
                    ent[k[1]][0].append(i)
                else:
                    if k[1] is None:
                        ent.clear()
                    ent[k[1]] = [[i], []]
            d.discard(i)
            if o["eng"] == "pe":
                d = {j for j in d if not (ops[j]["eng"] == "pe" and not ops[j]["dma"])}
            deps[i] = d
        need = [False] * len(ops)
        for d in deps:
            for j in d:
                need[j] = True
        last_on = {}
        for i, o in enumerate(ops):
            if not o["dma"]:
                last_on[o["eng"]] = i
        for i in last_on.values():
            need[i] = True

        sig = [None] * len(ops)
        waited = {}
        for i, o in enumerate(ops):
            e = o["eng"]
            eo = self.engs[e]
            wl = {}
            for j in deps[i]:
                s, v = sig[j]
                kk = id(s)
                if kk not in wl or wl[kk][1] < v:
                    wl[kk] = (s, v)
            pre = None
            if o["dma"]:
                k = o["writes"][0]
                name = k[0]
                pl = self.pool.get(name)
                if pl is None:
                    n = self.npool.get(name, 2)
                    sems_, cnt_ = [], []
                    for q in range(n):
                        if self.free_sems:
                            s_, c_ = self.free_sems.pop()
                        else:
                            self.n_dsem += 1
                            s_, c_ = self.es.enter_context(nc.semaphore(f"dma{self.n_dsem}")), 0
                        sems_.append(s_)
                        cnt_.append(c_)
                    pl = dict(sems=sems_, cnt=cnt_, last=[None] * n, rr=0)
                    self.pool[name] = pl
                idx = None
                if joined[i] and k in self.key_sem and pl["last"][self.key_sem[k]] == k:
                    idx = self.key_sem[k]
                else:
                    idx = pl["rr"]
                    pl["rr"] = (pl["rr"] + 1) % len(pl["sems"])
                    if pl["cnt"][idx] > 0:
                        s = pl["sems"][idx]
                        kk = id(s)
                        if kk not in wl or wl[kk][1] < pl["cnt"][idx]:
                            wl[kk] = (s, pl["cnt"][idx])
                self.key_sem[k] = idx
                pl["last"][idx] = k
                pre = (pl, idx)
            for kk, (s, v) in wl.items():
                if waited.get((e, kk), -1) >= v:
                    continue
                waited[(e, kk)] = v
                eo.wait_ge(s, v)
                self.tot_wait += 1
            ins = o["fn"](eo)
            if o["dma"]:
                pl, idx = pre
                pl["cnt"][idx] += 16
                ins.then_inc(pl["sems"][idx], 16)
                sig[i] = (pl["sems"][idx], pl["cnt"][idx])
            elif need[i]:
                s = self._esem(e)
                self.eng_cnt[e] += 1
                ins.then_inc(s, 1)
                sig[i] = (s, self.eng_cnt[e])
        self.tot_ops += len(ops)
        self.ops = []
        if self.fence_sem is None:
            self.fence_sem = self.es.enter_context(nc.semaphore("fence"))
        for e, s in self.eng_sem.items():
            if self.eng_cnt[e] > 0:
                nc.sync.wait_ge(s, self.eng_cnt[e])
        for pl in self.pool.values():
            for s, c in zip(pl["sems"], pl["cnt"]):
                if c > 0:
                    nc.sync.wait_ge(s, c)
        for pl in self.pool.values():
            for s, c in zip(pl["sems"], pl["cnt"]):
                self.free_sems.append((s, c))
        self.pool = {}
        self.key_sem = {}
        self.fence_cnt += 1
        nc.sync.drain().then_inc(self.fence_sem, 1)
        for e in ("pe", "act", "dve", "pool"):
            self.engs[e].wait_ge(self.fence_sem, self.fence_cnt)

    def dma(self, q, out, in_, okey=None, ikey=None, **kw):
        self.op(q, lambda e: e.dma_start(out=out, in_=in_, **kw), reads=[ikey or in_], writes=[okey or out], dma=True)

    def cdma(self, out, in_, okey=None, ikey=None):
        n = out.shape[-1]
        if n > 2048:
            d = 2048
            while n % d:
                d //= 2
            names = " ".join(f"a{i}" for i in range(len(out.shape) - 1))
            pat = f"{names} (x d) -> {names} x d"
            self.dma("pool", out.rearrange(pat, d=d), in_.rearrange(pat, d=d), okey=okey or out, ikey=ikey or in_)
        else:
            self.dma("pool", out, in_, okey=okey, ikey=ikey)

    def mm(self, out, lhsT, rhs, start=True, stop=True, okey=None, rkeys=None):
        self.op("pe", lambda e: e.matmul(out, lhsT, rhs, start=start, stop=stop), reads=rkeys or [lhsT, rhs], writes=[okey or out])

    def tr(self, out, in_, ident, okey=None, ikey=None):
        self.op("pe", lambda e: e.transpose(out, in_, ident), reads=[ikey or in_, ident], writes=[okey or out])

    def act(self, out, in_, func, scale=1.0, bias=0.0, accum_out=None, okey=None, ikey=None):
        rd = [ikey or in_] + [x for x in (scale, bias) if not isinstance(x, (int, float))]
        wr = [okey or out] + ([accum_out] if accum_out is not None else [])
        if accum_out is not None:
            self.op("act", lambda e: e.activation(out, in_, func, bias=bias, scale=scale, accum_out=accum_out), reads=rd, writes=wr)
        else:
            self.op("act", lambda e: e.activation(out, in_, func, bias=bias, scale=scale), reads=rd, writes=wr)

    def ts(self, eng, out, in0, s1, s2=None, op0=ALU.mult, op1=None, accum_out=None, okey=None, ikey=None):
        rd = [ikey or in0] + [x for x in (s1, s2) if x is not None and not isinstance(x, (int, float))]
        wr = [okey or out] + ([accum_out] if accum_out is not None else [])
        kw = {}
        if op1 is not None:
            kw["op1"] = op1
        if accum_out is not None:
            kw["accum_out"] = accum_out
        self.op(eng, lambda e: e.tensor_scalar(out, in0, s1, s2, op0, **kw), reads=rd, writes=wr)

    def tt(self, eng, out, in0, in1, op, okey=None, rkeys=None):
        self.op(eng, lambda e: e.tensor_tensor(out, in0, in1, op), reads=rkeys or [in0, in1], writes=[okey or out])

    def stt(self, out, in0, scalar, in1, op0, op1, okey=None, rkeys=None):
        rd = list(rkeys or [in0, in1]) + ([scalar] if not isinstance(scalar, (int, float)) else [])
        self.op("dve", lambda e: e.scalar_tensor_tensor(out, in0, scalar, in1, op0, op1), reads=rd, writes=[okey or out])

    def copy(self, eng, out, in_, okey=None, ikey=None):
        if eng == "act":
            self.op("act", lambda e: e.copy(out, in_), reads=[ikey or in_], writes=[okey or out])
        else:
            self.op(eng, lambda e: e.tensor_copy(out, in_), reads=[ikey or in_], writes=[okey or out])

    def max8(self, out, in_, okey=None):
        self.op("dve", lambda e: e.max(out, in_), reads=[in_], writes=[okey or out])

    def mrep(self, out, in_to_replace, in_values, imm, rkeys=None):
        self.op("dve", lambda e: e.match_replace(out, in_to_replace, in_values, imm), reads=rkeys or [in_to_replace, in_values], writes=[out])

    def memset(self, eng, ap, val):
        self.op(eng, lambda e: e.memset(ap, val), reads=[], writes=[ap])

    def recip(self, out, in_, okey=None):
        self.op("dve", lambda e: e.reciprocal(out, in_), reads=[in_], writes=[okey or out])

    def reduce(self, out, in_, op, axis=AX.X):
        self.op("dve", lambda e: e.tensor_reduce(out, in_, axis, op), reads=[in_], writes=[out])


def mkcfg(D=4096, SEQ=4096, B=4, DB=16, DS=64, PAST=4096, IH=32, TOPK_MAX=256, MEMT=256):
    c = dict(D=D, SEQ=SEQ, B=B, DB=DB, DS=DS, PAST=PAST, IH=IH, MEMT=MEMT)
    c["KC"] = D // 128
    c["CCH"] = D // 2
    c["CC"] = c["CCH"] // 128
    c["NH"] = (D // 2) // 128
    c["NKV"] = 4
    c["GQ"] = c["NH"] // 4
    c["NP"] = SEQ // 2 // 128
    c["NCX"] = SEQ // 128
    c["NT"] = c["NP"] + 2
    c["TOPK_P"] = min(TOPK_MAX, SEQ // 4)
    c["TOPK_S"] = min(TOPK_MAX, (PAST + DS) // 4)
    c["SS"] = PAST + 128
    c["MH"] = 4
    c["MC"] = MEMT // 128
    return c


PEER_KEYS = 128
PEER_HEADS = 8
PEER_TOPK = 16


def build(cfg, stages=("M", "KV", "MAIN", "B", "C", "D", "E"), dbg=False):
    D, KC, CCH, CC, NH, NKV, GQ, NP, NCX, NT, IH, SEQ, PAST, SS, MEMT, MC = (cfg[k] for k in (
        "D", "KC", "CCH", "CC", "NH", "NKV", "GQ", "NP", "NCX", "NT", "IH", "SEQ", "PAST", "SS", "MEMT", "MC"))
    NTOK = NT * 128
    IDX_SCALE = (IH ** -0.5) * (128 ** -0.5)
    ATT_SCALE = 128 ** -0.5
    nc = bass.Bass("TRN2", target_bir_lowering=False)

    def din(name, shape, dt=F32):
        return nc.dram_tensor(name, list(shape), dt, kind="ExternalInput").ap()

    def dout(name, shape, dt=F32):
        return nc.dram_tensor(name, list(shape), dt, kind="ExternalOutput").ap()

    def dscr(name, shape, dt=F32):
        return nc.dram_tensor(name, list(shape), dt, kind="Internal").ap()

    xctx = din("xctx", [SEQ, D])
    xsp = din("xsp", [2, 128, D])
    xhalo = din("xhalo", [128, D])
    mem = din("mem", [MEMT, D])
    ckT = din("ckT", [2, 128, NKV, PAST])
    cv = din("cv", [2, PAST, NKV * 128])
    ckiT = din("ckiT", [2, 128, PAST])
    stT = din("stT", [2, CCH, 30])
    cmkT = din("cmkT", [2, 128, 4, MEMT])
    cmv = din("cmv", [2, MEMT, 512])
    w_glu = din("w_glu", [D, 2 * CCH])
    w_q = din("w_q", [D, NH * 128])
    w_qi = din("w_qi", [D, IH * 128])
    w_wi = din("w_wi", [D, IH])
    w_kv = din("w_kv", [D, 1152])
    w_out = din("w_out", [D, D])
    w_qm = din("w_qm", [D, 512])
    w_km = din("w_km", [D, 512])
    w_vm = din("w_vm", [D, 512])
    w_om = din("w_om", [512, D])
    w_pq = din("w_pq", [D, 2048])
    subk = din("subk", [128, 16, 128])
    uT = din("uT", [128, 128, KC * 128])
    vtab = din("vtab", [PEER_KEYS * PEER_KEYS, D])
    g_mix = din("g_mix", [1, D])
    g_memn = din("g_memn", [1, D])
    g_ffn = din("g_ffn", [1, D])
    g_mem = din("g_mem", [1, D])
    g_q = din("g_q", [1, 128])
    g_k = din("g_k", [1, 128])
    g_mq = din("g_mq", [1, 128])
    g_mk = din("g_mk", [1, 128])
    dww = din("dww", [128, CC, 31])
    dwb = din("dwb", [128, CC])
    lng = din("lng", [128, CC])
    lnb = din("lnb", [128, CC])
    rope_c = din("rope_c", [SEQ, 32])
    rope_s = din("rope_s", [128, 32])
    kc_p = din("kc_p", [1, SEQ])
    kc_s = din("kc_s", [1, SS])
    qch = din("qch", [128, NT])
    c_idb = din("c_idb", [128, 128], BF16)
    c_idf = din("c_idf", [128, 128])
    c_oneb = din("c_oneb", [128, 128], BF16)
    c_onef = din("c_onef", [128, 128])

    y = dout("y", [NTOK, D])
    o_k = dout("o_k", [SEQ, NKV * 128])
    o_v = dout("o_v", [SEQ, NKV * 128])
    o_ki = dout("o_ki", [SEQ, 128])
    o_conv = dout("o_conv", [30, CCH])
    o_mk = dout("o_mk", [MEMT, 512])
    o_mv = dout("o_mv", [MEMT, 512])
    o_ks = dout("o_ks", [2, 128, NKV * 128])
    o_vs = dout("o_vs", [2, 128, NKV * 128])
    o_kis = dout("o_kis", [2, 128, 128])
    o_convs = dout("o_convs", [2, 30, CCH])

    UTp = dscr("UTp", [CCH, 128 + NP * 128])
    UTs = dscr("UTs", [2, CCH, 160])
    KT = dscr("KT", [128, NKV, SEQ], BF16)
    Vc = dscr("Vc", [SEQ, NKV * 128], BF16)
    KIT = dscr("KIT", [128, SEQ], BF16)
    KTs = dscr("KTs", [2, 128, NKV, 128], BF16)
    Vs = dscr("Vs", [2, 128, NKV * 128], BF16)
    KITs = dscr("KITs", [2, 128, 128], BF16)
    MKT = dscr("MKT", [128, 4, MEMT], BF16)
    MV = dscr("MV", [MEMT, 512], BF16)
    QT = dscr("QT", [NT, 128, NH, 128], BF16)
    QIT = dscr("QIT", [NT, 128, IH, 128], BF16)
    WI = dscr("WI", [NT, 128, IH])
    MIXT = dscr("MIXT", [D, NTOK], BF16)
    H2 = dscr("H2", [NTOK, D])
    HN2T = dscr("HN2T", [D, NTOK], BF16)
    S12 = dscr("S12", [NTOK, 16, 128])
    GALL = dscr("GALL", [128, 128, NTOK], BF16)

    dbg_out = {}

    with contextlib.ExitStack() as es:
        P = Prog(nc, es)
        idb = P.sb([128, 128], BF16, "idb")
        idf = P.sb([128, 128], F32, "idf")
        oneb = P.sb([128, 128], BF16, "oneb")
        onef = P.sb([128, 128], F32, "onef")
        P.dma("sp", idb[:], c_idb)
        P.dma("sp", idf[:], c_idf)
        P.dma("sp", oneb[:], c_oneb)
        P.dma("sp", onef[:], c_onef)
        P.flush()

        def bcast_row(dst, src_row, n):
            P.dma("sp", dst, src_row.to_broadcast([128, n]))

        def rstd_from_ss(ss, n, out, tmp):
            P.ts("dve", tmp, ss, 1.0 / n, EPS, op0=ALU.mult, op1=ALU.add)
            P.act(tmp, tmp, AF.Sqrt)
            P.recip(out, tmp)

        def load_w(dst, src, ncols):
            sv = src.rearrange("(c p) n -> p c n", p=128)
            nq = 4 if KC % 4 == 0 else 1
            step = KC // nq
            for q in range(nq):
                P.dma("pool", dst[:, q * step:(q + 1) * step, 0:ncols], sv[:, q * step:(q + 1) * step, :], okey=(dst, q))

        def wkeys(dst):
            return [(dst, q) for q in range(4 if KC % 4 == 0 else 1)]

        class NormT:
            def __init__(self, gsrc):
                self.gbc = P.sb([128, D], F32)
                bcast_row(self.gbc[:], gsrc[0:1, :], D)
                self.sq = P.sb([128, D], BF16)
                self.xs = [P.sb([128, D], BF16) for _ in range(2)]
                self.sm = [P.sb([128, 4], F32) for _ in range(2)]
                self.pt = [P.ps([128, 1024], BF16) for _ in range(2)]
                self.k = 0

            def run(self, x_t, dst_fn):
                k = self.k
                self.k += 1
                sm = self.sm[k % 2]
                xs = self.xs[k % 2]
                P.act(self.sq[:], x_t, AF.Square, accum_out=sm[:, 0:1])
                rstd_from_ss(sm[:, 0:1], D, sm[:, 1:2], sm[:, 2:3])
                P.stt(xs[:], x_t, sm[:, 1:2], self.gbc[:], ALU.mult, ALU.mult)
                nb = 8 if KC % 8 == 0 else KC
                for b0 in range(0, KC, nb):
                    pt = self.pt[(b0 // nb) % 2]
                    for j in range(nb):
                        P.tr(pt[:, j * 128:(j + 1) * 128], xs[:, (b0 + j) * 128:(b0 + j + 1) * 128], idb[:])
                    eng = "act" if (b0 // nb) % 2 == 0 else "dve"
                    P.copy(eng, dst_fn(b0, nb), pt[:, 0:nb * 128].rearrange("p (n t) -> p n t", t=128))

        def head_norm(ps_ap, nh, gain_bc, out_f, sq_t, sm_t):
            P.act(sq_t[:, 0:nh * 128], ps_ap, AF.Square)
            P.reduce(sm_t[:, 0:nh], sq_t[:, 0:nh * 128].rearrange("p (h d) -> p h d", d=128), ALU.add)
            rstd_from_ss(sm_t[:, 0:nh], 128, sm_t[:, 4:4 + nh], sm_t[:, 8:8 + nh])
            P.tt("dve", out_f, ps_ap.rearrange("p (h d) -> p h d", d=128),
                 sm_t[:, 4:4 + nh].unsqueeze(2).to_broadcast([128, nh, 128]), ALU.mult)
            P.tt("dve", out_f, out_f, gain_bc[:, 0:128].unsqueeze(1).to_broadcast([128, nh, 128]), ALU.mult)

        def rope(f, nh, cs, tmp):
            x1 = f[:, :, 0:16]
            x2 = f[:, :, 16:32]
            cosb = cs[:, 0:16].unsqueeze(1).to_broadcast([128, nh, 16])
            sinb = cs[:, 16:32].unsqueeze(1).to_broadcast([128, nh, 16])
            P.tt("dve", tmp[:, 0, 0:nh, :], x1, cosb, ALU.mult)
            P.tt("dve", tmp[:, 1, 0:nh, :], x2, sinb, ALU.mult)
            P.tt("dve", tmp[:, 2, 0:nh, :], x2, cosb, ALU.mult)
            P.tt("dve", tmp[:, 3, 0:nh, :], x1, sinb, ALU.mult)
            P.tt("dve", x1, tmp[:, 0, 0:nh, :], tmp[:, 1, 0:nh, :], ALU.subtract)
            P.tt("dve", x2, tmp[:, 2, 0:nh, :], tmp[:, 3, 0:nh, :], ALU.add)

        def tok_mm(ps_ap, hnT, tcol, wb, ncols, wk):
            for c in range(KC):
                P.mm(ps_ap, hnT[:, c, tcol:tcol + 128], wb[:, c, 0:ncols], start=(c == 0), stop=(c == KC - 1),
                     rkeys=[hnT] + wk)

        if "M" in stages:
            with contextlib.ExitStack() as st:
                P.st = st
                nt = NormT(g_mem)
                wk_b = P.sb([128, KC, 512], BF16)
                wv_b = P.sb([128, KC, 512], BF16)
                load_w(wk_b, w_km, 512)
                load_w(wv_b, w_vm, 512)
                gk = P.sb([128, 128], F32)
                bcast_row(gk[:], g_mk[0:1, :], 128)
                xt = [P.sb([128, D], F32) for _ in range(2)]
                hn = [P.sb([128, KC, 128], BF16) for _ in range(2)]
                pk = P.ps()
                pv = P.ps()
                ptr = P.ps([128, 1024], BF16)
                sq_t = P.sb([128, 512], F32)
                sm_t = P.sb([128, 12], F32)
                kf = P.sb([128, 4, 128], F32)
                kb = P.sb([128, 512], BF16)
                kTt = P.sb([128, 4, 128], BF16)
                vf = P.sb([128, 512], F32)
                vb = P.sb([128, 512], BF16)
                for m in range(MC):
                    x_t = xt[m % 2]
                    h_t = hn[m % 2]
                    P.dma("sp", x_t[:], mem[m * 128:(m + 1) * 128, :])
                    nt.run(x_t[:], lambda c0, n, h_t=h_t: h_t[:, c0:c0 + n, :])
                    tok_mm(pk[:, 0:512], h_t, 0, wk_b, 512, wkeys(wk_b))
                    tok_mm(pv[:, 0:512], h_t, 0, wv_b, 512, wkeys(wv_b))
                    head_norm(pk[:, 0:512], 4, gk, kf[:], sq_t, sm_t)
                    P.dma("sp", o_mk[m * 128:(m + 1) * 128, :], kf[:].rearrange("p h d -> p (h d)"))
                    P.copy("act", kb[:], kf[:].rearrange("p h d -> p (h d)"))
                    for h in range(4):
                        P.tr(ptr[:, h * 128:(h + 1) * 128], kb[:, h * 128:(h + 1) * 128], idb[:])
                    P.copy("dve", kTt[:], ptr[:, 0:512].rearrange("p (h t) -> p h t", t=128))
                    P.dma("sp", MKT[:, :, m * 128:(m + 1) * 128], kTt[:])
                    P.copy("act", vf[:], pv[:, 0:512])
                    P.dma("sp", o_mv[m * 128:(m + 1) * 128, :], vf[:])
                    P.copy("dve", vb[:], pv[:, 0:512])
                    P.dma("sp", MV[m * 128:(m + 1) * 128, :], vb[:])
                P.flush()
            P.st = es

        if "KV" in stages:
            with contextlib.ExitStack() as st:
                P.st = st
                nt = NormT(g_mix)
                wb = P.sb([128, KC, 1152], BF16)
                load_w(wb, w_kv, 1152)
                wk = wkeys(wb)
                gk = P.sb([128, 128], F32)
                bcast_row(gk[:], g_k[0:1, :], 128)
                xt = [P.sb([128, D], F32) for _ in range(2)]
                hn = [P.sb([128, KC, 128], BF16) for _ in range(2)]
                cs = [P.sb([128, 32], F32) for _ in range(2)]
                pk = P.ps()
                pv = P.ps()
                pki = P.ps()
                ptr = P.ps([128, 1024], BF16)
                sq_t = P.sb([128, 512], F32)
                sm_t = P.sb([128, 12], F32)
                rtmp = P.sb([128, 4, 4, 16], F32)
                kf = [P.sb([128, 4, 128], F32) for _ in range(2)]
                kb = P.sb([128, 512], BF16)
                kTt = [P.sb([128, 4, 128], BF16) for _ in range(2)]
                vf = [P.sb([128, 512], F32) for _ in range(2)]
                vb = [P.sb([128, 512], BF16) for _ in range(2)]
                kif = [P.sb([128, 1, 128], F32) for _ in range(2)]
                kib = P.sb([128, 128], BF16)
                kiTt = [P.sb([128, 128], BF16) for _ in range(2)]
                tiles = [("p", i) for i in range(NCX)] + [("s", 0), ("s", 1)]
                for n_, (kind, i) in enumerate(tiles):
                    x_t = xt[n_ % 2]
                    h_t = hn[n_ % 2]
                    c_t = cs[n_ % 2]
                    if kind == "p":
                        P.dma("sp", x_t[:], xctx[i * 128:(i + 1) * 128, :])
                        P.dma("sp", c_t[:], rope_c[i * 128:(i + 1) * 128, :])
                    else:
                        P.dma("sp", x_t[:], xsp[i])
                        P.dma("sp", c_t[:], rope_s)
                    nt.run(x_t[:], lambda c0, n, h_t=h_t: h_t[:, c0:c0 + n, :])
                    for c in range(KC):
                        P.mm(pk[:, 0:512], h_t[:, c, :], wb[:, c, 0:512], start=(c == 0), stop=(c == KC - 1), rkeys=[h_t] + wk)
                    for c in range(KC):
                        P.mm(pv[:, 0:512], h_t[:, c, :], wb[:, c, 512:1024], start=(c == 0), stop=(c == KC - 1), rkeys=[h_t] + wk)
                    for c in range(KC):
                        P.mm(pki[:, 0:128], h_t[:, c, :], wb[:, c, 1024:1152], start=(c == 0), stop=(c == KC - 1), rkeys=[h_t] + wk)
                    kf_t = kf[n_ % 2]
                    head_norm(pk[:, 0:512], 4, gk, kf_t[:], sq_t, sm_t)
                    rope(kf_t, 4, c_t, rtmp)
                    kflat = kf_t[:].rearrange("p h d -> p (h d)")
                    if kind == "p":
                        P.dma("sp", o_k[i * 128:(i + 1) * 128, :], kflat)
                    else:
                        P.dma("sp", o_ks[i], kflat)
                    P.copy("act", kb[:], kflat)
                    for h in range(4):
                        P.tr(ptr[:, h * 128:(h + 1) * 128], kb[:, h * 128:(h + 1) * 128], idb[:])
                    kT_t = kTt[n_ % 2]
                    P.copy("dve", kT_t[:], ptr[:, 0:512].rearrange("p (h t) -> p h t", t=128))
                    if kind == "p":
                        P.dma("sp", KT[:, :, i * 128:(i + 1) * 128], kT_t[:])
                    else:
                        P.dma("sp", KTs[i], kT_t[:])
                    vf_t = vf[n_ % 2]
                    vb_t = vb[n_ % 2]
                    P.copy("act", vf_t[:], pv[:, 0:512])
                    P.copy("dve", vb_t[:], pv[:, 0:512])
                    if kind == "p":
                        P.dma("sp", o_v[i * 128:(i + 1) * 128, :], vf_t[:])
                        P.dma("sp", Vc[i * 128:(i + 1) * 128, :], vb_t[:])
                    else:
                        P.dma("sp", o_vs[i], vf_t[:])
                        P.dma("sp", Vs[i], vb_t[:])
                    ki_t = kif[n_ % 2]
                    P.copy("act", ki_t[:, 0, :], pki[:, 0:128])
                    rope(ki_t, 1, c_t, rtmp)
                    if kind == "p":
                        P.dma("sp", o_ki[i * 128:(i + 1) * 128, :], ki_t[:, 0, :])
                    else:
                        P.dma("sp", o_kis[i], ki_t[:, 0, :])
                    P.copy("act", kib[:], ki_t[:, 0, :])
                    P.tr(ptr[:, 512:640], kib[:], idb[:])
                    kiT_t = kiTt[n_ % 2]
                    P.copy("dve", kiT_t[:], ptr[:, 512:640])
                    if kind == "p":
                        P.dma("sp", KIT[:, i * 128:(i + 1) * 128], kiT_t[:])
                    else:
                        P.dma("sp", KITs[i], kiT_t[:])
                P.flush()
            P.st = es

        own = [("h", -1)] + [("p", i) for i in range(NP)] + [("s", 0), ("s", 1)]
        groups = [own[i:i + 4] for i in range(0, len(own), 4)]

        def tile_index(kind, i):
            return i if kind == "p" else NP + i

        if "MAIN" in stages:
            with contextlib.ExitStack() as st:
                P.st = st
                nt = NormT(g_mix)
                gq = P.sb([128, 128], F32)
                bcast_row(gq[:], g_q[0:1, :], 128)
                for s in range(2):
                    P.dma("sp", UTs[s][:, 2:32], stT[s], okey=("UTs", "st"))
                xt = [P.sb([128, D], F32) for _ in range(2)]
                hnT = P.sb([128, KC, 512], BF16)
                wbuf = [P.sb([128, KC, 512], BF16) for _ in range(2)]
                cst = P.sb([128, 4, 32], F32)
                pa = P.ps()
                pg = P.ps()
                pq = [P.ps() for _ in range(2)]
                ptr = P.ps([128, 1024], BF16)
                sg = P.sb([128, 512], F32)
                ut = [P.sb([128, 512], F32) for _ in range(2)]
                sq_t = P.sb([128, 512], F32)
                sm_t = P.sb([128, 12], F32)
                rtmp = P.sb([128, 4, 4, 16], F32)
                qf = P.sb([128, 4, 128], F32)
                qb = P.sb([128, 512], BF16)
                qTt = [P.sb([128, 4, 128], BF16) for _ in range(2)]
                wis = [P.sb([128, IH], F32) for _ in range(2)]
                wcnt = 0
                xcnt = 0
                for grp in groups:
                    ng = len(grp)
                    N = ng * 128
                    for tt, (kind, i) in enumerate(grp):
                        x_t = xt[xcnt % 2]
                        xcnt += 1
                        if kind == "h":
                            P.dma("sp", x_t[:], xhalo)
                        elif kind == "p":
                            P.dma("sp", x_t[:], xctx[i * 128:(i + 1) * 128, :])
                            P.dma("sp", cst[:, tt, :], rope_c[i * 128:(i + 1) * 128, :], okey=(cst, tt))
                        else:
                            P.dma("sp", x_t[:], xsp[i])
                            P.dma("sp", cst[:, tt, :], rope_s, okey=(cst, tt))
                        nt.run(x_t[:], lambda c0, n, tt=tt: hnT[:, c0:c0 + n, tt * 128:(tt + 1) * 128])
                    for b in range(CC // 2):
                        wb = wbuf[wcnt % 2]
                        wcnt += 1
                        load_w(wb, w_glu[:, b * 512:(b + 1) * 512], 512)
                        wk = wkeys(wb)
                        for s in range(2):
                            j = 2 * b + s
                            for c in range(KC):
                                P.mm(pa[:, 0:N], wb[:, c, (2 * s) * 128:(2 * s + 1) * 128], hnT[:, c, 0:N],
                                     start=(c == 0), stop=(c == KC - 1), rkeys=[hnT] + wk)
                            for c in range(KC):
                                P.mm(pg[:, 0:N], wb[:, c, (2 * s + 1) * 128:(2 * s + 2) * 128], hnT[:, c, 0:N],
                                     start=(c == 0), stop=(c == KC - 1), rkeys=[hnT] + wk)
                            P.act(sg[:, 0:N], pg[:, 0:N], AF.Sigmoid)
                            u_t = ut[j % 2]
                            P.tt("dve", u_t[:, 0:N], pa[:, 0:N], sg[:, 0:N], ALU.mult)
                            for tt, (kind, i) in enumerate(grp):
                                src = u_t[:, tt * 128:(tt + 1) * 128]
                                if kind == "h":
                                    P.dma("sp", UTp[j * 128:(j + 1) * 128, 0:128], src, okey=("UTp", None))
                                elif kind == "p":
                                    P.dma("sp", UTp[j * 128:(j + 1) * 128, 128 + i * 128:128 + (i + 1) * 128], src, okey=("UTp", None))
                                else:
                                    P.dma("sp", UTs[i][j * 128:(j + 1) * 128, 32:160], src, okey=("UTs", "tok"))
                    for which, nblk, wsrc, dst in (("q", NH // 4, w_q, QT), ("qi", IH // 4, w_qi, QIT)):
                        for b in range(nblk):
                            wb = wbuf[wcnt % 2]
                            wcnt += 1
                            load_w(wb, wsrc[:, b * 512:(b + 1) * 512], 512)
                            wk = wkeys(wb)
                            for tt, (kind, i) in enumerate(grp):
                                if kind == "h":
                                    continue
                                ti = tile_index(kind, i)
                                pq_t = pq[(tt) % 2]
                                tok_mm(pq_t[:, 0:512], hnT, tt * 128, wb, 512, wk)
                                if which == "q":
                                    head_norm(pq_t[:, 0:512], 4, gq, qf[:], sq_t, sm_t)
                                else:
                                    P.copy("act", qf[:].rearrange("p h d -> p (h d)"), pq_t[:, 0:512])
                                rope(qf, 4, cst[:, tt, :], rtmp)
                                P.copy("act", qb[:], qf[:].rearrange("p h d -> p (h d)"))
                                for h in range(4):
                                    P.tr(ptr[:, h * 128:(h + 1) * 128], qb[:, h * 128:(h + 1) * 128], idb[:])
                                q_T = qTt[(tt) % 2]
                                P.copy("dve", q_T[:], ptr[:, 0:512].rearrange("p (h t) -> p h t", t=128))
                                P.dma("sp", dst[ti][:, b * 4:(b + 1) * 4, :], q_T[:], okey=(dst.tensor.name, None))
                    wb = wbuf[wcnt % 2]
                    wcnt += 1
                    load_w(wb, w_wi, IH)
                    wk = wkeys(wb)
                    for tt, (kind, i) in enumerate(grp):
                        if kind == "h":
                            continue
                        ti = tile_index(kind, i)
                        pq_t = pq[tt % 2]
                        tok_mm(pq_t[:, 0:IH], hnT, tt * 128, wb, IH, wk)
                        w_s = wis[tt % 2]
                        P.act(w_s[:], pq_t[:, 0:IH], AF.Copy, scale=IDX_SCALE)
                        P.dma("sp", WI[ti], w_s[:], okey=("WI", None))
                P.flush()
            P.st = es

        if "B" in stages:
            with contextlib.ExitStack() as st:
                P.st = st
                P.npool["uin"] = 4
                wt = P.sb([128, CC, 31], F32)
                bt = P.sb([128, CC], F32)
                lg = P.sb([128, CC], F32)
                lb = P.sb([128, CC], F32)
                P.dma("sp", wt[:], dww)
                P.dma("sp", bt[:], dwb)
                P.dma("sp", lg[:], lng)
                P.dma("sp", lb[:], lnb)
                uin = P.sb([128, CC, 544], F32, "uin")
                cc_t = P.sb([128, CC, 512], F32)
                sqt = [P.sb([128, 512], F32) for _ in range(2)]
                p1 = P.ps()
                p2 = P.ps()
                mean = P.sb([128, 512], F32)
                var = P.sb([128, 512], F32)
                rstd = P.sb([128, 512], F32)
                tmp = [P.sb([128, 512], F32) for _ in range(2)]
                co = [P.sb([128, 512], BF16) for _ in range(2)]
                P.dma("sp", o_conv.rearrange("t c -> c t"), UTp[:, 128 + NP * 128 - 30:128 + NP * 128], okey=("o_conv", None), ikey="UTp",
                      allow_slow_non_contiguous=True)
                for s in range(2):
                    P.dma("sp", o_convs[s].rearrange("t c -> c t"), UTs[s][:, 32 + 64 - 30:32 + 64], okey=("o_convs", None), ikey="UTs",
                          allow_slow_non_contiguous=True)
                jobs = []
                for tb in range(max(1, NP * 128 // 512)):
                    ntk = min(512, NP * 128)
                    jobs.append(("p", tb, ntk))
                jobs += [("s", 0, 128), ("s", 1, 128)]
                for kind, tb, ntk in jobs:
                    if kind == "p":
                        c0 = 128 + tb * ntk
                        src = UTp[:, c0 - 30:c0 + ntk].rearrange("(j p) t -> p j t", p=128)
                        mcol = tb * ntk
                        sk = "UTp"
                    else:
                        src = UTs[tb][:, 2:160].rearrange("(j p) t -> p j t", p=128)
                        mcol = (NP + tb) * 128
                        sk = "UTs"
                    W = 30 + ntk
                    P.dma("sp", uin[:, :, 0:W], src, ikey=sk)
                    for j in range(CC):
                        acc = cc_t[:, j, 0:ntk]
                        P.ts("dve", acc, uin[:, j, 0:ntk], wt[:, j, 0:1], bt[:, j:j + 1], op0=ALU.mult, op1=ALU.add, okey=(cc_t, j))
                        for k in range(1, 31):
                            P.stt(acc, uin[:, j, k:k + ntk], wt[:, j, k:k + 1], acc, ALU.mult, ALU.add, okey=(cc_t, j),
                                  rkeys=[uin, (cc_t, j)])
                        s_t = sqt[j % 2]
                        P.act(s_t[:, 0:ntk], acc, AF.Square, ikey=(cc_t, j))
                        P.mm(p1[:, 0:ntk], onef[:], acc, start=(j == 0), stop=(j == CC - 1), rkeys=[onef, (cc_t, j)])
                        P.mm(p2[:, 0:ntk], onef[:], s_t[:, 0:ntk], start=(j == 0), stop=(j == CC - 1))
                    P.ts("dve", mean[:, 0:ntk], p1[:, 0:ntk], 1.0 / CCH, None, op0=ALU.mult)
                    P.tt("dve", var[:, 0:ntk], mean[:, 0:ntk], mean[:, 0:ntk], ALU.mult)
                    P.stt(var[:, 0:ntk], p2[:, 0:ntk], 1.0 / CCH, var[:, 0:ntk], ALU.mult, ALU.subtract)
                    P.ts("dve", var[:, 0:ntk], var[:, 0:ntk], EPS, None, op0=ALU.add)
                    P.act(var[:, 0:ntk], var[:, 0:ntk], AF.Sqrt)
                    P.recip(rstd[:, 0:ntk], var[:, 0:ntk])
                    for j in range(CC):
                        t_ = tmp[j % 2]
                        P.tt("dve", t_[:, 0:ntk], cc_t[:, j, 0:ntk], mean[:, 0:ntk], ALU.subtract, rkeys=[(cc_t, j), mean])
                        P.tt("dve", t_[:, 0:ntk], t_[:, 0:ntk], rstd[:, 0:ntk], ALU.mult)
                        c_o = co[j % 2]
                        P.act(c_o[:, 0:ntk], t_[:, 0:ntk], AF.Silu, scale=lg[:, j:j + 1], bias=lb[:, j:j + 1])
                        P.dma("sp", MIXT[j * 128:(j + 1) * 128, mcol:mcol + ntk], c_o[:, 0:ntk], okey=("MIXT", "conv"))
                P.flush()
            P.st = es

        if "C" in stages:
            with contextlib.ExitStack() as st:
                P.st = st
                SMAX = max(SEQ, SS)
                P.npool["kiT_c"] = 4
                P.npool["kT_c"] = 4
                P.npool["v_c"] = 4
                kiT_c = P.sb([128, SMAX], BF16, "kiT_c")
                kT_c = P.sb([128, NKV, SMAX], BF16, "kT_c")
                v_c = P.sb([128, SMAX // 128, NKV * 128], BF16, "v_c")
                kc_t = P.sb([128, SMAX], BF16)
                qch_t = P.sb([128, NT], F32)
                P.dma("sp", qch_t[:], qch)
                qiT = [P.sb([128, IH, 128], BF16) for _ in range(2)]
                qT = [P.sb([128, NH, 128], BF16) for _ in range(2)]
                wi_t = [P.sb([128, IH], F32) for _ in range(2)]
                acc2 = [P.sb([128, SMAX], F32) for _ in range(2)]
                work = P.sb([128, SMAX], F32)
                m8 = P.sb([128, 256], F32)
                thr = P.sb([128, 1], F32)
                mask = P.sb([128, SMAX], BF16)
                maskT = P.sb([128, SMAX // 128, 128], BF16)
                rl = [P.sb([128, 512], F32) for _ in range(2)]
                pe_ = [P.sb([128, GQ, 128], BF16) for _ in range(2)]
                pm = [P.sb([128, GQ, 128], BF16) for _ in range(2)]
                rz = P.sb([128, GQ * 128], F32)
                ob = [P.sb([128, GQ, 128], BF16) for _ in range(2)]
                ps_s = [P.ps() for _ in range(2)]
                ps_qk = [P.ps() for _ in range(2)]
                ps_o = P.ps()
                ps_z = P.ps()
                ptr = [P.ps([128, 1024], BF16) for _ in range(2)]

                def seglist(blocks):
                    nb = len(blocks)
                    segs = []
                    a = 0
                    while a < nb:
                        b_ = a + 1
                        while b_ < nb and b_ - a < 4 and blocks[b_] == blocks[b_ - 1] + 1:
                            b_ += 1
                        segs.append((a, b_))
                        a = b_
                    return segs

                cnt = dict(n=0, it=0)

                def idx_phase(job):
                    ti, blocks = job["ti"], job["blocks"]
                    if job.get("pre_idx"):
                        job["pre_idx"]()
                    k2 = ti % 2
                    acc = acc2[k2]
                    P.dma("sp", qiT[k2][:], QIT[ti], ikey="QIT")
                    P.dma("sp", wi_t[k2][:], WI[ti], ikey="WI")
                    segs = seglist(blocks)
                    for (a, b_) in segs:
                        P.ts("dve", acc[:, a * 128:b_ * 128], kc_t[:, blocks[a] * 128:(blocks[a] + b_ - a) * 128],
                             qch_t[:, ti:ti + 1], NEG, op0=ALU.is_gt, op1=ALU.mult)
                    for (a, b_) in segs:
                        w = (b_ - a) * 128
                        for h in range(IH):
                            n_ = cnt["n"]
                            cnt["n"] += 1
                            p_ = ps_s[n_ % 2]
                            r_ = rl[n_ % 2]
                            P.mm(p_[:, 0:w], qiT[k2][:, h, :], kiT_c[:, blocks[a] * 128:blocks[a] * 128 + w], rkeys=[qiT[k2], kiT_c])
                            P.act(r_[:, 0:w], p_[:, 0:w], AF.Relu)
                            P.stt(acc[:, a * 128:b_ * 128], r_[:, 0:w], wi_t[k2][:, h:h + 1], acc[:, a * 128:b_ * 128], ALU.mult, ALU.add)

                def topk_rounds(job, r0, r1):
                    ti, N, topk = job["ti"], len(job["blocks"]) * 128, job["topk"]
                    acc = acc2[ti % 2]
                    nr = topk // 8
                    for r in range(r0, min(r1, nr)):
                        cur = acc if r == 0 else work
                        P.max8(m8[:, r * 8:(r + 1) * 8], cur[:, 0:N])
                        if r < nr - 1:
                            P.mrep(work[:, 0:N], m8[:, r * 8:(r + 1) * 8], cur[:, 0:N], -3.0e38)

                def topk_final(job):
                    ti, blocks, topk = job["ti"], job["blocks"], job["topk"]
                    nb = len(blocks)
                    N = nb * 128
                    acc = acc2[ti % 2]
                    P.ts("dve", thr[:], m8[:, topk - 1:topk], 0.5 * NEG, None, op0=ALU.max)
                    P.ts("dve", mask[:, 0:N], acc[:, 0:N], thr[:, 0:1], None, op0=ALU.is_ge)
                    for b0 in range(0, nb, 8):
                        n8 = min(8, nb - b0)
                        pt = ptr[(b0 // 8) % 2]
                        for j in range(n8):
                            P.tr(pt[:, j * 128:(j + 1) * 128], mask[:, (b0 + j) * 128:(b0 + j + 1) * 128], idb[:])
                        P.copy("act", maskT[:, b0:b0 + n8, :], pt[:, 0:n8 * 128].rearrange("p (n t) -> p n t", t=128))

                def attn_group(job, g):
                    ti, blocks = job["ti"], job["blocks"]
                    k2 = ti % 2
                    nb = len(blocks)
                    if g == 0:
                        if job.get("pre_attn"):
                            job["pre_attn"]()
                        P.dma("sp", qT[k2][:], QT[ti], ikey="QT")
                    W = GQ * 128
                    for ci, blk in enumerate(blocks):
                        it = cnt["it"]
                        cnt["it"] += 1
                        pq_ = ps_qk[it % 2]
                        e_ = pe_[it % 2]
                        m_ = pm[it % 2]
                        P.mm(pq_[:, 0:W], kT_c[:, g, blk * 128:(blk + 1) * 128],
                             qT[k2][:, g * GQ:(g + 1) * GQ, :].rearrange("p r t -> p (r t)"), rkeys=[kT_c, qT[k2]])
                        P.act(e_[:].rearrange("p r t -> p (r t)"), pq_[:, 0:W], AF.Exp, scale=ATT_SCALE)
                        P.tt("pool", m_[:], e_[:], maskT[:, ci, :].unsqueeze(1).to_broadcast([128, GQ, 128]), ALU.mult)
                        mf = m_[:].rearrange("p r t -> p (r t)")
                        P.mm(ps_o[:, 0:W], v_c[:, blk, g * 128:(g + 1) * 128], mf, start=(ci == 0), stop=(ci == nb - 1), rkeys=[v_c, m_])
                        P.mm(ps_z[:, 0:W], oneb[:], mf, start=(ci == 0), stop=(ci == nb - 1))
                    P.recip(rz[:, 0:W], ps_z[:, 0:W])
                    o_ = ob[g % 2]
                    P.tt("dve", o_[:].rearrange("p r t -> p (r t)"), ps_o[:, 0:W], rz[:, 0:W], ALU.mult)
                    P.dma("sp", MIXT[CCH + g * GQ * 128:CCH + (g + 1) * GQ * 128, ti * 128:(ti + 1) * 128].rearrange("(r d) t -> d r t", d=128),
                          o_[:], okey=("MIXT", "attn"))

                def load_prompt_ki():
                    P.cdma(kc_t[:, 0:SEQ], kc_p[0:1, :].to_broadcast([128, SEQ]))
                    P.dma("sp", kiT_c[:, 0:SEQ], KIT, ikey="KIT", okey=(kiT_c, 0))

                def load_prompt_kv():
                    P.dma("sp", kT_c[:, :, 0:SEQ], KT, ikey="KT", okey=(kT_c, 0))
                    P.dma("sp", v_c[:, 0:NCX, :], Vc.rearrange("(c p) n -> p c n", p=128), ikey="Vc", okey=(v_c, 0))

                def mk_sample_ki(s):
                    def f():
                        P.cdma(kc_t[:, 0:SS], kc_s[0:1, :].to_broadcast([128, SS]))
                        P.cdma(kiT_c[:, 0:PAST], ckiT[s], okey=(kiT_c, 0))
                        P.dma("sp", kiT_c[:, PAST:SS], KITs[s], ikey="KITs", okey=(kiT_c, 1))
                    return f

                def mk_sample_kv(s):
                    def f():
                        for g in range(NKV):
                            P.cdma(kT_c[:, g, 0:PAST], ckT[s][:, g, :], okey=(kT_c, 0))
                        P.dma("sp", kT_c[:, :, PAST:SS], KTs[s], ikey="KTs", okey=(kT_c, 1))
                        cvv = cv[s].rearrange("(c p) n -> p c n", p=128)
                        nq = 4 if (PAST // 128) % 4 == 0 else 1
                        stp = (PAST // 128) // nq
                        for q in range(nq):
                            P.dma("pool", v_c[:, q * stp:(q + 1) * stp, :], cvv[:, q * stp:(q + 1) * stp, :], okey=(v_c, 0))
                        P.dma("sp", v_c[:, PAST // 128, :], Vs[s], ikey="Vs", okey=(v_c, 1))
                    return f

                jobs = []
                for i in range(NP):
                    jobs.append(dict(ti=i, blocks=list(range(0, i + 1)) + list(range(NP, 2 * NP)), topk=cfg["TOPK_P"]))
                jobs[0]["pre_idx"] = load_prompt_ki
                jobs[0]["pre_attn"] = load_prompt_kv
                for s in range(2):
                    jobs.append(dict(ti=NP + s, blocks=list(range(SS // 128)), topk=cfg["TOPK_S"],
                                     pre_idx=mk_sample_ki(s), pre_attn=mk_sample_kv(s)))
                idx_phase(jobs[0])
                topk_rounds(jobs[0], 0, 10 ** 6)
                topk_final(jobs[0])
                for k, job in enumerate(jobs):
                    nxt = jobs[k + 1] if k + 1 < len(jobs) else None
                    if nxt is not None:
                        idx_phase(nxt)
                        nr = nxt["topk"] // 8
                        per = -(-nr // NKV)
                    for g in range(NKV):
                        if nxt is not None:
                            topk_rounds(nxt, g * per, (g + 1) * per)
                        attn_group(job, g)
                    if nxt is not None:
                        topk_final(nxt)
                P.flush()
            P.st = es

        otiles = list(range(NT))
        ogroups = [otiles[i:i + 4] for i in range(0, NT, 4)]
        Hs = dscr("Hs", [NTOK, D])
        RC = dscr("RC", [NT, 128, 3, 128])

        def x_rows(ti):
            return xctx[ti * 128:(ti + 1) * 128, :] if ti < NP else xsp[ti - NP]

        if "D" in stages:
            with contextlib.ExitStack() as st:
                P.st = st
                mixT = P.sb([128, KC, 512], BF16)
                wbuf = [P.sb([128, KC, 512], BF16) for _ in range(2)]
                xb_ = [P.sb([128, 512], F32) for _ in range(3)]
                hb_ = [P.sb([128, 512], F32) for _ in range(3)]
                pp = [P.ps() for _ in range(4)]
                wcnt = 0
                k_ = 0
                for grp in ogroups:
                    N = len(grp) * 128
                    c0 = grp[0] * 128
                    P.dma("sp", mixT[:, :, 0:N], MIXT[:, c0:c0 + N].rearrange("(c p) n -> p c n", p=128), ikey="MIXT")
                    for b in range(D // 512):
                        wb = wbuf[wcnt % 2]
                        wcnt += 1
                        load_w(wb, w_out[:, b * 512:(b + 1) * 512], 512)
                        wk = wkeys(wb)
                        for tt, ti in enumerate(grp):
                            xb = xb_[k_ % 3]
                            hb = hb_[k_ % 3]
                            p_ = pp[k_ % 4]
                            k_ += 1
                            P.dma("sp", xb[:], x_rows(ti)[:, b * 512:(b + 1) * 512])
                            tok_mm(p_[:, 0:512], mixT, tt * 128, wb, 512, wk)
                            P.tt("dve", hb[:], p_[:, 0:512], xb[:], ALU.add)
                            P.dma("sp", Hs[ti * 128:(ti + 1) * 128, b * 512:(b + 1) * 512], hb[:], okey=("Hs", None))
                P.flush()
            P.st = es

            with contextlib.ExitStack() as st:
                P.st = st
                nt = NormT(g_memn)
                gbc2 = P.sb([128, D], F32)
                bcast_row(gbc2[:], g_ffn[0:1, :], D)
                gmq = P.sb([128, 128], F32)
                bcast_row(gmq[:], g_mq[0:1, :], 128)
                wqm_b = P.sb([128, KC, 512], BF16)
                load_w(wqm_b, w_qm, 512)
                wom_b = P.sb([128, 4, D], BF16)
                P.cdma(wom_b[:], w_om.rearrange("(h p) d -> p h d", p=128))
                mkT_c = P.sb([128, 4, MEMT], BF16)
                mv_c = P.sb([128, MC, 512], BF16)
                ht = [P.sb([128, D], F32) for _ in range(2)]
                hn = [P.sb([128, KC, 128], BF16) for _ in range(2)]
                pq_ = P.ps()
                pl_ = [P.ps() for _ in range(2)]
                po_ = P.ps()
                pz_ = pq_
                pw_ = [P.ps() for _ in range(1)]
                ptr = P.ps([128, 1024], BF16)
                sq_t = P.sb([128, 512], F32)
                sm_t = P.sb([128, 12], F32)
                qmf = P.sb([128, 4, 128], F32)
                qmb = P.sb([128, 512], BF16)
                qmT = P.sb([128, 4, 128], BF16)
                pmT = [P.sb([128, 4, 128], BF16) for _ in range(MC)]
                rz = P.sb([128, 512], F32)
                omT = P.sb([128, 4, 128], BF16)
                for ti in otiles:
                    if ti == 0:
                        P.dma("sp", mkT_c[:], MKT, ikey="MKT")
                        P.dma("sp", mv_c[:], MV.rearrange("(c p) n -> p c n", p=128), ikey="MV")
                    elif ti >= NP:
                        P.dma("pool", mkT_c[:], cmkT[ti - NP])
                        P.dma("pool", mv_c[:], cmv[ti - NP].rearrange("(c p) n -> p c n", p=128))
                    h_t = ht[ti % 2]
                    hn_t = hn[ti % 2]
                    P.dma("sp", h_t[:], Hs[ti * 128:(ti + 1) * 128, :], ikey="Hs")
                    nt.run(h_t[:], lambda c0, n, hn_t=hn_t: hn_t[:, c0:c0 + n, :])
                    tok_mm(pq_[:, 0:512], hn_t, 0, wqm_b, 512, wkeys(wqm_b))
                    head_norm(pq_[:, 0:512], 4, gmq, qmf[:], sq_t, sm_t)
                    P.copy("act", qmb[:], qmf[:].rearrange("p h d -> p (h d)"))
                    for h in range(4):
                        P.tr(ptr[:, h * 128:(h + 1) * 128], qmb[:, h * 128:(h + 1) * 128], idb[:])
                    P.copy("dve", qmT[:], ptr[:, 0:512].rearrange("p (h t) -> p h t", t=128))
                    for mc in range(MC):
                        for h in range(4):
                            P.mm(pl_[mc % 2][:, h * 128:(h + 1) * 128], mkT_c[:, h, mc * 128:(mc + 1) * 128], qmT[:, h, :])
                        P.act(pmT[mc][:].rearrange("p h t -> p (h t)"), pl_[mc % 2][:, 0:512], AF.Exp, scale=ATT_SCALE)
                    for h in range(4):
                        for mc in range(MC):
                            P.mm(po_[:, h * 128:(h + 1) * 128], mv_c[:, mc, h * 128:(h + 1) * 128], pmT[mc][:, h, :],
                                 start=(mc == 0), stop=(mc == MC - 1))
                    for mc in range(MC):
                        P.mm(pz_[:, 0:512], oneb[:], pmT[mc][:].rearrange("p h t -> p (h t)"), start=(mc == 0), stop=(mc == MC - 1))
                    P.recip(rz[:], pz_[:, 0:512])
                    P.tt("dve", omT[:].rearrange("p h t -> p (h t)"), po_[:, 0:512], rz[:], ALU.mult)
                    for b in range(D // 512):
                        p_ = pw_[0]
                        for h in range(4):
                            P.mm(p_[:, 0:512], omT[:, h, :], wom_b[:, h, b * 512:(b + 1) * 512], start=(h == 0), stop=(h == 3))
                        P.tt("dve", h_t[:, b * 512:(b + 1) * 512], p_[:, 0:512], h_t[:, b * 512:(b + 1) * 512], ALU.add)
                    P.dma("sp", H2[ti * 128:(ti + 1) * 128, :], h_t[:], okey=("H2", None))
                    nt.gbc, g_save = gbc2, nt.gbc
                    nt.run(h_t[:], lambda c0, n, hn_t=hn_t: hn_t[:, c0:c0 + n, :])
                    nt.gbc = g_save
                    P.dma("sp", HN2T[:, ti * 128:(ti + 1) * 128].rearrange("(c p) t -> p c t", p=128), hn_t[:], okey=("HN2T", None))
                P.flush()
            P.st = es

            with contextlib.ExitStack() as st:
                P.st = st
                hn2 = P.sb([128, KC, 512], BF16)
                wbuf = [P.sb([128, KC, 512], BF16) for _ in range(2)]
                qpT = P.sb([128, 16, 512], F32)
                sk_t = P.sb([128, 16, 128], F32)
                P.dma("sp", sk_t[:], subk)
                pq_ = [P.ps() for _ in range(2)]
                ps_ = [P.ps() for _ in range(2)]
                ptf = P.ps()
                s12 = [P.sb([128, 16, 128], F32) for _ in range(2)]
                v16 = P.sb([128, 16, 16], F32)
                tmp128 = P.sb([128, 128], F32)
                cand = P.sb([128, 8, 256], F32)
                tmpc = P.sb([128, 256], F32)
                t16 = P.sb([128, 8, 16], F32)
                e16 = P.sb([128, 8, 16], F32)
                zz = P.sb([128, 8], F32)
                mlz = P.sb([128, 8], F32)
                rc3 = P.sb([128, 3, 8, 16], F32)
                rcT = [P.sb([128, 3, 128], F32) for _ in range(2)]
                wcnt = 0
                for grp in ogroups:
                    N = len(grp) * 128
                    c0 = grp[0] * 128
                    P.dma("sp", hn2[:, :, 0:N], HN2T[:, c0:c0 + N].rearrange("(c p) n -> p c n", p=128), ikey="HN2T")
                    for b in range(4):
                        wb = wbuf[wcnt % 2]
                        wcnt += 1
                        load_w(wb, w_pq[:, b * 512:(b + 1) * 512], 512)
                        wk = wkeys(wb)
                        for jj in range(4):
                            j = b * 4 + jj
                            p_ = pq_[j % 2]
                            for c in range(KC):
                                P.mm(p_[:, 0:N], wb[:, c, jj * 128:(jj + 1) * 128], hn2[:, c, 0:N], start=(c == 0), stop=(c == KC - 1),
                                     rkeys=[hn2] + wk)
                            P.copy("act", qpT[:, j, 0:N], p_[:, 0:N], okey=(qpT, j))
                    for tt, ti in enumerate(grp):
                        s_t = s12[ti % 2]
                        for jb in range(4):
                            p_ = ps_[jb % 2]
                            for jj in range(4):
                                j = jb * 4 + jj
                                P.mm(p_[:, jj * 128:(jj + 1) * 128], qpT[:, j, tt * 128:(tt + 1) * 128], sk_t[:, j, :], rkeys=[(qpT, j), sk_t])
                            P.copy("act", s_t[:, jb * 4:(jb + 1) * 4, :].rearrange("p j k -> p (j k)"), p_[:, 0:512])
                        P.dma("sp", S12[ti * 128:(ti + 1) * 128], s_t[:], okey=("S12", None))
                        for j in range(16):
                            P.max8(v16[:, j, 0:8], s_t[:, j, :])
                            P.mrep(tmp128[:], v16[:, j, 0:8], s_t[:, j, :], -3.0e38)
                            P.max8(v16[:, j, 8:16], tmp128[:])
                        v16v = v16[:].rearrange("p (h two) k -> p h two k", two=2)
                        for h in range(8):
                            P.tt("dve", cand[:, h, :].rearrange("p (a b) -> p a b", b=16),
                                 v16[:, 2 * h, :].unsqueeze(2).to_broadcast([128, 16, 16]),
                                 v16[:, 2 * h + 1, :].unsqueeze(1).to_broadcast([128, 16, 16]), ALU.add)
                        for h in range(8):
                            P.max8(t16[:, h, 0:8], cand[:, h, :])
                            P.mrep(tmpc[:], t16[:, h, 0:8], cand[:, h, :], -3.0e38)
                            P.max8(t16[:, h, 8:16], tmpc[:])
                        P.tt("dve", e16[:], t16[:], t16[:, :, 0:1].to_broadcast([128, 8, 16]), ALU.subtract)
                        P.act(e16[:], e16[:], AF.Exp)
                        P.reduce(zz[:], e16[:], ALU.add)
                        P.act(mlz[:], zz[:], AF.Ln)
                        P.tt("dve", mlz[:], mlz[:], t16[:, :, 0], ALU.add)
                        P.copy("dve", rc3[:, 0, :, :], v16v[:, :, 0, :])
                        P.tt("dve", rc3[:, 1, :, :], t16[:, :, 15:16].to_broadcast([128, 8, 16]), rc3[:, 0, :, :], ALU.subtract)
                        P.tt("dve", rc3[:, 2, :, :], rc3[:, 0, :, :], mlz[:].unsqueeze(2).to_broadcast([128, 8, 16]), ALU.subtract)
                        for q in range(3):
                            P.tr(ptf[:, q * 128:(q + 1) * 128], rc3[:, q, :, :].rearrange("p h a -> p (h a)"), idf[:])
                        r_T = rcT[ti % 2]
                        P.copy("act", r_T[:].rearrange("p q t -> p (q t)"), ptf[:, 0:384])
                        P.dma("sp", RC[ti], r_T[:], okey=("RC", None))
                P.flush()
            P.st = es

            with contextlib.ExitStack() as st:
                P.st = st
                TB = 32
                s1r = [P.sb([128, TB, 128], F32) for _ in range(2)]
                s2r = [P.sb([128, TB, 128], F32) for _ in range(2)]
                rct = [P.sb([128, 3, 128], F32) for _ in range(2)]
                o1 = [P.sb([128, 128], BF16) for _ in range(4)]
                ee = [P.sb([128, 128], F32) for _ in range(4)]
                rr = [P.sb([128, 128], BF16) for _ in range(4)]
                gst = [P.sb([128, 128, 128], BF16) for _ in range(2)]
                pg_ = [P.ps() for _ in range(2)]
                kk = 0
                for ti in otiles:
                    rc_ = rct[ti % 2]
                    g_s = gst[ti % 2]
                    P.dma("sp", rc_[:], RC[ti], ikey="RC")
                    for tb in range(128 // TB):
                        t0 = ti * 128 + tb * TB
                        a1 = s1r[tb % 2]
                        a2 = s2r[tb % 2]
                        for half, dst in ((0, a1), (1, a2)):
                            for h in range(8):
                                src = S12[t0:t0 + TB, 2 * h + half, :]
                                P.dma("sp", dst[h * 16:(h + 1) * 16, :, :], src.unsqueeze(0).to_broadcast([16, TB, 128]), ikey="S12", okey=(dst, None))
                        for tq in range(0, TB, 4):
                            p_ = pg_[(kk) % 2]
                            kk += 1
                            for u4 in range(4):
                                tl = tq + u4
                                t = tb * TB + tl
                                o_ = o1[u4]
                                e_ = ee[u4]
                                r_ = rr[u4]
                                P.ts("dve", o_[:], a1[:, tl, :], rc_[:, 0, t:t + 1], None, op0=ALU.is_equal)
                                P.act(e_[:], a2[:, tl, :], AF.Exp, bias=rc_[:, 2, t:t + 1])
                                P.stt(r_[:], a2[:, tl, :], rc_[:, 1, t:t + 1], e_[:], ALU.is_ge, ALU.mult)
                                P.mm(p_[:, u4 * 128:(u4 + 1) * 128], o_[:], r_[:])
                            tbase = tb * TB + tq
                            P.copy("act", g_s[:, :, tbase:tbase + 4].rearrange("p i t -> p t i"),
                                   p_[:, 0:512].rearrange("p (t i) -> p t i", i=128))
                    P.dma("sp", GALL[:, :, ti * 128:(ti + 1) * 128], g_s[:], okey=("GALL", None))
                P.flush()
            P.st = es

        if "E" in stages:
            with contextlib.ExitStack() as st:
                P.st = st
                NCH = PEER_KEYS
                EB = 4
                hn2 = P.sb([128, KC, 512], BF16)
                oacc = P.sb([128, 4, D], F32)
                ub = [P.sb([128, KC, 128], BF16) for _ in range(3)]
                vb = [P.sb([128, EB, D], BF16) for _ in range(2)]
                coef = [P.sb([128, EB, 512], BF16) for _ in range(2)]
                gl = [P.sb([128, 512], BF16) for _ in range(2)]
                gc = [P.sb([128, 512], BF16) for _ in range(2)]
                pa_ = [P.ps() for _ in range(2)]
                pv_ = [P.ps() for _ in range(4)]
                ucnt = 0
                vcnt = 0
                pcnt = 0
                DH = 2048 if D % 2048 == 0 else D
                for grp in ogroups:
                    ng = len(grp)
                    N = ng * 128
                    c0 = grp[0] * 128
                    P.dma("sp", hn2[:, :, 0:N], HN2T[:, c0:c0 + N].rearrange("(c p) n -> p c n", p=128), ikey="HN2T")
                    for tt, ti in enumerate(grp):
                        P.dma("sp", oacc[:, tt, :], H2[ti * 128:(ti + 1) * 128, :], ikey="H2", okey=(oacc, tt))
                    def v_load(eb):
                        v_b = vb[eb % 2]
                        vsrc = vtab[eb * EB * 128:(eb + 1) * EB * 128, :].rearrange("(cc p) (x d) -> p cc x d", p=128, d=DH)
                        P.dma("pool", v_b[:].rearrange("p cc (x d) -> p cc x d", d=DH), vsrc)

                    def u_phase(eb):
                        nonlocal ucnt
                        cf = coef[eb % 2]
                        for cc in range(EB):
                            c = eb * EB + cc
                            u_b = ub[ucnt % 3]
                            g_l = gl[ucnt % 2]
                            g_c = gc[ucnt % 2]
                            p_ = pa_[ucnt % 2]
                            ucnt += 1
                            P.dma("pool", u_b[:].rearrange("p c e -> p (c e)").rearrange("p (x d) -> p x d", d=min(2048, KC * 128)),
                                  uT[c].rearrange("p (x d) -> p x d", d=min(2048, KC * 128)))
                            P.dma("sp", g_c[:, 0:N], GALL[c][:, c0:c0 + N], ikey="GALL")
                            for dc in range(KC):
                                P.mm(p_[:, 0:N], u_b[:, dc, :], hn2[:, dc, 0:N], start=(dc == 0), stop=(dc == KC - 1))
                            P.act(g_l[:, 0:N], p_[:, 0:N], AF.Gelu)
                            P.tt("dve", cf[:, cc, 0:N], g_l[:, 0:N], g_c[:, 0:N], ALU.mult, okey=(cf, cc))

                    def v_phase(eb):
                        nonlocal pcnt
                        cf = coef[eb % 2]
                        v_b = vb[eb % 2]
                        for tt in range(ng):
                            for db in range(D // 512):
                                pv = pv_[pcnt % 4]
                                pcnt += 1
                                for cc in range(EB):
                                    P.mm(pv[:, 0:512], cf[:, cc, tt * 128:(tt + 1) * 128], v_b[:, cc, db * 512:(db + 1) * 512],
                                         start=(cc == 0), stop=(cc == EB - 1), rkeys=[(cf, cc), v_b])
                                P.tt("dve", oacc[:, tt, db * 512:(db + 1) * 512], pv[:, 0:512], oacc[:, tt, db * 512:(db + 1) * 512], ALU.add,
                                     okey=(oacc, tt), rkeys=[pv, (oacc, tt)])

                    nE = NCH // EB
                    v_load(0)
                    u_phase(0)
                    for eb in range(nE):
                        if eb + 1 < nE:
                            v_load(eb + 1)
                            u_phase(eb + 1)
                        v_phase(eb)
                    for tt, ti in enumerate(grp):
                        P.dma("sp", y[ti * 128:(ti + 1) * 128, :], oacc[:, tt, :], ikey=(oacc, tt), okey=("y", None))
                P.flush()
            P.st = es

        if dbg:
            for nm, ap_ in (("MIXT", MIXT), ("UTp", UTp), ("UTs", UTs), ("QT", QT), ("QIT", QIT), ("WI", WI), ("KT", KT), ("KIT", KIT),
                            ("Vc", Vc), ("H2", H2), ("HN2T", HN2T), ("S12", S12), ("GALL", GALL), ("MKT", MKT), ("MV", MV)):
                if nm in dbg:
                    o_ = dout("dbg_" + nm, list(ap_.shape), ap_.dtype)
                    P.dma("sp", o_, ap_)
        P.flush()
    return nc


def _rope_table(pos):
    half = 16
    inv_freq = np.power(np.float32(ROPE_THETA), -np.arange(half, dtype=np.float32) / np.float32(half)).astype(np.float32)
    ang = pos.astype(np.float32)[:, None] * inv_freq[None, :]
    return np.concatenate([np.cos(ang), np.sin(ang)], axis=1).astype(np.float32)


def host_prep(inp, cfg):
    D, KC, CCH, CC, NH, NKV, NP, NT, IH, SEQ, PAST, SS, MEMT = (cfg[k] for k in (
        "D", "KC", "CCH", "CC", "NH", "NKV", "NP", "NT", "IH", "SEQ", "PAST", "SS", "MEMT"))
    DS = cfg["DS"]
    f = lambda a: np.ascontiguousarray(a, dtype=np.float32)
    half = SEQ // 2
    w_in = inp["w_in"][0]
    OFF_Q = 2 * CCH
    OFF_K = OFF_Q + NH * 128
    OFF_V = OFF_K + NKV * 128
    OFF_QI = OFF_V + NKV * 128
    OFF_KI = OFF_QI + IH * 128
    OFF_WI = OFF_KI + 128
    a_ = w_in[:, :CCH].reshape(D, CC, 128)
    g_ = w_in[:, CCH:2 * CCH].reshape(D, CC, 128)
    w_glu = f(np.stack([a_, g_], axis=2).reshape(D, 2 * CCH))
    shared = dict(
        w_glu=w_glu,
        w_q=f(w_in[:, OFF_Q:OFF_K]),
        w_qi=f(w_in[:, OFF_QI:OFF_KI]),
        w_wi=f(w_in[:, OFF_WI:OFF_WI + IH]),
        w_kv=f(np.concatenate([w_in[:, OFF_K:OFF_V], w_in[:, OFF_V:OFF_QI], w_in[:, OFF_KI:OFF_WI]], axis=1)),
        w_out=f(inp["w_out"][0]),
        w_qm=f(inp["w_q_mem"][0]), w_km=f(inp["w_k_mem"][0]), w_vm=f(inp["w_v_mem"][0]), w_om=f(inp["w_o_mem"][0]),
        w_pq=f(inp["peer_wq"][0]),
        g_mix=f(inp["norm_mix_g"]), g_memn=f(inp["norm_mem_g"]), g_ffn=f(inp["norm_ffn_g"]), g_mem=f(inp["mem_norm_g"]),
        g_q=f(inp["q_norm_g"]), g_k=f(inp["k_norm_g"]), g_mq=f(inp["mem_q_norm_g"]), g_mk=f(inp["mem_k_norm_g"]),
        dww=f(inp["dw_w"][0].reshape(31, CC, 128).transpose(2, 1, 0)),
        dwb=f(inp["dw_b"][0].reshape(CC, 128).T), lng=f(inp["conv_ln_g"][0].reshape(CC, 128).T),
        lnb=f(inp["conv_ln_b"][0].reshape(CC, 128).T),
        vtab=f(inp["peer_v"][0]),
        c_idb=np.eye(128).astype(ml_dtypes.bfloat16), c_idf=np.eye(128, dtype=np.float32),
        c_oneb=np.ones((128, 128)).astype(ml_dtypes.bfloat16), c_onef=np.ones((128, 128), dtype=np.float32),
    )
    sk = np.stack([inp["peer_sub_k1"][0], inp["peer_sub_k2"][0]], axis=1)
    shared["subk"] = f(sk.reshape(16, 128, 128).transpose(2, 0, 1))
    u = inp["peer_u"][0]
    shared["uT"] = f(u.reshape(128, 128, KC, 128).transpose(0, 3, 2, 1).reshape(128, 128, KC * 128))
    kcs = (np.arange(SS) // 64).astype(np.float32)
    kcs[PAST + DS:] = 1.0e9
    shared["kc_s"] = kcs[None, :]
    shared["rope_s"] = _rope_table(PAST + np.arange(128))
    maps = []
    for c in range(8):
        b, hf = c // 2, c % 2
        xb = inp["x_prompt"][b]
        own = xb[hf * half:(hf + 1) * half]
        oth = xb[(1 - hf) * half:(2 - hf) * half]
        pos = np.concatenate([hf * half + np.arange(half), (1 - hf) * half + np.arange(half)])
        m = dict(shared)
        m["xctx"] = f(np.concatenate([own, oth], axis=0))
        m["xhalo"] = f(xb[half - 128:half]) if hf == 1 else np.zeros((128, D), np.float32)
        xsp = np.zeros((2, 128, D), np.float32)
        for s in range(2):
            xsp[s, :DS] = inp["x_sample"][2 * c + s]
        m["xsp"] = xsp
        m["mem"] = f(inp["mem_prompt"][b])
        m["ckT"] = f(np.stack([inp["cache_k"][0, 2 * c + s].transpose(2, 1, 0) for s in range(2)]))
        m["cv"] = f(np.stack([inp["cache_v"][0, 2 * c + s].reshape(PAST, NKV * 128) for s in range(2)]))
        m["ckiT"] = f(np.stack([inp["cache_k_idx"][0, 2 * c + s].T for s in range(2)]))
        m["stT"] = f(np.stack([inp["state_conv"][0, 2 * c + s].T for s in range(2)]))
        m["cmkT"] = f(np.stack([inp["cache_mem_k"][0, 2 * c + s].transpose(2, 1, 0) for s in range(2)]))
        m["cmv"] = f(np.stack([inp["cache_mem_v"][0, 2 * c + s].reshape(MEMT, 512) for s in range(2)]))
        m["rope_c"] = _rope_table(pos)
        m["kc_p"] = (pos // 64).astype(np.float32)[None, :]
        q = np.zeros((128, NT), np.float32)
        for i in range(NP):
            q[:, i] = (hf * half + i * 128 + np.arange(128)) // 64
        q[:, NP:] = PAST // 64
        m["qch"] = q
        maps.append(m)
    return maps


def assemble(res, cfg):
    D, CCH, NKV, NP, SEQ, DS, MEMT, B, DB = (cfg[k] for k in ("D", "CCH", "NKV", "NP", "SEQ", "DS", "MEMT", "B", "DB"))
    half = SEQ // 2
    y_p = np.zeros((B, SEQ, D), np.float32)
    y_s = np.zeros((DB, DS, D), np.float32)
    k_p = np.zeros((1, B, SEQ, NKV, 128), np.float32)
    v_p = np.zeros_like(k_p)
    ki_p = np.zeros((1, B, SEQ, 128), np.float32)
    conv_p = np.zeros((1, B, 30, CCH), np.float32)
    mk_p = np.zeros((1, B, MEMT, 4, 128), np.float32)
    mv_p = np.zeros_like(mk_p)
    k_s = np.zeros((1, DB, DS, NKV, 128), np.float32)
    v_s = np.zeros_like(k_s)
    ki_s = np.zeros((1, DB, DS, 128), np.float32)
    conv_s = np.zeros((1, DB, 30, CCH), np.float32)
    for c in range(8):
        r = res[c]
        b, hf = c // 2, c % 2
        y_p[b, hf * half:(hf + 1) * half] = r["y"][:NP * 128]
        if hf == 0:
            k_p[0, b] = r["o_k"].reshape(SEQ, NKV, 128)
            v_p[0, b] = r["o_v"].reshape(SEQ, NKV, 128)
            ki_p[0, b] = r["o_ki"]
            mk_p[0, b] = r["o_mk"].reshape(MEMT, 4, 128)
            mv_p[0, b] = r["o_mv"].reshape(MEMT, 4, 128)
        else:
            conv_p[0, b] = r["o_conv"]
        for s in range(2):
            q = 2 * c + s
            y_s[q] = r["y"][(NP + s) * 128:(NP + s) * 128 + DS]
            k_s[0, q] = r["o_ks"][s, :DS].reshape(DS, NKV, 128)
            v_s[0, q] = r["o_vs"][s, :DS].reshape(DS, NKV, 128)
            ki_s[0, q] = r["o_kis"][s, :DS]
            conv_s[0, q] = r["o_convs"][s]
    return (y_p, y_s, k_p, v_p, ki_p, conv_p, mk_p, mv_p, k_s, v_s, ki_s, conv_s)


def kernel(**inputs):
    cfg = mkcfg()
    inp = {k: np.asarray(v) for k, v in inputs.items()}
    maps = host_prep(inp, cfg)
    nc = build(cfg)
    res = run_bass_kernel_spmd(nc, maps, core_ids=list(range(8)))
    return assemble(res.results, cfg)
```

```python
import contextlib
import math
import numpy as np
import ml_dtypes
import concourse.bass as bass
import concourse.mybir as mybir
from concourse.bass_utils import run_bass_kernel_spmd

F32 = mybir.dt.float32
BF16 = mybir.dt.bfloat16
ALU = mybir.AluOpType
AF = mybir.ActivationFunctionType
AX = mybir.AxisListType

EPS = 1e-6
ROPE_THETA = 500000.0
NEG = -1.0e30


class Prog:
    def __init__(self, nc, es):
        self.nc = nc
        self.es = es
        self.st = es
        self.ops = []
        self.engs = {"pe": nc.tensor, "act": nc.scalar, "dve": nc.vector, "pool": nc.gpsimd, "sp": nc.sync}
        self.n_t = 0
        self.eng_sem = {}
        self.eng_cnt = {}
        self.pool = {}
        self.npool = {}
        self.key_sem = {}
        self.fence_sem = None
        self.fence_cnt = 0
        self.tot_ops = 0
        self.tot_wait = 0
        self.free_sems = []
        self.n_dsem = 0

    def sb(self, shape, dt=F32, name=None):
        self.n_t += 1
        return self.st.enter_context(self.nc.sbuf_tensor(name or f"sb{self.n_t}", list(shape), dt))

    def ps(self, shape=(128, 512), dt=F32, name=None):
        self.n_t += 1
        return self.st.enter_context(self.nc.psum_tensor(name or f"ps{self.n_t}", list(shape), dt))

    @staticmethod
    def key(x):
        def nm(a):
            if isinstance(a, str):
                return a
            t = getattr(a, "tensor", None)
            return t.name if t is not None else a.name
        if isinstance(x, tuple):
            return (nm(x[0]), x[1])
        return (nm(x), None)

    def op(self, eng, fn, reads=(), writes=(), dma=False):
        rk = []
        for r in reads:
            if r is None or isinstance(r, (int, float)):
                continue
            k = self.key(r)
            if k not in rk:
                rk.append(k)
        wk = []
        for w in writes:
            k = self.key(w)
            if k not in wk:
                wk.append(k)
        self.ops.append(dict(eng=eng, fn=fn, reads=rk, writes=wk, dma=dma))

    def _esem(self, e):
        if e not in self.eng_sem:
            self.eng_sem[e] = self.es.enter_context(self.nc.semaphore(f"s_{e}"))
            self.eng_cnt[e] = 0
        return self.eng_sem[e]

    def flush(self):
        nc = self.nc
        ops = self.ops
        state = {}
        deps = [None] * len(ops)

        def confl(k):
            ent = state.get(k[0])
            if not ent:
                return []
            if k[1] is None:
                return list(ent.values())
            return [ent[s_] for s_ in (k[1], None) if s_ in ent]

        joined = [False] * len(ops)
        for i, o in enumerate(ops):
            d = set()
            for k in o["reads"]:
                for st in confl(k):
                    d.update(st[0])
            joins = {}
            for k in o["writes"]:
                own = state.get(k[0], {}).get(k[1])
                joinable = bool(o["dma"] and own and own[0] and all(ops[j]["dma"] for j in own[0]) and not own[1])
                joins[k] = joinable
                for st in confl(k):
                    d.update(st[1])
                    if not (joinable and st is own):
                        d.update(st[0])
            if o["dma"]:
                joined[i] = joins[o["writes"][0]]
            for k in o["reads"]:
                st = state.setdefault(k[0], {}).setdefault(k[1], [[], []])
                st[1].append(i)
            for k in o["writes"]:
                ent = state.setdefault(k[0], {})
                if joins[k]:
                    ent[k[1]][0].append(i)
                else:
                    if k[1] is None:
                        ent.clear()
                    ent[k[1]] = [[i], []]
            d.discard(i)
            if o["eng"] == "pe":
                d = {j for j in d if not (ops[j]["eng"] == "pe" and not ops[j]["dma"])}
            deps[i] = d
        need = [False] * len(ops)
        for d in deps:
            for j in d:
                need[j] = True
        last_on = {}
        for i, o in enumerate(ops):
            if not o["dma"]:
                last_on[o["eng"]] = i
        for i in last_on.values():
            need[i] = True

        sig = [None] * len(ops)
        waited = {}
        for i, o in enumerate(ops):
            e = o["eng"]
            eo = self.engs[e]
            wl = {}
            for j in deps[i]:
                s, v = sig[j]
                kk = id(s)
                if kk not in wl or wl[kk][1] < v:
                    wl[kk] = (s, v)
            pre = None
            if o["dma"]:
                k = o["writes"][0]
                name = k[0]
                pl = self.pool.get(name)
                if pl is None:
                    n = self.npool.get(name, 2)
                    sems_, cnt_ = [], []
                    for q in range(n):
                        if self.free_sems:
                            s_, c_ = self.free_sems.pop()
                        else:
                            self.n_dsem += 1
                            s_, c_ = self.es.enter_context(nc.semaphore(f"dma{self.n_dsem}")), 0
                        sems_.append(s_)
                        cnt_.append(c_)
                    pl = dict(sems=sems_, cnt=cnt_, last=[None] * n, rr=0)
                    self.pool[name] = pl
                idx = None
                if joined[i] and k in self.key_sem and pl["last"][self.key_sem[k]] == k:
                    idx = self.key_sem[k]
                else:
                    idx = pl["rr"]
                    pl["rr"] = (pl["rr"] + 1) % len(pl["sems"])
                    if pl["cnt"][idx] > 0:
                        s = pl["sems"][idx]
                        kk = id(s)
                        if kk not in wl or wl[kk][1] < pl["cnt"][idx]:
                            wl[kk] = (s, pl["cnt"][idx])
                self.key_sem[k] = idx
                pl["last"][idx] = k
                pre = (pl, idx)
            for kk, (s, v) in wl.items():
                if waited.get((e, kk), -1) >= v:
                    continue
                waited[(e, kk)] = v
                eo.wait_ge(s, v)
                self.tot_wait += 1
            ins = o["fn"](eo)
            if o["dma"]:
                pl, idx = pre
                pl["cnt"][idx] += 16
                ins.then_inc(pl["sems"][idx], 16)
                sig[i] = (pl["sems"][idx], pl["cnt"][idx])
            elif need[i]:
                s = self._esem(e)
                self.eng_cnt[e] += 1
                ins.then_inc(s, 1)
                sig[i] = (s, self.eng_cnt[e])
        self.tot_ops += len(ops)
        self.ops = []
        if self.fence_sem is None:
            self.fence_sem = self.es.enter_context(nc.semaphore("fence"))
        for e, s in self.eng_sem.items():
            if self.eng_cnt[e] > 0:
                nc.sync.wait_ge(s, self.eng_cnt[e])
        for pl in self.pool.values():
            for s, c in zip(pl["sems"], pl["cnt"]):
                if c > 0:
                    nc.sync.wait_ge(s, c)
        for pl in self.pool.values():
            for s, c in zip(pl["sems"], pl["cnt"]):
                self.free_sems.append((s, c))
        self.pool = {}
        self.key_sem = {}
        self.fence_cnt += 1
        nc.sync.drain().then_inc(self.fence_sem, 1)
        for e in ("pe", "act", "dve", "pool"):
            self.engs[e].wait_ge(self.fence_sem, self.fence_cnt)

    def dma(self, q, out, in_, okey=None, ikey=None, **kw):
        self.op(q, lambda e: e.dma_start(out=out, in_=in_, **kw), reads=[ikey or in_], writes=[okey or out], dma=True)

    def cdma(self, out, in_, okey=None, ikey=None):
        n = out.shape[-1]
        if n > 2048:
            d = 2048
            while n % d:
                d //= 2
            names = " ".join(f"a{i}" for i in range(len(out.shape) - 1))
            pat = f"{names} (x d) -> {names} x d"
            self.dma("pool", out.rearrange(pat, d=d), in_.rearrange(pat, d=d), okey=okey or out, ikey=ikey or in_)
        else:
            self.dma("pool", out, in_, okey=okey, ikey=ikey)

    def mm(self, out, lhsT, rhs, start=True, stop=True, okey=None, rkeys=None):
        self.op("pe", lambda e: e.matmul(out, lhsT, rhs, start=start, stop=stop), reads=rkeys or [lhsT, rhs], writes=[okey or out])

    def tr(self, out, in_, ident, okey=None, ikey=None):
        self.op("pe", lambda e: e.transpose(out, in_, ident), reads=[ikey or in_, ident], writes=[okey or out])

    def act(self, out, in_, func, scale=1.0, bias=0.0, accum_out=None, okey=None, ikey=None):
        rd = [ikey or in_] + [x for x in (scale, bias) if not isinstance(x, (int, float))]
        wr = [okey or out] + ([accum_out] if accum_out is not None else [])
        if accum_out is not None:
            self.op("act", lambda e: e.activation(out, in_, func, bias=bias, scale=scale, accum_out=accum_out), reads=rd, writes=wr)
        else:
            self.op("act", lambda e: e.activation(out, in_, func, bias=bias, scale=scale), reads=rd, writes=wr)

    def ts(self, eng, out, in0, s1, s2=None, op0=ALU.mult, op1=None, accum_out=None, okey=None, ikey=None):
        rd = [ikey or in0] + [x for x in (s1, s2) if x is not None and not isinstance(x, (int, float))]
        wr = [okey or out] + ([accum_out] if accum_out is not None else [])
        kw = {}
        if op1 is not None:
            kw["op1"] = op1
        if accum_out is not None:
            kw["accum_out"] = accum_out
        self.op(eng, lambda e: e.tensor_scalar(out, in0, s1, s2, op0, **kw), reads=rd, writes=wr)

    def tt(self, eng, out, in0, in1, op, okey=None, rkeys=None):
        self.op(eng, lambda e: e.tensor_tensor(out, in0, in1, op), reads=rkeys or [in0, in1], writes=[okey or out])

    def stt(self, out, in0, scalar, in1, op0, op1, okey=None, rkeys=None):
        rd = list(rkeys or [in0, in1]) + ([scalar] if not isinstance(scalar, (int, float)) else [])
        self.op("dve", lambda e: e.scalar_tensor_tensor(out, in0, scalar, in1, op0, op1), reads=rd, writes=[okey or out])

    def copy(self, eng, out, in_, okey=None, ikey=None):
        if eng == "act":
            self.op("act", lambda e: e.copy(out, in_), reads=[ikey or in_], writes=[okey or out])
        else:
            self.op(eng, lambda e: e.tensor_copy(out, in_), reads=[ikey or in_], writes=[okey or out])

    def max8(self, out, in_, okey=None):
        self.op("dve", lambda e: e.max(out, in_), reads=[in_], writes=[okey or out])

    def mrep(self, out, in_to_replace, in_values, imm, rkeys=None):
        self.op("dve", lambda e: e.match_replace(out, in_to_replace, in_values, imm), reads=rkeys or [in_to_replace, in_values], writes=[out])

    def memset(self, eng, ap, val):
        self.op(eng, lambda e: e.memset(ap, val), reads=[], writes=[ap])

    def recip(self, out, in_, okey=None):
        self.op("dve", lambda e: e.reciprocal(out, in_), reads=[in_], writes=[okey or out])

    def reduce(self, out, in_, op, axis=AX.X):
        self.op("dve", lambda e: e.tensor_reduce(out, in_, axis, op), reads=[in_], writes=[out])


def mkcfg(D=4096, SEQ=4096, B=4, DB=16, DS=64, PAST=4096, IH=32, TOPK_MAX=256, MEMT=256):
    c = dict(D=D, SEQ=SEQ, B=B, DB=DB, DS=DS, PAST=PAST, IH=IH, MEMT=MEMT)
    c["KC"] = D // 128
    c["CCH"] = D // 2
    c["CC"] = c["CCH"] // 128
    c["NH"] = (D // 2) // 128
    c["NKV"] = 4
    c["GQ"] = c["NH"] // 4
    c["NP"] = SEQ // 2 // 128
    c["NCX"] = SEQ // 128
    c["NT"] = c["NP"] + 2
    c["TOPK_P"] = min(TOPK_MAX, SEQ // 4)
    c["TOPK_S"] = min(TOPK_MAX, (PAST + DS) // 4)
    c["SS"] = PAST + 128
    c["MH"] = 4
    c["MC"] = MEMT // 128
    return c


PEER_KEYS = 128
PEER_HEADS = 8
PEER_TOPK = 16


def build(cfg, stages=("M", "KV", "MAIN", "B", "C", "D", "E"), dbg=False):
    D, KC, CCH, CC, NH, NKV, GQ, NP, NCX, NT, IH, SEQ, PAST, SS, MEMT, MC = (cfg[k] for k in (
        "D", "KC", "CCH", "CC", "NH", "NKV", "GQ", "NP", "NCX", "NT", "IH", "SEQ", "PAST", "SS", "MEMT", "MC"))
    NTOK = NT * 128
    IDX_SCALE = (IH ** -0.5) * (128 ** -0.5)
    ATT_SCALE = 128 ** -0.5
    nc = bass.Bass("TRN2", target_bir_lowering=False)

    def din(name, shape, dt=F32):
        return nc.dram_tensor(name, list(shape), dt, kind="ExternalInput").ap()

    def dout(name, shape, dt=F32):
        return nc.dram_tensor(name, list(shape), dt, kind="ExternalOutput").ap()

    def dscr(name, shape, dt=F32):
        return nc.dram_tensor(name, list(shape), dt, kind="Internal").ap()

    xctx = din("xctx", [SEQ, D])
    xsp = din("xsp", [2, 128, D])
    xhalo = din("xhalo", [128, D])
    mem = din("mem", [MEMT, D])
    ckT = din("ckT", [2, 128, NKV, PAST])
    cv = din("cv", [2, PAST, NKV * 128])
    ckiT = din("ckiT", [2, 128, PAST])
    stT = din("stT", [2, CCH, 30])
    cmkT = din("cmkT", [2, 128, 4, MEMT])
    cmv = din("cmv", [2, MEMT, 512])
    w_glu = din("w_glu", [D, 2 * CCH])
    w_q = din("w_q", [D, NH * 128])
    w_qi = din("w_qi", [D, IH * 128])
    w_wi = din("w_wi", [D, IH])
    w_kv = din("w_kv", [D, 1152])
    w_out = din("w_out", [D, D])
    w_qm = din("w_qm", [D, 512])
    w_km = din("w_km", [D, 512])
    w_vm = din("w_vm", [D, 512])
    w_om = din("w_om", [512, D])
    w_pq = din("w_pq", [D, 2048])
    subk = din("subk", [128, 16, 128])
    uT = din("uT", [128, 128, KC * 128])
    vtab = din("vtab", [PEER_KEYS * PEER_KEYS, D])
    g_mix = din("g_mix", [1, D])
    g_memn = din("g_memn", [1, D])
    g_ffn = din("g_ffn", [1, D])
    g_mem = din("g_mem", [1, D])
    g_q = din("g_q", [1, 128])
    g_k = din("g_k", [1, 128])
    g_mq = din("g_mq", [1, 128])
    g_mk = din("g_mk", [1, 128])
    dww = din("dww", [128, CC, 31])
    dwb = din("dwb", [128, CC])
    lng = din("lng", [128, CC])
    lnb = din("lnb", [128, CC])
    rope_c = din("rope_c", [SEQ, 32])
    rope_s = din("rope_s", [128, 32])
    kc_p = din("kc_p", [1, SEQ])
    kc_s = din("kc_s", [1, SS])
    qch = din("qch", [128, NT])
    c_idb = din("c_idb", [128, 128], BF16)
    c_idf = din("c_idf", [128, 128])
    c_oneb = din("c_oneb", [128, 128], BF16)
    c_onef = din("c_onef", [128, 128])

    y = dout("y", [NTOK, D])
    o_k = dout("o_k", [SEQ, NKV * 128])
    o_v = dout("o_v", [SEQ, NKV * 128])
    o_ki = dout("o_ki", [SEQ, 128])
    o_conv = dout("o_conv", [30, CCH])
    o_mk = dout("o_mk", [MEMT, 512])
    o_mv = dout("o_mv", [MEMT, 512])
    o_ks = dout("o_ks", [2, 128, NKV * 128])
    o_vs = dout("o_vs", [2, 128, NKV * 128])
    o_kis = dout("o_kis", [2, 128, 128])
    o_convs = dout("o_convs", [2, 30, CCH])

    UTp = dscr("UTp", [CCH, 128 + NP * 128])
    UTs = dscr("UTs", [2, CCH, 160])
    KT = dscr("KT", [128, NKV, SEQ], BF16)
    Vc = dscr("Vc", [SEQ, NKV * 128], BF16)
    KIT = dscr("KIT", [128, SEQ], BF16)
    KTs = dscr("KTs", [2, 128, NKV, 128], BF16)
    Vs = dscr("Vs", [2, 128, NKV * 128], BF16)
    KITs = dscr("KITs", [2, 128, 128], BF16)
    MKT = dscr("MKT", [128, 4, MEMT], BF16)
    MV = dscr("MV", [MEMT, 512], BF16)
    QT = dscr("QT", [NT, 128, NH, 128], BF16)
    QIT = dscr("QIT", [NT, 128, IH, 128], BF16)
    WI = dscr("WI", [NT, 128, IH])
    MIXT = dscr("MIXT", [D, NTOK], BF16)
    H2 = dscr("H2", [NTOK, D])
    HN2T = dscr("HN2T", [D, NTOK], BF16)
    S12 = dscr("S12", [NTOK, 16, 128])
    GALL = dscr("GALL", [128, 128, NTOK], BF16)

    UB16 = dscr("UB16", [128, 128, KC * 128], BF16)
    VB16 = dscr("VB16", [PEER_KEYS * PEER_KEYS, D], BF16)
    dbg_out = {}

    with contextlib.ExitStack() as es:
        P = Prog(nc, es)
        idb = P.sb([128, 128], BF16, "idb")
        idf = P.sb([128, 128], F32, "idf")
        oneb = P.sb([128, 128], BF16, "oneb")
        onef = P.sb([128, 128], F32, "onef")
        P.dma("sp", idb[:], c_idb)
        P.dma("sp", idf[:], c_idf)
        P.dma("sp", oneb[:], c_oneb)
        P.dma("sp", onef[:], c_onef)
        P.flush()

        def bcast_row(dst, src_row, n):
            P.dma("sp", dst, src_row.to_broadcast([128, n]))

        def rstd_from_ss(ss, n, out, tmp):
            P.ts("dve", tmp, ss, 1.0 / n, EPS, op0=ALU.mult, op1=ALU.add)
            P.act(tmp, tmp, AF.Sqrt)
            P.recip(out, tmp)

        def load_w(dst, src, ncols):
            sv = src.rearrange("(c p) n -> p c n", p=128)
            nq = 4 if KC % 4 == 0 else 1
            step = KC // nq
            for q in range(nq):
                P.dma("pool", dst[:, q * step:(q + 1) * step, 0:ncols], sv[:, q * step:(q + 1) * step, :], okey=(dst, q))

        def wkeys(dst):
            return [(dst, q) for q in range(4 if KC % 4 == 0 else 1)]

        class NormT:
            def __init__(self, gsrc):
                self.gbc = P.sb([128, D], F32)
                bcast_row(self.gbc[:], gsrc[0:1, :], D)
                self.sq = P.sb([128, D], BF16)
                self.xs = [P.sb([128, D], BF16) for _ in range(2)]
                self.sm = [P.sb([128, 4], F32) for _ in range(2)]
                self.pt = [P.ps([128, 1024], BF16) for _ in range(2)]
                self.k = 0

            def run(self, x_t, dst_fn):
                k = self.k
                self.k += 1
                sm = self.sm[k % 2]
                xs = self.xs[k % 2]
                P.act(self.sq[:], x_t, AF.Square, accum_out=sm[:, 0:1])
                rstd_from_ss(sm[:, 0:1], D, sm[:, 1:2], sm[:, 2:3])
                P.stt(xs[:], x_t, sm[:, 1:2], self.gbc[:], ALU.mult, ALU.mult)
                nb = 8 if KC % 8 == 0 else KC
                for b0 in range(0, KC, nb):
                    pt = self.pt[(b0 // nb) % 2]
                    for j in range(nb):
                        P.tr(pt[:, j * 128:(j + 1) * 128], xs[:, (b0 + j) * 128:(b0 + j + 1) * 128], idb[:])
                    eng = "act" if (b0 // nb) % 2 == 0 else "dve"
                    P.copy(eng, dst_fn(b0, nb), pt[:, 0:nb * 128].rearrange("p (n t) -> p n t", t=128))

        def head_norm(ps_ap, nh, gain_bc, out_f, sq_t, sm_t):
            P.act(sq_t[:, 0:nh * 128], ps_ap, AF.Square)
            P.reduce(sm_t[:, 0:nh], sq_t[:, 0:nh * 128].rearrange("p (h d) -> p h d", d=128), ALU.add)
            rstd_from_ss(sm_t[:, 0:nh], 128, sm_t[:, 4:4 + nh], sm_t[:, 8:8 + nh])
            P.tt("dve", out_f, ps_ap.rearrange("p (h d) -> p h d", d=128),
                 sm_t[:, 4:4 + nh].unsqueeze(2).to_broadcast([128, nh, 128]), ALU.mult)
            P.tt("dve", out_f, out_f, gain_bc[:, 0:128].unsqueeze(1).to_broadcast([128, nh, 128]), ALU.mult)

        def rope(f, nh, cs, tmp):
            x1 = f[:, :, 0:16]
            x2 = f[:, :, 16:32]
            cosb = cs[:, 0:16].unsqueeze(1).to_broadcast([128, nh, 16])
            sinb = cs[:, 16:32].unsqueeze(1).to_broadcast([128, nh, 16])
            P.tt("dve", tmp[:, 0, 0:nh, :], x1, cosb, ALU.mult)
            P.tt("dve", tmp[:, 1, 0:nh, :], x2, sinb, ALU.mult)
            P.tt("dve", tmp[:, 2, 0:nh, :], x2, cosb, ALU.mult)
            P.tt("dve", tmp[:, 3, 0:nh, :], x1, sinb, ALU.mult)
            P.tt("dve", x1, tmp[:, 0, 0:nh, :], tmp[:, 1, 0:nh, :], ALU.subtract)
            P.tt("dve", x2, tmp[:, 2, 0:nh, :], tmp[:, 3, 0:nh, :], ALU.add)

        def tok_mm(ps_ap, hnT, tcol, wb, ncols, wk):
            for c in range(KC):
                P.mm(ps_ap, hnT[:, c, tcol:tcol + 128], wb[:, c, 0:ncols], start=(c == 0), stop=(c == KC - 1),
                     rkeys=[hnT] + wk)

        if "M" in stages:
            with contextlib.ExitStack() as st:
                P.st = st
                nt = NormT(g_mem)
                wk_b = P.sb([128, KC, 512], BF16)
                wv_b = P.sb([128, KC, 512], BF16)
                load_w(wk_b, w_km, 512)
                load_w(wv_b, w_vm, 512)
                gk = P.sb([128, 128], F32)
                bcast_row(gk[:], g_mk[0:1, :], 128)
                xt = [P.sb([128, D], F32) for _ in range(2)]
                hn = [P.sb([128, KC, 128], BF16) for _ in range(2)]
                pk = P.ps()
                pv = P.ps()
                ptr = P.ps([128, 1024], BF16)
                sq_t = P.sb([128, 512], F32)
                sm_t = P.sb([128, 12], F32)
                kf = P.sb([128, 4, 128], F32)
                kb = P.sb([128, 512], BF16)
                kTt = P.sb([128, 4, 128], BF16)
                vf = P.sb([128, 512], F32)
                vb = P.sb([128, 512], BF16)
                for m in range(MC):
                    x_t = xt[m % 2]
                    h_t = hn[m % 2]
                    P.dma("sp", x_t[:], mem[m * 128:(m + 1) * 128, :])
                    nt.run(x_t[:], lambda c0, n, h_t=h_t: h_t[:, c0:c0 + n, :])
                    tok_mm(pk[:, 0:512], h_t, 0, wk_b, 512, wkeys(wk_b))
                    tok_mm(pv[:, 0:512], h_t, 0, wv_b, 512, wkeys(wv_b))
                    head_norm(pk[:, 0:512], 4, gk, kf[:], sq_t, sm_t)
                    P.dma("sp", o_mk[m * 128:(m + 1) * 128, :], kf[:].rearrange("p h d -> p (h d)"))
                    P.copy("act", kb[:], kf[:].rearrange("p h d -> p (h d)"))
                    for h in range(4):
                        P.tr(ptr[:, h * 128:(h + 1) * 128], kb[:, h * 128:(h + 1) * 128], idb[:])
                    P.copy("dve", kTt[:], ptr[:, 0:512].rearrange("p (h t) -> p h t", t=128))
                    P.dma("sp", MKT[:, :, m * 128:(m + 1) * 128], kTt[:])
                    P.copy("act", vf[:], pv[:, 0:512])
                    P.dma("sp", o_mv[m * 128:(m + 1) * 128, :], vf[:])
                    P.copy("dve", vb[:], pv[:, 0:512])
                    P.dma("sp", MV[m * 128:(m + 1) * 128, :], vb[:])
                P.flush()
            P.st = es

        if "KV" in stages:
            with contextlib.ExitStack() as st:
                P.st = st
                nt = NormT(g_mix)
                wb = P.sb([128, KC, 1152], BF16)
                load_w(wb, w_kv, 1152)
                wk = wkeys(wb)
                gk = P.sb([128, 128], F32)
                bcast_row(gk[:], g_k[0:1, :], 128)
                xt = [P.sb([128, D], F32) for _ in range(2)]
                hn = [P.sb([128, KC, 128], BF16) for _ in range(2)]
                cs = [P.sb([128, 32], F32) for _ in range(2)]
                pk = P.ps()
                pv = P.ps()
                pki = P.ps()
                ptr = P.ps([128, 1024], BF16)
                sq_t = P.sb([128, 512], F32)
                sm_t = P.sb([128, 12], F32)
                rtmp = P.sb([128, 4, 4, 16], F32)
                kf = [P.sb([128, 4, 128], F32) for _ in range(2)]
                kb = P.sb([128, 512], BF16)
                kTt = [P.sb([128, 4, 128], BF16) for _ in range(2)]
                vf = [P.sb([128, 512], F32) for _ in range(2)]
                vb = [P.sb([128, 512], BF16) for _ in range(2)]
                kif = [P.sb([128, 1, 128], F32) for _ in range(2)]
                kib = P.sb([128, 128], BF16)
                kiTt = [P.sb([128, 128], BF16) for _ in range(2)]
                tiles = [("p", i) for i in range(NCX)] + [("s", 0), ("s", 1)]
                for n_, (kind, i) in enumerate(tiles):
                    x_t = xt[n_ % 2]
                    h_t = hn[n_ % 2]
                    c_t = cs[n_ % 2]
                    if kind == "p":
                        P.dma("sp", x_t[:], xctx[i * 128:(i + 1) * 128, :])
                        P.dma("sp", c_t[:], rope_c[i * 128:(i + 1) * 128, :])
                    else:
                        P.dma("sp", x_t[:], xsp[i])
                        P.dma("sp", c_t[:], rope_s)
                    nt.run(x_t[:], lambda c0, n, h_t=h_t: h_t[:, c0:c0 + n, :])
                    for c in range(KC):
                        P.mm(pk[:, 0:512], h_t[:, c, :], wb[:, c, 0:512], start=(c == 0), stop=(c == KC - 1), rkeys=[h_t] + wk)
                    for c in range(KC):
                        P.mm(pv[:, 0:512], h_t[:, c, :], wb[:, c, 512:1024], start=(c == 0), stop=(c == KC - 1), rkeys=[h_t] + wk)
                    for c in range(KC):
                        P.mm(pki[:, 0:128], h_t[:, c, :], wb[:, c, 1024:1152], start=(c == 0), stop=(c == KC - 1), rkeys=[h_t] + wk)
                    kf_t = kf[n_ % 2]
                    head_norm(pk[:, 0:512], 4, gk, kf_t[:], sq_t, sm_t)
                    rope(kf_t, 4, c_t, rtmp)
                    kflat = kf_t[:].rearrange("p h d -> p (h d)")
                    if kind == "p":
                        P.dma("sp", o_k[i * 128:(i + 1) * 128, :], kflat)
                    else:
                        P.dma("sp", o_ks[i], kflat)
                    P.copy("act", kb[:], kflat)
                    for h in range(4):
                        P.tr(ptr[:, h * 128:(h + 1) * 128], kb[:, h * 128:(h + 1) * 128], idb[:])
                    kT_t = kTt[n_ % 2]
                    P.copy("dve", kT_t[:], ptr[:, 0:512].rearrange("p (h t) -> p h t", t=128))
                    if kind == "p":
                        P.dma("sp", KT[:, :, i * 128:(i + 1) * 128], kT_t[:])
                    else:
                        P.dma("sp", KTs[i], kT_t[:])
                    vf_t = vf[n_ % 2]
                    vb_t = vb[n_ % 2]
                    P.copy("act", vf_t[:], pv[:, 0:512])
                    P.copy("dve", vb_t[:], pv[:, 0:512])
                    if kind == "p":
                        P.dma("sp", o_v[i * 128:(i + 1) * 128, :], vf_t[:])
                        P.dma("sp", Vc[i * 128:(i + 1) * 128, :], vb_t[:])
                    else:
                        P.dma("sp", o_vs[i], vf_t[:])
                        P.dma("sp", Vs[i], vb_t[:])
                    ki_t = kif[n_ % 2]
                    P.copy("act", ki_t[:, 0, :], pki[:, 0:128])
                    rope(ki_t, 1, c_t, rtmp)
                    if kind == "p":
                        P.dma("sp", o_ki[i * 128:(i + 1) * 128, :], ki_t[:, 0, :])
                    else:
                        P.dma("sp", o_kis[i], ki_t[:, 0, :])
                    P.copy("act", kib[:], ki_t[:, 0, :])
                    P.tr(ptr[:, 512:640], kib[:], idb[:])
                    kiT_t = kiTt[n_ % 2]
                    P.copy("dve", kiT_t[:], ptr[:, 512:640])
                    if kind == "p":
                        P.dma("sp", KIT[:, i * 128:(i + 1) * 128], kiT_t[:])
                    else:
                        P.dma("sp", KITs[i], kiT_t[:])
                P.flush()
            P.st = es

        own = [("h", -1)] + [("p", i) for i in range(NP)] + [("s", 0), ("s", 1)]
        groups = [own[i:i + 4] for i in range(0, len(own), 4)]

        def tile_index(kind, i):
            return i if kind == "p" else NP + i

        if "MAIN" in stages:
            with contextlib.ExitStack() as st:
                P.st = st
                nt = NormT(g_mix)
                gq = P.sb([128, 128], F32)
                bcast_row(gq[:], g_q[0:1, :], 128)
                for s in range(2):
                    P.dma("sp", UTs[s][:, 2:32], stT[s], okey=("UTs", "st"))
                xt = [P.sb([128, D], F32) for _ in range(2)]
                hnT = P.sb([128, KC, 512], BF16)
                wbuf = [P.sb([128, KC, 512], BF16) for _ in range(2)]
                cst = P.sb([128, 4, 32], F32)
                pa = P.ps()
                pg = P.ps()
                pq = [P.ps() for _ in range(2)]
                ptr = P.ps([128, 1024], BF16)
                sg = P.sb([128, 512], F32)
                ut = [P.sb([128, 512], F32) for _ in range(2)]
                sq_t = P.sb([128, 512], F32)
                sm_t = P.sb([128, 12], F32)
                rtmp = P.sb([128, 4, 4, 16], F32)
                qf = P.sb([128, 4, 128], F32)
                qb = P.sb([128, 512], BF16)
                qTt = [P.sb([128, 4, 128], BF16) for _ in range(2)]
                wis = [P.sb([128, IH], F32) for _ in range(2)]
                wcnt = 0
                xcnt = 0
                for grp in groups:
                    ng = len(grp)
                    N = ng * 128
                    for tt, (kind, i) in enumerate(grp):
                        x_t = xt[xcnt % 2]
                        xcnt += 1
                        if kind == "h":
                            P.dma("sp", x_t[:], xhalo)
                        elif kind == "p":
                            P.dma("sp", x_t[:], xctx[i * 128:(i + 1) * 128, :])
                            P.dma("sp", cst[:, tt, :], rope_c[i * 128:(i + 1) * 128, :], okey=(cst, tt))
                        else:
                            P.dma("sp", x_t[:], xsp[i])
                            P.dma("sp", cst[:, tt, :], rope_s, okey=(cst, tt))
                        nt.run(x_t[:], lambda c0, n, tt=tt: hnT[:, c0:c0 + n, tt * 128:(tt + 1) * 128])
                    for b in range(CC // 2):
                        wb = wbuf[wcnt % 2]
                        wcnt += 1
                        load_w(wb, w_glu[:, b * 512:(b + 1) * 512], 512)
                        wk = wkeys(wb)
                        for s in range(2):
                            j = 2 * b + s
                            for c in range(KC):
                                P.mm(pa[:, 0:N], wb[:, c, (2 * s) * 128:(2 * s + 1) * 128], hnT[:, c, 0:N],
                                     start=(c == 0), stop=(c == KC - 1), rkeys=[hnT] + wk)
                            for c in range(KC):
                                P.mm(pg[:, 0:N], wb[:, c, (2 * s + 1) * 128:(2 * s + 2) * 128], hnT[:, c, 0:N],
                                     start=(c == 0), stop=(c == KC - 1), rkeys=[hnT] + wk)
                            P.act(sg[:, 0:N], pg[:, 0:N], AF.Sigmoid)
                            u_t = ut[j % 2]
                            P.tt("dve", u_t[:, 0:N], pa[:, 0:N], sg[:, 0:N], ALU.mult)
                            for tt, (kind, i) in enumerate(grp):
                                src = u_t[:, tt * 128:(tt + 1) * 128]
                                if kind == "h":
                                    P.dma("sp", UTp[j * 128:(j + 1) * 128, 0:128], src, okey=("UTp", None))
                                elif kind == "p":
                                    P.dma("sp", UTp[j * 128:(j + 1) * 128, 128 + i * 128:128 + (i + 1) * 128], src, okey=("UTp", None))
                                else:
                                    P.dma("sp", UTs[i][j * 128:(j + 1) * 128, 32:160], src, okey=("UTs", "tok"))
                    for which, nblk, wsrc, dst in (("q", NH // 4, w_q, QT), ("qi", IH // 4, w_qi, QIT)):
                        for b in range(nblk):
                            wb = wbuf[wcnt % 2]
                            wcnt += 1
                            load_w(wb, wsrc[:, b * 512:(b + 1) * 512], 512)
                            wk = wkeys(wb)
                            for tt, (kind, i) in enumerate(grp):
                                if kind == "h":
                                    continue
                                ti = tile_index(kind, i)
                                pq_t = pq[(tt) % 2]
                                tok_mm(pq_t[:, 0:512], hnT, tt * 128, wb, 512, wk)
                                if which == "q":
                                    head_norm(pq_t[:, 0:512], 4, gq, qf[:], sq_t, sm_t)
                                else:
                                    P.copy("act", qf[:].rearrange("p h d -> p (h d)"), pq_t[:, 0:512])
                                rope(qf, 4, cst[:, tt, :], rtmp)
                                P.copy("act", qb[:], qf[:].rearrange("p h d -> p (h d)"))
                                for h in range(4):
                                    P.tr(ptr[:, h * 128:(h + 1) * 128], qb[:, h * 128:(h + 1) * 128], idb[:])
                                q_T = qTt[(tt) % 2]
                                P.copy("dve", q_T[:], ptr[:, 0:512].rearrange("p (h t) -> p h t", t=128))
                                P.dma("sp", dst[ti][:, b * 4:(b + 1) * 4, :], q_T[:], okey=(dst.tensor.name, None))
                    wb = wbuf[wcnt % 2]
                    wcnt += 1
                    load_w(wb, w_wi, IH)
                    wk = wkeys(wb)
                    for tt, (kind, i) in enumerate(grp):
                        if kind == "h":
                            continue
                        ti = tile_index(kind, i)
                        pq_t = pq[tt % 2]
                        tok_mm(pq_t[:, 0:IH], hnT, tt * 128, wb, IH, wk)
                        w_s = wis[tt % 2]
                        P.act(w_s[:], pq_t[:, 0:IH], AF.Copy, scale=IDX_SCALE)
                        P.dma("sp", WI[ti], w_s[:], okey=("WI", None))
                P.flush()
            P.st = es

        if "B" in stages:
            with contextlib.ExitStack() as st:
                P.st = st
                P.npool["uin"] = 4
                wt = P.sb([128, CC, 31], F32)
                bt = P.sb([128, CC], F32)
                lg = P.sb([128, CC], F32)
                lb = P.sb([128, CC], F32)
                P.dma("sp", wt[:], dww)
                P.dma("sp", bt[:], dwb)
                P.dma("sp", lg[:], lng)
                P.dma("sp", lb[:], lnb)
                uin = P.sb([128, CC, 544], F32, "uin")
                cc_t = P.sb([128, CC, 512], F32)
                sqt = [P.sb([128, 512], F32) for _ in range(2)]
                p1 = P.ps()
                p2 = P.ps()
                mean = P.sb([128, 512], F32)
                var = P.sb([128, 512], F32)
                rstd = P.sb([128, 512], F32)
                tmp = [P.sb([128, 512], F32) for _ in range(2)]
                co = [P.sb([128, 512], BF16) for _ in range(2)]
                P.dma("sp", o_conv.rearrange("t c -> c t"), UTp[:, 128 + NP * 128 - 30:128 + NP * 128], okey=("o_conv", None), ikey="UTp",
                      allow_slow_non_contiguous=True)
                for s in range(2):
                    P.dma("sp", o_convs[s].rearrange("t c -> c t"), UTs[s][:, 32 + 64 - 30:32 + 64], okey=("o_convs", None), ikey="UTs",
                          allow_slow_non_contiguous=True)
                jobs = []
                for tb in range(max(1, NP * 128 // 512)):
                    ntk = min(512, NP * 128)
                    jobs.append(("p", tb, ntk))
                jobs += [("s", 0, 128), ("s", 1, 128)]
                for kind, tb, ntk in jobs:
                    if kind == "p":
                        c0 = 128 + tb * ntk
                        src = UTp[:, c0 - 30:c0 + ntk].rearrange("(j p) t -> p j t", p=128)
                        mcol = tb * ntk
                        sk = "UTp"
                    else:
                        src = UTs[tb][:, 2:160].rearrange("(j p) t -> p j t", p=128)
                        mcol = (NP + tb) * 128
                        sk = "UTs"
                    W = 30 + ntk
                    P.dma("sp", uin[:, :, 0:W], src, ikey=sk)
                    for j in range(CC):
                        acc = cc_t[:, j, 0:ntk]
                        P.ts("dve", acc, uin[:, j, 0:ntk], wt[:, j, 0:1], bt[:, j:j + 1], op0=ALU.mult, op1=ALU.add, okey=(cc_t, j))
                        for k in range(1, 31):
                            P.stt(acc, uin[:, j, k:k + ntk], wt[:, j, k:k + 1], acc, ALU.mult, ALU.add, okey=(cc_t, j),
                                  rkeys=[uin, (cc_t, j)])
                        s_t = sqt[j % 2]
                        P.act(s_t[:, 0:ntk], acc, AF.Square, ikey=(cc_t, j))
                        P.mm(p1[:, 0:ntk], onef[:], acc, start=(j == 0), stop=(j == CC - 1), rkeys=[onef, (cc_t, j)])
                        P.mm(p2[:, 0:ntk], onef[:], s_t[:, 0:ntk], start=(j == 0), stop=(j == CC - 1))
                    P.ts("dve", mean[:, 0:ntk], p1[:, 0:ntk], 1.0 / CCH, None, op0=ALU.mult)
                    P.tt("dve", var[:, 0:ntk], mean[:, 0:ntk], mean[:, 0:ntk], ALU.mult)
                    P.stt(var[:, 0:ntk], p2[:, 0:ntk], 1.0 / CCH, var[:, 0:ntk], ALU.mult, ALU.subtract)
                    P.ts("dve", var[:, 0:ntk], var[:, 0:ntk], EPS, None, op0=ALU.add)
                    P.act(var[:, 0:ntk], var[:, 0:ntk], AF.Sqrt)
                    P.recip(rstd[:, 0:ntk], var[:, 0:ntk])
                    for j in range(CC):
                        t_ = tmp[j % 2]
                        P.tt("dve", t_[:, 0:ntk], cc_t[:, j, 0:ntk], mean[:, 0:ntk], ALU.subtract, rkeys=[(cc_t, j), mean])
                        P.tt("dve", t_[:, 0:ntk], t_[:, 0:ntk], rstd[:, 0:ntk], ALU.mult)
                        c_o = co[j % 2]
                        P.act(c_o[:, 0:ntk], t_[:, 0:ntk], AF.Silu, scale=lg[:, j:j + 1], bias=lb[:, j:j + 1])
                        P.dma("sp", MIXT[j * 128:(j + 1) * 128, mcol:mcol + ntk], c_o[:, 0:ntk], okey=("MIXT", "conv"))
                P.flush()
            P.st = es

        if "C" in stages:
            with contextlib.ExitStack() as st:
                P.st = st
                SMAX = max(SEQ, SS)
                P.npool["kiT_c"] = 4
                P.npool["kT_c"] = 4
                P.npool["v_c"] = 4
                kiT_c = P.sb([128, SMAX], BF16, "kiT_c")
                kT_c = P.sb([128, NKV, SMAX], BF16, "kT_c")
                v_c = P.sb([128, SMAX // 128, NKV * 128], BF16, "v_c")
                kc_t = P.sb([128, SMAX], BF16)
                qch_t = P.sb([128, NT], F32)
                P.dma("sp", qch_t[:], qch)
                qiT = [P.sb([128, IH, 128], BF16) for _ in range(2)]
                qT = [P.sb([128, NH, 128], BF16) for _ in range(2)]
                wi_t = [P.sb([128, IH], F32) for _ in range(2)]
                acc2 = [P.sb([128, SMAX], F32) for _ in range(2)]
                madd = P.sb([128, SMAX], BF16)
                bs = P.sb([128, 8], F32)
                m8 = P.sb([128, 256], F32)
                thr = P.sb([128, 1], F32)
                mask = P.sb([128, SMAX], BF16)
                maskT = P.sb([128, SMAX // 128, 128], BF16)
                rl = [P.sb([128, 512], F32) for _ in range(4)]
                pe_ = [P.sb([128, GQ, 128], BF16) for _ in range(3)]
                pm = [P.sb([128, GQ, 128], BF16) for _ in range(3)]
                rz = P.sb([128, GQ * 128], F32)
                ob = [P.sb([128, GQ, 128], BF16) for _ in range(2)]
                ps_s = [P.ps() for _ in range(3)]
                ps_qk = [P.ps() for _ in range(2)]
                ps_o = P.ps()
                ps_z = P.ps()
                ptr = [P.ps([128, 1024], BF16) for _ in range(1)]

                def seglist(blocks):
                    nb = len(blocks)
                    segs = []
                    a = 0
                    while a < nb:
                        b_ = a + 1
                        while b_ < nb and b_ - a < 4 and blocks[b_] == blocks[b_ - 1] + 1:
                            b_ += 1
                        segs.append((a, b_))
                        a = b_
                    return segs

                cnt = dict(n=0, it=0)
                P.npool["UB16"] = 3
                P.npool["VB16"] = 3
                pre_ops = []
                for c in range(0, 128, 2):
                    pre_ops.append(("u", c))
                    pre_ops.append(("v", c))
                n_slots = (NP + 2) * NKV
                per_slot = -(-len(pre_ops) // n_slots)

                def precast_some():
                    dd = min(2048, KC * 128)
                    for _ in range(per_slot):
                        if not pre_ops:
                            return
                        kind, c = pre_ops.pop(0)
                        if kind == "u":
                            P.dma("pool", UB16[c:c + 2].rearrange("c p (x d) -> p c x d", d=dd),
                                  uT[c:c + 2].rearrange("c p (x d) -> p c x d", d=dd), okey=("UB16", None))
                        else:
                            dv = min(2048, D)
                            P.dma("pool", VB16[c * 128:(c + 2) * 128, :].rearrange("(c p) (x d) -> p c x d", p=128, d=dv),
                                  vtab[c * 128:(c + 2) * 128, :].rearrange("(c p) (x d) -> p c x d", p=128, d=dv), okey=("VB16", None))

                def idx_phase(job):
                    ti, blocks = job["ti"], job["blocks"]
                    if job.get("pre_idx"):
                        job["pre_idx"]()
                    k2 = ti % 2
                    acc = acc2[k2]
                    P.dma("sp", qiT[k2][:], QIT[ti], ikey="QIT")
                    P.dma("sp", wi_t[k2][:], WI[ti], ikey="WI")
                    segs = seglist(blocks)
                    N = len(blocks) * 128
                    for (a, b_) in segs:
                        P.ts("pool", madd[:, a * 128:b_ * 128], kc_t[:, blocks[a] * 128:(blocks[a] + b_ - a) * 128],
                             qch_t[:, ti:ti + 1], NEG, op0=ALU.is_gt, op1=ALU.mult)
                    for (a, b_) in segs:
                        w = (b_ - a) * 128
                        for h in range(IH):
                            n_ = cnt["n"]
                            cnt["n"] += 1
                            p_ = ps_s[n_ % 3]
                            r_ = rl[n_ % 4]
                            P.mm(p_[:, 0:w], qiT[k2][:, h, :], kiT_c[:, blocks[a] * 128:blocks[a] * 128 + w], rkeys=[qiT[k2], kiT_c])
                            P.act(r_[:, 0:w], p_[:, 0:w], AF.Relu)
                            if h == 0:
                                P.ts("dve", acc[:, a * 128:b_ * 128], r_[:, 0:w], wi_t[k2][:, 0:1], None, op0=ALU.mult)
                            else:
                                P.stt(acc[:, a * 128:b_ * 128], r_[:, 0:w], wi_t[k2][:, h:h + 1], acc[:, a * 128:b_ * 128], ALU.mult, ALU.add)
                    P.reduce(bs[:, 0:1], acc[:, 0:N], ALU.min)
                    P.tt("pool", acc[:, 0:N], acc[:, 0:N], madd[:, 0:N], ALU.add)
                    P.max8(m8[:, 0:8], acc[:, 0:N])
                    P.copy("dve", bs[:, 1:2], m8[:, 0:1])

                NITER = 22

                def topk_rounds(job, r0, r1):
                    ti, N, topk = job["ti"], len(job["blocks"]) * 128, job["topk"]
                    acc = acc2[ti % 2]
                    lo, hi, mid, tmp, cn, sel, dd = (bs[:, i:i + 1] for i in range(7))
                    for r in range(r0, min(r1, NITER)):
                        P.ts("dve", tmp, hi, 0.5, None, op0=ALU.mult)
                        P.stt(mid, lo, 0.5, tmp, ALU.mult, ALU.add)
                        P.ts("dve", mask[:, 0:N], acc[:, 0:N], mid, None, op0=ALU.is_ge, op1=ALU.add, accum_out=cn)
                        P.ts("dve", sel, cn, float(topk) - 0.5, None, op0=ALU.is_ge)
                        P.tt("dve", dd, mid, lo, ALU.subtract)
                        P.stt(lo, dd, sel, lo, ALU.mult, ALU.add)
                        P.tt("dve", dd, hi, mid, ALU.subtract)
                        P.stt(hi, dd, sel, mid, ALU.mult, ALU.add)

                def topk_final(job):
                    ti, blocks, topk = job["ti"], job["blocks"], job["topk"]
                    nb = len(blocks)
                    N = nb * 128
                    acc = acc2[ti % 2]
                    P.ts("dve", thr[:], bs[:, 0:1], 0.5 * NEG, None, op0=ALU.max)
                    P.ts("dve", mask[:, 0:N], acc[:, 0:N], thr[:, 0:1], None, op0=ALU.is_ge)
                    for b0 in range(0, nb, 8):
                        n8 = min(8, nb - b0)
                        pt = ptr[0]
                        for j in range(n8):
                            P.tr(pt[:, j * 128:(j + 1) * 128], mask[:, (b0 + j) * 128:(b0 + j + 1) * 128], idb[:])
                        P.copy("act", maskT[:, b0:b0 + n8, :], pt[:, 0:n8 * 128].rearrange("p (n t) -> p n t", t=128))

                def attn_group(job, g):
                    ti, blocks = job["ti"], job["blocks"]
                    k2 = ti % 2
                    nb = len(blocks)
                    if g == 0:
                        if job.get("pre_attn"):
                            job["pre_attn"]()
                        P.dma("sp", qT[k2][:], QT[ti], ikey="QT")
                    W = GQ * 128
                    for ci, blk in enumerate(blocks):
                        it = cnt["it"]
                        cnt["it"] += 1
                        pq_ = ps_qk[it % 2]
                        e_ = pe_[it % 3]
                        m_ = pm[it % 3]
                        P.mm(pq_[:, 0:W], kT_c[:, g, blk * 128:(blk + 1) * 128],
                             qT[k2][:, g * GQ:(g + 1) * GQ, :].rearrange("p r t -> p (r t)"), rkeys=[kT_c, qT[k2]])
                        P.act(e_[:].rearrange("p r t -> p (r t)"), pq_[:, 0:W], AF.Exp, scale=ATT_SCALE)
                        P.tt("pool", m_[:], e_[:], maskT[:, ci, :].unsqueeze(1).to_broadcast([128, GQ, 128]), ALU.mult)
                        mf = m_[:].rearrange("p r t -> p (r t)")
                        P.mm(ps_o[:, 0:W], v_c[:, blk, g * 128:(g + 1) * 128], mf, start=(ci == 0), stop=(ci == nb - 1), rkeys=[v_c, m_])
                        P.mm(ps_z[:, 0:W], oneb[:], mf, start=(ci == 0), stop=(ci == nb - 1))
                    P.recip(rz[:, 0:W], ps_z[:, 0:W])
                    o_ = ob[g % 2]
                    P.tt("dve", o_[:].rearrange("p r t -> p (r t)"), ps_o[:, 0:W], rz[:, 0:W], ALU.mult)
                    P.dma("sp", MIXT[CCH + g * GQ * 128:CCH + (g + 1) * GQ * 128, ti * 128:(ti + 1) * 128].rearrange("(r d) t -> d r t", d=128),
                          o_[:], okey=("MIXT", "attn"))

                def load_prompt_ki():
                    P.cdma(kc_t[:, 0:SEQ], kc_p[0:1, :].to_broadcast([128, SEQ]))
                    P.dma("sp", kiT_c[:, 0:SEQ], KIT, ikey="KIT", okey=(kiT_c, 0))

                def load_prompt_kv():
                    P.dma("sp", kT_c[:, :, 0:SEQ], KT, ikey="KT", okey=(kT_c, 0))
                    P.dma("sp", v_c[:, 0:NCX, :], Vc.rearrange("(c p) n -> p c n", p=128), ikey="Vc", okey=(v_c, 0))

                def mk_sample_ki(s):
                    def f():
                        P.cdma(kc_t[:, 0:SS], kc_s[0:1, :].to_broadcast([128, SS]))
                        P.cdma(kiT_c[:, 0:PAST], ckiT[s], okey=(kiT_c, 0))
                        P.dma("sp", kiT_c[:, PAST:SS], KITs[s], ikey="KITs", okey=(kiT_c, 1))
                    return f

                def mk_sample_kv(s):
                    def f():
                        for g in range(NKV):
                            P.cdma(kT_c[:, g, 0:PAST], ckT[s][:, g, :], okey=(kT_c, 0))
                        P.dma("sp", kT_c[:, :, PAST:SS], KTs[s], ikey="KTs", okey=(kT_c, 1))
                        cvv = cv[s].rearrange("(c p) n -> p c n", p=128)
                        nq = 4 if (PAST // 128) % 4 == 0 else 1
                        stp = (PAST // 128) // nq
                        for q in range(nq):
                            P.dma("pool", v_c[:, q * stp:(q + 1) * stp, :], cvv[:, q * stp:(q + 1) * stp, :], okey=(v_c, 0))
                        P.dma("sp", v_c[:, PAST // 128, :], Vs[s], ikey="Vs", okey=(v_c, 1))
                    return f

                jobs = []
                for i in range(NP):
                    jobs.append(dict(ti=i, blocks=list(range(0, i + 1)) + list(range(NP, 2 * NP)), topk=cfg["TOPK_P"]))
                jobs[0]["pre_idx"] = load_prompt_ki
                jobs[0]["pre_attn"] = load_prompt_kv
                for s in range(2):
                    jobs.append(dict(ti=NP + s, blocks=list(range(SS // 128)), topk=cfg["TOPK_S"],
                                     pre_idx=mk_sample_ki(s), pre_attn=mk_sample_kv(s)))
                idx_phase(jobs[0])
                topk_rounds(jobs[0], 0, 10 ** 6)
                topk_final(jobs[0])
                for k, job in enumerate(jobs):
                    nxt = jobs[k + 1] if k + 1 < len(jobs) else None
                    if nxt is not None:
                        idx_phase(nxt)
                        per = -(-NITER // NKV)
                    for g in range(NKV):
                        if nxt is not None:
                            topk_rounds(nxt, g * per, (g + 1) * per)
                        precast_some()
                        attn_group(job, g)
                    if nxt is not None:
                        topk_final(nxt)
                while pre_ops:
                    precast_some()
                P.flush()
            P.st = es

        otiles = list(range(NT))
        ogroups = [otiles[i:i + 4] for i in range(0, NT, 4)]
        Hs = dscr("Hs", [NTOK, D])
        RC = dscr("RC", [NT, 128, 3, 128])

        def x_rows(ti):
            return xctx[ti * 128:(ti + 1) * 128, :] if ti < NP else xsp[ti - NP]

        if "D" in stages:
            with contextlib.ExitStack() as st:
                P.st = st
                mixT = P.sb([128, KC, 512], BF16)
                wbuf = [P.sb([128, KC, 512], BF16) for _ in range(2)]
                xb_ = [P.sb([128, 512], F32) for _ in range(3)]
                hb_ = [P.sb([128, 512], F32) for _ in range(3)]
                pp = [P.ps() for _ in range(4)]
                wcnt = 0
                k_ = 0
                for grp in ogroups:
                    N = len(grp) * 128
                    c0 = grp[0] * 128
                    P.dma("sp", mixT[:, :, 0:N], MIXT[:, c0:c0 + N].rearrange("(c p) n -> p c n", p=128), ikey="MIXT")
                    for b in range(D // 512):
                        wb = wbuf[wcnt % 2]
                        wcnt += 1
                        load_w(wb, w_out[:, b * 512:(b + 1) * 512], 512)
                        wk = wkeys(wb)
                        for tt, ti in enumerate(grp):
                            xb = xb_[k_ % 3]
                            hb = hb_[k_ % 3]
                            p_ = pp[k_ % 4]
                            k_ += 1
                            P.dma("sp", xb[:], x_rows(ti)[:, b * 512:(b + 1) * 512])
                            tok_mm(p_[:, 0:512], mixT, tt * 128, wb, 512, wk)
                            P.tt("dve", hb[:], p_[:, 0:512], xb[:], ALU.add)
                            P.dma("sp", Hs[ti * 128:(ti + 1) * 128, b * 512:(b + 1) * 512], hb[:], okey=("Hs", None))
                P.flush()
            P.st = es

            with contextlib.ExitStack() as st:
                P.st = st
                nt = NormT(g_memn)
                gbc2 = P.sb([128, D], F32)
                bcast_row(gbc2[:], g_ffn[0:1, :], D)
                gmq = P.sb([128, 128], F32)
                bcast_row(gmq[:], g_mq[0:1, :], 128)
                wqm_b = P.sb([128, KC, 512], BF16)
                load_w(wqm_b, w_qm, 512)
                wom_b = P.sb([128, 4, D], BF16)
                P.cdma(wom_b[:], w_om.rearrange("(h p) d -> p h d", p=128))
                mkT_c = P.sb([128, 4, MEMT], BF16)
                mv_c = P.sb([128, MC, 512], BF16)
                ht = [P.sb([128, D], F32) for _ in range(2)]
                hn = [P.sb([128, KC, 128], BF16) for _ in range(2)]
                pq_ = P.ps()
                pl_ = [P.ps() for _ in range(2)]
                po_ = P.ps()
                pz_ = pq_
                pw_ = [P.ps() for _ in range(1)]
                ptr = P.ps([128, 1024], BF16)
                sq_t = P.sb([128, 512], F32)
                sm_t = P.sb([128, 12], F32)
                qmf = P.sb([128, 4, 128], F32)
                qmb = P.sb([128, 512], BF16)
                qmT = P.sb([128, 4, 128], BF16)
                pmT = [P.sb([128, 4, 128], BF16) for _ in range(MC)]
                rz = P.sb([128, 512], F32)
                omT = P.sb([128, 4, 128], BF16)
                for ti in otiles:
                    if ti == 0:
                        P.dma("sp", mkT_c[:], MKT, ikey="MKT")
                        P.dma("sp", mv_c[:], MV.rearrange("(c p) n -> p c n", p=128), ikey="MV")
                    elif ti >= NP:
                        P.dma("pool", mkT_c[:], cmkT[ti - NP])
                        P.dma("pool", mv_c[:], cmv[ti - NP].rearrange("(c p) n -> p c n", p=128))
                    h_t = ht[ti % 2]
                    hn_t = hn[ti % 2]
                    P.dma("sp", h_t[:], Hs[ti * 128:(ti + 1) * 128, :], ikey="Hs")
                    nt.run(h_t[:], lambda c0, n, hn_t=hn_t: hn_t[:, c0:c0 + n, :])
                    tok_mm(pq_[:, 0:512], hn_t, 0, wqm_b, 512, wkeys(wqm_b))
                    head_norm(pq_[:, 0:512], 4, gmq, qmf[:], sq_t, sm_t)
                    P.copy("act", qmb[:], qmf[:].rearrange("p h d -> p (h d)"))
                    for h in range(4):
                        P.tr(ptr[:, h * 128:(h + 1) * 128], qmb[:, h * 128:(h + 1) * 128], idb[:])
                    P.copy("dve", qmT[:], ptr[:, 0:512].rearrange("p (h t) -> p h t", t=128))
                    for mc in range(MC):
                        for h in range(4):
                            P.mm(pl_[mc % 2][:, h * 128:(h + 1) * 128], mkT_c[:, h, mc * 128:(mc + 1) * 128], qmT[:, h, :])
                        P.act(pmT[mc][:].rearrange("p h t -> p (h t)"), pl_[mc % 2][:, 0:512], AF.Exp, scale=ATT_SCALE)
                    for h in range(4):
                        for mc in range(MC):
                            P.mm(po_[:, h * 128:(h + 1) * 128], mv_c[:, mc, h * 128:(h + 1) * 128], pmT[mc][:, h, :],
                                 start=(mc == 0), stop=(mc == MC - 1))
                    for mc in range(MC):
                        P.mm(pz_[:, 0:512], oneb[:], pmT[mc][:].rearrange("p h t -> p (h t)"), start=(mc == 0), stop=(mc == MC - 1))
                    P.recip(rz[:], pz_[:, 0:512])
                    P.tt("dve", omT[:].rearrange("p h t -> p (h t)"), po_[:, 0:512], rz[:], ALU.mult)
                    for b in range(D // 512):
                        p_ = pw_[0]
                        for h in range(4):
                            P.mm(p_[:, 0:512], omT[:, h, :], wom_b[:, h, b * 512:(b + 1) * 512], start=(h == 0), stop=(h == 3))
                        P.tt("dve", h_t[:, b * 512:(b + 1) * 512], p_[:, 0:512], h_t[:, b * 512:(b + 1) * 512], ALU.add)
                    P.dma("sp", H2[ti * 128:(ti + 1) * 128, :], h_t[:], okey=("H2", None))
                    nt.gbc, g_save = gbc2, nt.gbc
                    nt.run(h_t[:], lambda c0, n, hn_t=hn_t: hn_t[:, c0:c0 + n, :])
                    nt.gbc = g_save
                    P.dma("sp", HN2T[:, ti * 128:(ti + 1) * 128].rearrange("(c p) t -> p c t", p=128), hn_t[:], okey=("HN2T", None))
                P.flush()
            P.st = es

            with contextlib.ExitStack() as st:
                P.st = st
                hn2 = P.sb([128, KC, 512], BF16)
                wbuf = [P.sb([128, KC, 512], BF16) for _ in range(2)]
                qpT = P.sb([128, 16, 512], F32)
                sk_t = P.sb([128, 16, 128], F32)
                P.dma("sp", sk_t[:], subk)
                pq_ = [P.ps() for _ in range(2)]
                ps_ = [P.ps() for _ in range(2)]
                ptf = P.ps()
                s12 = [P.sb([128, 16, 128], F32) for _ in range(2)]
                v16 = P.sb([128, 16, 16], F32)
                tmp128 = P.sb([128, 128], F32)
                cand = P.sb([128, 8, 256], F32)
                tmpc = P.sb([128, 256], F32)
                t16 = P.sb([128, 8, 16], F32)
                e16 = P.sb([128, 8, 16], F32)
                zz = P.sb([128, 8], F32)
                mlz = P.sb([128, 8], F32)
                rc3 = P.sb([128, 3, 8, 16], F32)
                rcT = [P.sb([128, 3, 128], F32) for _ in range(2)]
                wcnt = 0
                for grp in ogroups:
                    N = len(grp) * 128
                    c0 = grp[0] * 128
                    P.dma("sp", hn2[:, :, 0:N], HN2T[:, c0:c0 + N].rearrange("(c p) n -> p c n", p=128), ikey="HN2T")
                    for b in range(4):
                        wb = wbuf[wcnt % 2]
                        wcnt += 1
                        load_w(wb, w_pq[:, b * 512:(b + 1) * 512], 512)
                        wk = wkeys(wb)
                        for jj in range(4):
                            j = b * 4 + jj
                            p_ = pq_[j % 2]
                            for c in range(KC):
                                P.mm(p_[:, 0:N], wb[:, c, jj * 128:(jj + 1) * 128], hn2[:, c, 0:N], start=(c == 0), stop=(c == KC - 1),
                                     rkeys=[hn2] + wk)
                            P.copy("act", qpT[:, j, 0:N], p_[:, 0:N], okey=(qpT, j))
                    for tt, ti in enumerate(grp):
                        s_t = s12[ti % 2]
                        for jb in range(4):
                            p_ = ps_[jb % 2]
                            for jj in range(4):
                                j = jb * 4 + jj
                                P.mm(p_[:, jj * 128:(jj + 1) * 128], qpT[:, j, tt * 128:(tt + 1) * 128], sk_t[:, j, :], rkeys=[(qpT, j), sk_t])
                            P.copy("act", s_t[:, jb * 4:(jb + 1) * 4, :].rearrange("p j k -> p (j k)"), p_[:, 0:512])
                        P.dma("sp", S12[ti * 128:(ti + 1) * 128], s_t[:], okey=("S12", None))
                        for j in range(16):
                            P.max8(v16[:, j, 0:8], s_t[:, j, :])
                            P.mrep(tmp128[:], v16[:, j, 0:8], s_t[:, j, :], -3.0e38)
                            P.max8(v16[:, j, 8:16], tmp128[:])
                        v16v = v16[:].rearrange("p (h two) k -> p h two k", two=2)
                        for h in range(8):
                            P.tt("dve", cand[:, h, :].rearrange("p (a b) -> p a b", b=16),
                                 v16[:, 2 * h, :].unsqueeze(2).to_broadcast([128, 16, 16]),
                                 v16[:, 2 * h + 1, :].unsqueeze(1).to_broadcast([128, 16, 16]), ALU.add)
                        for h in range(8):
                            P.max8(t16[:, h, 0:8], cand[:, h, :])
                            P.mrep(tmpc[:], t16[:, h, 0:8], cand[:, h, :], -3.0e38)
                            P.max8(t16[:, h, 8:16], tmpc[:])
                        P.tt("dve", e16[:], t16[:], t16[:, :, 0:1].to_broadcast([128, 8, 16]), ALU.subtract)
                        P.act(e16[:], e16[:], AF.Exp)
                        P.reduce(zz[:], e16[:], ALU.add)
                        P.act(mlz[:], zz[:], AF.Ln)
                        P.tt("dve", mlz[:], mlz[:], t16[:, :, 0], ALU.add)
                        P.copy("dve", rc3[:, 0, :, :], v16v[:, :, 0, :])
                        P.tt("dve", rc3[:, 1, :, :], t16[:, :, 15:16].to_broadcast([128, 8, 16]), rc3[:, 0, :, :], ALU.subtract)
                        P.tt("dve", rc3[:, 2, :, :], rc3[:, 0, :, :], mlz[:].unsqueeze(2).to_broadcast([128, 8, 16]), ALU.subtract)
                        for q in range(3):
                            P.tr(ptf[:, q * 128:(q + 1) * 128], rc3[:, q, :, :].rearrange("p h a -> p (h a)"), idf[:])
                        r_T = rcT[ti % 2]
                        P.copy("act", r_T[:].rearrange("p q t -> p (q t)"), ptf[:, 0:384])
                        P.dma("sp", RC[ti], r_T[:], okey=("RC", None))
                P.flush()
            P.st = es

            with contextlib.ExitStack() as st:
                P.st = st
                TB = 32
                s1r = [P.sb([128, TB, 128], F32) for _ in range(2)]
                s2r = [P.sb([128, TB, 128], F32) for _ in range(2)]
                rct = [P.sb([128, 3, 128], F32) for _ in range(2)]
                o1 = [P.sb([128, 128], BF16) for _ in range(4)]
                ee = [P.sb([128, 128], F32) for _ in range(4)]
                rr = [P.sb([128, 128], BF16) for _ in range(4)]
                gst = [P.sb([128, 128, 128], BF16) for _ in range(2)]
                pg_ = [P.ps() for _ in range(2)]
                kk = 0
                for ti in otiles:
                    rc_ = rct[ti % 2]
                    g_s = gst[ti % 2]
                    P.dma("sp", rc_[:], RC[ti], ikey="RC")
                    for tb in range(128 // TB):
                        t0 = ti * 128 + tb * TB
                        a1 = s1r[tb % 2]
                        a2 = s2r[tb % 2]
                        for half, dst in ((0, a1), (1, a2)):
                            for h in range(8):
                                src = S12[t0:t0 + TB, 2 * h + half, :]
                                P.dma("sp", dst[h * 16:(h + 1) * 16, :, :], src.unsqueeze(0).to_broadcast([16, TB, 128]), ikey="S12", okey=(dst, None))
                        for tq in range(0, TB, 4):
                            p_ = pg_[(kk) % 2]
                            kk += 1
                            for u4 in range(4):
                                tl = tq + u4
                                t = tb * TB + tl
                                o_ = o1[u4]
                                e_ = ee[u4]
                                r_ = rr[u4]
                                P.ts("dve", o_[:], a1[:, tl, :], rc_[:, 0, t:t + 1], None, op0=ALU.is_equal)
                                P.act(e_[:], a2[:, tl, :], AF.Exp, bias=rc_[:, 2, t:t + 1])
                                P.stt(r_[:], a2[:, tl, :], rc_[:, 1, t:t + 1], e_[:], ALU.is_ge, ALU.mult)
                                P.mm(p_[:, u4 * 128:(u4 + 1) * 128], o_[:], r_[:])
                            tbase = tb * TB + tq
                            P.copy("act", g_s[:, :, tbase:tbase + 4].rearrange("p i t -> p t i"),
                                   p_[:, 0:512].rearrange("p (t i) -> p t i", i=128))
                    P.dma("sp", GALL[:, :, ti * 128:(ti + 1) * 128], g_s[:], okey=("GALL", None))
                P.flush()
            P.st = es

        if "E" in stages:
            with contextlib.ExitStack() as st:
                P.st = st
                NCH = PEER_KEYS
                EB = 4
                hn2 = P.sb([128, KC, 512], BF16)
                oacc = P.sb([128, 4, D], F32)
                ub = [P.sb([128, KC, 128], BF16) for _ in range(3)]
                vb = [P.sb([128, EB, D], BF16) for _ in range(2)]
                coef = [P.sb([128, EB, 512], BF16) for _ in range(2)]
                gl = [P.sb([128, 512], BF16) for _ in range(2)]
                gc = [P.sb([128, 512], BF16) for _ in range(2)]
                pa_ = [P.ps() for _ in range(2)]
                pv_ = [P.ps() for _ in range(4)]
                ucnt = 0
                vcnt = 0
                pcnt = 0
                DH = 2048 if D % 2048 == 0 else D
                for grp in ogroups:
                    ng = len(grp)
                    N = ng * 128
                    c0 = grp[0] * 128
                    P.dma("sp", hn2[:, :, 0:N], HN2T[:, c0:c0 + N].rearrange("(c p) n -> p c n", p=128), ikey="HN2T")
                    for tt, ti in enumerate(grp):
                        P.dma("sp", oacc[:, tt, :], H2[ti * 128:(ti + 1) * 128, :], ikey="H2", okey=(oacc, tt))
                    def v_load(eb):
                        v_b = vb[eb % 2]
                        vsrc = VB16[eb * EB * 128:(eb + 1) * EB * 128, :].rearrange("(cc p) d -> p cc d", p=128)
                        P.dma("sp", v_b[:], vsrc, ikey="VB16")

                    def u_phase(eb):
                        nonlocal ucnt
                        cf = coef[eb % 2]
                        for cc in range(EB):
                            c = eb * EB + cc
                            u_b = ub[ucnt % 3]
                            g_l = gl[ucnt % 2]
                            g_c = gc[ucnt % 2]
                            p_ = pa_[ucnt % 2]
                            ucnt += 1
                            P.dma("sp", u_b[:].rearrange("p c e -> p (c e)"), UB16[c], ikey="UB16")
                            P.dma("sp", g_c[:, 0:N], GALL[c][:, c0:c0 + N], ikey="GALL")
                            for dc in range(KC):
                                P.mm(p_[:, 0:N], u_b[:, dc, :], hn2[:, dc, 0:N], start=(dc == 0), stop=(dc == KC - 1))
                            P.act(g_l[:, 0:N], p_[:, 0:N], AF.Gelu)
                            P.tt("dve", cf[:, cc, 0:N], g_l[:, 0:N], g_c[:, 0:N], ALU.mult, okey=(cf, cc))

                    def v_phase(eb):
                        nonlocal pcnt
                        cf = coef[eb % 2]
                        v_b = vb[eb % 2]
                        for tt in range(ng):
                            for db in range(D // 512):
                                pv = pv_[pcnt % 4]
                                pcnt += 1
                                for cc in range(EB):
                                    P.mm(pv[:, 0:512], cf[:, cc, tt * 128:(tt + 1) * 128], v_b[:, cc, db * 512:(db + 1) * 512],
                                         start=(cc == 0), stop=(cc == EB - 1), rkeys=[(cf, cc), v_b])
                                P.tt("dve", oacc[:, tt, db * 512:(db + 1) * 512], pv[:, 0:512], oacc[:, tt, db * 512:(db + 1) * 512], ALU.add,
                                     okey=(oacc, tt), rkeys=[pv, (oacc, tt)])

                    nE = NCH // EB
                    v_load(0)
                    u_phase(0)
                    for eb in range(nE):
                        if eb + 1 < nE:
                            v_load(eb + 1)
                            u_phase(eb + 1)
                        v_phase(eb)
                    for tt, ti in enumerate(grp):
                        P.dma("sp", y[ti * 128:(ti + 1) * 128, :], oacc[:, tt, :], ikey=(oacc, tt), okey=("y", None))
                P.flush()
            P.st = es

        if dbg:
            for nm, ap_ in (("MIXT", MIXT), ("UTp", UTp), ("UTs", UTs), ("QT", QT), ("QIT", QIT), ("WI", WI), ("KT", KT), ("KIT", KIT),
                            ("Vc", Vc), ("H2", H2), ("HN2T", HN2T), ("S12", S12), ("GALL", GALL), ("MKT", MKT), ("MV", MV)):
                if nm in dbg:
                    o_ = dout("dbg_" + nm, list(ap_.shape), ap_.dtype)
                    P.dma("sp", o_, ap_)
        P.flush()
    return nc


def _rope_table(pos):
    half = 16
    inv_freq = np.power(np.float32(ROPE_THETA), -np.arange(half, dtype=np.float32) / np.float32(half)).astype(np.float32)
    ang = pos.astype(np.float32)[:, None] * inv_freq[None, :]
    return np.concatenate([np.cos(ang), np.sin(ang)], axis=1).astype(np.float32)


def host_prep(inp, cfg):
    D, KC, CCH, CC, NH, NKV, NP, NT, IH, SEQ, PAST, SS, MEMT = (cfg[k] for k in (
        "D", "KC", "CCH", "CC", "NH", "NKV", "NP", "NT", "IH", "SEQ", "PAST", "SS", "MEMT"))
    DS = cfg["DS"]
    f = lambda a: np.ascontiguousarray(a, dtype=np.float32)
    half = SEQ // 2
    w_in = inp["w_in"][0]
    OFF_Q = 2 * CCH
    OFF_K = OFF_Q + NH * 128
    OFF_V = OFF_K + NKV * 128
    OFF_QI = OFF_V + NKV * 128
    OFF_KI = OFF_QI + IH * 128
    OFF_WI = OFF_KI + 128
    a_ = w_in[:, :CCH].reshape(D, CC, 128)
    g_ = w_in[:, CCH:2 * CCH].reshape(D, CC, 128)
    w_glu = f(np.stack([a_, g_], axis=2).reshape(D, 2 * CCH))
    shared = dict(
        w_glu=w_glu,
        w_q=f(w_in[:, OFF_Q:OFF_K]),
        w_qi=f(w_in[:, OFF_QI:OFF_KI]),
        w_wi=f(w_in[:, OFF_WI:OFF_WI + IH]),
        w_kv=f(np.concatenate([w_in[:, OFF_K:OFF_V], w_in[:, OFF_V:OFF_QI], w_in[:, OFF_KI:OFF_WI]], axis=1)),
        w_out=f(inp["w_out"][0]),
        w_qm=f(inp["w_q_mem"][0]), w_km=f(inp["w_k_mem"][0]), w_vm=f(inp["w_v_mem"][0]), w_om=f(inp["w_o_mem"][0]),
        w_pq=f(inp["peer_wq"][0]),
        g_mix=f(inp["norm_mix_g"]), g_memn=f(inp["norm_mem_g"]), g_ffn=f(inp["norm_ffn_g"]), g_mem=f(inp["mem_norm_g"]),
        g_q=f(inp["q_norm_g"]), g_k=f(inp["k_norm_g"]), g_mq=f(inp["mem_q_norm_g"]), g_mk=f(inp["mem_k_norm_g"]),
        dww=f(inp["dw_w"][0].reshape(31, CC, 128).transpose(2, 1, 0)),
        dwb=f(inp["dw_b"][0].reshape(CC, 128).T), lng=f(inp["conv_ln_g"][0].reshape(CC, 128).T),
        lnb=f(inp["conv_ln_b"][0].reshape(CC, 128).T),
        vtab=f(inp["peer_v"][0]),
        c_idb=np.eye(128).astype(ml_dtypes.bfloat16), c_idf=np.eye(128, dtype=np.float32),
        c_oneb=np.ones((128, 128)).astype(ml_dtypes.bfloat16), c_onef=np.ones((128, 128), dtype=np.float32),
    )
    sk = np.stack([inp["peer_sub_k1"][0], inp["peer_sub_k2"][0]], axis=1)
    shared["subk"] = f(sk.reshape(16, 128, 128).transpose(2, 0, 1))
    u = inp["peer_u"][0]
    shared["uT"] = f(u.reshape(128, 128, KC, 128).transpose(0, 3, 2, 1).reshape(128, 128, KC * 128))
    kcs = (np.arange(SS) // 64).astype(np.float32)
    kcs[PAST + DS:] = 1.0e9
    shared["kc_s"] = kcs[None, :]
    shared["rope_s"] = _rope_table(PAST + np.arange(128))
    maps = []
    for c in range(8):
        b, hf = c // 2, c % 2
        xb = inp["x_prompt"][b]
        own = xb[hf * half:(hf + 1) * half]
        oth = xb[(1 - hf) * half:(2 - hf) * half]
        pos = np.concatenate([hf * half + np.arange(half), (1 - hf) * half + np.arange(half)])
        m = dict(shared)
        m["xctx"] = f(np.concatenate([own, oth], axis=0))
        m["xhalo"] = f(xb[half - 128:half]) if hf == 1 else np.zeros((128, D), np.float32)
        xsp = np.zeros((2, 128, D), np.float32)
        for s in range(2):
            xsp[s, :DS] = inp["x_sample"][2 * c + s]
        m["xsp"] = xsp
        m["mem"] = f(inp["mem_prompt"][b])
        m["ckT"] = f(np.stack([inp["cache_k"][0, 2 * c + s].transpose(2, 1, 0) for s in range(2)]))
        m["cv"] = f(np.stack([inp["cache_v"][0, 2 * c + s].reshape(PAST, NKV * 128) for s in range(2)]))
        m["ckiT"] = f(np.stack([inp["cache_k_idx"][0, 2 * c + s].T for s in range(2)]))
        m["stT"] = f(np.stack([inp["state_conv"][0, 2 * c + s].T for s in range(2)]))
        m["cmkT"] = f(np.stack([inp["cache_mem_k"][0, 2 * c + s].transpose(2, 1, 0) for s in range(2)]))
        m["cmv"] = f(np.stack([inp["cache_mem_v"][0, 2 * c + s].reshape(MEMT, 512) for s in range(2)]))
        m["rope_c"] = _rope_table(pos)
        m["kc_p"] = (pos // 64).astype(np.float32)[None, :]
        q = np.zeros((128, NT), np.float32)
        for i in range(NP):
            q[:, i] = (hf * half + i * 128 + np.arange(128)) // 64
        q[:, NP:] = PAST // 64
        m["qch"] = q
        maps.append(m)
    return maps


def assemble(res, cfg):
    D, CCH, NKV, NP, SEQ, DS, MEMT, B, DB = (cfg[k] for k in ("D", "CCH", "NKV", "NP", "SEQ", "DS", "MEMT", "B", "DB"))
    half = SEQ // 2
    y_p = np.zeros((B, SEQ, D), np.float32)
    y_s = np.zeros((DB, DS, D), np.float32)
    k_p = np.zeros((1, B, SEQ, NKV, 128), np.float32)
    v_p = np.zeros_like(k_p)
    ki_p = np.zeros((1, B, SEQ, 128), np.float32)
    conv_p = np.zeros((1, B, 30, CCH), np.float32)
    mk_p = np.zeros((1, B, MEMT, 4, 128), np.float32)
    mv_p = np.zeros_like(mk_p)
    k_s = np.zeros((1, DB, DS, NKV, 128), np.float32)
    v_s = np.zeros_like(k_s)
    ki_s = np.zeros((1, DB, DS, 128), np.float32)
    conv_s = np.zeros((1, DB, 30, CCH), np.float32)
    for c in range(8):
        r = res[c]
        b, hf = c // 2, c % 2
        y_p[b, hf * half:(hf + 1) * half] = r["y"][:NP * 128]
        if hf == 0:
            k_p[0, b] = r["o_k"].reshape(SEQ, NKV, 128)
            v_p[0, b] = r["o_v"].reshape(SEQ, NKV, 128)
            ki_p[0, b] = r["o_ki"]
            mk_p[0, b] = r["o_mk"].reshape(MEMT, 4, 128)
            mv_p[0, b] = r["o_mv"].reshape(MEMT, 4, 128)
        else:
            conv_p[0, b] = r["o_conv"]
        for s in range(2):
            q = 2 * c + s
            y_s[q] = r["y"][(NP + s) * 128:(NP + s) * 128 + DS]
            k_s[0, q] = r["o_ks"][s, :DS].reshape(DS, NKV, 128)
            v_s[0, q] = r["o_vs"][s, :DS].reshape(DS, NKV, 128)
            ki_s[0, q] = r["o_kis"][s, :DS]
            conv_s[0, q] = r["o_convs"][s]
    return (y_p, y_s, k_p, v_p, ki_p, conv_p, mk_p, mv_p, k_s, v_s, ki_s, conv_s)


def kernel(**inputs):
    cfg = mkcfg()
    inp = {k: np.asarray(v) for k, v in inputs.items()}
    maps = host_prep(inp, cfg)
    nc = build(cfg)
    res = run_bass_kernel_spmd(nc, maps, core_ids=list(range(8)))
    return assemble(res.results, cfg)
```

```python
import contextlib
import math
import numpy as np
import ml_dtypes
import concourse.bass as bass
import concourse.mybir as mybir
from concourse.bass_utils import run_bass_kernel_spmd

F32 = mybir.dt.float32
BF16 = mybir.dt.bfloat16
ALU = mybir.AluOpType
AF = mybir.ActivationFunctionType
AX = mybir.AxisListType

EPS = 1e-6
ROPE_THETA = 500000.0
NEG = -1.0e30


class Prog:
    def __init__(self, nc, es):
        self.nc = nc
        self.es = es
        self.st = es
        self.ops = []
        self.engs = {"pe": nc.tensor, "act": nc.scalar, "dve": nc.vector, "pool": nc.gpsimd, "sp": nc.sync}
        self.n_t = 0
        self.eng_sem = {}
        self.eng_cnt = {}
        self.pool = {}
        self.npool = {}
        self.key_sem = {}
        self.fence_sem = None
        self.fence_cnt = 0
        self.tot_ops = 0
        self.tot_wait = 0
        self.free_sems = []
        self.n_dsem = 0

    def sb(self, shape, dt=F32, name=None):
        self.n_t += 1
        return self.st.enter_context(self.nc.sbuf_tensor(name or f"sb{self.n_t}", list(shape), dt))

    def ps(self, shape=(128, 512), dt=F32, name=None):
        self.n_t += 1
        return self.st.enter_context(self.nc.psum_tensor(name or f"ps{self.n_t}", list(shape), dt))

    @staticmethod
    def key(x):
        def nm(a):
            if isinstance(a, str):
                return a
            t = getattr(a, "tensor", None)
            return t.name if t is not None else a.name
        if isinstance(x, tuple):
            return (nm(x[0]), x[1])
        return (nm(x), None)

    def op(self, eng, fn, reads=(), writes=(), dma=False):
        rk = []
        for r in reads:
            if r is None or isinstance(r, (int, float)):
                continue
            k = self.key(r)
            if k not in rk:
                rk.append(k)
        wk = []
        for w in writes:
            k = self.key(w)
            if k not in wk:
                wk.append(k)
        self.ops.append(dict(eng=eng, fn=fn, reads=rk, writes=wk, dma=dma))

    def _esem(self, e):
        if e not in self.eng_sem:
            self.eng_sem[e] = self.es.enter_context(self.nc.semaphore(f"s_{e}"))
            self.eng_cnt[e] = 0
        return self.eng_sem[e]

    def flush(self):
        nc = self.nc
        ops = self.ops
        state = {}
        deps = [None] * len(ops)

        def confl(k):
            ent = state.get(k[0])
            if not ent:
                return []
            if k[1] is None:
                return list(ent.values())
            return [ent[s_] for s_ in (k[1], None) if s_ in ent]

        joined = [False] * len(ops)
        for i, o in enumerate(ops):
            d = set()
            for k in o["reads"]:
                for st in confl(k):
                    d.update(st[0])
            joins = {}
            for k in o["writes"]:
                own = state.get(k[0], {}).get(k[1])
                joinable = bool(o["dma"] and own and own[0] and all(ops[j]["dma"] for j in own[0]) and not own[1])
                joins[k] = joinable
                for st in confl(k):
                    d.update(st[1])
                    if not (joinable and st is own):
                        d.update(st[0])
            if o["dma"]:
                joined[i] = joins[o["writes"][0]]
            for k in o["reads"]:
                st = state.setdefault(k[0], {}).setdefault(k[1], [[], []])
                st[1].append(i)
            for k in o["writes"]:
                ent = state.setdefault(k[0], {})
                if joins[k]:
                    ent[k[1]][0].append(i)
                else:
                    if k[1] is None:
                        ent.clear()
                    ent[k[1]] = [[i], []]
            d.discard(i)
            if o["eng"] == "pe":
                d = {j for j in d if not (ops[j]["eng"] == "pe" and not ops[j]["dma"])}
            deps[i] = d
        need = [False] * len(ops)
        for d in deps:
            for j in d:
                need[j] = True
        last_on = {}
        for i, o in enumerate(ops):
            if not o["dma"]:
                last_on[o["eng"]] = i
        for i in last_on.values():
            need[i] = True

        sig = [None] * len(ops)
        waited = {}
        for i, o in enumerate(ops):
            e = o["eng"]
            eo = self.engs[e]
            wl = {}
            for j in deps[i]:
                s, v = sig[j]
                kk = id(s)
                if kk not in wl or wl[kk][1] < v:
                    wl[kk] = (s, v)
            pre = None
            if o["dma"]:
                k = o["writes"][0]
                name = k[0]
                pl = self.pool.get(name)
                if pl is None:
                    n = self.npool.get(name, 2)
                    sems_, cnt_ = [], []
                    for q in range(n):
                        if self.free_sems:
                            s_, c_ = self.free_sems.pop()
                        else:
                            self.n_dsem += 1
                            s_, c_ = self.es.enter_context(nc.semaphore(f"dma{self.n_dsem}")), 0
                        sems_.append(s_)
                        cnt_.append(c_)
                    pl = dict(sems=sems_, cnt=cnt_, last=[None] * n, rr=0)
                    self.pool[name] = pl
                idx = None
                if joined[i] and k in self.key_sem and pl["last"][self.key_sem[k]] == k:
                    idx = self.key_sem[k]
                else:
                    idx = pl["rr"]
                    pl["rr"] = (pl["rr"] + 1) % len(pl["sems"])
                    if pl["cnt"][idx] > 0:
                        s = pl["sems"][idx]
                        kk = id(s)
                        if kk not in wl or wl[kk][1] < pl["cnt"][idx]:
                            wl[kk] = (s, pl["cnt"][idx])
                self.key_sem[k] = idx
                pl["last"][idx] = k
                pre = (pl, idx)
            for kk, (s, v) in wl.items():
                if waited.get((e, kk), -1) >= v:
                    continue
                waited[(e, kk)] = v
                eo.wait_ge(s, v)
                self.tot_wait += 1
            ins = o["fn"](eo)
            if o["dma"]:
                pl, idx = pre
                pl["cnt"][idx] += 16
                ins.then_inc(pl["sems"][idx], 16)
                sig[i] = (pl["sems"][idx], pl["cnt"][idx])
            elif need[i]:
                s = self._esem(e)
                self.eng_cnt[e] += 1
                ins.then_inc(s, 1)
                sig[i] = (s, self.eng_cnt[e])
        self.tot_ops += len(ops)
        self.ops = []
        if self.fence_sem is None:
            self.fence_sem = self.es.enter_context(nc.semaphore("fence"))
        for e, s in self.eng_sem.items():
            if self.eng_cnt[e] > 0:
                nc.sync.wait_ge(s, self.eng_cnt[e])
        for pl in self.pool.values():
            for s, c in zip(pl["sems"], pl["cnt"]):
                if c > 0:
                    nc.sync.wait_ge(s, c)
        for pl in self.pool.values():
            for s, c in zip(pl["sems"], pl["cnt"]):
                self.free_sems.append((s, c))
        self.pool = {}
        self.key_sem = {}
        self.fence_cnt += 1
        nc.sync.drain().then_inc(self.fence_sem, 1)
        for e in ("pe", "act", "dve", "pool"):
            self.engs[e].wait_ge(self.fence_sem, self.fence_cnt)

    def dma(self, q, out, in_, okey=None, ikey=None, **kw):
        self.op(q, lambda e: e.dma_start(out=out, in_=in_, **kw), reads=[ikey or in_], writes=[okey or out], dma=True)

    def cdma(self, out, in_, okey=None, ikey=None):
        n = out.shape[-1]
        if n > 2048:
            d = 2048
            while n % d:
                d //= 2
            names = " ".join(f"a{i}" for i in range(len(out.shape) - 1))
            pat = f"{names} (x d) -> {names} x d"
            self.dma("pool", out.rearrange(pat, d=d), in_.rearrange(pat, d=d), okey=okey or out, ikey=ikey or in_)
        else:
            self.dma("pool", out, in_, okey=okey, ikey=ikey)

    def mm(self, out, lhsT, rhs, start=True, stop=True, okey=None, rkeys=None):
        self.op("pe", lambda e: e.matmul(out, lhsT, rhs, start=start, stop=stop), reads=rkeys or [lhsT, rhs], writes=[okey or out])

    def tr(self, out, in_, ident, okey=None, ikey=None):
        self.op("pe", lambda e: e.transpose(out, in_, ident), reads=[ikey or in_, ident], writes=[okey or out])

    def act(self, out, in_, func, scale=1.0, bias=0.0, accum_out=None, okey=None, ikey=None):
        rd = [ikey or in_] + [x for x in (scale, bias) if not isinstance(x, (int, float))]
        wr = [okey or out] + ([accum_out] if accum_out is not None else [])
        if accum_out is not None:
            self.op("act", lambda e: e.activation(out, in_, func, bias=bias, scale=scale, accum_out=accum_out), reads=rd, writes=wr)
        else:
            self.op("act", lambda e: e.activation(out, in_, func, bias=bias, scale=scale), reads=rd, writes=wr)

    def ts(self, eng, out, in0, s1, s2=None, op0=ALU.mult, op1=None, accum_out=None, okey=None, ikey=None):
        rd = [ikey or in0] + [x for x in (s1, s2) if x is not None and not isinstance(x, (int, float))]
        wr = [okey or out] + ([accum_out] if accum_out is not None else [])
        kw = {}
        if op1 is not None:
            kw["op1"] = op1
        if accum_out is not None:
            kw["accum_out"] = accum_out
        self.op(eng, lambda e: e.tensor_scalar(out, in0, s1, s2, op0, **kw), reads=rd, writes=wr)

    def tt(self, eng, out, in0, in1, op, okey=None, rkeys=None):
        self.op(eng, lambda e: e.tensor_tensor(out, in0, in1, op), reads=rkeys or [in0, in1], writes=[okey or out])

    def stt(self, out, in0, scalar, in1, op0, op1, okey=None, rkeys=None):
        rd = list(rkeys or [in0, in1]) + ([scalar] if not isinstance(scalar, (int, float)) else [])
        self.op("dve", lambda e: e.scalar_tensor_tensor(out, in0, scalar, in1, op0, op1), reads=rd, writes=[okey or out])

    def copy(self, eng, out, in_, okey=None, ikey=None):
        if eng == "act":
            self.op("act", lambda e: e.copy(out, in_), reads=[ikey or in_], writes=[okey or out])
        else:
            self.op(eng, lambda e: e.tensor_copy(out, in_), reads=[ikey or in_], writes=[okey or out])

    def max8(self, out, in_, okey=None):
        self.op("dve", lambda e: e.max(out, in_), reads=[in_], writes=[okey or out])

    def mrep(self, out, in_to_replace, in_values, imm, rkeys=None):
        self.op("dve", lambda e: e.match_replace(out, in_to_replace, in_values, imm), reads=rkeys or [in_to_replace, in_values], writes=[out])

    def memset(self, eng, ap, val):
        self.op(eng, lambda e: e.memset(ap, val), reads=[], writes=[ap])

    def recip(self, out, in_, okey=None):
        self.op("dve", lambda e: e.reciprocal(out, in_), reads=[in_], writes=[okey or out])

    def reduce(self, out, in_, op, axis=AX.X):
        self.op("dve", lambda e: e.tensor_reduce(out, in_, axis, op), reads=[in_], writes=[out])


def mkcfg(D=4096, SEQ=4096, B=4, DB=16, DS=64, PAST=4096, IH=32, TOPK_MAX=256, MEMT=256):
    c = dict(D=D, SEQ=SEQ, B=B, DB=DB, DS=DS, PAST=PAST, IH=IH, MEMT=MEMT)
    c["KC"] = D // 128
    c["CCH"] = D // 2
    c["CC"] = c["CCH"] // 128
    c["NH"] = (D // 2) // 128
    c["NKV"] = 4
    c["GQ"] = c["NH"] // 4
    c["NP"] = SEQ // 2 // 128
    c["NCX"] = SEQ // 128
    c["NT"] = c["NP"] + 2
    c["TOPK_P"] = min(TOPK_MAX, SEQ // 4)
    c["TOPK_S"] = min(TOPK_MAX, (PAST + DS) // 4)
    c["SS"] = PAST + 128
    c["MH"] = 4
    c["MC"] = MEMT // 128
    return c


PEER_KEYS = 128
PEER_HEADS = 8
PEER_TOPK = 16


def build(cfg, stages=("M", "KV", "MAIN", "B", "C", "D", "E"), dbg=False):
    D, KC, CCH, CC, NH, NKV, GQ, NP, NCX, NT, IH, SEQ, PAST, SS, MEMT, MC = (cfg[k] for k in (
        "D", "KC", "CCH", "CC", "NH", "NKV", "GQ", "NP", "NCX", "NT", "IH", "SEQ", "PAST", "SS", "MEMT", "MC"))
    NTOK = NT * 128
    IDX_SCALE = (IH ** -0.5) * (128 ** -0.5)
    ATT_SCALE = 128 ** -0.5
    nc = bass.Bass("TRN2", target_bir_lowering=False)

    def din(name, shape, dt=F32):
        return nc.dram_tensor(name, list(shape), dt, kind="ExternalInput").ap()

    def dout(name, shape, dt=F32):
        return nc.dram_tensor(name, list(shape), dt, kind="ExternalOutput").ap()

    def dscr(name, shape, dt=F32):
        return nc.dram_tensor(name, list(shape), dt, kind="Internal").ap()

    xctx = din("xctx", [SEQ, D])
    xsp = din("xsp", [2, 128, D])
    xhalo = din("xhalo", [128, D])
    mem = din("mem", [MEMT, D])
    ckT = din("ckT", [2, 128, NKV, PAST])
    cv = din("cv", [2, PAST, NKV * 128])
    ckiT = din("ckiT", [2, 128, PAST])
    stT = din("stT", [2, CCH, 30])
    cmkT = din("cmkT", [2, 128, 4, MEMT])
    cmv = din("cmv", [2, MEMT, 512])
    w_glu = din("w_glu", [D, 2 * CCH])
    w_q = din("w_q", [D, NH * 128])
    w_qi = din("w_qi", [D, IH * 128])
    w_wi = din("w_wi", [D, IH])
    w_kv = din("w_kv", [D, 1152])
    w_out = din("w_out", [D, D])
    w_qm = din("w_qm", [D, 512])
    w_km = din("w_km", [D, 512])
    w_vm = din("w_vm", [D, 512])
    w_om = din("w_om", [512, D])
    w_pq = din("w_pq", [D, 2048])
    subk = din("subk", [128, 16, 128])
    uT = din("uT", [128, 128, KC * 128])
    vtab = din("vtab", [PEER_KEYS * PEER_KEYS, D])
    g_mix = din("g_mix", [1, D])
    g_memn = din("g_memn", [1, D])
    g_ffn = din("g_ffn", [1, D])
    g_mem = din("g_mem", [1, D])
    g_q = din("g_q", [1, 128])
    g_k = din("g_k", [1, 128])
    g_mq = din("g_mq", [1, 128])
    g_mk = din("g_mk", [1, 128])
    dww = din("dww", [128, CC, 31])
    dwb = din("dwb", [128, CC])
    lng = din("lng", [128, CC])
    lnb = din("lnb", [128, CC])
    rope_c = din("rope_c", [SEQ, 32])
    rope_s = din("rope_s", [128, 32])
    kc_p = din("kc_p", [1, SEQ])
    kc_s = din("kc_s", [1, SS])
    qch = din("qch", [128, NT])
    c_idb = din("c_idb", [128, 128], BF16)
    c_idf = din("c_idf", [128, 128])
    c_oneb = din("c_oneb", [128, 128], BF16)
    c_onef = din("c_onef", [128, 128])

    y = dout("y", [NTOK, D])
    o_k = dout("o_k", [SEQ, NKV * 128])
    o_v = dout("o_v", [SEQ, NKV * 128])
    o_ki = dout("o_ki", [SEQ, 128])
    o_conv = dout("o_conv", [30, CCH])
    o_mk = dout("o_mk", [MEMT, 512])
    o_mv = dout("o_mv", [MEMT, 512])
    o_ks = dout("o_ks", [2, 128, NKV * 128])
    o_vs = dout("o_vs", [2, 128, NKV * 128])
    o_kis = dout("o_kis", [2, 128, 128])
    o_convs = dout("o_convs", [2, 30, CCH])

    UTp = dscr("UTp", [CCH, 128 + NP * 128])
    UTs = dscr("UTs", [2, CCH, 160])
    KT = dscr("KT", [128, NKV, SEQ], BF16)
    Vc = dscr("Vc", [SEQ, NKV * 128], BF16)
    KIT = dscr("KIT", [128, SEQ], BF16)
    KTs = dscr("KTs", [2, 128, NKV, 128], BF16)
    Vs = dscr("Vs", [2, 128, NKV * 128], BF16)
    KITs = dscr("KITs", [2, 128, 128], BF16)
    MKT = dscr("MKT", [128, 4, MEMT], BF16)
    MV = dscr("MV", [MEMT, 512], BF16)
    QT = dscr("QT", [NT, 128, NH, 128], BF16)
    QIT = dscr("QIT", [NT, 128, IH, 128], BF16)
    WI = dscr("WI", [NT, 128, IH])
    MIXT = dscr("MIXT", [D, NTOK], BF16)
    H2 = dscr("H2", [NTOK, D])
    HN2T = dscr("HN2T", [D, NTOK], BF16)
    S12 = dscr("S12", [16, NTOK, 128])
    GALL = dscr("GALL", [128, 128, NTOK], BF16)

    UB16 = dscr("UB16", [128, 128, KC * 128], BF16)
    VB16 = dscr("VB16", [PEER_KEYS * PEER_KEYS, D], BF16)
    dbg_out = {}

    with contextlib.ExitStack() as es:
        P = Prog(nc, es)
        idb = P.sb([128, 128], BF16, "idb")
        idf = P.sb([128, 128], F32, "idf")
        oneb = P.sb([128, 128], BF16, "oneb")
        onef = P.sb([128, 128], F32, "onef")
        P.dma("sp", idb[:], c_idb)
        P.dma("sp", idf[:], c_idf)
        P.dma("sp", oneb[:], c_oneb)
        P.dma("sp", onef[:], c_onef)
        P.flush()

        def bcast_row(dst, src_row, n):
            P.dma("sp", dst, src_row.to_broadcast([128, n]))

        def rstd_from_ss(ss, n, out, tmp):
            P.ts("dve", tmp, ss, 1.0 / n, EPS, op0=ALU.mult, op1=ALU.add)
            P.act(tmp, tmp, AF.Sqrt)
            P.recip(out, tmp)

        def load_w(dst, src, ncols):
            sv = src.rearrange("(c p) n -> p c n", p=128)
            nq = 4 if KC % 4 == 0 else 1
            step = KC // nq
            for q in range(nq):
                P.dma("pool", dst[:, q * step:(q + 1) * step, 0:ncols], sv[:, q * step:(q + 1) * step, :], okey=(dst, q))

        def wkeys(dst):
            return [(dst, q) for q in range(4 if KC % 4 == 0 else 1)]

        class NormT:
            def __init__(self, gsrc):
                self.gbc = P.sb([128, D], F32)
                bcast_row(self.gbc[:], gsrc[0:1, :], D)
                self.sq = P.sb([128, D], BF16)
                self.xs = [P.sb([128, D], BF16) for _ in range(2)]
                self.sm = [P.sb([128, 4], F32) for _ in range(2)]
                self.pt = [P.ps([128, 1024], BF16) for _ in range(2)]
                self.k = 0

            def run(self, x_t, dst_fn):
                k = self.k
                self.k += 1
                sm = self.sm[k % 2]
                xs = self.xs[k % 2]
                P.act(self.sq[:], x_t, AF.Square, accum_out=sm[:, 0:1])
                rstd_from_ss(sm[:, 0:1], D, sm[:, 1:2], sm[:, 2:3])
                P.stt(xs[:], x_t, sm[:, 1:2], self.gbc[:], ALU.mult, ALU.mult)
                nb = 8 if KC % 8 == 0 else KC
                for b0 in range(0, KC, nb):
                    pt = self.pt[(b0 // nb) % 2]
                    for j in range(nb):
                        P.tr(pt[:, j * 128:(j + 1) * 128], xs[:, (b0 + j) * 128:(b0 + j + 1) * 128], idb[:])
                    eng = "act" if (b0 // nb) % 2 == 0 else "dve"
                    P.copy(eng, dst_fn(b0, nb), pt[:, 0:nb * 128].rearrange("p (n t) -> p n t", t=128))

        def head_norm(ps_ap, nh, gain_bc, out_f, sq_t, sm_t):
            P.act(sq_t[:, 0:nh * 128], ps_ap, AF.Square)
            P.reduce(sm_t[:, 0:nh], sq_t[:, 0:nh * 128].rearrange("p (h d) -> p h d", d=128), ALU.add)
            rstd_from_ss(sm_t[:, 0:nh], 128, sm_t[:, 4:4 + nh], sm_t[:, 8:8 + nh])
            P.tt("dve", out_f, ps_ap.rearrange("p (h d) -> p h d", d=128),
                 sm_t[:, 4:4 + nh].unsqueeze(2).to_broadcast([128, nh, 128]), ALU.mult)
            P.tt("dve", out_f, out_f, gain_bc[:, 0:128].unsqueeze(1).to_broadcast([128, nh, 128]), ALU.mult)

        def rope(f, nh, cs, tmp):
            x1 = f[:, :, 0:16]
            x2 = f[:, :, 16:32]
            cosb = cs[:, 0:16].unsqueeze(1).to_broadcast([128, nh, 16])
            sinb = cs[:, 16:32].unsqueeze(1).to_broadcast([128, nh, 16])
            P.tt("dve", tmp[:, 0, 0:nh, :], x1, cosb, ALU.mult)
            P.tt("dve", tmp[:, 1, 0:nh, :], x2, sinb, ALU.mult)
            P.tt("dve", tmp[:, 2, 0:nh, :], x2, cosb, ALU.mult)
            P.tt("dve", tmp[:, 3, 0:nh, :], x1, sinb, ALU.mult)
            P.tt("dve", x1, tmp[:, 0, 0:nh, :], tmp[:, 1, 0:nh, :], ALU.subtract)
            P.tt("dve", x2, tmp[:, 2, 0:nh, :], tmp[:, 3, 0:nh, :], ALU.add)

        def tok_mm(ps_ap, hnT, tcol, wb, ncols, wk):
            for c in range(KC):
                P.mm(ps_ap, hnT[:, c, tcol:tcol + 128], wb[:, c, 0:ncols], start=(c == 0), stop=(c == KC - 1),
                     rkeys=[hnT] + wk)

        if "M" in stages:
            with contextlib.ExitStack() as st:
                P.st = st
                nt = NormT(g_mem)
                wk_b = P.sb([128, KC, 512], BF16)
                wv_b = P.sb([128, KC, 512], BF16)
                load_w(wk_b, w_km, 512)
                load_w(wv_b, w_vm, 512)
                gk = P.sb([128, 128], F32)
                bcast_row(gk[:], g_mk[0:1, :], 128)
                xt = [P.sb([128, D], F32) for _ in range(2)]
                hn = [P.sb([128, KC, 128], BF16) for _ in range(2)]
                pk = P.ps()
                pv = P.ps()
                ptr = P.ps([128, 1024], BF16)
                sq_t = P.sb([128, 512], F32)
                sm_t = P.sb([128, 12], F32)
                kf = P.sb([128, 4, 128], F32)
                kb = P.sb([128, 512], BF16)
                kTt = P.sb([128, 4, 128], BF16)
                vf = P.sb([128, 512], F32)
                vb = P.sb([128, 512], BF16)
                for m in range(MC):
                    x_t = xt[m % 2]
                    h_t = hn[m % 2]
                    P.dma("sp", x_t[:], mem[m * 128:(m + 1) * 128, :])
                    nt.run(x_t[:], lambda c0, n, h_t=h_t: h_t[:, c0:c0 + n, :])
                    tok_mm(pk[:, 0:512], h_t, 0, wk_b, 512, wkeys(wk_b))
                    tok_mm(pv[:, 0:512], h_t, 0, wv_b, 512, wkeys(wv_b))
                    head_norm(pk[:, 0:512], 4, gk, kf[:], sq_t, sm_t)
                    P.dma("sp", o_mk[m * 128:(m + 1) * 128, :], kf[:].rearrange("p h d -> p (h d)"))
                    P.copy("act", kb[:], kf[:].rearrange("p h d -> p (h d)"))
                    for h in range(4):
                        P.tr(ptr[:, h * 128:(h + 1) * 128], kb[:, h * 128:(h + 1) * 128], idb[:])
                    P.copy("dve", kTt[:], ptr[:, 0:512].rearrange("p (h t) -> p h t", t=128))
                    P.dma("sp", MKT[:, :, m * 128:(m + 1) * 128], kTt[:])
                    P.copy("act", vf[:], pv[:, 0:512])
                    P.dma("sp", o_mv[m * 128:(m + 1) * 128, :], vf[:])
                    P.copy("dve", vb[:], pv[:, 0:512])
                    P.dma("sp", MV[m * 128:(m + 1) * 128, :], vb[:])
                P.flush()
            P.st = es

        if "KV" in stages:
            with contextlib.ExitStack() as st:
                P.st = st
                nt = NormT(g_mix)
                wb = P.sb([128, KC, 1152], BF16)
                load_w(wb, w_kv, 1152)
                wk = wkeys(wb)
                gk = P.sb([128, 128], F32)
                bcast_row(gk[:], g_k[0:1, :], 128)
                xt = [P.sb([128, D], F32) for _ in range(2)]
                hn = [P.sb([128, KC, 128], BF16) for _ in range(2)]
                cs = [P.sb([128, 32], F32) for _ in range(2)]
                pk = P.ps()
                pv = P.ps()
                pki = P.ps()
                ptr = P.ps([128, 1024], BF16)
                sq_t = P.sb([128, 512], F32)
                sm_t = P.sb([128, 12], F32)
                rtmp = P.sb([128, 4, 4, 16], F32)
                kf = [P.sb([128, 4, 128], F32) for _ in range(2)]
                kb = P.sb([128, 512], BF16)
                kTt = [P.sb([128, 4, 128], BF16) for _ in range(2)]
                vf = [P.sb([128, 512], F32) for _ in range(2)]
                vb = [P.sb([128, 512], BF16) for _ in range(2)]
                kif = [P.sb([128, 1, 128], F32) for _ in range(2)]
                kib = P.sb([128, 128], BF16)
                kiTt = [P.sb([128, 128], BF16) for _ in range(2)]
                tiles = [("p", i) for i in range(NCX)] + [("s", 0), ("s", 1)]
                for n_, (kind, i) in enumerate(tiles):
                    x_t = xt[n_ % 2]
                    h_t = hn[n_ % 2]
                    c_t = cs[n_ % 2]
                    if kind == "p":
                        P.dma("sp", x_t[:], xctx[i * 128:(i + 1) * 128, :])
                        P.dma("sp", c_t[:], rope_c[i * 128:(i + 1) * 128, :])
                    else:
                        P.dma("sp", x_t[:], xsp[i])
                        P.dma("sp", c_t[:], rope_s)
                    nt.run(x_t[:], lambda c0, n, h_t=h_t: h_t[:, c0:c0 + n, :])
                    for c in range(KC):
                        P.mm(pk[:, 0:512], h_t[:, c, :], wb[:, c, 0:512], start=(c == 0), stop=(c == KC - 1), rkeys=[h_t] + wk)
                    for c in range(KC):
                        P.mm(pv[:, 0:512], h_t[:, c, :], wb[:, c, 512:1024], start=(c == 0), stop=(c == KC - 1), rkeys=[h_t] + wk)
                    for c in range(KC):
                        P.mm(pki[:, 0:128], h_t[:, c, :], wb[:, c, 1024:1152], start=(c == 0), stop=(c == KC - 1), rkeys=[h_t] + wk)
                    kf_t = kf[n_ % 2]
                    head_norm(pk[:, 0:512], 4, gk, kf_t[:], sq_t, sm_t)
                    rope(kf_t, 4, c_t, rtmp)
                    kflat = kf_t[:].rearrange("p h d -> p (h d)")
                    if kind == "p":
                        P.dma("sp", o_k[i * 128:(i + 1) * 128, :], kflat)
                    else:
                        P.dma("sp", o_ks[i], kflat)
                    P.copy("act", kb[:], kflat)
                    for h in range(4):
                        P.tr(ptr[:, h * 128:(h + 1) * 128], kb[:, h * 128:(h + 1) * 128], idb[:])
                    kT_t = kTt[n_ % 2]
                    P.copy("dve", kT_t[:], ptr[:, 0:512].rearrange("p (h t) -> p h t", t=128))
                    if kind == "p":
                        P.dma("sp", KT[:, :, i * 128:(i + 1) * 128], kT_t[:])
                    else:
                        P.dma("sp", KTs[i], kT_t[:])
                    vf_t = vf[n_ % 2]
                    vb_t = vb[n_ % 2]
                    P.copy("act", vf_t[:], pv[:, 0:512])
                    P.copy("dve", vb_t[:], pv[:, 0:512])
                    if kind == "p":
                        P.dma("sp", o_v[i * 128:(i + 1) * 128, :], vf_t[:])
                        P.dma("sp", Vc[i * 128:(i + 1) * 128, :], vb_t[:])
                    else:
                        P.dma("sp", o_vs[i], vf_t[:])
                        P.dma("sp", Vs[i], vb_t[:])
                    ki_t = kif[n_ % 2]
                    P.copy("act", ki_t[:, 0, :], pki[:, 0:128])
                    rope(ki_t, 1, c_t, rtmp)
                    if kind == "p":
                        P.dma("sp", o_ki[i * 128:(i + 1) * 128, :], ki_t[:, 0, :])
                    else:
                        P.dma("sp", o_kis[i], ki_t[:, 0, :])
                    P.copy("act", kib[:], ki_t[:, 0, :])
                    P.tr(ptr[:, 512:640], kib[:], idb[:])
                    kiT_t = kiTt[n_ % 2]
                    P.copy("dve", kiT_t[:], ptr[:, 512:640])
                    if kind == "p":
                        P.dma("sp", KIT[:, i * 128:(i + 1) * 128], kiT_t[:])
                    else:
                        P.dma("sp", KITs[i], kiT_t[:])
                P.flush()
            P.st = es

        own = [("h", -1)] + [("p", i) for i in range(NP)] + [("s", 0), ("s", 1)]
        groups = [own[i:i + 4] for i in range(0, len(own), 4)]

        def tile_index(kind, i):
            return i if kind == "p" else NP + i

        if "MAIN" in stages:
            with contextlib.ExitStack() as st:
                P.st = st
                nt = NormT(g_mix)
                gq = P.sb([128, 128], F32)
                bcast_row(gq[:], g_q[0:1, :], 128)
                for s in range(2):
                    P.dma("sp", UTs[s][:, 2:32], stT[s], okey=("UTs", "st"))
                xt = [P.sb([128, D], F32) for _ in range(2)]
                hnT = P.sb([128, KC, 512], BF16)
                wbuf = [P.sb([128, KC, 512], BF16) for _ in range(2)]
                cst = P.sb([128, 4, 32], F32)
                pa = P.ps()
                pg = P.ps()
                pq = [P.ps() for _ in range(2)]
                ptr = P.ps([128, 1024], BF16)
                sg = P.sb([128, 512], F32)
                ut = [P.sb([128, 512], F32) for _ in range(2)]
                sq_t = P.sb([128, 512], F32)
                sm_t = P.sb([128, 12], F32)
                rtmp = P.sb([128, 4, 4, 16], F32)
                qf = P.sb([128, 4, 128], F32)
                qb = P.sb([128, 512], BF16)
                qTt = [P.sb([128, 4, 128], BF16) for _ in range(2)]
                wis = [P.sb([128, IH], F32) for _ in range(2)]
                wcnt = 0
                xcnt = 0
                for grp in groups:
                    ng = len(grp)
                    N = ng * 128
                    for tt, (kind, i) in enumerate(grp):
                        x_t = xt[xcnt % 2]
                        xcnt += 1
                        if kind == "h":
                            P.dma("sp", x_t[:], xhalo)
                        elif kind == "p":
                            P.dma("sp", x_t[:], xctx[i * 128:(i + 1) * 128, :])
                            P.dma("sp", cst[:, tt, :], rope_c[i * 128:(i + 1) * 128, :], okey=(cst, tt))
                        else:
                            P.dma("sp", x_t[:], xsp[i])
                            P.dma("sp", cst[:, tt, :], rope_s, okey=(cst, tt))
                        nt.run(x_t[:], lambda c0, n, tt=tt: hnT[:, c0:c0 + n, tt * 128:(tt + 1) * 128])
                    for b in range(CC // 2):
                        wb = wbuf[wcnt % 2]
                        wcnt += 1
                        load_w(wb, w_glu[:, b * 512:(b + 1) * 512], 512)
                        wk = wkeys(wb)
                        for s in range(2):
                            j = 2 * b + s
                            for c in range(KC):
                                P.mm(pa[:, 0:N], wb[:, c, (2 * s) * 128:(2 * s + 1) * 128], hnT[:, c, 0:N],
                                     start=(c == 0), stop=(c == KC - 1), rkeys=[hnT] + wk)
                            for c in range(KC):
                                P.mm(pg[:, 0:N], wb[:, c, (2 * s + 1) * 128:(2 * s + 2) * 128], hnT[:, c, 0:N],
                                     start=(c == 0), stop=(c == KC - 1), rkeys=[hnT] + wk)
                            P.act(sg[:, 0:N], pg[:, 0:N], AF.Sigmoid)
                            u_t = ut[j % 2]
                            P.tt("dve", u_t[:, 0:N], pa[:, 0:N], sg[:, 0:N], ALU.mult)
                            for tt, (kind, i) in enumerate(grp):
                                src = u_t[:, tt * 128:(tt + 1) * 128]
                                if kind == "h":
                                    P.dma("sp", UTp[j * 128:(j + 1) * 128, 0:128], src, okey=("UTp", None))
                                elif kind == "p":
                                    P.dma("sp", UTp[j * 128:(j + 1) * 128, 128 + i * 128:128 + (i + 1) * 128], src, okey=("UTp", None))
                                else:
                                    P.dma("sp", UTs[i][j * 128:(j + 1) * 128, 32:160], src, okey=("UTs", "tok"))
                    for which, nblk, wsrc, dst in (("q", NH // 4, w_q, QT), ("qi", IH // 4, w_qi, QIT)):
                        for b in range(nblk):
                            wb = wbuf[wcnt % 2]
                            wcnt += 1
                            load_w(wb, wsrc[:, b * 512:(b + 1) * 512], 512)
                            wk = wkeys(wb)
                            real = [(tt, kind, i) for tt, (kind, i) in enumerate(grp) if kind != "h"]
                            for n2, (tt, kind, i) in enumerate(real):
                                if n2 == 0:
                                    tok_mm(pq[n2 % 2][:, 0:512], hnT, tt * 128, wb, 512, wk)
                                if n2 + 1 < len(real):
                                    tok_mm(pq[(n2 + 1) % 2][:, 0:512], hnT, real[n2 + 1][0] * 128, wb, 512, wk)
                                ti = tile_index(kind, i)
                                pq_t = pq[n2 % 2]
                                if which == "q":
                                    head_norm(pq_t[:, 0:512], 4, gq, qf[:], sq_t, sm_t)
                                else:
                                    P.copy("act", qf[:].rearrange("p h d -> p (h d)"), pq_t[:, 0:512])
                                rope(qf, 4, cst[:, tt, :], rtmp)
                                P.copy("act", qb[:], qf[:].rearrange("p h d -> p (h d)"))
                                for h in range(4):
                                    P.tr(ptr[:, h * 128:(h + 1) * 128], qb[:, h * 128:(h + 1) * 128], idb[:])
                                q_T = qTt[n2 % 2]
                                P.copy("dve", q_T[:], ptr[:, 0:512].rearrange("p (h t) -> p h t", t=128))
                                P.dma("sp", dst[ti][:, b * 4:(b + 1) * 4, :], q_T[:], okey=(dst.tensor.name, None))
                    wb = wbuf[wcnt % 2]
                    wcnt += 1
                    load_w(wb, w_wi, IH)
                    wk = wkeys(wb)
                    for tt, (kind, i) in enumerate(grp):
                        if kind == "h":
                            continue
                        ti = tile_index(kind, i)
                        pq_t = pq[tt % 2]
                        tok_mm(pq_t[:, 0:IH], hnT, tt * 128, wb, IH, wk)
                        w_s = wis[tt % 2]
                        P.act(w_s[:], pq_t[:, 0:IH], AF.Copy, scale=IDX_SCALE)
                        P.dma("sp", WI[ti], w_s[:], okey=("WI", None))
                P.flush()
            P.st = es

        if "B" in stages:
            with contextlib.ExitStack() as st:
                P.st = st
                P.npool["uin"] = 4
                wt = P.sb([128, CC, 31], F32)
                bt = P.sb([128, CC], F32)
                lg = P.sb([128, CC], F32)
                lb = P.sb([128, CC], F32)
                P.dma("sp", wt[:], dww)
                P.dma("sp", bt[:], dwb)
                P.dma("sp", lg[:], lng)
                P.dma("sp", lb[:], lnb)
                uin = P.sb([128, CC, 544], F32, "uin")
                cc_t = P.sb([128, CC, 512], F32)
                sqt = [P.sb([128, 512], F32) for _ in range(2)]
                p1 = P.ps()
                p2 = P.ps()
                mean = P.sb([128, 512], F32)
                var = P.sb([128, 512], F32)
                rstd = P.sb([128, 512], F32)
                tmp = [P.sb([128, 512], F32) for _ in range(2)]
                co = [P.sb([128, 512], BF16) for _ in range(2)]
                P.dma("sp", o_conv.rearrange("t c -> c t"), UTp[:, 128 + NP * 128 - 30:128 + NP * 128], okey=("o_conv", None), ikey="UTp",
                      allow_slow_non_contiguous=True)
                for s in range(2):
                    P.dma("sp", o_convs[s].rearrange("t c -> c t"), UTs[s][:, 32 + 64 - 30:32 + 64], okey=("o_convs", None), ikey="UTs",
                          allow_slow_non_contiguous=True)
                jobs = []
                for tb in range(max(1, NP * 128 // 512)):
                    ntk = min(512, NP * 128)
                    jobs.append(("p", tb, ntk))
                jobs += [("s", 0, 128), ("s", 1, 128)]
                for kind, tb, ntk in jobs:
                    if kind == "p":
                        c0 = 128 + tb * ntk
                        src = UTp[:, c0 - 30:c0 + ntk].rearrange("(j p) t -> p j t", p=128)
                        mcol = tb * ntk
                        sk = "UTp"
                    else:
                        src = UTs[tb][:, 2:160].rearrange("(j p) t -> p j t", p=128)
                        mcol = (NP + tb) * 128
                        sk = "UTs"
                    W = 30 + ntk
                    P.dma("sp", uin[:, :, 0:W], src, ikey=sk)
                    for j in range(CC):
                        acc = cc_t[:, j, 0:ntk]
                        P.ts("dve", acc, uin[:, j, 0:ntk], wt[:, j, 0:1], bt[:, j:j + 1], op0=ALU.mult, op1=ALU.add, okey=(cc_t, j))
                        for k in range(1, 31):
                            P.stt(acc, uin[:, j, k:k + ntk], wt[:, j, k:k + 1], acc, ALU.mult, ALU.add, okey=(cc_t, j),
                                  rkeys=[uin, (cc_t, j)])
                        s_t = sqt[j % 2]
                        P.act(s_t[:, 0:ntk], acc, AF.Square, ikey=(cc_t, j))
                        P.mm(p1[:, 0:ntk], onef[:], acc, start=(j == 0), stop=(j == CC - 1), rkeys=[onef, (cc_t, j)])
                        P.mm(p2[:, 0:ntk], onef[:], s_t[:, 0:ntk], start=(j == 0), stop=(j == CC - 1))
                    P.ts("dve", mean[:, 0:ntk], p1[:, 0:ntk], 1.0 / CCH, None, op0=ALU.mult)
                    P.tt("dve", var[:, 0:ntk], mean[:, 0:ntk], mean[:, 0:ntk], ALU.mult)
                    P.stt(var[:, 0:ntk], p2[:, 0:ntk], 1.0 / CCH, var[:, 0:ntk], ALU.mult, ALU.subtract)
                    P.ts("dve", var[:, 0:ntk], var[:, 0:ntk], EPS, None, op0=ALU.add)
                    P.act(var[:, 0:ntk], var[:, 0:ntk], AF.Sqrt)
                    P.recip(rstd[:, 0:ntk], var[:, 0:ntk])
                    for j in range(CC):
                        t_ = tmp[j % 2]
                        P.tt("dve", t_[:, 0:ntk], cc_t[:, j, 0:ntk], mean[:, 0:ntk], ALU.subtract, rkeys=[(cc_t, j), mean])
                        P.tt("dve", t_[:, 0:ntk], t_[:, 0:ntk], rstd[:, 0:ntk], ALU.mult)
                        c_o = co[j % 2]
                        P.act(c_o[:, 0:ntk], t_[:, 0:ntk], AF.Silu, scale=lg[:, j:j + 1], bias=lb[:, j:j + 1])
                        P.dma("sp", MIXT[j * 128:(j + 1) * 128, mcol:mcol + ntk], c_o[:, 0:ntk], okey=("MIXT", "conv"))
                P.flush()
            P.st = es

        if "C" in stages:
            with contextlib.ExitStack() as st:
                P.st = st
                SMAX = max(SEQ, SS)
                P.npool["kiT_c"] = 4
                P.npool["kT_c"] = 4
                P.npool["v_c"] = 4
                kiT_c = P.sb([128, SMAX], BF16, "kiT_c")
                kT_c = P.sb([128, NKV, SMAX], BF16, "kT_c")
                v_c = P.sb([128, SMAX // 128, NKV * 128], BF16, "v_c")
                kc_t = P.sb([128, SMAX], BF16)
                qch_t = P.sb([128, NT], F32)
                P.dma("sp", qch_t[:], qch)
                qiT = [P.sb([128, IH, 128], BF16) for _ in range(2)]
                qT = [P.sb([128, NH, 128], BF16) for _ in range(2)]
                wi_t = [P.sb([128, IH], F32) for _ in range(2)]
                acc2 = [P.sb([128, SMAX], F32) for _ in range(2)]
                madd = P.sb([128, SMAX], BF16)
                bs = P.sb([128, 8], F32)
                m8 = P.sb([128, 256], F32)
                thr = P.sb([128, 1], F32)
                mask = P.sb([128, SMAX], BF16)
                maskT = P.sb([128, SMAX // 128, 128], BF16)
                rl = [P.sb([128, 512], F32) for _ in range(4)]
                pe_ = [P.sb([128, GQ, 128], BF16) for _ in range(3)]
                pm = [P.sb([128, GQ, 128], BF16) for _ in range(4)]
                rz = P.sb([128, GQ * 128], F32)
                ob = [P.sb([128, GQ, 128], BF16) for _ in range(2)]
                ps_s = [P.ps() for _ in range(2)]
                ps_qk = [P.ps() for _ in range(3)]
                ps_o = P.ps()
                ps_z = P.ps()
                ptr = [P.ps([128, 1024], BF16) for _ in range(1)]

                def seglist(blocks):
                    nb = len(blocks)
                    segs = []
                    a = 0
                    while a < nb:
                        b_ = a + 1
                        while b_ < nb and b_ - a < 4 and blocks[b_] == blocks[b_ - 1] + 1:
                            b_ += 1
                        segs.append((a, b_))
                        a = b_
                    return segs

                cnt = dict(n=0, it=0)
                P.npool["UB16"] = 3
                P.npool["VB16"] = 3
                pre_ops = []
                for c in range(0, 128, 2):
                    pre_ops.append(("u", c))
                    pre_ops.append(("v", c))
                n_slots = (NP + 2) * NKV
                per_slot = -(-len(pre_ops) // n_slots)

                def precast_some():
                    dd = min(2048, KC * 128)
                    for _ in range(per_slot):
                        if not pre_ops:
                            return
                        kind, c = pre_ops.pop(0)
                        if kind == "u":
                            P.dma("pool", UB16[c:c + 2].rearrange("c p (x d) -> p c x d", d=dd),
                                  uT[c:c + 2].rearrange("c p (x d) -> p c x d", d=dd), okey=("UB16", None))
                        else:
                            dv = min(2048, D)
                            P.dma("pool", VB16[c * 128:(c + 2) * 128, :].rearrange("(c p) (x d) -> p c x d", p=128, d=dv),
                                  vtab[c * 128:(c + 2) * 128, :].rearrange("(c p) (x d) -> p c x d", p=128, d=dv), okey=("VB16", None))

                def idx_phase(job):
                    ti, blocks = job["ti"], job["blocks"]
                    if job.get("pre_idx"):
                        job["pre_idx"]()
                    k2 = ti % 2
                    acc = acc2[k2]
                    P.dma("sp", qiT[k2][:], QIT[ti], ikey="QIT")
                    P.dma("sp", wi_t[k2][:], WI[ti], ikey="WI")
                    segs = seglist(blocks)
                    N = len(blocks) * 128
                    for (a, b_) in segs:
                        P.ts("pool", madd[:, a * 128:b_ * 128], kc_t[:, blocks[a] * 128:(blocks[a] + b_ - a) * 128],
                             qch_t[:, ti:ti + 1], NEG, op0=ALU.is_gt, op1=ALU.mult)
                    for (a, b_) in segs:
                        w = (b_ - a) * 128
                        for h in range(IH):
                            n_ = cnt["n"]
                            cnt["n"] += 1
                            p_ = ps_s[n_ % 2]
                            r_ = rl[n_ % 4]
                            P.mm(p_[:, 0:w], qiT[k2][:, h, :], kiT_c[:, blocks[a] * 128:blocks[a] * 128 + w], rkeys=[qiT[k2], kiT_c])
                            P.act(r_[:, 0:w], p_[:, 0:w], AF.Relu)
                            if h == 0:
                                P.ts("dve", acc[:, a * 128:b_ * 128], r_[:, 0:w], wi_t[k2][:, 0:1], None, op0=ALU.mult)
                            else:
                                P.stt(acc[:, a * 128:b_ * 128], r_[:, 0:w], wi_t[k2][:, h:h + 1], acc[:, a * 128:b_ * 128], ALU.mult, ALU.add)
                    P.reduce(bs[:, 0:1], acc[:, 0:N], ALU.min)
                    P.tt("pool", acc[:, 0:N], acc[:, 0:N], madd[:, 0:N], ALU.add)
                    P.max8(m8[:, 0:8], acc[:, 0:N])
                    P.copy("dve", bs[:, 1:2], m8[:, 0:1])

                NITER = 22

                def topk_rounds(job, r0, r1):
                    ti, N, topk = job["ti"], len(job["blocks"]) * 128, job["topk"]
                    acc = acc2[ti % 2]
                    lo, hi, mid, tmp, cn, sel, dd = (bs[:, i:i + 1] for i in range(7))
                    for r in range(r0, min(r1, NITER)):
                        P.ts("dve", tmp, hi, 0.5, None, op0=ALU.mult)
                        P.stt(mid, lo, 0.5, tmp, ALU.mult, ALU.add)
                        P.ts("dve", mask[:, 0:N], acc[:, 0:N], mid, None, op0=ALU.is_ge, op1=ALU.add, accum_out=cn)
                        P.ts("dve", sel, cn, float(topk) - 0.5, None, op0=ALU.is_ge)
                        P.tt("dve", dd, mid, lo, ALU.subtract)
                        P.stt(lo, dd, sel, lo, ALU.mult, ALU.add)
                        P.tt("dve", dd, hi, mid, ALU.subtract)
                        P.stt(hi, dd, sel, mid, ALU.mult, ALU.add)

                def topk_final(job):
                    ti, blocks, topk = job["ti"], job["blocks"], job["topk"]
                    nb = len(blocks)
                    N = nb * 128
                    acc = acc2[ti % 2]
                    P.ts("dve", thr[:], bs[:, 0:1], 0.5 * NEG, None, op0=ALU.max)
                    P.ts("dve", mask[:, 0:N], acc[:, 0:N], thr[:, 0:1], None, op0=ALU.is_ge)
                    for b0 in range(0, nb, 8):
                        n8 = min(8, nb - b0)
                        pt = ptr[0]
                        for j in range(n8):
                            P.tr(pt[:, j * 128:(j + 1) * 128], mask[:, (b0 + j) * 128:(b0 + j + 1) * 128], idb[:])
                        P.copy("act", maskT[:, b0:b0 + n8, :], pt[:, 0:n8 * 128].rearrange("p (n t) -> p n t", t=128))

                def attn_group(job, g):
                    ti, blocks = job["ti"], job["blocks"]
                    k2 = ti % 2
                    nb = len(blocks)
                    if g == 0:
                        if job.get("pre_attn"):
                            job["pre_attn"]()
                        P.dma("sp", qT[k2][:], QT[ti], ikey="QT")
                    W = GQ * 128
                    LA = 2
                    bufs = {}

                    def front(ci):
                        blk = blocks[ci]
                        it = cnt["it"]
                        cnt["it"] += 1
                        pq_ = ps_qk[it % 3]
                        e_ = pe_[it % 3]
                        m_ = pm[it % 4]
                        bufs[ci] = m_
                        P.mm(pq_[:, 0:W], kT_c[:, g, blk * 128:(blk + 1) * 128],
                             qT[k2][:, g * GQ:(g + 1) * GQ, :].rearrange("p r t -> p (r t)"), rkeys=[kT_c, qT[k2]])
                        P.act(e_[:].rearrange("p r t -> p (r t)"), pq_[:, 0:W], AF.Exp, scale=ATT_SCALE)
                        P.tt("pool" if it % 2 == 0 else "dve", m_[:], e_[:], maskT[:, ci, :].unsqueeze(1).to_broadcast([128, GQ, 128]), ALU.mult)

                    def back(ci):
                        blk = blocks[ci]
                        m_ = bufs[ci]
                        mf = m_[:].rearrange("p r t -> p (r t)")
                        P.mm(ps_o[:, 0:W], v_c[:, blk, g * 128:(g + 1) * 128], mf, start=(ci == 0), stop=(ci == nb - 1), rkeys=[v_c, m_])
                        P.mm(ps_z[:, 0:W], oneb[:], mf, start=(ci == 0), stop=(ci == nb - 1))

                    for ci in range(min(LA, nb)):
                        front(ci)
                    for ci in range(nb):
                        if ci + LA < nb:
                            front(ci + LA)
                        back(ci)
                    P.recip(rz[:, 0:W], ps_z[:, 0:W])
                    o_ = ob[g % 2]
                    P.tt("dve", o_[:].rearrange("p r t -> p (r t)"), ps_o[:, 0:W], rz[:, 0:W], ALU.mult)
                    P.dma("sp", MIXT[CCH + g * GQ * 128:CCH + (g + 1) * GQ * 128, ti * 128:(ti + 1) * 128].rearrange("(r d) t -> d r t", d=128),
                          o_[:], okey=("MIXT", "attn"))

                def load_prompt_ki():
                    P.cdma(kc_t[:, 0:SEQ], kc_p[0:1, :].to_broadcast([128, SEQ]))
                    P.dma("sp", kiT_c[:, 0:SEQ], KIT, ikey="KIT", okey=(kiT_c, 0))

                def load_prompt_kv():
                    P.dma("sp", kT_c[:, :, 0:SEQ], KT, ikey="KT", okey=(kT_c, 0))
                    P.dma("sp", v_c[:, 0:NCX, :], Vc.rearrange("(c p) n -> p c n", p=128), ikey="Vc", okey=(v_c, 0))

                def mk_sample_ki(s):
                    def f():
                        P.cdma(kc_t[:, 0:SS], kc_s[0:1, :].to_broadcast([128, SS]))
                        P.cdma(kiT_c[:, 0:PAST], ckiT[s], okey=(kiT_c, 0))
                        P.dma("sp", kiT_c[:, PAST:SS], KITs[s], ikey="KITs", okey=(kiT_c, 1))
                    return f

                def mk_sample_kv(s):
                    def f():
                        for g in range(NKV):
                            P.cdma(kT_c[:, g, 0:PAST], ckT[s][:, g, :], okey=(kT_c, 0))
                        P.dma("sp", kT_c[:, :, PAST:SS], KTs[s], ikey="KTs", okey=(kT_c, 1))
                        cvv = cv[s].rearrange("(c p) n -> p c n", p=128)
                        nq = 4 if (PAST // 128) % 4 == 0 else 1
                        stp = (PAST // 128) // nq
                        for q in range(nq):
                            P.dma("pool", v_c[:, q * stp:(q + 1) * stp, :], cvv[:, q * stp:(q + 1) * stp, :], okey=(v_c, 0))
                        P.dma("sp", v_c[:, PAST // 128, :], Vs[s], ikey="Vs", okey=(v_c, 1))
                    return f

                jobs = []
                for i in range(NP):
                    jobs.append(dict(ti=i, blocks=list(range(0, i + 1)) + list(range(NP, 2 * NP)), topk=cfg["TOPK_P"]))
                jobs[0]["pre_idx"] = load_prompt_ki
                jobs[0]["pre_attn"] = load_prompt_kv
                for s in range(2):
                    jobs.append(dict(ti=NP + s, blocks=list(range(SS // 128)), topk=cfg["TOPK_S"],
                                     pre_idx=mk_sample_ki(s), pre_attn=mk_sample_kv(s)))
                idx_phase(jobs[0])
                topk_rounds(jobs[0], 0, 10 ** 6)
                topk_final(jobs[0])
                for k, job in enumerate(jobs):
                    nxt = jobs[k + 1] if k + 1 < len(jobs) else None
                    if nxt is not None:
                        idx_phase(nxt)
                        per = -(-NITER // NKV)
                    for g in range(NKV):
                        if nxt is not None:
                            topk_rounds(nxt, g * per, (g + 1) * per)
                        precast_some()
                        attn_group(job, g)
                    if nxt is not None:
                        topk_final(nxt)
                while pre_ops:
                    precast_some()
                P.flush()
            P.st = es

        otiles = list(range(NT))
        ogroups = [otiles[i:i + 4] for i in range(0, NT, 4)]
        Hs = dscr("Hs", [NTOK, D])
        RC = dscr("RC", [NT, 128, 3, 128])

        def x_rows(ti):
            return xctx[ti * 128:(ti + 1) * 128, :] if ti < NP else xsp[ti - NP]

        if "D" in stages:
            with contextlib.ExitStack() as st:
                P.st = st
                mixT = P.sb([128, KC, 512], BF16)
                wbuf = [P.sb([128, KC, 512], BF16) for _ in range(2)]
                xb_ = [P.sb([128, 512], F32) for _ in range(3)]
                hb_ = [P.sb([128, 512], F32) for _ in range(3)]
                pp = [P.ps() for _ in range(4)]
                wcnt = 0
                k_ = 0
                for grp in ogroups:
                    N = len(grp) * 128
                    c0 = grp[0] * 128
                    P.dma("sp", mixT[:, :, 0:N], MIXT[:, c0:c0 + N].rearrange("(c p) n -> p c n", p=128), ikey="MIXT")
                    for b in range(D // 512):
                        wb = wbuf[wcnt % 2]
                        wcnt += 1
                        load_w(wb, w_out[:, b * 512:(b + 1) * 512], 512)
                        wk = wkeys(wb)
                        for tt, ti in enumerate(grp):
                            xb = xb_[k_ % 3]
                            hb = hb_[k_ % 3]
                            p_ = pp[k_ % 4]
                            k_ += 1
                            P.dma("sp", xb[:], x_rows(ti)[:, b * 512:(b + 1) * 512])
                            tok_mm(p_[:, 0:512], mixT, tt * 128, wb, 512, wk)
                            P.tt("dve", hb[:], p_[:, 0:512], xb[:], ALU.add)
                            P.dma("sp", Hs[ti * 128:(ti + 1) * 128, b * 512:(b + 1) * 512], hb[:], okey=("Hs", None))
                P.flush()
            P.st = es

            with contextlib.ExitStack() as st:
                P.st = st
                nt = NormT(g_memn)
                gbc2 = P.sb([128, D], F32)
                bcast_row(gbc2[:], g_ffn[0:1, :], D)
                gmq = P.sb([128, 128], F32)
                bcast_row(gmq[:], g_mq[0:1, :], 128)
                wqm_b = P.sb([128, KC, 512], BF16)
                load_w(wqm_b, w_qm, 512)
                wom_b = P.sb([128, 4, D], BF16)
                P.cdma(wom_b[:], w_om.rearrange("(h p) d -> p h d", p=128))
                mkT_c = P.sb([128, 4, MEMT], BF16)
                mv_c = P.sb([128, MC, 512], BF16)
                ht = [P.sb([128, D], F32) for _ in range(2)]
                hn = [P.sb([128, KC, 128], BF16) for _ in range(2)]
                pq_ = P.ps()
                pl_ = [P.ps() for _ in range(2)]
                po_ = P.ps()
                pz_ = pq_
                pw_ = [P.ps() for _ in range(1)]
                ptr = P.ps([128, 1024], BF16)
                sq_t = P.sb([128, 512], F32)
                sm_t = P.sb([128, 12], F32)
                qmf = P.sb([128, 4, 128], F32)
                qmb = P.sb([128, 512], BF16)
                qmT = P.sb([128, 4, 128], BF16)
                pmT = [P.sb([128, 4, 128], BF16) for _ in range(MC)]
                rz = P.sb([128, 512], F32)
                omT = P.sb([128, 4, 128], BF16)
                for ti in otiles:
                    if ti == 0:
                        P.dma("sp", mkT_c[:], MKT, ikey="MKT")
                        P.dma("sp", mv_c[:], MV.rearrange("(c p) n -> p c n", p=128), ikey="MV")
                    elif ti >= NP:
                        P.dma("pool", mkT_c[:], cmkT[ti - NP])
                        P.dma("pool", mv_c[:], cmv[ti - NP].rearrange("(c p) n -> p c n", p=128))
                    h_t = ht[ti % 2]
                    hn_t = hn[ti % 2]
                    P.dma("sp", h_t[:], Hs[ti * 128:(ti + 1) * 128, :], ikey="Hs")
                    nt.run(h_t[:], lambda c0, n, hn_t=hn_t: hn_t[:, c0:c0 + n, :])
                    tok_mm(pq_[:, 0:512], hn_t, 0, wqm_b, 512, wkeys(wqm_b))
                    head_norm(pq_[:, 0:512], 4, gmq, qmf[:], sq_t, sm_t)
                    P.copy("act", qmb[:], qmf[:].rearrange("p h d -> p (h d)"))
                    for h in range(4):
                        P.tr(ptr[:, h * 128:(h + 1) * 128], qmb[:, h * 128:(h + 1) * 128], idb[:])
                    P.copy("dve", qmT[:], ptr[:, 0:512].rearrange("p (h t) -> p h t", t=128))
                    for mc in range(MC):
                        for h in range(4):
                            P.mm(pl_[mc % 2][:, h * 128:(h + 1) * 128], mkT_c[:, h, mc * 128:(mc + 1) * 128], qmT[:, h, :])
                        P.act(pmT[mc][:].rearrange("p h t -> p (h t)"), pl_[mc % 2][:, 0:512], AF.Exp, scale=ATT_SCALE)
                    for h in range(4):
                        for mc in range(MC):
                            P.mm(po_[:, h * 128:(h + 1) * 128], mv_c[:, mc, h * 128:(h + 1) * 128], pmT[mc][:, h, :],
                                 start=(mc == 0), stop=(mc == MC - 1))
                    for mc in range(MC):
                        P.mm(pz_[:, 0:512], oneb[:], pmT[mc][:].rearrange("p h t -> p (h t)"), start=(mc == 0), stop=(mc == MC - 1))
                    P.recip(rz[:], pz_[:, 0:512])
                    P.tt("dve", omT[:].rearrange("p h t -> p (h t)"), po_[:, 0:512], rz[:], ALU.mult)
                    for b in range(D // 512):
                        p_ = pw_[0]
                        for h in range(4):
                            P.mm(p_[:, 0:512], omT[:, h, :], wom_b[:, h, b * 512:(b + 1) * 512], start=(h == 0), stop=(h == 3))
                        P.tt("dve", h_t[:, b * 512:(b + 1) * 512], p_[:, 0:512], h_t[:, b * 512:(b + 1) * 512], ALU.add)
                    P.dma("sp", H2[ti * 128:(ti + 1) * 128, :], h_t[:], okey=("H2", None))
                    nt.gbc, g_save = gbc2, nt.gbc
                    nt.run(h_t[:], lambda c0, n, hn_t=hn_t: hn_t[:, c0:c0 + n, :])
                    nt.gbc = g_save
                    P.dma("sp", HN2T[:, ti * 128:(ti + 1) * 128].rearrange("(c p) t -> p c t", p=128), hn_t[:], okey=("HN2T", None))
                P.flush()
            P.st = es

            with contextlib.ExitStack() as st:
                P.st = st
                hn2 = P.sb([128, KC, 512], BF16)
                wbuf = [P.sb([128, KC, 512], BF16) for _ in range(2)]
                qpT = P.sb([128, 16, 512], F32)
                sk_t = P.sb([128, 16, 128], F32)
                P.dma("sp", sk_t[:], subk)
                pq_ = [P.ps() for _ in range(2)]
                ps_ = [P.ps() for _ in range(2)]
                ptf = P.ps()
                s12 = [P.sb([128, 16, 128], F32) for _ in range(2)]
                v16 = P.sb([128, 16, 16], F32)
                tmp128 = P.sb([128, 128], F32)
                cand = P.sb([128, 8, 256], F32)
                tmpc = P.sb([128, 256], F32)
                t16 = P.sb([128, 8, 16], F32)
                e16 = P.sb([128, 8, 16], F32)
                zz = P.sb([128, 8], F32)
                mlz = P.sb([128, 8], F32)
                rc3 = P.sb([128, 3, 8, 16], F32)
                rcT = [P.sb([128, 3, 128], F32) for _ in range(2)]
                wcnt = 0
                for grp in ogroups:
                    N = len(grp) * 128
                    c0 = grp[0] * 128
                    P.dma("sp", hn2[:, :, 0:N], HN2T[:, c0:c0 + N].rearrange("(c p) n -> p c n", p=128), ikey="HN2T")
                    for b in range(4):
                        wb = wbuf[wcnt % 2]
                        wcnt += 1
                        load_w(wb, w_pq[:, b * 512:(b + 1) * 512], 512)
                        wk = wkeys(wb)
                        for jj in range(4):
                            j = b * 4 + jj
                            p_ = pq_[j % 2]
                            for c in range(KC):
                                P.mm(p_[:, 0:N], wb[:, c, jj * 128:(jj + 1) * 128], hn2[:, c, 0:N], start=(c == 0), stop=(c == KC - 1),
                                     rkeys=[hn2] + wk)
                            P.copy("act", qpT[:, j, 0:N], p_[:, 0:N], okey=(qpT, j))
                    for tt, ti in enumerate(grp):
                        s_t = s12[ti % 2]
                        for jb in range(4):
                            p_ = ps_[jb % 2]
                            for jj in range(4):
                                j = jb * 4 + jj
                                P.mm(p_[:, jj * 128:(jj + 1) * 128], qpT[:, j, tt * 128:(tt + 1) * 128], sk_t[:, j, :], rkeys=[(qpT, j), sk_t])
                            P.copy("act", s_t[:, jb * 4:(jb + 1) * 4, :].rearrange("p j k -> p (j k)"), p_[:, 0:512])
                        P.dma("sp", S12[:, ti * 128:(ti + 1) * 128, :].rearrange("j t i -> t j i"), s_t[:], okey=("S12", None))
                        for j in range(16):
                            P.max8(v16[:, j, 0:8], s_t[:, j, :])
                            P.mrep(tmp128[:], v16[:, j, 0:8], s_t[:, j, :], -3.0e38)
                            P.max8(v16[:, j, 8:16], tmp128[:])
                        v16v = v16[:].rearrange("p (h two) k -> p h two k", two=2)
                        for h in range(8):
                            P.tt("dve", cand[:, h, :].rearrange("p (a b) -> p a b", b=16),
                                 v16[:, 2 * h, :].unsqueeze(2).to_broadcast([128, 16, 16]),
                                 v16[:, 2 * h + 1, :].unsqueeze(1).to_broadcast([128, 16, 16]), ALU.add)
                        for h in range(8):
                            P.max8(t16[:, h, 0:8], cand[:, h, :])
                            P.mrep(tmpc[:], t16[:, h, 0:8], cand[:, h, :], -3.0e38)
                            P.max8(t16[:, h, 8:16], tmpc[:])
                        P.tt("dve", e16[:], t16[:], t16[:, :, 0:1].to_broadcast([128, 8, 16]), ALU.subtract)
                        P.act(e16[:], e16[:], AF.Exp)
                        P.reduce(zz[:], e16[:], ALU.add)
                        P.act(mlz[:], zz[:], AF.Ln)
                        P.tt("dve", mlz[:], mlz[:], t16[:, :, 0], ALU.add)
                        P.copy("dve", rc3[:, 0, :, :], v16v[:, :, 0, :])
                        P.tt("dve", rc3[:, 1, :, :], t16[:, :, 15:16].to_broadcast([128, 8, 16]), rc3[:, 0, :, :], ALU.subtract)
                        P.tt("dve", rc3[:, 2, :, :], rc3[:, 0, :, :], mlz[:].unsqueeze(2).to_broadcast([128, 8, 16]), ALU.subtract)
                        for q in range(3):
                            P.tr(ptf[:, q * 128:(q + 1) * 128], rc3[:, q, :, :].rearrange("p h a -> p (h a)"), idf[:])
                        r_T = rcT[ti % 2]
                        P.copy("act", r_T[:].rearrange("p q t -> p (q t)"), ptf[:, 0:384])
                        P.dma("sp", RC[ti], r_T[:], okey=("RC", None))
                P.flush()
            P.st = es

            with contextlib.ExitStack() as st:
                P.st = st
                TB = 32
                s1r = [P.sb([128, TB, 128], F32) for _ in range(2)]
                s2r = [P.sb([128, TB, 128], F32) for _ in range(2)]
                rct = [P.sb([128, 3, 128], F32) for _ in range(2)]
                o1 = [P.sb([128, 128], BF16) for _ in range(4)]
                ee = [P.sb([128, 128], F32) for _ in range(4)]
                rr = [P.sb([128, 128], BF16) for _ in range(4)]
                gst = [P.sb([128, 128, 128], BF16) for _ in range(2)]
                pg_ = [P.ps() for _ in range(2)]
                kk = 0
                for ti in otiles:
                    rc_ = rct[ti % 2]
                    g_s = gst[ti % 2]
                    P.dma("sp", rc_[:], RC[ti], ikey="RC")
                    for tb in range(128 // TB):
                        t0 = ti * 128 + tb * TB
                        a1 = s1r[tb % 2]
                        a2 = s2r[tb % 2]
                        for half, dst in ((0, a1), (1, a2)):
                            src = S12[:, t0:t0 + TB, :].rearrange("(h two) t i -> two h (t i)", two=2)[half]
                            P.dma("sp", dst[:].rearrange("p t i -> p (t i)"), src.unsqueeze(1).to_broadcast([8, 16, TB * 128]),
                                  ikey="S12", okey=(dst, None))
                        for tq in range(0, TB, 4):
                            p_ = pg_[(kk) % 2]
                            kk += 1
                            for u4 in range(4):
                                tl = tq + u4
                                t = tb * TB + tl
                                o_ = o1[u4]
                                e_ = ee[u4]
                                r_ = rr[u4]
                                P.ts("dve", o_[:], a1[:, tl, :], rc_[:, 0, t:t + 1], None, op0=ALU.is_equal)
                                P.act(e_[:], a2[:, tl, :], AF.Exp, bias=rc_[:, 2, t:t + 1])
                                P.stt(r_[:], a2[:, tl, :], rc_[:, 1, t:t + 1], e_[:], ALU.is_ge, ALU.mult)
                                P.mm(p_[:, u4 * 128:(u4 + 1) * 128], o_[:], r_[:])
                            tbase = tb * TB + tq
                            P.copy("act", g_s[:, :, tbase:tbase + 4].rearrange("p i t -> p t i"),
                                   p_[:, 0:512].rearrange("p (t i) -> p t i", i=128))
                    P.dma("sp", GALL[:, :, ti * 128:(ti + 1) * 128], g_s[:], okey=("GALL", None))
                P.flush()
            P.st = es

        if "E" in stages:
            with contextlib.ExitStack() as st:
                P.st = st
                NCH = PEER_KEYS
                EB = 4
                hn2 = P.sb([128, KC, 512], BF16)
                oacc = P.sb([128, 4, D], F32)
                ub = [P.sb([128, KC, 128], BF16) for _ in range(3)]
                vb = [P.sb([128, EB, D], BF16) for _ in range(2)]
                coef = [P.sb([128, EB, 512], BF16) for _ in range(2)]
                gl = [P.sb([128, 512], BF16) for _ in range(2)]
                gc = [P.sb([128, 512], BF16) for _ in range(2)]
                pa_ = [P.ps() for _ in range(2)]
                pv_ = [P.ps() for _ in range(4)]
                ucnt = 0
                vcnt = 0
                pcnt = 0
                DH = 2048 if D % 2048 == 0 else D
                for grp in ogroups:
                    ng = len(grp)
                    N = ng * 128
                    c0 = grp[0] * 128
                    P.dma("sp", hn2[:, :, 0:N], HN2T[:, c0:c0 + N].rearrange("(c p) n -> p c n", p=128), ikey="HN2T")
                    for tt, ti in enumerate(grp):
                        P.dma("sp", oacc[:, tt, :], H2[ti * 128:(ti + 1) * 128, :], ikey="H2", okey=(oacc, tt))
                    def v_load(eb):
                        v_b = vb[eb % 2]
                        vsrc = VB16[eb * EB * 128:(eb + 1) * EB * 128, :].rearrange("(cc p) d -> p cc d", p=128)
                        P.dma("sp", v_b[:], vsrc, ikey="VB16")

                    def u_phase(eb):
                        nonlocal ucnt
                        cf = coef[eb % 2]
                        for cc in range(EB):
                            c = eb * EB + cc
                            u_b = ub[ucnt % 3]
                            g_l = gl[ucnt % 2]
                            g_c = gc[ucnt % 2]
                            p_ = pa_[ucnt % 2]
                            ucnt += 1
                            P.dma("sp", u_b[:].rearrange("p c e -> p (c e)"), UB16[c], ikey="UB16")
                            P.dma("sp", g_c[:, 0:N], GALL[c][:, c0:c0 + N], ikey="GALL")
                            for dc in range(KC):
                                P.mm(p_[:, 0:N], u_b[:, dc, :], hn2[:, dc, 0:N], start=(dc == 0), stop=(dc == KC - 1))
                            P.act(g_l[:, 0:N], p_[:, 0:N], AF.Gelu)
                            P.tt("dve", cf[:, cc, 0:N], g_l[:, 0:N], g_c[:, 0:N], ALU.mult, okey=(cf, cc))

                    def v_phase(eb):
                        nonlocal pcnt
                        cf = coef[eb % 2]
                        v_b = vb[eb % 2]
                        for tt in range(ng):
                            for db in range(D // 512):
                                pv = pv_[pcnt % 4]
                                pcnt += 1
                                for cc in range(EB):
                                    P.mm(pv[:, 0:512], cf[:, cc, tt * 128:(tt + 1) * 128], v_b[:, cc, db * 512:(db + 1) * 512],
                                         start=(cc == 0), stop=(cc == EB - 1), rkeys=[(cf, cc), v_b])
                                P.tt("dve", oacc[:, tt, db * 512:(db + 1) * 512], pv[:, 0:512], oacc[:, tt, db * 512:(db + 1) * 512], ALU.add,
                                     okey=(oacc, tt), rkeys=[pv, (oacc, tt)])

                    nE = NCH // EB
                    v_load(0)
                    u_phase(0)
                    for eb in range(nE):
                        if eb + 1 < nE:
                            v_load(eb + 1)
                            u_phase(eb + 1)
                        v_phase(eb)
                    for tt, ti in enumerate(grp):
                        P.dma("sp", y[ti * 128:(ti + 1) * 128, :], oacc[:, tt, :], ikey=(oacc, tt), okey=("y", None))
                P.flush()
            P.st = es

        if dbg:
            for nm, ap_ in (("MIXT", MIXT), ("UTp", UTp), ("UTs", UTs), ("QT", QT), ("QIT", QIT), ("WI", WI), ("KT", KT), ("KIT", KIT),
                            ("Vc", Vc), ("H2", H2), ("HN2T", HN2T), ("S12", S12), ("GALL", GALL), ("MKT", MKT), ("MV", MV)):
                if nm in dbg:
                    o_ = dout("dbg_" + nm, list(ap_.shape), ap_.dtype)
                    P.dma("sp", o_, ap_)
        P.flush()
    return nc


def _rope_table(pos):
    half = 16
    inv_freq = np.power(np.float32(ROPE_THETA), -np.arange(half, dtype=np.float32) / np.float32(half)).astype(np.float32)
    ang = pos.astype(np.float32)[:, None] * inv_freq[None, :]
    return np.concatenate([np.cos(ang), np.sin(ang)], axis=1).astype(np.float32)


def host_prep(inp, cfg):
    D, KC, CCH, CC, NH, NKV, NP, NT, IH, SEQ, PAST, SS, MEMT = (cfg[k] for k in (
        "D", "KC", "CCH", "CC", "NH", "NKV", "NP", "NT", "IH", "SEQ", "PAST", "SS", "MEMT"))
    DS = cfg["DS"]
    f = lambda a: np.ascontiguousarray(a, dtype=np.float32)
    half = SEQ // 2
    w_in = inp["w_in"][0]
    OFF_Q = 2 * CCH
    OFF_K = OFF_Q + NH * 128
    OFF_V = OFF_K + NKV * 128
    OFF_QI = OFF_V + NKV * 128
    OFF_KI = OFF_QI + IH * 128
    OFF_WI = OFF_KI + 128
    a_ = w_in[:, :CCH].reshape(D, CC, 128)
    g_ = w_in[:, CCH:2 * CCH].reshape(D, CC, 128)
    w_glu = f(np.stack([a_, g_], axis=2).reshape(D, 2 * CCH))
    shared = dict(
        w_glu=w_glu,
        w_q=f(w_in[:, OFF_Q:OFF_K]),
        w_qi=f(w_in[:, OFF_QI:OFF_KI]),
        w_wi=f(w_in[:, OFF_WI:OFF_WI + IH]),
        w_kv=f(np.concatenate([w_in[:, OFF_K:OFF_V], w_in[:, OFF_V:OFF_QI], w_in[:, OFF_KI:OFF_WI]], axis=1)),
        w_out=f(inp["w_out"][0]),
        w_qm=f(inp["w_q_mem"][0]), w_km=f(inp["w_k_mem"][0]), w_vm=f(inp["w_v_mem"][0]), w_om=f(inp["w_o_mem"][0]),
        w_pq=f(inp["peer_wq"][0]),
        g_mix=f(inp["norm_mix_g"]), g_memn=f(inp["norm_mem_g"]), g_ffn=f(inp["norm_ffn_g"]), g_mem=f(inp["mem_norm_g"]),
        g_q=f(inp["q_norm_g"]), g_k=f(inp["k_norm_g"]), g_mq=f(inp["mem_q_norm_g"]), g_mk=f(inp["mem_k_norm_g"]),
        dww=f(inp["dw_w"][0].reshape(31, CC, 128).transpose(2, 1, 0)),
        dwb=f(inp["dw_b"][0].reshape(CC, 128).T), lng=f(inp["conv_ln_g"][0].reshape(CC, 128).T),
        lnb=f(inp["conv_ln_b"][0].reshape(CC, 128).T),
        vtab=f(inp["peer_v"][0]),
        c_idb=np.eye(128).astype(ml_dtypes.bfloat16), c_idf=np.eye(128, dtype=np.float32),
        c_oneb=np.ones((128, 128)).astype(ml_dtypes.bfloat16), c_onef=np.ones((128, 128), dtype=np.float32),
    )
    sk = np.stack([inp["peer_sub_k1"][0], inp["peer_sub_k2"][0]], axis=1)
    shared["subk"] = f(sk.reshape(16, 128, 128).transpose(2, 0, 1))
    u = inp["peer_u"][0]
    shared["uT"] = f(u.reshape(128, 128, KC, 128).transpose(0, 3, 2, 1).reshape(128, 128, KC * 128))
    kcs = (np.arange(SS) // 64).astype(np.float32)
    kcs[PAST + DS:] = 1.0e9
    shared["kc_s"] = kcs[None, :]
    shared["rope_s"] = _rope_table(PAST + np.arange(128))
    maps = []
    for c in range(8):
        b, hf = c // 2, c % 2
        xb = inp["x_prompt"][b]
        own = xb[hf * half:(hf + 1) * half]
        oth = xb[(1 - hf) * half:(2 - hf) * half]
        pos = np.concatenate([hf * half + np.arange(half), (1 - hf) * half + np.arange(half)])
        m = dict(shared)
        m["xctx"] = f(np.concatenate([own, oth], axis=0))
        m["xhalo"] = f(xb[half - 128:half]) if hf == 1 else np.zeros((128, D), np.float32)
        xsp = np.zeros((2, 128, D), np.float32)
        for s in range(2):
            xsp[s, :DS] = inp["x_sample"][2 * c + s]
        m["xsp"] = xsp
        m["mem"] = f(inp["mem_prompt"][b])
        m["ckT"] = f(np.stack([inp["cache_k"][0, 2 * c + s].transpose(2, 1, 0) for s in range(2)]))
        m["cv"] = f(np.stack([inp["cache_v"][0, 2 * c + s].reshape(PAST, NKV * 128) for s in range(2)]))
        m["ckiT"] = f(np.stack([inp["cache_k_idx"][0, 2 * c + s].T for s in range(2)]))
        m["stT"] = f(np.stack([inp["state_conv"][0, 2 * c + s].T for s in range(2)]))
        m["cmkT"] = f(np.stack([inp["cache_mem_k"][0, 2 * c + s].transpose(2, 1, 0) for s in range(2)]))
        m["cmv"] = f(np.stack([inp["cache_mem_v"][0, 2 * c + s].reshape(MEMT, 512) for s in range(2)]))
        m["rope_c"] = _rope_table(pos)
        m["kc_p"] = (pos // 64).astype(np.float32)[None, :]
        q = np.zeros((128, NT), np.float32)
        for i in range(NP):
            q[:, i] = (hf * half + i * 128 + np.arange(128)) // 64
        q[:, NP:] = PAST // 64
        m["qch"] = q
        maps.append(m)
    return maps


def assemble(res, cfg):
    D, CCH, NKV, NP, SEQ, DS, MEMT, B, DB = (cfg[k] for k in ("D", "CCH", "NKV", "NP", "SEQ", "DS", "MEMT", "B", "DB"))
    half = SEQ // 2
    y_p = np.zeros((B, SEQ, D), np.float32)
    y_s = np.zeros((DB, DS, D), np.float32)
    k_p = np.zeros((1, B, SEQ, NKV, 128), np.float32)
    v_p = np.zeros_like(k_p)
    ki_p = np.zeros((1, B, SEQ, 128), np.float32)
    conv_p = np.zeros((1, B, 30, CCH), np.float32)
    mk_p = np.zeros((1, B, MEMT, 4, 128), np.float32)
    mv_p = np.zeros_like(mk_p)
    k_s = np.zeros((1, DB, DS, NKV, 128), np.float32)
    v_s = np.zeros_like(k_s)
    ki_s = np.zeros((1, DB, DS, 128), np.float32)
    conv_s = np.zeros((1, DB, 30, CCH), np.float32)
    for c in range(8):
        r = res[c]
        b, hf = c // 2, c % 2
        y_p[b, hf * half:(hf + 1) * half] = r["y"][:NP * 128]
        if hf == 0:
            k_p[0, b] = r["o_k"].reshape(SEQ, NKV, 128)
            v_p[0, b] = r["o_v"].reshape(SEQ, NKV, 128)
            ki_p[0, b] = r["o_ki"]
            mk_p[0, b] = r["o_mk"].reshape(MEMT, 4, 128)
            mv_p[0, b] = r["o_mv"].reshape(MEMT, 4, 128)
        else:
            conv_p[0, b] = r["o_conv"]
        for s in range(2):
            q = 2 * c + s
            y_s[q] = r["y"][(NP + s) * 128:(NP + s) * 128 + DS]
            k_s[0, q] = r["o_ks"][s, :DS].reshape(DS, NKV, 128)
            v_s[0, q] = r["o_vs"][s, :DS].reshape(DS, NKV, 128)
            ki_s[0, q] = r["o_kis"][s, :DS]
            conv_s[0, q] = r["o_convs"][s]
    return (y_p, y_s, k_p, v_p, ki_p, conv_p, mk_p, mv_p, k_s, v_s, ki_s, conv_s)


def kernel(**inputs):
    cfg = mkcfg()
    inp = {k: np.asarray(v) for k, v in inputs.items()}
    maps = host_prep(inp, cfg)
    nc = build(cfg)
    res = run_bass_kernel_spmd(nc, maps, core_ids=list(range(8)))
    return assemble(res.results, cfg)
```

```python
import contextlib
import math
import numpy as np
import ml_dtypes
import concourse.bass as bass
import concourse.mybir as mybir
from concourse.bass_utils import run_bass_kernel_spmd

F32 = mybir.dt.float32
BF16 = mybir.dt.bfloat16
ALU = mybir.AluOpType
AF = mybir.ActivationFunctionType
AX = mybir.AxisListType

EPS = 1e-6
ROPE_THETA = 500000.0
NEG = -1.0e30


class Prog:
    def __init__(self, nc, es):
        self.nc = nc
        self.es = es
        self.st = es
        self.ops = []
        self.engs = {"pe": nc.tensor, "act": nc.scalar, "dve": nc.vector, "pool": nc.gpsimd, "sp": nc.sync}
        self.n_t = 0
        self.eng_sem = {}
        self.eng_cnt = {}
        self.pool = {}
        self.npool = {}
        self.key_sem = {}
        self.fence_sem = None
        self.fence_cnt = 0
        self.tot_ops = 0
        self.tot_wait = 0
        self.free_sems = []
        self.n_dsem = 0

    def sb(self, shape, dt=F32, name=None):
        self.n_t += 1
        return self.st.enter_context(self.nc.sbuf_tensor(name or f"sb{self.n_t}", list(shape), dt))

    def ps(self, shape=(128, 512), dt=F32, name=None):
        self.n_t += 1
        return self.st.enter_context(self.nc.psum_tensor(name or f"ps{self.n_t}", list(shape), dt))

    @staticmethod
    def key(x):
        def nm(a):
            if isinstance(a, str):
                return a
            t = getattr(a, "tensor", None)
            return t.name if t is not None else a.name
        if isinstance(x, tuple):
            return (nm(x[0]), x[1])
        return (nm(x), None)

    def op(self, eng, fn, reads=(), writes=(), dma=False):
        rk = []
        for r in reads:
            if r is None or isinstance(r, (int, float)):
                continue
            k = self.key(r)
            if k not in rk:
                rk.append(k)
        wk = []
        for w in writes:
            k = self.key(w)
            if k not in wk:
                wk.append(k)
        self.ops.append(dict(eng=eng, fn=fn, reads=rk, writes=wk, dma=dma))

    def _esem(self, e):
        if e not in self.eng_sem:
            self.eng_sem[e] = self.es.enter_context(self.nc.semaphore(f"s_{e}"))
            self.eng_cnt[e] = 0
        return self.eng_sem[e]

    def flush(self):
        nc = self.nc
        ops = self.ops
        state = {}
        deps = [None] * len(ops)

        def confl(k):
            ent = state.get(k[0])
            if not ent:
                return []
            if k[1] is None:
                return list(ent.values())
            return [ent[s_] for s_ in (k[1], None) if s_ in ent]

        joined = [False] * len(ops)
        for i, o in enumerate(ops):
            d = set()
            for k in o["reads"]:
                for st in confl(k):
                    d.update(st[0])
            joins = {}
            for k in o["writes"]:
                own = state.get(k[0], {}).get(k[1])
                joinable = bool(o["dma"] and own and own[0] and all(ops[j]["dma"] for j in own[0]) and not own[1])
                joins[k] = joinable
                for st in confl(k):
                    d.update(st[1])
                    if not (joinable and st is own):
                        d.update(st[0])
            if o["dma"]:
                joined[i] = joins[o["writes"][0]]
            for k in o["reads"]:
                st = state.setdefault(k[0], {}).setdefault(k[1], [[], []])
                st[1].append(i)
            for k in o["writes"]:
                ent = state.setdefault(k[0], {})
                if joins[k]:
                    ent[k[1]][0].append(i)
                else:
                    if k[1] is None:
                        ent.clear()
                    ent[k[1]] = [[i], []]
            d.discard(i)
            if o["eng"] == "pe":
                d = {j for j in d if not (ops[j]["eng"] == "pe" and not ops[j]["dma"])}
            deps[i] = d
        need = [False] * len(ops)
        for d in deps:
            for j in d:
                need[j] = True
        last_on = {}
        for i, o in enumerate(ops):
            if not o["dma"]:
                last_on[o["eng"]] = i
        for i in last_on.values():
            need[i] = True

        sig = [None] * len(ops)
        waited = {}
        for i, o in enumerate(ops):
            e = o["eng"]
            eo = self.engs[e]
            wl = {}
            for j in deps[i]:
                s, v = sig[j]
                kk = id(s)
                if kk not in wl or wl[kk][1] < v:
                    wl[kk] = (s, v)
            pre = None
            if o["dma"]:
                k = o["writes"][0]
                name = k[0]
                pl = self.pool.get(name)
                if pl is None:
                    n = self.npool.get(name, 2)
                    sems_, cnt_ = [], []
                    for q in range(n):
                        if self.free_sems:
                            s_, c_ = self.free_sems.pop()
                        else:
                            self.n_dsem += 1
                            s_, c_ = self.es.enter_context(nc.semaphore(f"dma{self.n_dsem}")), 0
                        sems_.append(s_)
                        cnt_.append(c_)
                    pl = dict(sems=sems_, cnt=cnt_, last=[None] * n, rr=0)
                    self.pool[name] = pl
                idx = None
                if joined[i] and k in self.key_sem and pl["last"][self.key_sem[k]] == k:
                    idx = self.key_sem[k]
                else:
                    idx = pl["rr"]
                    pl["rr"] = (pl["rr"] + 1) % len(pl["sems"])
                    if pl["cnt"][idx] > 0:
                        s = pl["sems"][idx]
                        kk = id(s)
                        if kk not in wl or wl[kk][1] < pl["cnt"][idx]:
                            wl[kk] = (s, pl["cnt"][idx])
                self.key_sem[k] = idx
                pl["last"][idx] = k
                pre = (pl, idx)
            for kk, (s, v) in wl.items():
                if waited.get((e, kk), -1) >= v:
                    continue
                waited[(e, kk)] = v
                eo.wait_ge(s, v)
                self.tot_wait += 1
            ins = o["fn"](eo)
            if o["dma"]:
                pl, idx = pre
                pl["cnt"][idx] += 16
                ins.then_inc(pl["sems"][idx], 16)
                sig[i] = (pl["sems"][idx], pl["cnt"][idx])
            elif need[i]:
                s = self._esem(e)
                self.eng_cnt[e] += 1
                ins.then_inc(s, 1)
                sig[i] = (s, self.eng_cnt[e])
        self.tot_ops += len(ops)
        self.ops = []
        if self.fence_sem is None:
            self.fence_sem = self.es.enter_context(nc.semaphore("fence"))
        for e, s in self.eng_sem.items():
            if self.eng_cnt[e] > 0:
                nc.sync.wait_ge(s, self.eng_cnt[e])
        for pl in self.pool.values():
            for s, c in zip(pl["sems"], pl["cnt"]):
                if c > 0:
                    nc.sync.wait_ge(s, c)
        for pl in self.pool.values():
            for s, c in zip(pl["sems"], pl["cnt"]):
                self.free_sems.append((s, c))
        self.pool = {}
        self.key_sem = {}
        self.fence_cnt += 1
        nc.sync.drain().then_inc(self.fence_sem, 1)
        for e in ("pe", "act", "dve", "pool"):
            self.engs[e].wait_ge(self.fence_sem, self.fence_cnt)

    def dma(self, q, out, in_, okey=None, ikey=None, **kw):
        self.op(q, lambda e: e.dma_start(out=out, in_=in_, **kw), reads=[ikey or in_], writes=[okey or out], dma=True)

    def cdma(self, out, in_, okey=None, ikey=None):
        n = out.shape[-1]
        if n > 2048:
            d = 2048
            while n % d:
                d //= 2
            names = " ".join(f"a{i}" for i in range(len(out.shape) - 1))
            pat = f"{names} (x d) -> {names} x d"
            self.dma("pool", out.rearrange(pat, d=d), in_.rearrange(pat, d=d), okey=okey or out, ikey=ikey or in_)
        else:
            self.dma("pool", out, in_, okey=okey, ikey=ikey)

    def mm(self, out, lhsT, rhs, start=True, stop=True, okey=None, rkeys=None):
        self.op("pe", lambda e: e.matmul(out, lhsT, rhs, start=start, stop=stop), reads=rkeys or [lhsT, rhs], writes=[okey or out])

    def tr(self, out, in_, ident, okey=None, ikey=None):
        self.op("pe", lambda e: e.transpose(out, in_, ident), reads=[ikey or in_, ident], writes=[okey or out])

    def act(self, out, in_, func, scale=1.0, bias=0.0, accum_out=None, okey=None, ikey=None):
        rd = [ikey or in_] + [x for x in (scale, bias) if not isinstance(x, (int, float))]
        wr = [okey or out] + ([accum_out] if accum_out is not None else [])
        if accum_out is not None:
            self.op("act", lambda e: e.activation(out, in_, func, bias=bias, scale=scale, accum_out=accum_out), reads=rd, writes=wr)
        else:
            self.op("act", lambda e: e.activation(out, in_, func, bias=bias, scale=scale), reads=rd, writes=wr)

    def ts(self, eng, out, in0, s1, s2=None, op0=ALU.mult, op1=None, accum_out=None, okey=None, ikey=None):
        rd = [ikey or in0] + [x for x in (s1, s2) if x is not None and not isinstance(x, (int, float))]
        wr = [okey or out] + ([accum_out] if accum_out is not None else [])
        kw = {}
        if op1 is not None:
            kw["op1"] = op1
        if accum_out is not None:
            kw["accum_out"] = accum_out
        self.op(eng, lambda e: e.tensor_scalar(out, in0, s1, s2, op0, **kw), reads=rd, writes=wr)

    def tt(self, eng, out, in0, in1, op, okey=None, rkeys=None):
        self.op(eng, lambda e: e.tensor_tensor(out, in0, in1, op), reads=rkeys or [in0, in1], writes=[okey or out])

    def stt(self, out, in0, scalar, in1, op0, op1, okey=None, rkeys=None):
        rd = list(rkeys or [in0, in1]) + ([scalar] if not isinstance(scalar, (int, float)) else [])
        self.op("dve", lambda e: e.scalar_tensor_tensor(out, in0, scalar, in1, op0, op1), reads=rd, writes=[okey or out])

    def copy(self, eng, out, in_, okey=None, ikey=None):
        if eng == "act":
            self.op("act", lambda e: e.copy(out, in_), reads=[ikey or in_], writes=[okey or out])
        else:
            self.op(eng, lambda e: e.tensor_copy(out, in_), reads=[ikey or in_], writes=[okey or out])

    def max8(self, out, in_, okey=None):
        self.op("dve", lambda e: e.max(out, in_), reads=[in_], writes=[okey or out])

    def mrep(self, out, in_to_replace, in_values, imm, rkeys=None):
        self.op("dve", lambda e: e.match_replace(out, in_to_replace, in_values, imm), reads=rkeys or [in_to_replace, in_values], writes=[out])

    def memset(self, eng, ap, val):
        self.op(eng, lambda e: e.memset(ap, val), reads=[], writes=[ap])

    def recip(self, out, in_, okey=None):
        self.op("dve", lambda e: e.reciprocal(out, in_), reads=[in_], writes=[okey or out])

    def reduce(self, out, in_, op, axis=AX.X):
        self.op("dve", lambda e: e.tensor_reduce(out, in_, axis, op), reads=[in_], writes=[out])


def mkcfg(D=4096, SEQ=4096, B=4, DB=16, DS=64, PAST=4096, IH=32, TOPK_MAX=256, MEMT=256):
    c = dict(D=D, SEQ=SEQ, B=B, DB=DB, DS=DS, PAST=PAST, IH=IH, MEMT=MEMT)
    c["KC"] = D // 128
    c["CCH"] = D // 2
    c["CC"] = c["CCH"] // 128
    c["NH"] = (D // 2) // 128
    c["NKV"] = 4
    c["GQ"] = c["NH"] // 4
    c["NP"] = SEQ // 2 // 128
    c["NCX"] = SEQ // 128
    c["NT"] = c["NP"] + 2
    c["TOPK_P"] = min(TOPK_MAX, SEQ // 4)
    c["TOPK_S"] = min(TOPK_MAX, (PAST + DS) // 4)
    c["SS"] = PAST + 128
    c["MH"] = 4
    c["MC"] = MEMT // 128
    return c


PEER_KEYS = 128
PEER_HEADS = 8
PEER_TOPK = 16


def build(cfg, stages=("M", "KV", "MAIN", "B", "C", "D", "E"), dbg=False):
    D, KC, CCH, CC, NH, NKV, GQ, NP, NCX, NT, IH, SEQ, PAST, SS, MEMT, MC = (cfg[k] for k in (
        "D", "KC", "CCH", "CC", "NH", "NKV", "GQ", "NP", "NCX", "NT", "IH", "SEQ", "PAST", "SS", "MEMT", "MC"))
    NTOK = NT * 128
    IDX_SCALE = (IH ** -0.5) * (128 ** -0.5)
    ATT_SCALE = 128 ** -0.5
    nc = bass.Bass("TRN2", target_bir_lowering=False)

    def din(name, shape, dt=F32):
        return nc.dram_tensor(name, list(shape), dt, kind="ExternalInput").ap()

    def dout(name, shape, dt=F32):
        return nc.dram_tensor(name, list(shape), dt, kind="ExternalOutput").ap()

    def dscr(name, shape, dt=F32):
        return nc.dram_tensor(name, list(shape), dt, kind="Internal").ap()

    xctx = din("xctx", [SEQ, D])
    xsp = din("xsp", [2, 128, D])
    xhalo = din("xhalo", [128, D])
    mem = din("mem", [MEMT, D])
    ckT = din("ckT", [2, 128, NKV, PAST])
    cv = din("cv", [2, PAST, NKV * 128])
    ckiT = din("ckiT", [2, 128, PAST])
    stT = din("stT", [2, CCH, 30])
    cmkT = din("cmkT", [2, 128, 4, MEMT])
    cmv = din("cmv", [2, MEMT, 512])
    w_glu = din("w_glu", [D, 2 * CCH])
    w_q = din("w_q", [D, NH * 128])
    w_qi = din("w_qi", [D, IH * 128])
    w_wi = din("w_wi", [D, IH])
    w_kv = din("w_kv", [D, 1152])
    w_out = din("w_out", [D, D])
    w_qm = din("w_qm", [D, 512])
    w_km = din("w_km", [D, 512])
    w_vm = din("w_vm", [D, 512])
    w_om = din("w_om", [512, D])
    w_pq = din("w_pq", [D, 2048])
    subk = din("subk", [128, 16, 128])
    uT = din("uT", [128, 128, KC * 128])
    vtab = din("vtab", [PEER_KEYS * PEER_KEYS, D])
    g_mix = din("g_mix", [1, D])
    g_memn = din("g_memn", [1, D])
    g_ffn = din("g_ffn", [1, D])
    g_mem = din("g_mem", [1, D])
    g_q = din("g_q", [1, 128])
    g_k = din("g_k", [1, 128])
    g_mq = din("g_mq", [1, 128])
    g_mk = din("g_mk", [1, 128])
    dww = din("dww", [128, CC, 31])
    dwb = din("dwb", [128, CC])
    lng = din("lng", [128, CC])
    lnb = din("lnb", [128, CC])
    rope_c = din("rope_c", [SEQ, 32])
    rope_s = din("rope_s", [128, 32])
    kc_p = din("kc_p", [1, SEQ])
    kc_s = din("kc_s", [1, SS])
    qch = din("qch", [128, NT])
    c_idb = din("c_idb", [128, 128], BF16)
    c_idf = din("c_idf", [128, 128])
    c_oneb = din("c_oneb", [128, 128], BF16)
    c_onef = din("c_onef", [128, 128])

    y = dout("y", [NTOK, D])
    o_k = dout("o_k", [SEQ, NKV * 128])
    o_v = dout("o_v", [SEQ, NKV * 128])
    o_ki = dout("o_ki", [SEQ, 128])
    o_conv = dout("o_conv", [30, CCH])
    o_mk = dout("o_mk", [MEMT, 512])
    o_mv = dout("o_mv", [MEMT, 512])
    o_ks = dout("o_ks", [2, 128, NKV * 128])
    o_vs = dout("o_vs", [2, 128, NKV * 128])
    o_kis = dout("o_kis", [2, 128, 128])
    o_convs = dout("o_convs", [2, 30, CCH])

    UTp = dscr("UTp", [CCH, 128 + NP * 128])
    UTs = dscr("UTs", [2, CCH, 160])
    KT = dscr("KT", [128, NKV, SEQ], BF16)
    Vc = dscr("Vc", [SEQ, NKV * 128], BF16)
    KIT = dscr("KIT", [128, SEQ], BF16)
    KTs = dscr("KTs", [2, 128, NKV, 128], BF16)
    Vs = dscr("Vs", [2, 128, NKV * 128], BF16)
    KITs = dscr("KITs", [2, 128, 128], BF16)
    MKT = dscr("MKT", [128, 4, MEMT], BF16)
    MV = dscr("MV", [MEMT, 512], BF16)
    QT = dscr("QT", [NT, 128, NH, 128], BF16)
    QIT = dscr("QIT", [NT, 128, IH, 128], BF16)
    WI = dscr("WI", [NT, 128, IH])
    MIXT = dscr("MIXT", [D, NTOK], BF16)
    H2 = dscr("H2", [NTOK, D])
    HN2T = dscr("HN2T", [D, NTOK], BF16)
    S12 = dscr("S12", [16, NTOK, 128])
    GALL = dscr("GALL", [128, 128, NTOK], BF16)

    UB16 = dscr("UB16", [128, 128, KC * 128], BF16)
    VB16 = dscr("VB16", [PEER_KEYS * PEER_KEYS, D], BF16)
    dbg_out = {}

    with contextlib.ExitStack() as es:
        P = Prog(nc, es)
        idb = P.sb([128, 128], BF16, "idb")
        idf = P.sb([128, 128], F32, "idf")
        oneb = P.sb([128, 128], BF16, "oneb")
        onef = P.sb([128, 128], F32, "onef")
        P.dma("sp", idb[:], c_idb)
        P.dma("sp", idf[:], c_idf)
        P.dma("sp", oneb[:], c_oneb)
        P.dma("sp", onef[:], c_onef)
        P.flush()

        def bcast_row(dst, src_row, n):
            P.dma("sp", dst, src_row.to_broadcast([128, n]))

        def rstd_from_ss(ss, n, out, tmp):
            P.ts("dve", tmp, ss, 1.0 / n, EPS, op0=ALU.mult, op1=ALU.add)
            P.act(tmp, tmp, AF.Sqrt)
            P.recip(out, tmp)

        def load_w(dst, src, ncols):
            sv = src.rearrange("(c p) n -> p c n", p=128)
            nq = 4 if KC % 4 == 0 else 1
            step = KC // nq
            for q in range(nq):
                P.dma("pool", dst[:, q * step:(q + 1) * step, 0:ncols], sv[:, q * step:(q + 1) * step, :], okey=(dst, q))

        def wkeys(dst):
            return [(dst, q) for q in range(4 if KC % 4 == 0 else 1)]

        class NormT:
            def __init__(self, gsrc, npt=2):
                self.npt = npt
                self.gbc = P.sb([128, D], F32)
                bcast_row(self.gbc[:], gsrc[0:1, :], D)
                self.sq = P.sb([128, D], BF16)
                self.xs = [P.sb([128, D], BF16) for _ in range(2)]
                self.sm = [P.sb([128, 4], F32) for _ in range(2)]
                self.pt = [P.ps([128, 1024], BF16) for _ in range(npt)]
                self.k = 0

            def run(self, x_t, dst_fn):
                k = self.k
                self.k += 1
                sm = self.sm[k % 2]
                xs = self.xs[k % 2]
                P.act(self.sq[:], x_t, AF.Square, accum_out=sm[:, 0:1])
                rstd_from_ss(sm[:, 0:1], D, sm[:, 1:2], sm[:, 2:3])
                P.stt(xs[:], x_t, sm[:, 1:2], self.gbc[:], ALU.mult, ALU.mult)
                nb = 8 if KC % 8 == 0 else KC
                for b0 in range(0, KC, nb):
                    pt = self.pt[(b0 // nb) % self.npt]
                    for j in range(nb):
                        P.tr(pt[:, j * 128:(j + 1) * 128], xs[:, (b0 + j) * 128:(b0 + j + 1) * 128], idb[:])
                    eng = "act" if (b0 // nb) % 2 == 0 else "dve"
                    P.copy(eng, dst_fn(b0, nb), pt[:, 0:nb * 128].rearrange("p (n t) -> p n t", t=128))

        def head_norm(ps_ap, nh, gain_bc, out_f, sq_t, sm_t):
            P.act(sq_t[:, 0:nh * 128], ps_ap, AF.Square)
            P.reduce(sm_t[:, 0:nh], sq_t[:, 0:nh * 128].rearrange("p (h d) -> p h d", d=128), ALU.add)
            rstd_from_ss(sm_t[:, 0:nh], 128, sm_t[:, 4:4 + nh], sm_t[:, 8:8 + nh])
            P.tt("dve", out_f, ps_ap.rearrange("p (h d) -> p h d", d=128),
                 sm_t[:, 4:4 + nh].unsqueeze(2).to_broadcast([128, nh, 128]), ALU.mult)
            P.tt("dve", out_f, out_f, gain_bc[:, 0:128].unsqueeze(1).to_broadcast([128, nh, 128]), ALU.mult)

        def rope(f, nh, cs, tmp):
            x1 = f[:, :, 0:16]
            x2 = f[:, :, 16:32]
            cosb = cs[:, 0:16].unsqueeze(1).to_broadcast([128, nh, 16])
            sinb = cs[:, 16:32].unsqueeze(1).to_broadcast([128, nh, 16])
            P.tt("dve", tmp[:, 0, 0:nh, :], x1, cosb, ALU.mult)
            P.tt("dve", tmp[:, 1, 0:nh, :], x2, sinb, ALU.mult)
            P.tt("dve", tmp[:, 2, 0:nh, :], x2, cosb, ALU.mult)
            P.tt("dve", tmp[:, 3, 0:nh, :], x1, sinb, ALU.mult)
            P.tt("dve", x1, tmp[:, 0, 0:nh, :], tmp[:, 1, 0:nh, :], ALU.subtract)
            P.tt("dve", x2, tmp[:, 2, 0:nh, :], tmp[:, 3, 0:nh, :], ALU.add)

        def tok_mm(ps_ap, hnT, tcol, wb, ncols, wk):
            for c in range(KC):
                P.mm(ps_ap, hnT[:, c, tcol:tcol + 128], wb[:, c, 0:ncols], start=(c == 0), stop=(c == KC - 1),
                     rkeys=[hnT] + wk)

        if "M" in stages:
            with contextlib.ExitStack() as st:
                P.st = st
                nt = NormT(g_mem)
                wk_b = P.sb([128, KC, 512], BF16)
                wv_b = P.sb([128, KC, 512], BF16)
                load_w(wk_b, w_km, 512)
                load_w(wv_b, w_vm, 512)
                gk = P.sb([128, 128], F32)
                bcast_row(gk[:], g_mk[0:1, :], 128)
                xt = [P.sb([128, D], F32) for _ in range(2)]
                hn = [P.sb([128, KC, 128], BF16) for _ in range(2)]
                pk = P.ps()
                pv = P.ps()
                ptr = P.ps([128, 1024], BF16)
                sq_t = P.sb([128, 512], F32)
                sm_t = P.sb([128, 12], F32)
                kf = P.sb([128, 4, 128], F32)
                kb = P.sb([128, 512], BF16)
                kTt = P.sb([128, 4, 128], BF16)
                vf = P.sb([128, 512], F32)
                vb = P.sb([128, 512], BF16)
                for m in range(MC):
                    x_t = xt[m % 2]
                    h_t = hn[m % 2]
                    P.dma("sp", x_t[:], mem[m * 128:(m + 1) * 128, :])
                    nt.run(x_t[:], lambda c0, n, h_t=h_t: h_t[:, c0:c0 + n, :])
                    tok_mm(pk[:, 0:512], h_t, 0, wk_b, 512, wkeys(wk_b))
                    tok_mm(pv[:, 0:512], h_t, 0, wv_b, 512, wkeys(wv_b))
                    head_norm(pk[:, 0:512], 4, gk, kf[:], sq_t, sm_t)
                    P.dma("sp", o_mk[m * 128:(m + 1) * 128, :], kf[:].rearrange("p h d -> p (h d)"))
                    P.copy("act", kb[:], kf[:].rearrange("p h d -> p (h d)"))
                    for h in range(4):
                        P.tr(ptr[:, h * 128:(h + 1) * 128], kb[:, h * 128:(h + 1) * 128], idb[:])
                    P.copy("dve", kTt[:], ptr[:, 0:512].rearrange("p (h t) -> p h t", t=128))
                    P.dma("sp", MKT[:, :, m * 128:(m + 1) * 128], kTt[:])
                    P.copy("act", vf[:], pv[:, 0:512])
                    P.dma("sp", o_mv[m * 128:(m + 1) * 128, :], vf[:])
                    P.copy("dve", vb[:], pv[:, 0:512])
                    P.dma("sp", MV[m * 128:(m + 1) * 128, :], vb[:])
                P.flush()
            P.st = es

        if "KV" in stages:
            with contextlib.ExitStack() as st:
                P.st = st
                nt = NormT(g_mix, npt=1)
                wb = P.sb([128, KC, 1152], BF16)
                load_w(wb, w_kv, 1152)
                wk = wkeys(wb)
                gk = P.sb([128, 128], F32)
                bcast_row(gk[:], g_k[0:1, :], 128)
                xt = [P.sb([128, D], F32) for _ in range(2)]
                hn = [P.sb([128, KC, 128], BF16) for _ in range(2)]
                cs = [P.sb([128, 32], F32) for _ in range(2)]
                pk2 = [P.ps() for _ in range(2)]
                pv2 = [P.ps() for _ in range(2)]
                pki2 = [P.ps() for _ in range(2)]
                ptr = P.ps([128, 1024], BF16)
                sq_t = P.sb([128, 512], F32)
                sm_t = P.sb([128, 12], F32)
                rtmp = P.sb([128, 4, 4, 16], F32)
                kf = [P.sb([128, 4, 128], F32) for _ in range(2)]
                kb = P.sb([128, 512], BF16)
                kTt = [P.sb([128, 4, 128], BF16) for _ in range(2)]
                vf = [P.sb([128, 512], F32) for _ in range(2)]
                vb = [P.sb([128, 512], BF16) for _ in range(2)]
                kif = [P.sb([128, 1, 128], F32) for _ in range(2)]
                kib = P.sb([128, 128], BF16)
                kiTt = [P.sb([128, 128], BF16) for _ in range(2)]
                tiles = [("p", i) for i in range(NCX)] + [("s", 0), ("s", 1)]
                def kv_front(n_):
                    kind, i = tiles[n_]
                    pk, pv, pki = pk2[n_ % 2], pv2[n_ % 2], pki2[n_ % 2]
                    x_t = xt[n_ % 2]
                    h_t = hn[n_ % 2]
                    c_t = cs[n_ % 2]
                    if kind == "p":
                        P.dma("sp", x_t[:], xctx[i * 128:(i + 1) * 128, :])
                        P.dma("sp", c_t[:], rope_c[i * 128:(i + 1) * 128, :])
                    else:
                        P.dma("sp", x_t[:], xsp[i])
                        P.dma("sp", c_t[:], rope_s)
                    nt.run(x_t[:], lambda c0, n, h_t=h_t: h_t[:, c0:c0 + n, :])
                    for c in range(KC):
                        P.mm(pk[:, 0:512], h_t[:, c, :], wb[:, c, 0:512], start=(c == 0), stop=(c == KC - 1), rkeys=[h_t] + wk)
                    for c in range(KC):
                        P.mm(pv[:, 0:512], h_t[:, c, :], wb[:, c, 512:1024], start=(c == 0), stop=(c == KC - 1), rkeys=[h_t] + wk)
                    for c in range(KC):
                        P.mm(pki[:, 0:128], h_t[:, c, :], wb[:, c, 1024:1152], start=(c == 0), stop=(c == KC - 1), rkeys=[h_t] + wk)

                def kv_back(n_):
                    kind, i = tiles[n_]
                    pk, pv, pki = pk2[n_ % 2], pv2[n_ % 2], pki2[n_ % 2]
                    c_t = cs[n_ % 2]
                    kf_t = kf[n_ % 2]
                    head_norm(pk[:, 0:512], 4, gk, kf_t[:], sq_t, sm_t)
                    rope(kf_t, 4, c_t, rtmp)
                    kflat = kf_t[:].rearrange("p h d -> p (h d)")
                    if kind == "p":
                        P.dma("sp", o_k[i * 128:(i + 1) * 128, :], kflat)
                    else:
                        P.dma("sp", o_ks[i], kflat)
                    P.copy("act", kb[:], kflat)
                    for h in range(4):
                        P.tr(ptr[:, h * 128:(h + 1) * 128], kb[:, h * 128:(h + 1) * 128], idb[:])
                    kT_t = kTt[n_ % 2]
                    P.copy("dve", kT_t[:], ptr[:, 0:512].rearrange("p (h t) -> p h t", t=128))
                    if kind == "p":
                        P.dma("sp", KT[:, :, i * 128:(i + 1) * 128], kT_t[:])
                    else:
                        P.dma("sp", KTs[i], kT_t[:])
                    vf_t = vf[n_ % 2]
                    vb_t = vb[n_ % 2]
                    P.copy("act", vf_t[:], pv[:, 0:512])
                    P.copy("dve", vb_t[:], pv[:, 0:512])
                    if kind == "p":
                        P.dma("sp", o_v[i * 128:(i + 1) * 128, :], vf_t[:])
                        P.dma("sp", Vc[i * 128:(i + 1) * 128, :], vb_t[:])
                    else:
                        P.dma("sp", o_vs[i], vf_t[:])
                        P.dma("sp", Vs[i], vb_t[:])
                    ki_t = kif[n_ % 2]
                    P.copy("act", ki_t[:, 0, :], pki[:, 0:128])
                    rope(ki_t, 1, c_t, rtmp)
                    if kind == "p":
                        P.dma("sp", o_ki[i * 128:(i + 1) * 128, :], ki_t[:, 0, :])
                    else:
                        P.dma("sp", o_kis[i], ki_t[:, 0, :])
                    P.copy("act", kib[:], ki_t[:, 0, :])
                    P.tr(ptr[:, 512:640], kib[:], idb[:])
                    kiT_t = kiTt[n_ % 2]
                    P.copy("dve", kiT_t[:], ptr[:, 512:640])
                    if kind == "p":
                        P.dma("sp", KIT[:, i * 128:(i + 1) * 128], kiT_t[:])
                    else:
                        P.dma("sp", KITs[i], kiT_t[:])
                kv_front(0)
                for n_ in range(len(tiles)):
                    if n_ + 1 < len(tiles):
                        kv_front(n_ + 1)
                    kv_back(n_)
                P.flush()
            P.st = es

        own = [("h", -1)] + [("p", i) for i in range(NP)] + [("s", 0), ("s", 1)]
        groups = [own[i:i + 4] for i in range(0, len(own), 4)]

        def tile_index(kind, i):
            return i if kind == "p" else NP + i

        if "MAIN" in stages:
            with contextlib.ExitStack() as st:
                P.st = st
                nt = NormT(g_mix)
                gq = P.sb([128, 128], F32)
                bcast_row(gq[:], g_q[0:1, :], 128)
                for s in range(2):
                    P.dma("sp", UTs[s][:, 2:32], stT[s], okey=("UTs", "st"))
                xt = [P.sb([128, D], F32) for _ in range(2)]
                hnT = P.sb([128, KC, 512], BF16)
                wbuf = [P.sb([128, KC, 512], BF16) for _ in range(2)]
                cst = P.sb([128, 4, 32], F32)
                pa = P.ps()
                pg = P.ps()
                pq = [P.ps() for _ in range(2)]
                ptr = P.ps([128, 1024], BF16)
                sg = P.sb([128, 512], F32)
                ut = [P.sb([128, 512], F32) for _ in range(2)]
                sq_t = P.sb([128, 512], F32)
                sm_t = P.sb([128, 12], F32)
                rtmp = P.sb([128, 4, 4, 16], F32)
                qf = P.sb([128, 4, 128], F32)
                qb = P.sb([128, 512], BF16)
                qTt = [P.sb([128, 4, 128], BF16) for _ in range(2)]
                wis = [P.sb([128, IH], F32) for _ in range(2)]
                wcnt = 0
                xcnt = 0
                for grp in groups:
                    ng = len(grp)
                    N = ng * 128
                    for tt, (kind, i) in enumerate(grp):
                        x_t = xt[xcnt % 2]
                        xcnt += 1
                        if kind == "h":
                            P.dma("sp", x_t[:], xhalo)
                        elif kind == "p":
                            P.dma("sp", x_t[:], xctx[i * 128:(i + 1) * 128, :])
                            P.dma("sp", cst[:, tt, :], rope_c[i * 128:(i + 1) * 128, :], okey=(cst, tt))
                        else:
                            P.dma("sp", x_t[:], xsp[i])
                            P.dma("sp", cst[:, tt, :], rope_s, okey=(cst, tt))
                        nt.run(x_t[:], lambda c0, n, tt=tt: hnT[:, c0:c0 + n, tt * 128:(tt + 1) * 128])
                    for b in range(CC // 2):
                        wb = wbuf[wcnt % 2]
                        wcnt += 1
                        load_w(wb, w_glu[:, b * 512:(b + 1) * 512], 512)
                        wk = wkeys(wb)
                        for s in range(2):
                            j = 2 * b + s
                            for c in range(KC):
                                P.mm(pa[:, 0:N], wb[:, c, (2 * s) * 128:(2 * s + 1) * 128], hnT[:, c, 0:N],
                                     start=(c == 0), stop=(c == KC - 1), rkeys=[hnT] + wk)
                            for c in range(KC):
                                P.mm(pg[:, 0:N], wb[:, c, (2 * s + 1) * 128:(2 * s + 2) * 128], hnT[:, c, 0:N],
                                     start=(c == 0), stop=(c == KC - 1), rkeys=[hnT] + wk)
                            P.act(sg[:, 0:N], pg[:, 0:N], AF.Sigmoid)
                            u_t = ut[j % 2]
                            P.tt("dve", u_t[:, 0:N], pa[:, 0:N], sg[:, 0:N], ALU.mult)
                            for tt, (kind, i) in enumerate(grp):
                                src = u_t[:, tt * 128:(tt + 1) * 128]
                                if kind == "h":
                                    P.dma("sp", UTp[j * 128:(j + 1) * 128, 0:128], src, okey=("UTp", None))
                                elif kind == "p":
                                    P.dma("sp", UTp[j * 128:(j + 1) * 128, 128 + i * 128:128 + (i + 1) * 128], src, okey=("UTp", None))
                                else:
                                    P.dma("sp", UTs[i][j * 128:(j + 1) * 128, 32:160], src, okey=("UTs", "tok"))
                    for which, nblk, wsrc, dst in (("q", NH // 4, w_q, QT), ("qi", IH // 4, w_qi, QIT)):
                        for b in range(nblk):
                            wb = wbuf[wcnt % 2]
                            wcnt += 1
                            load_w(wb, wsrc[:, b * 512:(b + 1) * 512], 512)
                            wk = wkeys(wb)
                            real = [(tt, kind, i) for tt, (kind, i) in enumerate(grp) if kind != "h"]
                            for n2, (tt, kind, i) in enumerate(real):
                                if n2 == 0:
                                    tok_mm(pq[n2 % 2][:, 0:512], hnT, tt * 128, wb, 512, wk)
                                if n2 + 1 < len(real):
                                    tok_mm(pq[(n2 + 1) % 2][:, 0:512], hnT, real[n2 + 1][0] * 128, wb, 512, wk)
                                ti = tile_index(kind, i)
                                pq_t = pq[n2 % 2]
                                if which == "q":
                                    head_norm(pq_t[:, 0:512], 4, gq, qf[:], sq_t, sm_t)
                                else:
                                    P.copy("act", qf[:].rearrange("p h d -> p (h d)"), pq_t[:, 0:512])
                                rope(qf, 4, cst[:, tt, :], rtmp)
                                P.copy("act", qb[:], qf[:].rearrange("p h d -> p (h d)"))
                                for h in range(4):
                                    P.tr(ptr[:, h * 128:(h + 1) * 128], qb[:, h * 128:(h + 1) * 128], idb[:])
                                q_T = qTt[n2 % 2]
                                P.copy("dve", q_T[:], ptr[:, 0:512].rearrange("p (h t) -> p h t", t=128))
                                P.dma("sp", dst[ti][:, b * 4:(b + 1) * 4, :], q_T[:], okey=(dst.tensor.name, None))
                    wb = wbuf[wcnt % 2]
                    wcnt += 1
                    load_w(wb, w_wi, IH)
                    wk = wkeys(wb)
                    for tt, (kind, i) in enumerate(grp):
                        if kind == "h":
                            continue
                        ti = tile_index(kind, i)
                        pq_t = pq[tt % 2]
                        tok_mm(pq_t[:, 0:IH], hnT, tt * 128, wb, IH, wk)
                        w_s = wis[tt % 2]
                        P.act(w_s[:], pq_t[:, 0:IH], AF.Copy, scale=IDX_SCALE)
                        P.dma("sp", WI[ti], w_s[:], okey=("WI", None))
                P.flush()
            P.st = es

        if "B" in stages:
            with contextlib.ExitStack() as st:
                P.st = st
                P.npool["uin"] = 4
                wt = P.sb([128, CC, 31], F32)
                bt = P.sb([128, CC], F32)
                lg = P.sb([128, CC], F32)
                lb = P.sb([128, CC], F32)
                P.dma("sp", wt[:], dww)
                P.dma("sp", bt[:], dwb)
                P.dma("sp", lg[:], lng)
                P.dma("sp", lb[:], lnb)
                uin = P.sb([128, CC, 544], F32, "uin")
                cc_t = P.sb([128, CC, 512], F32)
                sqt = [P.sb([128, 512], F32) for _ in range(2)]
                p1 = P.ps()
                p2 = P.ps()
                mean = P.sb([128, 512], F32)
                var = P.sb([128, 512], F32)
                rstd = P.sb([128, 512], F32)
                tmp = [P.sb([128, 512], F32) for _ in range(2)]
                co = [P.sb([128, 512], BF16) for _ in range(2)]
                ptc = P.ps()
                cnew = P.sb([32, CCH], F32)
                jobs = []
                for tb in range(max(1, NP * 128 // 512)):
                    ntk = min(512, NP * 128)
                    jobs.append(("p", tb, ntk))
                jobs += [("s", 0, 128), ("s", 1, 128)]
                for kind, tb, ntk in jobs:
                    if kind == "p":
                        c0 = 128 + tb * ntk
                        src = UTp[:, c0 - 30:c0 + ntk].rearrange("(j p) t -> p j t", p=128)
                        mcol = tb * ntk
                        sk = "UTp"
                    else:
                        src = UTs[tb][:, 2:160].rearrange("(j p) t -> p j t", p=128)
                        mcol = (NP + tb) * 128
                        sk = "UTs"
                    W = 30 + ntk
                    P.dma("sp", uin[:, :, 0:W], src, ikey=sk)
                    last_p = (kind == "p" and (tb + 1) * ntk == NP * 128)
                    if last_p or kind == "s":
                        a0 = (30 + ntk - 30) if kind == "p" else (30 + 64 - 30)
                        for j0 in range(0, CC, 4):
                            for j in range(j0, min(CC, j0 + 4)):
                                P.tr(ptc[0:30, (j - j0) * 128:(j - j0 + 1) * 128], uin[:, j, a0:a0 + 30], idf[:])
                            nj = min(CC, j0 + 4) - j0
                            P.copy("act", cnew[0:30, j0 * 128:(j0 + nj) * 128], ptc[0:30, 0:nj * 128])
                        P.dma("sp", o_conv if kind == "p" else o_convs[tb], cnew[0:30, :])
                    for j in range(CC):
                        acc = cc_t[:, j, 0:ntk]
                        P.ts("dve", acc, uin[:, j, 0:ntk], wt[:, j, 0:1], bt[:, j:j + 1], op0=ALU.mult, op1=ALU.add, okey=(cc_t, j))
                        for k in range(1, 31):
                            P.stt(acc, uin[:, j, k:k + ntk], wt[:, j, k:k + 1], acc, ALU.mult, ALU.add, okey=(cc_t, j),
                                  rkeys=[uin, (cc_t, j)])
                        s_t = sqt[j % 2]
                        P.act(s_t[:, 0:ntk], acc, AF.Square, ikey=(cc_t, j))
                        P.mm(p1[:, 0:ntk], onef[:], acc, start=(j == 0), stop=(j == CC - 1), rkeys=[onef, (cc_t, j)])
                        P.mm(p2[:, 0:ntk], onef[:], s_t[:, 0:ntk], start=(j == 0), stop=(j == CC - 1))
                    P.ts("dve", mean[:, 0:ntk], p1[:, 0:ntk], 1.0 / CCH, None, op0=ALU.mult)
                    P.tt("dve", var[:, 0:ntk], mean[:, 0:ntk], mean[:, 0:ntk], ALU.mult)
                    P.stt(var[:, 0:ntk], p2[:, 0:ntk], 1.0 / CCH, var[:, 0:ntk], ALU.mult, ALU.subtract)
                    P.ts("dve", var[:, 0:ntk], var[:, 0:ntk], EPS, None, op0=ALU.add)
                    P.act(var[:, 0:ntk], var[:, 0:ntk], AF.Sqrt)
                    P.recip(rstd[:, 0:ntk], var[:, 0:ntk])
                    for j in range(CC):
                        t_ = tmp[j % 2]
                        P.tt("dve", t_[:, 0:ntk], cc_t[:, j, 0:ntk], mean[:, 0:ntk], ALU.subtract, rkeys=[(cc_t, j), mean])
                        P.tt("dve", t_[:, 0:ntk], t_[:, 0:ntk], rstd[:, 0:ntk], ALU.mult)
                        c_o = co[j % 2]
                        P.act(c_o[:, 0:ntk], t_[:, 0:ntk], AF.Silu, scale=lg[:, j:j + 1], bias=lb[:, j:j + 1])
                        P.dma("sp", MIXT[j * 128:(j + 1) * 128, mcol:mcol + ntk], c_o[:, 0:ntk], okey=("MIXT", "conv"))
                P.flush()
            P.st = es

        if "C" in stages:
            with contextlib.ExitStack() as st:
                P.st = st
                SMAX = max(SEQ, SS)
                P.npool["kiT_c"] = 4
                P.npool["kT_c"] = 4
                P.npool["v_c"] = 4
                kiT_c = P.sb([128, SMAX], BF16, "kiT_c")
                kT_c = P.sb([128, NKV, SMAX], BF16, "kT_c")
                v_c = P.sb([128, SMAX // 128, NKV * 128], BF16, "v_c")
                kc_t = P.sb([128, SMAX], BF16)
                qch_t = P.sb([128, NT], F32)
                P.dma("sp", qch_t[:], qch)
                qiT = [P.sb([128, IH, 128], BF16) for _ in range(2)]
                qT = [P.sb([128, NH, 128], BF16) for _ in range(2)]
                wi_t = [P.sb([128, IH], F32) for _ in range(2)]
                acc2 = [P.sb([128, SMAX], F32) for _ in range(2)]
                madd = P.sb([128, SMAX], BF16)
                bs = P.sb([128, 8], F32)
                m8 = P.sb([128, 256], F32)
                thr = P.sb([128, 1], F32)
                mask = P.sb([128, SMAX], BF16)
                maskT = P.sb([128, SMAX // 128, 128], BF16)
                rl = [P.sb([128, 512], F32) for _ in range(4)]
                pe_ = [P.sb([128, GQ, 128], BF16) for _ in range(3)]
                pm = [P.sb([128, GQ, 128], BF16) for _ in range(4)]
                rz = P.sb([128, GQ * 128], F32)
                ob = [P.sb([128, GQ, 128], BF16) for _ in range(2)]
                ps_s = [P.ps() for _ in range(2)]
                ps_qk = [P.ps() for _ in range(3)]
                ps_o = P.ps()
                ps_z = P.ps()
                ptr = [P.ps([128, 1024], BF16) for _ in range(1)]

                def seglist(blocks):
                    nb = len(blocks)
                    segs = []
                    a = 0
                    while a < nb:
                        b_ = a + 1
                        while b_ < nb and b_ - a < 4 and blocks[b_] == blocks[b_ - 1] + 1:
                            b_ += 1
                        segs.append((a, b_))
                        a = b_
                    return segs

                cnt = dict(n=0, it=0)
                P.npool["UB16"] = 3
                P.npool["VB16"] = 3
                pre_ops = []
                for c in range(0, 128, 2):
                    pre_ops.append(("u", c))
                    pre_ops.append(("v", c))
                n_slots = (NP + 2) * NKV
                per_slot = -(-len(pre_ops) // n_slots)

                def precast_some():
                    dd = min(2048, KC * 128)
                    for _ in range(per_slot):
                        if not pre_ops:
                            return
                        kind, c = pre_ops.pop(0)
                        if kind == "u":
                            P.dma("pool", UB16[c:c + 2].rearrange("c p (x d) -> p c x d", d=dd),
                                  uT[c:c + 2].rearrange("c p (x d) -> p c x d", d=dd), okey=("UB16", None))
                        else:
                            dv = min(2048, D)
                            P.dma("pool", VB16[c * 128:(c + 2) * 128, :].rearrange("(c p) (x d) -> p c x d", p=128, d=dv),
                                  vtab[c * 128:(c + 2) * 128, :].rearrange("(c p) (x d) -> p c x d", p=128, d=dv), okey=("VB16", None))

                def idx_phase(job):
                    ti, blocks = job["ti"], job["blocks"]
                    if job.get("pre_idx"):
                        job["pre_idx"]()
                    k2 = ti % 2
                    acc = acc2[k2]
                    P.dma("sp", qiT[k2][:], QIT[ti], ikey="QIT")
                    P.dma("sp", wi_t[k2][:], WI[ti], ikey="WI")
                    segs = seglist(blocks)
                    N = len(blocks) * 128
                    for (a, b_) in segs:
                        P.ts("pool", madd[:, a * 128:b_ * 128], kc_t[:, blocks[a] * 128:(blocks[a] + b_ - a) * 128],
                             qch_t[:, ti:ti + 1], NEG, op0=ALU.is_gt, op1=ALU.mult)
                    for (a, b_) in segs:
                        w = (b_ - a) * 128
                        for h in range(IH):
                            n_ = cnt["n"]
                            cnt["n"] += 1
                            p_ = ps_s[n_ % 2]
                            r_ = rl[n_ % 4]
                            P.mm(p_[:, 0:w], qiT[k2][:, h, :], kiT_c[:, blocks[a] * 128:blocks[a] * 128 + w], rkeys=[qiT[k2], kiT_c])
                            P.act(r_[:, 0:w], p_[:, 0:w], AF.Relu)
                            if h == 0:
                                P.ts("dve", acc[:, a * 128:b_ * 128], r_[:, 0:w], wi_t[k2][:, 0:1], None, op0=ALU.mult)
                            else:
                                P.stt(acc[:, a * 128:b_ * 128], r_[:, 0:w], wi_t[k2][:, h:h + 1], acc[:, a * 128:b_ * 128], ALU.mult, ALU.add)
                    P.reduce(bs[:, 0:1], acc[:, 0:N], ALU.min)
                    P.tt("pool", acc[:, 0:N], acc[:, 0:N], madd[:, 0:N], ALU.add)
                    P.max8(m8[:, 0:8], acc[:, 0:N])
                    P.copy("dve", bs[:, 1:2], m8[:, 0:1])

                NITER = 22

                def topk_rounds(job, r0, r1):
                    ti, N, topk = job["ti"], len(job["blocks"]) * 128, job["topk"]
                    acc = acc2[ti % 2]
                    lo, hi, mid, tmp, cn, sel, dd = (bs[:, i:i + 1] for i in range(7))
                    for r in range(r0, min(r1, NITER)):
                        P.ts("dve", tmp, hi, 0.5, None, op0=ALU.mult)
                        P.stt(mid, lo, 0.5, tmp, ALU.mult, ALU.add)
                        P.ts("dve", mask[:, 0:N], acc[:, 0:N], mid, None, op0=ALU.is_ge, op1=ALU.add, accum_out=cn)
                        P.ts("dve", sel, cn, float(topk) - 0.5, None, op0=ALU.is_ge)
                        P.tt("dve", dd, mid, lo, ALU.subtract)
                        P.stt(lo, dd, sel, lo, ALU.mult, ALU.add)
                        P.tt("dve", dd, hi, mid, ALU.subtract)
                        P.stt(hi, dd, sel, mid, ALU.mult, ALU.add)

                def topk_final(job):
                    ti, blocks, topk = job["ti"], job["blocks"], job["topk"]
                    nb = len(blocks)
                    N = nb * 128
                    acc = acc2[ti % 2]
                    P.ts("dve", thr[:], bs[:, 0:1], 0.5 * NEG, None, op0=ALU.max)
                    P.ts("dve", mask[:, 0:N], acc[:, 0:N], thr[:, 0:1], None, op0=ALU.is_ge)
                    for b0 in range(0, nb, 8):
                        n8 = min(8, nb - b0)
                        pt = ptr[0]
                        for j in range(n8):
                            P.tr(pt[:, j * 128:(j + 1) * 128], mask[:, (b0 + j) * 128:(b0 + j + 1) * 128], idb[:])
                        P.copy("act", maskT[:, b0:b0 + n8, :], pt[:, 0:n8 * 128].rearrange("p (n t) -> p n t", t=128))

                def attn_group(job, g):
                    ti, blocks = job["ti"], job["blocks"]
                    k2 = ti % 2
                    nb = len(blocks)
                    if g == 0:
                        if job.get("pre_attn"):
                            job["pre_attn"]()
                        P.dma("sp", qT[k2][:], QT[ti], ikey="QT")
                    W = GQ * 128
                    LA = 2
                    bufs = {}

                    def front(ci):
                        blk = blocks[ci]
                        it = cnt["it"]
                        cnt["it"] += 1
                        pq_ = ps_qk[it % 3]
                        e_ = pe_[it % 3]
                        m_ = pm[it % 4]
                        bufs[ci] = m_
                        P.mm(pq_[:, 0:W], kT_c[:, g, blk * 128:(blk + 1) * 128],
                             qT[k2][:, g * GQ:(g + 1) * GQ, :].rearrange("p r t -> p (r t)"), rkeys=[kT_c, qT[k2]])
                        P.act(e_[:].rearrange("p r t -> p (r t)"), pq_[:, 0:W], AF.Exp, scale=ATT_SCALE)
                        P.tt("pool", m_[:], e_[:], maskT[:, ci, :].unsqueeze(1).to_broadcast([128, GQ, 128]), ALU.mult)

                    def back(ci):
                        blk = blocks[ci]
                        m_ = bufs[ci]
                        mf = m_[:].rearrange("p r t -> p (r t)")
                        P.mm(ps_o[:, 0:W], v_c[:, blk, g * 128:(g + 1) * 128], mf, start=(ci == 0), stop=(ci == nb - 1), rkeys=[v_c, m_])
                        P.mm(ps_z[:, 0:W], oneb[:], mf, start=(ci == 0), stop=(ci == nb - 1))

                    for ci in range(min(LA, nb)):
                        front(ci)
                    for ci in range(nb):
                        if ci + LA < nb:
                            front(ci + LA)
                        back(ci)
                    P.recip(rz[:, 0:W], ps_z[:, 0:W])
                    o_ = ob[g % 2]
                    P.tt("dve", o_[:].rearrange("p r t -> p (r t)"), ps_o[:, 0:W], rz[:, 0:W], ALU.mult)
                    P.dma("sp", MIXT[CCH + g * GQ * 128:CCH + (g + 1) * GQ * 128, ti * 128:(ti + 1) * 128].rearrange("(r d) t -> d r t", d=128),
                          o_[:], okey=("MIXT", "attn"))

                def load_prompt_ki():
                    P.cdma(kc_t[:, 0:SEQ], kc_p[0:1, :].to_broadcast([128, SEQ]))
                    P.dma("sp", kiT_c[:, 0:SEQ], KIT, ikey="KIT", okey=(kiT_c, 0))

                def load_prompt_kv():
                    P.dma("sp", kT_c[:, :, 0:SEQ], KT, ikey="KT", okey=(kT_c, 0))
                    P.dma("sp", v_c[:, 0:NCX, :], Vc.rearrange("(c p) n -> p c n", p=128), ikey="Vc", okey=(v_c, 0))

                def mk_sample_ki(s):
                    def f():
                        P.cdma(kc_t[:, 0:SS], kc_s[0:1, :].to_broadcast([128, SS]))
                        P.cdma(kiT_c[:, 0:PAST], ckiT[s], okey=(kiT_c, 0))
                        P.dma("sp", kiT_c[:, PAST:SS], KITs[s], ikey="KITs", okey=(kiT_c, 1))
                    return f

                def mk_sample_kv(s):
                    def f():
                        for g in range(NKV):
                            P.cdma(kT_c[:, g, 0:PAST], ckT[s][:, g, :], okey=(kT_c, 0))
                        P.dma("sp", kT_c[:, :, PAST:SS], KTs[s], ikey="KTs", okey=(kT_c, 1))
                        cvv = cv[s].rearrange("(c p) n -> p c n", p=128)
                        nq = 4 if (PAST // 128) % 4 == 0 else 1
                        stp = (PAST // 128) // nq
                        for q in range(nq):
                            P.dma("pool", v_c[:, q * stp:(q + 1) * stp, :], cvv[:, q * stp:(q + 1) * stp, :], okey=(v_c, 0))
                        P.dma("sp", v_c[:, PAST // 128, :], Vs[s], ikey="Vs", okey=(v_c, 1))
                    return f

                jobs = []
                for i in range(NP):
                    jobs.append(dict(ti=i, blocks=list(range(0, i + 1)) + list(range(NP, 2 * NP)), topk=cfg["TOPK_P"]))
                jobs[0]["pre_idx"] = load_prompt_ki
                jobs[0]["pre_attn"] = load_prompt_kv
                for s in range(2):
                    jobs.append(dict(ti=NP + s, blocks=list(range(SS // 128)), topk=cfg["TOPK_S"],
                                     pre_idx=mk_sample_ki(s), pre_attn=mk_sample_kv(s)))
                idx_phase(jobs[0])
                topk_rounds(jobs[0], 0, 10 ** 6)
                topk_final(jobs[0])
                for k, job in enumerate(jobs):
                    nxt = jobs[k + 1] if k + 1 < len(jobs) else None
                    if nxt is not None:
                        idx_phase(nxt)
                        per = -(-NITER // NKV)
                    for g in range(NKV):
                        if nxt is not None:
                            topk_rounds(nxt, g * per, (g + 1) * per)
                        precast_some()
                        attn_group(job, g)
                    if nxt is not None:
                        topk_final(nxt)
                while pre_ops:
                    precast_some()
                P.flush()
            P.st = es

        otiles = list(range(NT))
        ogroups = [otiles[i:i + 4] for i in range(0, NT, 4)]
        Hs = dscr("Hs", [NTOK, D])
        RC = dscr("RC", [NT, 128, 3, 128])

        def x_rows(ti):
            return xctx[ti * 128:(ti + 1) * 128, :] if ti < NP else xsp[ti - NP]

        if "D" in stages:
            with contextlib.ExitStack() as st:
                P.st = st
                mixT = P.sb([128, KC, 512], BF16)
                wbuf = [P.sb([128, KC, 512], BF16) for _ in range(2)]
                xb_ = [P.sb([128, 512], F32) for _ in range(3)]
                hb_ = [P.sb([128, 512], F32) for _ in range(3)]
                pp = [P.ps() for _ in range(4)]
                wcnt = 0
                k_ = 0
                for grp in ogroups:
                    N = len(grp) * 128
                    c0 = grp[0] * 128
                    P.dma("sp", mixT[:, :, 0:N], MIXT[:, c0:c0 + N].rearrange("(c p) n -> p c n", p=128), ikey="MIXT")
                    for b in range(D // 512):
                        wb = wbuf[wcnt % 2]
                        wcnt += 1
                        load_w(wb, w_out[:, b * 512:(b + 1) * 512], 512)
                        wk = wkeys(wb)
                        for tt, ti in enumerate(grp):
                            xb = xb_[k_ % 3]
                            hb = hb_[k_ % 3]
                            p_ = pp[k_ % 4]
                            k_ += 1
                            P.dma("sp", xb[:], x_rows(ti)[:, b * 512:(b + 1) * 512])
                            tok_mm(p_[:, 0:512], mixT, tt * 128, wb, 512, wk)
                            P.tt("dve", hb[:], p_[:, 0:512], xb[:], ALU.add)
                            P.dma("sp", Hs[ti * 128:(ti + 1) * 128, b * 512:(b + 1) * 512], hb[:], okey=("Hs", None))
                P.flush()
            P.st = es

            with contextlib.ExitStack() as st:
                P.st = st
                nt = NormT(g_memn)
                gbc2 = P.sb([128, D], F32)
                bcast_row(gbc2[:], g_ffn[0:1, :], D)
                gmq = P.sb([128, 128], F32)
                bcast_row(gmq[:], g_mq[0:1, :], 128)
                wqm_b = P.sb([128, KC, 512], BF16)
                load_w(wqm_b, w_qm, 512)
                wom_b = P.sb([128, 4, D], BF16)
                P.cdma(wom_b[:], w_om.rearrange("(h p) d -> p h d", p=128))
                mkT_c = P.sb([128, 4, MEMT], BF16)
                mv_c = P.sb([128, MC, 512], BF16)
                ht = [P.sb([128, D], F32) for _ in range(2)]
                hn = [P.sb([128, KC, 128], BF16) for _ in range(2)]
                pq_ = P.ps()
                pl_ = [P.ps() for _ in range(2)]
                po_ = P.ps()
                pz_ = pq_
                pw_ = [P.ps() for _ in range(1)]
                ptr = P.ps([128, 1024], BF16)
                sq_t = P.sb([128, 512], F32)
                sm_t = P.sb([128, 12], F32)
                qmf = P.sb([128, 4, 128], F32)
                qmb = P.sb([128, 512], BF16)
                qmT = P.sb([128, 4, 128], BF16)
                pmT = [P.sb([128, 4, 128], BF16) for _ in range(MC)]
                rz = P.sb([128, 512], F32)
                omT = P.sb([128, 4, 128], BF16)
                for ti in otiles:
                    if ti == 0:
                        P.dma("sp", mkT_c[:], MKT, ikey="MKT")
                        P.dma("sp", mv_c[:], MV.rearrange("(c p) n -> p c n", p=128), ikey="MV")
                    elif ti >= NP:
                        P.dma("pool", mkT_c[:], cmkT[ti - NP])
                        P.dma("pool", mv_c[:], cmv[ti - NP].rearrange("(c p) n -> p c n", p=128))
                    h_t = ht[ti % 2]
                    hn_t = hn[ti % 2]
                    P.dma("sp", h_t[:], Hs[ti * 128:(ti + 1) * 128, :], ikey="Hs")
                    nt.run(h_t[:], lambda c0, n, hn_t=hn_t: hn_t[:, c0:c0 + n, :])
                    tok_mm(pq_[:, 0:512], hn_t, 0, wqm_b, 512, wkeys(wqm_b))
                    head_norm(pq_[:, 0:512], 4, gmq, qmf[:], sq_t, sm_t)
                    P.copy("act", qmb[:], qmf[:].rearrange("p h d -> p (h d)"))
                    for h in range(4):
                        P.tr(ptr[:, h * 128:(h + 1) * 128], qmb[:, h * 128:(h + 1) * 128], idb[:])
                    P.copy("dve", qmT[:], ptr[:, 0:512].rearrange("p (h t) -> p h t", t=128))
                    for mc in range(MC):
                        for h in range(4):
                            P.mm(pl_[mc % 2][:, h * 128:(h + 1) * 128], mkT_c[:, h, mc * 128:(mc + 1) * 128], qmT[:, h, :])
                        P.act(pmT[mc][:].rearrange("p h t -> p (h t)"), pl_[mc % 2][:, 0:512], AF.Exp, scale=ATT_SCALE)
                    for h in range(4):
                        for mc in range(MC):
                            P.mm(po_[:, h * 128:(h + 1) * 128], mv_c[:, mc, h * 128:(h + 1) * 128], pmT[mc][:, h, :],
                                 start=(mc == 0), stop=(mc == MC - 1))
                    for mc in range(MC):
                        P.mm(pz_[:, 0:512], oneb[:], pmT[mc][:].rearrange("p h t -> p (h t)"), start=(mc == 0), stop=(mc == MC - 1))
                    P.recip(rz[:], pz_[:, 0:512])
                    P.tt("dve", omT[:].rearrange("p h t -> p (h t)"), po_[:, 0:512], rz[:], ALU.mult)
                    for b in range(D // 512):
                        p_ = pw_[0]
                        for h in range(4):
                            P.mm(p_[:, 0:512], omT[:, h, :], wom_b[:, h, b * 512:(b + 1) * 512], start=(h == 0), stop=(h == 3))
                        P.tt("dve", h_t[:, b * 512:(b + 1) * 512], p_[:, 0:512], h_t[:, b * 512:(b + 1) * 512], ALU.add)
                    P.dma("sp", H2[ti * 128:(ti + 1) * 128, :], h_t[:], okey=("H2", None))
                    nt.gbc, g_save = gbc2, nt.gbc
                    nt.run(h_t[:], lambda c0, n, hn_t=hn_t: hn_t[:, c0:c0 + n, :])
                    nt.gbc = g_save
                    P.dma("sp", HN2T[:, ti * 128:(ti + 1) * 128].rearrange("(c p) t -> p c t", p=128), hn_t[:], okey=("HN2T", None))
                P.flush()
            P.st = es

            with contextlib.ExitStack() as st:
                P.st = st
                hn2 = P.sb([128, KC, 512], BF16)
                wbuf = [P.sb([128, KC, 512], BF16) for _ in range(2)]
                qpT = P.sb([128, 16, 512], F32)
                sk_t = P.sb([128, 16, 128], F32)
                P.dma("sp", sk_t[:], subk)
                pq_ = [P.ps() for _ in range(2)]
                ps_ = [P.ps() for _ in range(2)]
                ptf = P.ps()
                s12 = [P.sb([128, 16, 128], F32) for _ in range(2)]
                v16 = P.sb([128, 16, 16], F32)
                tmp128 = P.sb([128, 128], F32)
                cand = P.sb([128, 8, 256], F32)
                tmpc = P.sb([128, 256], F32)
                t16 = P.sb([128, 8, 16], F32)
                e16 = P.sb([128, 8, 16], F32)
                zz = P.sb([128, 8], F32)
                mlz = P.sb([128, 8], F32)
                rc3 = P.sb([128, 3, 8, 16], F32)
                rcT = [P.sb([128, 3, 128], F32) for _ in range(2)]
                wcnt = 0
                for grp in ogroups:
                    N = len(grp) * 128
                    c0 = grp[0] * 128
                    P.dma("sp", hn2[:, :, 0:N], HN2T[:, c0:c0 + N].rearrange("(c p) n -> p c n", p=128), ikey="HN2T")
                    for b in range(4):
                        wb = wbuf[wcnt % 2]
                        wcnt += 1
                        load_w(wb, w_pq[:, b * 512:(b + 1) * 512], 512)
                        wk = wkeys(wb)
                        for jj in range(4):
                            j = b * 4 + jj
                            p_ = pq_[j % 2]
                            for c in range(KC):
                                P.mm(p_[:, 0:N], wb[:, c, jj * 128:(jj + 1) * 128], hn2[:, c, 0:N], start=(c == 0), stop=(c == KC - 1),
                                     rkeys=[hn2] + wk)
                            P.copy("act", qpT[:, j, 0:N], p_[:, 0:N], okey=(qpT, j))
                    for tt, ti in enumerate(grp):
                        s_t = s12[ti % 2]
                        for jb in range(4):
                            p_ = ps_[jb % 2]
                            for jj in range(4):
                                j = jb * 4 + jj
                                P.mm(p_[:, jj * 128:(jj + 1) * 128], qpT[:, j, tt * 128:(tt + 1) * 128], sk_t[:, j, :], rkeys=[(qpT, j), sk_t])
                            P.copy("act", s_t[:, jb * 4:(jb + 1) * 4, :].rearrange("p j k -> p (j k)"), p_[:, 0:512])
                        P.dma("sp", S12[:, ti * 128:(ti + 1) * 128, :].rearrange("j t i -> t j i"), s_t[:], okey=("S12", None))
                        for j in range(16):
                            P.max8(v16[:, j, 0:8], s_t[:, j, :])
                            P.mrep(tmp128[:], v16[:, j, 0:8], s_t[:, j, :], -3.0e38)
                            P.max8(v16[:, j, 8:16], tmp128[:])
                        v16v = v16[:].rearrange("p (h two) k -> p h two k", two=2)
                        for h in range(8):
                            P.tt("dve", cand[:, h, :].rearrange("p (a b) -> p a b", b=16),
                                 v16[:, 2 * h, :].unsqueeze(2).to_broadcast([128, 16, 16]),
                                 v16[:, 2 * h + 1, :].unsqueeze(1).to_broadcast([128, 16, 16]), ALU.add)
                        for h in range(8):
                            P.max8(t16[:, h, 0:8], cand[:, h, :])
                            P.mrep(tmpc[:], t16[:, h, 0:8], cand[:, h, :], -3.0e38)
                            P.max8(t16[:, h, 8:16], tmpc[:])
                        P.tt("dve", e16[:], t16[:], t16[:, :, 0:1].to_broadcast([128, 8, 16]), ALU.subtract)
                        P.act(e16[:], e16[:], AF.Exp)
                        P.reduce(zz[:], e16[:], ALU.add)
                        P.act(mlz[:], zz[:], AF.Ln)
                        P.tt("dve", mlz[:], mlz[:], t16[:, :, 0], ALU.add)
                        P.copy("dve", rc3[:, 0, :, :], v16v[:, :, 0, :])
                        P.tt("dve", rc3[:, 1, :, :], t16[:, :, 15:16].to_broadcast([128, 8, 16]), rc3[:, 0, :, :], ALU.subtract)
                        P.tt("dve", rc3[:, 2, :, :], rc3[:, 0, :, :], mlz[:].unsqueeze(2).to_broadcast([128, 8, 16]), ALU.subtract)
                        for q in range(3):
                            P.tr(ptf[:, q * 128:(q + 1) * 128], rc3[:, q, :, :].rearrange("p h a -> p (h a)"), idf[:])
                        r_T = rcT[ti % 2]
                        P.copy("act", r_T[:].rearrange("p q t -> p (q t)"), ptf[:, 0:384])
                        P.dma("sp", RC[ti], r_T[:], okey=("RC", None))
                P.flush()
            P.st = es

            with contextlib.ExitStack() as st:
                P.st = st
                TB = 32
                s1r = [P.sb([128, TB, 128], F32) for _ in range(2)]
                s2r = [P.sb([128, TB, 128], F32) for _ in range(2)]
                rct = [P.sb([128, 3, 128], F32) for _ in range(2)]
                o1 = [P.sb([128, 128], BF16) for _ in range(4)]
                ee = [P.sb([128, 128], F32) for _ in range(4)]
                rr = [P.sb([128, 128], BF16) for _ in range(4)]
                gst = [P.sb([128, 128, 128], BF16) for _ in range(2)]
                pg_ = [P.ps() for _ in range(2)]
                kk = 0
                for ti in otiles:
                    rc_ = rct[ti % 2]
                    g_s = gst[ti % 2]
                    P.dma("sp", rc_[:], RC[ti], ikey="RC")
                    for tb in range(128 // TB):
                        t0 = ti * 128 + tb * TB
                        a1 = s1r[tb % 2]
                        a2 = s2r[tb % 2]
                        for half, dst in ((0, a1), (1, a2)):
                            src = S12[:, t0:t0 + TB, :].rearrange("(h two) t i -> two h (t i)", two=2)[half]
                            P.dma("sp", dst[:].rearrange("p t i -> p (t i)"), src.unsqueeze(1).to_broadcast([8, 16, TB * 128]),
                                  ikey="S12", okey=(dst, None))
                        for tq in range(0, TB, 4):
                            p_ = pg_[(kk) % 2]
                            kk += 1
                            for u4 in range(4):
                                tl = tq + u4
                                t = tb * TB + tl
                                o_ = o1[u4]
                                e_ = ee[u4]
                                r_ = rr[u4]
                                P.ts("dve", o_[:], a1[:, tl, :], rc_[:, 0, t:t + 1], None, op0=ALU.is_equal)
                                P.act(e_[:], a2[:, tl, :], AF.Exp, bias=rc_[:, 2, t:t + 1])
                                P.stt(r_[:], a2[:, tl, :], rc_[:, 1, t:t + 1], e_[:], ALU.is_ge, ALU.mult)
                                P.mm(p_[:, u4 * 128:(u4 + 1) * 128], o_[:], r_[:])
                            tbase = tb * TB + tq
                            P.copy("act", g_s[:, :, tbase:tbase + 4].rearrange("p i t -> p t i"),
                                   p_[:, 0:512].rearrange("p (t i) -> p t i", i=128))
                    P.dma("sp", GALL[:, :, ti * 128:(ti + 1) * 128], g_s[:], okey=("GALL", None))
                P.flush()
            P.st = es

        if "E" in stages:
            with contextlib.ExitStack() as st:
                P.st = st
                NCH = PEER_KEYS
                EB = 4
                hn2 = P.sb([128, KC, 512], BF16)
                oacc = P.sb([128, 4, D], F32)
                ub = [P.sb([128, KC, 128], BF16) for _ in range(3)]
                vb = [P.sb([128, EB, D], BF16) for _ in range(2)]
                coef = [P.sb([128, EB, 512], BF16) for _ in range(2)]
                gl = [P.sb([128, 512], BF16) for _ in range(2)]
                gc = [P.sb([128, 512], BF16) for _ in range(2)]
                pa_ = [P.ps() for _ in range(2)]
                pv_ = [P.ps() for _ in range(4)]
                ucnt = 0
                vcnt = 0
                pcnt = 0
                DH = 2048 if D % 2048 == 0 else D
                for grp in ogroups:
                    ng = len(grp)
                    N = ng * 128
                    c0 = grp[0] * 128
                    P.dma("sp", hn2[:, :, 0:N], HN2T[:, c0:c0 + N].rearrange("(c p) n -> p c n", p=128), ikey="HN2T")
                    for tt, ti in enumerate(grp):
                        P.dma("sp", oacc[:, tt, :], H2[ti * 128:(ti + 1) * 128, :], ikey="H2", okey=(oacc, tt))
                    def v_load(eb):
                        v_b = vb[eb % 2]
                        vsrc = VB16[eb * EB * 128:(eb + 1) * EB * 128, :].rearrange("(cc p) d -> p cc d", p=128)
                        P.dma("sp", v_b[:], vsrc, ikey="VB16")

                    def u_phase(eb):
                        nonlocal ucnt
                        cf = coef[eb % 2]
                        for cc in range(EB):
                            c = eb * EB + cc
                            u_b = ub[ucnt % 3]
                            g_l = gl[ucnt % 2]
                            g_c = gc[ucnt % 2]
                            p_ = pa_[ucnt % 2]
                            ucnt += 1
                            P.dma("sp", u_b[:].rearrange("p c e -> p (c e)"), UB16[c], ikey="UB16")
                            P.dma("sp", g_c[:, 0:N], GALL[c][:, c0:c0 + N], ikey="GALL")
                            for dc in range(KC):
                                P.mm(p_[:, 0:N], u_b[:, dc, :], hn2[:, dc, 0:N], start=(dc == 0), stop=(dc == KC - 1))
                            P.act(g_l[:, 0:N], p_[:, 0:N], AF.Gelu)
                            P.tt("dve", cf[:, cc, 0:N], g_l[:, 0:N], g_c[:, 0:N], ALU.mult, okey=(cf, cc))

                    def v_phase(eb):
                        nonlocal pcnt
                        cf = coef[eb % 2]
                        v_b = vb[eb % 2]
                        for tt in range(ng):
                            for db in range(D // 512):
                                pv = pv_[pcnt % 4]
                                pcnt += 1
                                for cc in range(EB):
                                    P.mm(pv[:, 0:512], cf[:, cc, tt * 128:(tt + 1) * 128], v_b[:, cc, db * 512:(db + 1) * 512],
                                         start=(cc == 0), stop=(cc == EB - 1), rkeys=[(cf, cc), v_b])
                                P.tt("dve", oacc[:, tt, db * 512:(db + 1) * 512], pv[:, 0:512], oacc[:, tt, db * 512:(db + 1) * 512], ALU.add,
                                     okey=(oacc, tt), rkeys=[pv, (oacc, tt)])

                    nE = NCH // EB
                    v_load(0)
                    u_phase(0)
                    for eb in range(nE):
                        if eb + 1 < nE:
                            v_load(eb + 1)
                            u_phase(eb + 1)
                        v_phase(eb)
                    for tt, ti in enumerate(grp):
                        P.dma("sp", y[ti * 128:(ti + 1) * 128, :], oacc[:, tt, :], ikey=(oacc, tt), okey=("y", None))
                P.flush()
            P.st = es

        if dbg:
            for nm, ap_ in (("MIXT", MIXT), ("UTp", UTp), ("UTs", UTs), ("QT", QT), ("QIT", QIT), ("WI", WI), ("KT", KT), ("KIT", KIT),
                            ("Vc", Vc), ("H2", H2), ("HN2T", HN2T), ("S12", S12), ("GALL", GALL), ("MKT", MKT), ("MV", MV)):
                if nm in dbg:
                    o_ = dout("dbg_" + nm, list(ap_.shape), ap_.dtype)
                    P.dma("sp", o_, ap_)
        P.flush()
    return nc


def _rope_table(pos):
    half = 16
    inv_freq = np.power(np.float32(ROPE_THETA), -np.arange(half, dtype=np.float32) / np.float32(half)).astype(np.float32)
    ang = pos.astype(np.float32)[:, None] * inv_freq[None, :]
    return np.concatenate([np.cos(ang), np.sin(ang)], axis=1).astype(np.float32)


def host_prep(inp, cfg):
    D, KC, CCH, CC, NH, NKV, NP, NT, IH, SEQ, PAST, SS, MEMT = (cfg[k] for k in (
        "D", "KC", "CCH", "CC", "NH", "NKV", "NP", "NT", "IH", "SEQ", "PAST", "SS", "MEMT"))
    DS = cfg["DS"]
    f = lambda a: np.ascontiguousarray(a, dtype=np.float32)
    half = SEQ // 2
    w_in = inp["w_in"][0]
    OFF_Q = 2 * CCH
    OFF_K = OFF_Q + NH * 128
    OFF_V = OFF_K + NKV * 128
    OFF_QI = OFF_V + NKV * 128
    OFF_KI = OFF_QI + IH * 128
    OFF_WI = OFF_KI + 128
    a_ = w_in[:, :CCH].reshape(D, CC, 128)
    g_ = w_in[:, CCH:2 * CCH].reshape(D, CC, 128)
    w_glu = f(np.stack([a_, g_], axis=2).reshape(D, 2 * CCH))
    shared = dict(
        w_glu=w_glu,
        w_q=f(w_in[:, OFF_Q:OFF_K]),
        w_qi=f(w_in[:, OFF_QI:OFF_KI]),
        w_wi=f(w_in[:, OFF_WI:OFF_WI + IH]),
        w_kv=f(np.concatenate([w_in[:, OFF_K:OFF_V], w_in[:, OFF_V:OFF_QI], w_in[:, OFF_KI:OFF_WI]], axis=1)),
        w_out=f(inp["w_out"][0]),
        w_qm=f(inp["w_q_mem"][0]), w_km=f(inp["w_k_mem"][0]), w_vm=f(inp["w_v_mem"][0]), w_om=f(inp["w_o_mem"][0]),
        w_pq=f(inp["peer_wq"][0]),
        g_mix=f(inp["norm_mix_g"]), g_memn=f(inp["norm_mem_g"]), g_ffn=f(inp["norm_ffn_g"]), g_mem=f(inp["mem_norm_g"]),
        g_q=f(inp["q_norm_g"]), g_k=f(inp["k_norm_g"]), g_mq=f(inp["mem_q_norm_g"]), g_mk=f(inp["mem_k_norm_g"]),
        dww=f(inp["dw_w"][0].reshape(31, CC, 128).transpose(2, 1, 0)),
        dwb=f(inp["dw_b"][0].reshape(CC, 128).T), lng=f(inp["conv_ln_g"][0].reshape(CC, 128).T),
        lnb=f(inp["conv_ln_b"][0].reshape(CC, 128).T),
        vtab=f(inp["peer_v"][0]),
        c_idb=np.eye(128).astype(ml_dtypes.bfloat16), c_idf=np.eye(128, dtype=np.float32),
        c_oneb=np.ones((128, 128)).astype(ml_dtypes.bfloat16), c_onef=np.ones((128, 128), dtype=np.float32),
    )
    sk = np.stack([inp["peer_sub_k1"][0], inp["peer_sub_k2"][0]], axis=1)
    shared["subk"] = f(sk.reshape(16, 128, 128).transpose(2, 0, 1))
    u = inp["peer_u"][0]
    shared["uT"] = f(u.reshape(128, 128, KC, 128).transpose(0, 3, 2, 1).reshape(128, 128, KC * 128))
    kcs = (np.arange(SS) // 64).astype(np.float32)
    kcs[PAST + DS:] = 1.0e9
    shared["kc_s"] = kcs[None, :]
    shared["rope_s"] = _rope_table(PAST + np.arange(128))
    maps = []
    for c in range(8):
        b, hf = c // 2, c % 2
        xb = inp["x_prompt"][b]
        own = xb[hf * half:(hf + 1) * half]
        oth = xb[(1 - hf) * half:(2 - hf) * half]
        pos = np.concatenate([hf * half + np.arange(half), (1 - hf) * half + np.arange(half)])
        m = dict(shared)
        m["xctx"] = f(np.concatenate([own, oth], axis=0))
        m["xhalo"] = f(xb[half - 128:half]) if hf == 1 else np.zeros((128, D), np.float32)
        xsp = np.zeros((2, 128, D), np.float32)
        for s in range(2):
            xsp[s, :DS] = inp["x_sample"][2 * c + s]
        m["xsp"] = xsp
        m["mem"] = f(inp["mem_prompt"][b])
        m["ckT"] = f(np.stack([inp["cache_k"][0, 2 * c + s].transpose(2, 1, 0) for s in range(2)]))
        m["cv"] = f(np.stack([inp["cache_v"][0, 2 * c + s].reshape(PAST, NKV * 128) for s in range(2)]))
        m["ckiT"] = f(np.stack([inp["cache_k_idx"][0, 2 * c + s].T for s in range(2)]))
        m["stT"] = f(np.stack([inp["state_conv"][0, 2 * c + s].T for s in range(2)]))
        m["cmkT"] = f(np.stack([inp["cache_mem_k"][0, 2 * c + s].transpose(2, 1, 0) for s in range(2)]))
        m["cmv"] = f(np.stack([inp["cache_mem_v"][0, 2 * c + s].reshape(MEMT, 512) for s in range(2)]))
        m["rope_c"] = _rope_table(pos)
        m["kc_p"] = (pos // 64).astype(np.float32)[None, :]
        q = np.zeros((128, NT), np.float32)
        for i in range(NP):
            q[:, i] = (hf * half + i * 128 + np.arange(128)) // 64
        q[:, NP:] = PAST // 64
        m["qch"] = q
        maps.append(m)
    return maps


def assemble(res, cfg):
    D, CCH, NKV, NP, SEQ, DS, MEMT, B, DB = (cfg[k] for k in ("D", "CCH", "NKV", "NP", "SEQ", "DS", "MEMT", "B", "DB"))
    half = SEQ // 2
    y_p = np.zeros((B, SEQ, D), np.float32)
    y_s = np.zeros((DB, DS, D), np.float32)
    k_p = np.zeros((1, B, SEQ, NKV, 128), np.float32)
    v_p = np.zeros_like(k_p)
    ki_p = np.zeros((1, B, SEQ, 128), np.float32)
    conv_p = np.zeros((1, B, 30, CCH), np.float32)
    mk_p = np.zeros((1, B, MEMT, 4, 128), np.float32)
    mv_p = np.zeros_like(mk_p)
    k_s = np.zeros((1, DB, DS, NKV, 128), np.float32)
    v_s = np.zeros_like(k_s)
    ki_s = np.zeros((1, DB, DS, 128), np.float32)
    conv_s = np.zeros((1, DB, 30, CCH), np.float32)
    for c in range(8):
        r = res[c]
        b, hf = c // 2, c % 2
        y_p[b, hf * half:(hf + 1) * half] = r["y"][:NP * 128]
        if hf == 0:
            k_p[0, b] = r["o_k"].reshape(SEQ, NKV, 128)
            v_p[0, b] = r["o_v"].reshape(SEQ, NKV, 128)
            ki_p[0, b] = r["o_ki"]
            mk_p[0, b] = r["o_mk"].reshape(MEMT, 4, 128)
            mv_p[0, b] = r["o_mv"].reshape(MEMT, 4, 128)
        else:
            conv_p[0, b] = r["o_conv"]
        for s in range(2):
            q = 2 * c + s
            y_s[q] = r["y"][(NP + s) * 128:(NP + s) * 128 + DS]
            k_s[0, q] = r["o_ks"][s, :DS].reshape(DS, NKV, 128)
            v_s[0, q] = r["o_vs"][s, :DS].reshape(DS, NKV, 128)
            ki_s[0, q] = r["o_kis"][s, :DS]
            conv_s[0, q] = r["o_convs"][s]
    return (y_p, y_s, k_p, v_p, ki_p, conv_p, mk_p, mv_p, k_s, v_s, ki_s, conv_s)


def kernel(**inputs):
    cfg = mkcfg()
    inp = {k: np.asarray(v) for k, v in inputs.items()}
    maps = host_prep(inp, cfg)
    nc = build(cfg)
    res = run_bass_kernel_spmd(nc, maps, core_ids=list(range(8)))
    return assemble(res.results, cfg)
```

```python
import contextlib
import math
import numpy as np
import ml_dtypes
import concourse.bass as bass
import concourse.mybir as mybir
from concourse.bass_utils import run_bass_kernel_spmd

F32 = mybir.dt.float32
BF16 = mybir.dt.bfloat16
ALU = mybir.AluOpType
AF = mybir.ActivationFunctionType
AX = mybir.AxisListType

EPS = 1e-6
ROPE_THETA = 500000.0
NEG = -1.0e30


class Prog:
    def __init__(self, nc, es):
        self.nc = nc
        self.es = es
        self.st = es
        self.ops = []
        self.engs = {"pe": nc.tensor, "act": nc.scalar, "dve": nc.vector, "pool": nc.gpsimd, "sp": nc.sync}
        self.n_t = 0
        self.eng_sem = {}
        self.eng_cnt = {}
        self.pool = {}
        self.npool = {}
        self.key_sem = {}
        self.fence_sem = None
        self.fence_cnt = 0
        self.tot_ops = 0
        self.tot_wait = 0
        self.free_sems = []
        self.n_dsem = 0

    def sb(self, shape, dt=F32, name=None):
        self.n_t += 1
        return self.st.enter_context(self.nc.sbuf_tensor(name or f"sb{self.n_t}", list(shape), dt))

    def ps(self, shape=(128, 512), dt=F32, name=None):
        self.n_t += 1
        return self.st.enter_context(self.nc.psum_tensor(name or f"ps{self.n_t}", list(shape), dt))

    @staticmethod
    def key(x):
        def nm(a):
            if isinstance(a, str):
                return a
            t = getattr(a, "tensor", None)
            return t.name if t is not None else a.name
        if isinstance(x, tuple):
            return (nm(x[0]), x[1])
        return (nm(x), None)

    def op(self, eng, fn, reads=(), writes=(), dma=False):
        rk = []
        for r in reads:
            if r is None or isinstance(r, (int, float)):
                continue
            k = self.key(r)
            if k not in rk:
                rk.append(k)
        wk = []
        for w in writes:
            k = self.key(w)
            if k not in wk:
                wk.append(k)
        self.ops.append(dict(eng=eng, fn=fn, reads=rk, writes=wk, dma=dma))

    def _esem(self, e):
        if e not in self.eng_sem:
            self.eng_sem[e] = self.es.enter_context(self.nc.semaphore(f"s_{e}"))
            self.eng_cnt[e] = 0
        return self.eng_sem[e]

    def flush(self):
        nc = self.nc
        ops = self.ops
        state = {}
        deps = [None] * len(ops)

        def confl(k):
            ent = state.get(k[0])
            if not ent:
                return []
            if k[1] is None:
                return list(ent.values())
            return [ent[s_] for s_ in (k[1], None) if s_ in ent]

        joined = [False] * len(ops)
        for i, o in enumerate(ops):
            d = set()
            for k in o["reads"]:
                for st in confl(k):
                    d.update(st[0])
            joins = {}
            for k in o["writes"]:
                own = state.get(k[0], {}).get(k[1])
                joinable = bool(o["dma"] and own and own[0] and all(ops[j]["dma"] for j in own[0]) and not own[1])
                joins[k] = joinable
                for st in confl(k):
                    d.update(st[1])
                    if not (joinable and st is own):
                        d.update(st[0])
            if o["dma"]:
                joined[i] = joins[o["writes"][0]]
            for k in o["reads"]:
                st = state.setdefault(k[0], {}).setdefault(k[1], [[], []])
                st[1].append(i)
            for k in o["writes"]:
                ent = state.setdefault(k[0], {})
                if joins[k]:
                    ent[k[1]][0].append(i)
                else:
                    if k[1] is None:
                        ent.clear()
                    ent[k[1]] = [[i], []]
            d.discard(i)
            if o["eng"] == "pe":
                d = {j for j in d if not (ops[j]["eng"] == "pe" and not ops[j]["dma"])}
            deps[i] = d
        need = [False] * len(ops)
        for d in deps:
            for j in d:
                need[j] = True
        last_on = {}
        for i, o in enumerate(ops):
            if not o["dma"]:
                last_on[o["eng"]] = i
        for i in last_on.values():
            need[i] = True

        sig = [None] * len(ops)
        waited = {}
        for i, o in enumerate(ops):
            e = o["eng"]
            eo = self.engs[e]
            wl = {}
            for j in deps[i]:
                s, v = sig[j]
                kk = id(s)
                if kk not in wl or wl[kk][1] < v:
                    wl[kk] = (s, v)
            pre = None
            if o["dma"]:
                k = o["writes"][0]
                name = k[0]
                pl = self.pool.get(name)
                if pl is None:
                    n = self.npool.get(name, 2)
                    sems_, cnt_ = [], []
                    for q in range(n):
                        if self.free_sems:
                            s_, c_ = self.free_sems.pop()
                        else:
                            self.n_dsem += 1
                            s_, c_ = self.es.enter_context(nc.semaphore(f"dma{self.n_dsem}")), 0
                        sems_.append(s_)
                        cnt_.append(c_)
                    pl = dict(sems=sems_, cnt=cnt_, last=[None] * n, rr=0)
                    self.pool[name] = pl
                idx = None
                if joined[i] and k in self.key_sem and pl["last"][self.key_sem[k]] == k:
                    idx = self.key_sem[k]
                else:
                    idx = pl["rr"]
                    pl["rr"] = (pl["rr"] + 1) % len(pl["sems"])
                    if pl["cnt"][idx] > 0:
                        s = pl["sems"][idx]
                        kk = id(s)
                        if kk not in wl or wl[kk][1] < pl["cnt"][idx]:
                            wl[kk] = (s, pl["cnt"][idx])
                self.key_sem[k] = idx
                pl["last"][idx] = k
                pre = (pl, idx)
            for kk, (s, v) in wl.items():
                if waited.get((e, kk), -1) >= v:
                    continue
                waited[(e, kk)] = v
                eo.wait_ge(s, v)
                self.tot_wait += 1
            ins = o["fn"](eo)
            if o["dma"]:
                pl, idx = pre
                pl["cnt"][idx] += 16
                ins.then_inc(pl["sems"][idx], 16)
                sig[i] = (pl["sems"][idx], pl["cnt"][idx])
            elif need[i]:
                s = self._esem(e)
                self.eng_cnt[e] += 1
                ins.then_inc(s, 1)
                sig[i] = (s, self.eng_cnt[e])
        self.tot_ops += len(ops)
        self.ops = []
        if self.fence_sem is None:
            self.fence_sem = self.es.enter_context(nc.semaphore("fence"))
        for e, s in self.eng_sem.items():
            if self.eng_cnt[e] > 0:
                nc.sync.wait_ge(s, self.eng_cnt[e])
        for pl in self.pool.values():
            for s, c in zip(pl["sems"], pl["cnt"]):
                if c > 0:
                    nc.sync.wait_ge(s, c)
        for pl in self.pool.values():
            for s, c in zip(pl["sems"], pl["cnt"]):
                self.free_sems.append((s, c))
        self.pool = {}
        self.key_sem = {}
        self.fence_cnt += 1
        nc.sync.drain().then_inc(self.fence_sem, 1)
        for e in ("pe", "act", "dve", "pool"):
            self.engs[e].wait_ge(self.fence_sem, self.fence_cnt)

    def dma(self, q, out, in_, okey=None, ikey=None, **kw):
        self.op(q, lambda e: e.dma_start(out=out, in_=in_, **kw), reads=[ikey or in_], writes=[okey or out], dma=True)

    def cdma(self, out, in_, okey=None, ikey=None):
        n = out.shape[-1]
        if n > 2048:
            d = 2048
            while n % d:
                d //= 2
            names = " ".join(f"a{i}" for i in range(len(out.shape) - 1))
            pat = f"{names} (x d) -> {names} x d"
            self.dma("pool", out.rearrange(pat, d=d), in_.rearrange(pat, d=d), okey=okey or out, ikey=ikey or in_)
        else:
            self.dma("pool", out, in_, okey=okey, ikey=ikey)

    def mm(self, out, lhsT, rhs, start=True, stop=True, okey=None, rkeys=None):
        self.op("pe", lambda e: e.matmul(out, lhsT, rhs, start=start, stop=stop), reads=rkeys or [lhsT, rhs], writes=[okey or out])

    def tr(self, out, in_, ident, okey=None, ikey=None):
        self.op("pe", lambda e: e.transpose(out, in_, ident), reads=[ikey or in_, ident], writes=[okey or out])

    def act(self, out, in_, func, scale=1.0, bias=0.0, accum_out=None, okey=None, ikey=None):
        rd = [ikey or in_] + [x for x in (scale, bias) if not isinstance(x, (int, float))]
        wr = [okey or out] + ([accum_out] if accum_out is not None else [])
        if accum_out is not None:
            self.op("act", lambda e: e.activation(out, in_, func, bias=bias, scale=scale, accum_out=accum_out), reads=rd, writes=wr)
        else:
            self.op("act", lambda e: e.activation(out, in_, func, bias=bias, scale=scale), reads=rd, writes=wr)

    def ts(self, eng, out, in0, s1, s2=None, op0=ALU.mult, op1=None, accum_out=None, okey=None, ikey=None):
        rd = [ikey or in0] + [x for x in (s1, s2) if x is not None and not isinstance(x, (int, float))]
        wr = [okey or out] + ([accum_out] if accum_out is not None else [])
        kw = {}
        if op1 is not None:
            kw["op1"] = op1
        if accum_out is not None:
            kw["accum_out"] = accum_out
        self.op(eng, lambda e: e.tensor_scalar(out, in0, s1, s2, op0, **kw), reads=rd, writes=wr)

    def tt(self, eng, out, in0, in1, op, okey=None, rkeys=None):
        self.op(eng, lambda e: e.tensor_tensor(out, in0, in1, op), reads=rkeys or [in0, in1], writes=[okey or out])

    def stt(self, out, in0, scalar, in1, op0, op1, okey=None, rkeys=None):
        rd = list(rkeys or [in0, in1]) + ([scalar] if not isinstance(scalar, (int, float)) else [])
        self.op("dve", lambda e: e.scalar_tensor_tensor(out, in0, scalar, in1, op0, op1), reads=rd, writes=[okey or out])

    def copy(self, eng, out, in_, okey=None, ikey=None):
        if eng == "act":
            self.op("act", lambda e: e.copy(out, in_), reads=[ikey or in_], writes=[okey or out])
        else:
            self.op(eng, lambda e: e.tensor_copy(out, in_), reads=[ikey or in_], writes=[okey or out])

    def max8(self, out, in_, okey=None):
        self.op("dve", lambda e: e.max(out, in_), reads=[in_], writes=[okey or out])

    def mrep(self, out, in_to_replace, in_values, imm, rkeys=None):
        self.op("dve", lambda e: e.match_replace(out, in_to_replace, in_values, imm), reads=rkeys or [in_to_replace, in_values], writes=[out])

    def memset(self, eng, ap, val):
        self.op(eng, lambda e: e.memset(ap, val), reads=[], writes=[ap])

    def recip(self, out, in_, okey=None):
        self.op("dve", lambda e: e.reciprocal(out, in_), reads=[in_], writes=[okey or out])

    def reduce(self, out, in_, op, axis=AX.X):
        self.op("dve", lambda e: e.tensor_reduce(out, in_, axis, op), reads=[in_], writes=[out])


def mkcfg(D=4096, SEQ=4096, B=4, DB=16, DS=64, PAST=4096, IH=32, TOPK_MAX=256, MEMT=256):
    c = dict(D=D, SEQ=SEQ, B=B, DB=DB, DS=DS, PAST=PAST, IH=IH, MEMT=MEMT)
    c["KC"] = D // 128
    c["CCH"] = D // 2
    c["CC"] = c["CCH"] // 128
    c["NH"] = (D // 2) // 128
    c["NKV"] = 4
    c["GQ"] = c["NH"] // 4
    c["NP"] = SEQ // 2 // 128
    c["NCX"] = SEQ // 128
    c["NT"] = c["NP"] + 2
    c["TOPK_P"] = min(TOPK_MAX, SEQ // 4)
    c["TOPK_S"] = min(TOPK_MAX, (PAST + DS) // 4)
    c["SS"] = PAST + 128
    c["MH"] = 4
    c["MC"] = MEMT // 128
    return c


PEER_KEYS = 128
PEER_HEADS = 8
PEER_TOPK = 16


def build(cfg, stages=("M", "KV", "MAIN", "B", "C", "D", "E"), dbg=False):
    D, KC, CCH, CC, NH, NKV, GQ, NP, NCX, NT, IH, SEQ, PAST, SS, MEMT, MC = (cfg[k] for k in (
        "D", "KC", "CCH", "CC", "NH", "NKV", "GQ", "NP", "NCX", "NT", "IH", "SEQ", "PAST", "SS", "MEMT", "MC"))
    NTOK = NT * 128
    IDX_SCALE = (IH ** -0.5) * (128 ** -0.5)
    ATT_SCALE = 128 ** -0.5
    nc = bass.Bass("TRN2", target_bir_lowering=False)

    def din(name, shape, dt=F32):
        return nc.dram_tensor(name, list(shape), dt, kind="ExternalInput").ap()

    def dout(name, shape, dt=F32):
        return nc.dram_tensor(name, list(shape), dt, kind="ExternalOutput").ap()

    def dscr(name, shape, dt=F32):
        return nc.dram_tensor(name, list(shape), dt, kind="Internal").ap()

    xctx = din("xctx", [SEQ, D])
    xsp = din("xsp", [2, 128, D])
    xhalo = din("xhalo", [128, D])
    mem = din("mem", [MEMT, D])
    ckT = din("ckT", [2, 128, NKV, PAST])
    cv = din("cv", [2, PAST, NKV * 128])
    ckiT = din("ckiT", [2, 128, PAST])
    stT = din("stT", [2, CCH, 30])
    cmkT = din("cmkT", [2, 128, 4, MEMT])
    cmv = din("cmv", [2, MEMT, 512])
    w_glu = din("w_glu", [D, 2 * CCH])
    w_q = din("w_q", [D, NH * 128])
    w_qi = din("w_qi", [D, IH * 128])
    w_wi = din("w_wi", [D, IH])
    w_kv = din("w_kv", [D, 1152])
    w_out = din("w_out", [D, D])
    w_qm = din("w_qm", [D, 512])
    w_km = din("w_km", [D, 512])
    w_vm = din("w_vm", [D, 512])
    w_om = din("w_om", [512, D])
    w_pq = din("w_pq", [D, 2048])
    subk = din("subk", [128, 16, 128])
    uT = din("uT", [128, 128, KC * 128])
    vtab = din("vtab", [PEER_KEYS * PEER_KEYS, D])
    g_mix = din("g_mix", [1, D])
    g_memn = din("g_memn", [1, D])
    g_ffn = din("g_ffn", [1, D])
    g_mem = din("g_mem", [1, D])
    g_q = din("g_q", [1, 128])
    g_k = din("g_k", [1, 128])
    g_mq = din("g_mq", [1, 128])
    g_mk = din("g_mk", [1, 128])
    dww = din("dww", [128, CC, 31])
    dwb = din("dwb", [128, CC])
    lng = din("lng", [128, CC])
    lnb = din("lnb", [128, CC])
    rope_c = din("rope_c", [SEQ, 32])
    rope_s = din("rope_s", [128, 32])
    kc_p = din("kc_p", [1, SEQ])
    kc_s = din("kc_s", [1, SS])
    qch = din("qch", [128, NT])
    c_idb = din("c_idb", [128, 128], BF16)
    c_idf = din("c_idf", [128, 128])
    c_oneb = din("c_oneb", [128, 128], BF16)
    c_onef = din("c_onef", [128, 128])

    y = dout("y", [NTOK, D])
    o_k = dout("o_k", [SEQ, NKV * 128])
    o_v = dout("o_v", [SEQ, NKV * 128])
    o_ki = dout("o_ki", [SEQ, 128])
    o_conv = dout("o_conv", [30, CCH])
    o_mk = dout("o_mk", [MEMT, 512])
    o_mv = dout("o_mv", [MEMT, 512])
    o_ks = dout("o_ks", [2, 128, NKV * 128])
    o_vs = dout("o_vs", [2, 128, NKV * 128])
    o_kis = dout("o_kis", [2, 128, 128])
    o_convs = dout("o_convs", [2, 30, CCH])

    UTp = dscr("UTp", [CCH, 128 + NP * 128])
    UTs = dscr("UTs", [2, CCH, 160])
    KT = dscr("KT", [128, NKV, SEQ], BF16)
    Vc = dscr("Vc", [SEQ, NKV * 128], BF16)
    KIT = dscr("KIT", [128, SEQ], BF16)
    KTs = dscr("KTs", [2, 128, NKV, 128], BF16)
    Vs = dscr("Vs", [2, 128, NKV * 128], BF16)
    KITs = dscr("KITs", [2, 128, 128], BF16)
    MKT = dscr("MKT", [128, 4, MEMT], BF16)
    MV = dscr("MV", [MEMT, 512], BF16)
    QT = dscr("QT", [NT, 128, NH, 128], BF16)
    QIT = dscr("QIT", [NT, 128, IH, 128], BF16)
    WI = dscr("WI", [NT, 128, IH])
    MIXT = dscr("MIXT", [D, NTOK], BF16)
    H2 = dscr("H2", [NTOK, D])
    HN2T = dscr("HN2T", [D, NTOK], BF16)
    S12 = dscr("S12", [16, NTOK, 128])
    GALL = dscr("GALL", [128, 128, NTOK], BF16)

    UB16 = dscr("UB16", [128, 128, KC * 128], BF16)
    VB16 = dscr("VB16", [PEER_KEYS * PEER_KEYS, D], BF16)
    dbg_out = {}

    with contextlib.ExitStack() as es:
        P = Prog(nc, es)
        idb = P.sb([128, 128], BF16, "idb")
        idf = P.sb([128, 128], F32, "idf")
        oneb = P.sb([128, 128], BF16, "oneb")
        onef = P.sb([128, 128], F32, "onef")
        P.dma("sp", idb[:], c_idb)
        P.dma("sp", idf[:], c_idf)
        P.dma("sp", oneb[:], c_oneb)
        P.dma("sp", onef[:], c_onef)
        P.flush()

        def bcast_row(dst, src_row, n):
            P.dma("sp", dst, src_row.to_broadcast([128, n]))

        def rstd_from_ss(ss, n, out, tmp):
            P.ts("dve", tmp, ss, 1.0 / n, EPS, op0=ALU.mult, op1=ALU.add)
            P.act(tmp, tmp, AF.Sqrt)
            P.recip(out, tmp)

        def load_w(dst, src, ncols):
            sv = src.rearrange("(c p) n -> p c n", p=128)
            nq = 4 if KC % 4 == 0 else 1
            step = KC // nq
            for q in range(nq):
                P.dma("pool", dst[:, q * step:(q + 1) * step, 0:ncols], sv[:, q * step:(q + 1) * step, :], okey=(dst, q))

        def wkeys(dst):
            return [(dst, q) for q in range(4 if KC % 4 == 0 else 1)]

        class NormT:
            def __init__(self, gsrc, npt=2):
                self.npt = npt
                self.gbc = P.sb([128, D], F32)
                bcast_row(self.gbc[:], gsrc[0:1, :], D)
                self.sq = P.sb([128, D], BF16)
                self.xs = [P.sb([128, D], BF16) for _ in range(2)]
                self.sm = [P.sb([128, 4], F32) for _ in range(2)]
                self.pt = [P.ps([128, 1024], BF16) for _ in range(npt)]
                self.k = 0

            def run(self, x_t, dst_fn):
                k = self.k
                self.k += 1
                sm = self.sm[k % 2]
                xs = self.xs[k % 2]
                P.act(self.sq[:], x_t, AF.Square, accum_out=sm[:, 0:1])
                rstd_from_ss(sm[:, 0:1], D, sm[:, 1:2], sm[:, 2:3])
                P.stt(xs[:], x_t, sm[:, 1:2], self.gbc[:], ALU.mult, ALU.mult)
                nb = 8 if KC % 8 == 0 else KC
                for b0 in range(0, KC, nb):
                    pt = self.pt[(b0 // nb) % self.npt]
                    for j in range(nb):
                        P.tr(pt[:, j * 128:(j + 1) * 128], xs[:, (b0 + j) * 128:(b0 + j + 1) * 128], idb[:])
                    eng = "act" if (b0 // nb) % 2 == 0 else "dve"
                    P.copy(eng, dst_fn(b0, nb), pt[:, 0:nb * 128].rearrange("p (n t) -> p n t", t=128))

        def head_norm(ps_ap, nh, gain_bc, out_f, sq_t, sm_t):
            P.act(sq_t[:, 0:nh * 128], ps_ap, AF.Square)
            P.reduce(sm_t[:, 0:nh], sq_t[:, 0:nh * 128].rearrange("p (h d) -> p h d", d=128), ALU.add)
            rstd_from_ss(sm_t[:, 0:nh], 128, sm_t[:, 4:4 + nh], sm_t[:, 8:8 + nh])
            P.tt("dve", out_f, ps_ap.rearrange("p (h d) -> p h d", d=128),
                 sm_t[:, 4:4 + nh].unsqueeze(2).to_broadcast([128, nh, 128]), ALU.mult)
            P.tt("dve", out_f, out_f, gain_bc[:, 0:128].unsqueeze(1).to_broadcast([128, nh, 128]), ALU.mult)

        def rope(f, nh, cs, tmp):
            x1 = f[:, :, 0:16]
            x2 = f[:, :, 16:32]
            cosb = cs[:, 0:16].unsqueeze(1).to_broadcast([128, nh, 16])
            sinb = cs[:, 16:32].unsqueeze(1).to_broadcast([128, nh, 16])
            P.tt("dve", tmp[:, 0, 0:nh, :], x1, cosb, ALU.mult)
            P.tt("dve", tmp[:, 1, 0:nh, :], x2, sinb, ALU.mult)
            P.tt("dve", tmp[:, 2, 0:nh, :], x2, cosb, ALU.mult)
            P.tt("dve", tmp[:, 3, 0:nh, :], x1, sinb, ALU.mult)
            P.tt("dve", x1, tmp[:, 0, 0:nh, :], tmp[:, 1, 0:nh, :], ALU.subtract)
            P.tt("dve", x2, tmp[:, 2, 0:nh, :], tmp[:, 3, 0:nh, :], ALU.add)

        def tok_mm(ps_ap, hnT, tcol, wb, ncols, wk):
            for c in range(KC):
                P.mm(ps_ap, hnT[:, c, tcol:tcol + 128], wb[:, c, 0:ncols], start=(c == 0), stop=(c == KC - 1),
                     rkeys=[hnT] + wk)

        if "M" in stages:
            with contextlib.ExitStack() as st:
                P.st = st
                nt = NormT(g_mem)
                wk_b = P.sb([128, KC, 512], BF16)
                wv_b = P.sb([128, KC, 512], BF16)
                load_w(wk_b, w_km, 512)
                load_w(wv_b, w_vm, 512)
                gk = P.sb([128, 128], F32)
                bcast_row(gk[:], g_mk[0:1, :], 128)
                xt = [P.sb([128, D], F32) for _ in range(2)]
                hn = [P.sb([128, KC, 128], BF16) for _ in range(2)]
                pk = P.ps()
                pv = P.ps()
                ptr = P.ps([128, 1024], BF16)
                sq_t = P.sb([128, 512], F32)
                sm_t = P.sb([128, 12], F32)
                kf = P.sb([128, 4, 128], F32)
                kb = P.sb([128, 512], BF16)
                kTt = P.sb([128, 4, 128], BF16)
                vf = P.sb([128, 512], F32)
                vb = P.sb([128, 512], BF16)
                for m in range(MC):
                    x_t = xt[m % 2]
                    h_t = hn[m % 2]
                    P.dma("sp", x_t[:], mem[m * 128:(m + 1) * 128, :])
                    nt.run(x_t[:], lambda c0, n, h_t=h_t: h_t[:, c0:c0 + n, :])
                    tok_mm(pk[:, 0:512], h_t, 0, wk_b, 512, wkeys(wk_b))
                    tok_mm(pv[:, 0:512], h_t, 0, wv_b, 512, wkeys(wv_b))
                    head_norm(pk[:, 0:512], 4, gk, kf[:], sq_t, sm_t)
                    P.dma("sp", o_mk[m * 128:(m + 1) * 128, :], kf[:].rearrange("p h d -> p (h d)"))
                    P.copy("act", kb[:], kf[:].rearrange("p h d -> p (h d)"))
                    for h in range(4):
                        P.tr(ptr[:, h * 128:(h + 1) * 128], kb[:, h * 128:(h + 1) * 128], idb[:])
                    P.copy("dve", kTt[:], ptr[:, 0:512].rearrange("p (h t) -> p h t", t=128))
                    P.dma("sp", MKT[:, :, m * 128:(m + 1) * 128], kTt[:])
                    P.copy("act", vf[:], pv[:, 0:512])
                    P.dma("sp", o_mv[m * 128:(m + 1) * 128, :], vf[:])
                    P.copy("dve", vb[:], pv[:, 0:512])
                    P.dma("sp", MV[m * 128:(m + 1) * 128, :], vb[:])
                P.flush()
            P.st = es

        if "KV" in stages:
            with contextlib.ExitStack() as st:
                P.st = st
                nt = NormT(g_mix, npt=1)
                wb = P.sb([128, KC, 1152], BF16)
                load_w(wb, w_kv, 1152)
                wk = wkeys(wb)
                gk = P.sb([128, 128], F32)
                bcast_row(gk[:], g_k[0:1, :], 128)
                xt = [P.sb([128, D], F32) for _ in range(2)]
                hn = [P.sb([128, KC, 128], BF16) for _ in range(2)]
                cs = [P.sb([128, 32], F32) for _ in range(2)]
                pk2 = [P.ps() for _ in range(2)]
                pv2 = [P.ps() for _ in range(2)]
                pki2 = [P.ps() for _ in range(2)]
                ptr = P.ps([128, 1024], BF16)
                sq_t = P.sb([128, 512], F32)
                sm_t = P.sb([128, 12], F32)
                rtmp = P.sb([128, 4, 4, 16], F32)
                kf = [P.sb([128, 4, 128], F32) for _ in range(2)]
                kb = P.sb([128, 512], BF16)
                kTt = [P.sb([128, 4, 128], BF16) for _ in range(2)]
                vf = [P.sb([128, 512], F32) for _ in range(2)]
                vb = [P.sb([128, 512], BF16) for _ in range(2)]
                kif = [P.sb([128, 1, 128], F32) for _ in range(2)]
                kib = P.sb([128, 128], BF16)
                kiTt = [P.sb([128, 128], BF16) for _ in range(2)]
                tiles = [("p", i) for i in range(NCX)] + [("s", 0), ("s", 1)]
                def kv_front(n_):
                    kind, i = tiles[n_]
                    pk, pv, pki = pk2[n_ % 2], pv2[n_ % 2], pki2[n_ % 2]
                    x_t = xt[n_ % 2]
                    h_t = hn[n_ % 2]
                    c_t = cs[n_ % 2]
                    if kind == "p":
                        P.dma("sp", x_t[:], xctx[i * 128:(i + 1) * 128, :])
                        P.dma("sp", c_t[:], rope_c[i * 128:(i + 1) * 128, :])
                    else:
                        P.dma("sp", x_t[:], xsp[i])
                        P.dma("sp", c_t[:], rope_s)
                    nt.run(x_t[:], lambda c0, n, h_t=h_t: h_t[:, c0:c0 + n, :])
                    for c in range(KC):
                        P.mm(pk[:, 0:512], h_t[:, c, :], wb[:, c, 0:512], start=(c == 0), stop=(c == KC - 1), rkeys=[h_t] + wk)
                    for c in range(KC):
                        P.mm(pv[:, 0:512], h_t[:, c, :], wb[:, c, 512:1024], start=(c == 0), stop=(c == KC - 1), rkeys=[h_t] + wk)
                    for c in range(KC):
                        P.mm(pki[:, 0:128], h_t[:, c, :], wb[:, c, 1024:1152], start=(c == 0), stop=(c == KC - 1), rkeys=[h_t] + wk)

                def kv_back(n_):
                    kind, i = tiles[n_]
                    pk, pv, pki = pk2[n_ % 2], pv2[n_ % 2], pki2[n_ % 2]
                    c_t = cs[n_ % 2]
                    kf_t = kf[n_ % 2]
                    head_norm(pk[:, 0:512], 4, gk, kf_t[:], sq_t, sm_t)
                    rope(kf_t, 4, c_t, rtmp)
                    kflat = kf_t[:].rearrange("p h d -> p (h d)")
                    if kind == "p":
                        P.dma("sp", o_k[i * 128:(i + 1) * 128, :], kflat)
                    else:
                        P.dma("sp", o_ks[i], kflat)
                    P.copy("act", kb[:], kflat)
                    for h in range(4):
                        P.tr(ptr[:, h * 128:(h + 1) * 128], kb[:, h * 128:(h + 1) * 128], idb[:])
                    kT_t = kTt[n_ % 2]
                    P.copy("dve", kT_t[:], ptr[:, 0:512].rearrange("p (h t) -> p h t", t=128))
                    if kind == "p":
                        P.dma("sp", KT[:, :, i * 128:(i + 1) * 128], kT_t[:])
                    else:
                        P.dma("sp", KTs[i], kT_t[:])
                    vf_t = vf[n_ % 2]
                    vb_t = vb[n_ % 2]
                    P.copy("act", vf_t[:], pv[:, 0:512])
                    P.copy("dve", vb_t[:], pv[:, 0:512])
                    if kind == "p":
                        P.dma("sp", o_v[i * 128:(i + 1) * 128, :], vf_t[:])
                        P.dma("sp", Vc[i * 128:(i + 1) * 128, :], vb_t[:])
                    else:
                        P.dma("sp", o_vs[i], vf_t[:])
                        P.dma("sp", Vs[i], vb_t[:])
                    ki_t = kif[n_ % 2]
                    P.copy("act", ki_t[:, 0, :], pki[:, 0:128])
                    rope(ki_t, 1, c_t, rtmp)
                    if kind == "p":
                        P.dma("sp", o_ki[i * 128:(i + 1) * 128, :], ki_t[:, 0, :])
                    else:
                        P.dma("sp", o_kis[i], ki_t[:, 0, :])
                    P.copy("act", kib[:], ki_t[:, 0, :])
                    P.tr(ptr[:, 512:640], kib[:], idb[:])
                    kiT_t = kiTt[n_ % 2]
                    P.copy("dve", kiT_t[:], ptr[:, 512:640])
                    if kind == "p":
                        P.dma("sp", KIT[:, i * 128:(i + 1) * 128], kiT_t[:])
                    else:
                        P.dma("sp", KITs[i], kiT_t[:])
                kv_front(0)
                for n_ in range(len(tiles)):
                    if n_ + 1 < len(tiles):
                        kv_front(n_ + 1)
                    kv_back(n_)
                P.flush()
            P.st = es

        own = [("h", -1)] + [("p", i) for i in range(NP)] + [("s", 0), ("s", 1)]
        groups = [own[i:i + 4] for i in range(0, len(own), 4)]

        def tile_index(kind, i):
            return i if kind == "p" else NP + i

        if "MAIN" in stages:
            with contextlib.ExitStack() as st:
                P.st = st
                nt = NormT(g_mix)
                gq = P.sb([128, 128], F32)
                bcast_row(gq[:], g_q[0:1, :], 128)
                for s in range(2):
                    P.dma("sp", UTs[s][:, 2:32], stT[s], okey=("UTs", "st"))
                xt = [P.sb([128, D], F32) for _ in range(2)]
                hnT = P.sb([128, KC, 512], BF16)
                wbuf = [P.sb([128, KC, 512], BF16) for _ in range(2)]
                cst = P.sb([128, 4, 32], F32)
                pa = P.ps()
                pg = P.ps()
                pq = [P.ps() for _ in range(2)]
                ptr = P.ps([128, 1024], BF16)
                sg = P.sb([128, 512], F32)
                ut = [P.sb([128, 512], F32) for _ in range(2)]
                sq_t = P.sb([128, 512], F32)
                sm_t = P.sb([128, 12], F32)
                rtmp = P.sb([128, 4, 4, 16], F32)
                qf = P.sb([128, 4, 128], F32)
                qb = P.sb([128, 512], BF16)
                qTt = [P.sb([128, 4, 128], BF16) for _ in range(2)]
                wis = [P.sb([128, IH], F32) for _ in range(2)]
                wcnt = 0
                xcnt = 0
                for grp in groups:
                    ng = len(grp)
                    N = ng * 128
                    for tt, (kind, i) in enumerate(grp):
                        x_t = xt[xcnt % 2]
                        xcnt += 1
                        if kind == "h":
                            P.dma("sp", x_t[:], xhalo)
                        elif kind == "p":
                            P.dma("sp", x_t[:], xctx[i * 128:(i + 1) * 128, :])
                            P.dma("sp", cst[:, tt, :], rope_c[i * 128:(i + 1) * 128, :], okey=(cst, tt))
                        else:
                            P.dma("sp", x_t[:], xsp[i])
                            P.dma("sp", cst[:, tt, :], rope_s, okey=(cst, tt))
                        nt.run(x_t[:], lambda c0, n, tt=tt: hnT[:, c0:c0 + n, tt * 128:(tt + 1) * 128])
                    for b in range(CC // 2):
                        wb = wbuf[wcnt % 2]
                        wcnt += 1
                        load_w(wb, w_glu[:, b * 512:(b + 1) * 512], 512)
                        wk = wkeys(wb)
                        for s in range(2):
                            j = 2 * b + s
                            for c in range(KC):
                                P.mm(pa[:, 0:N], wb[:, c, (2 * s) * 128:(2 * s + 1) * 128], hnT[:, c, 0:N],
                                     start=(c == 0), stop=(c == KC - 1), rkeys=[hnT] + wk)
                            for c in range(KC):
                                P.mm(pg[:, 0:N], wb[:, c, (2 * s + 1) * 128:(2 * s + 2) * 128], hnT[:, c, 0:N],
                                     start=(c == 0), stop=(c == KC - 1), rkeys=[hnT] + wk)
                            P.act(sg[:, 0:N], pg[:, 0:N], AF.Sigmoid)
                            u_t = ut[j % 2]
                            P.tt("dve", u_t[:, 0:N], pa[:, 0:N], sg[:, 0:N], ALU.mult)
                            for tt, (kind, i) in enumerate(grp):
                                src = u_t[:, tt * 128:(tt + 1) * 128]
                                if kind == "h":
                                    P.dma("sp", UTp[j * 128:(j + 1) * 128, 0:128], src, okey=("UTp", None))
                                elif kind == "p":
                                    P.dma("sp", UTp[j * 128:(j + 1) * 128, 128 + i * 128:128 + (i + 1) * 128], src, okey=("UTp", None))
                                else:
                                    P.dma("sp", UTs[i][j * 128:(j + 1) * 128, 32:160], src, okey=("UTs", "tok"))
                    for which, nblk, wsrc, dst in (("q", NH // 4, w_q, QT), ("qi", IH // 4, w_qi, QIT)):
                        for b in range(nblk):
                            wb = wbuf[wcnt % 2]
                            wcnt += 1
                            load_w(wb, wsrc[:, b * 512:(b + 1) * 512], 512)
                            wk = wkeys(wb)
                            real = [(tt, kind, i) for tt, (kind, i) in enumerate(grp) if kind != "h"]
                            for n2, (tt, kind, i) in enumerate(real):
                                if n2 == 0:
                                    tok_mm(pq[n2 % 2][:, 0:512], hnT, tt * 128, wb, 512, wk)
                                if n2 + 1 < len(real):
                                    tok_mm(pq[(n2 + 1) % 2][:, 0:512], hnT, real[n2 + 1][0] * 128, wb, 512, wk)
                                ti = tile_index(kind, i)
                                pq_t = pq[n2 % 2]
                                if which == "q":
                                    head_norm(pq_t[:, 0:512], 4, gq, qf[:], sq_t, sm_t)
                                else:
                                    P.copy("act", qf[:].rearrange("p h d -> p (h d)"), pq_t[:, 0:512])
                                rope(qf, 4, cst[:, tt, :], rtmp)
                                P.copy("act", qb[:], qf[:].rearrange("p h d -> p (h d)"))
                                for h in range(4):
                                    P.tr(ptr[:, h * 128:(h + 1) * 128], qb[:, h * 128:(h + 1) * 128], idb[:])
                                q_T = qTt[n2 % 2]
                                P.copy("dve", q_T[:], ptr[:, 0:512].rearrange("p (h t) -> p h t", t=128))
                                P.dma("sp", dst[ti][:, b * 4:(b + 1) * 4, :], q_T[:], okey=(dst.tensor.name, None))
                    wb = wbuf[wcnt % 2]
                    wcnt += 1
                    load_w(wb, w_wi, IH)
                    wk = wkeys(wb)
                    for tt, (kind, i) in enumerate(grp):
                        if kind == "h":
                            continue
                        ti = tile_index(kind, i)
                        pq_t = pq[tt % 2]
                        tok_mm(pq_t[:, 0:IH], hnT, tt * 128, wb, IH, wk)
                        w_s = wis[tt % 2]
                        P.act(w_s[:], pq_t[:, 0:IH], AF.Copy, scale=IDX_SCALE)
                        P.dma("sp", WI[ti], w_s[:], okey=("WI", None))
                P.flush()
            P.st = es

        if "B" in stages:
            with contextlib.ExitStack() as st:
                P.st = st
                P.npool["uin"] = 4
                wt = P.sb([128, CC, 31], F32)
                bt = P.sb([128, CC], F32)
                lg = P.sb([128, CC], F32)
                lb = P.sb([128, CC], F32)
                P.dma("sp", wt[:], dww)
                P.dma("sp", bt[:], dwb)
                P.dma("sp", lg[:], lng)
                P.dma("sp", lb[:], lnb)
                uin = P.sb([128, CC, 544], F32, "uin")
                cc_t = P.sb([128, CC, 512], F32)
                sqt = [P.sb([128, 512], F32) for _ in range(2)]
                p1 = P.ps()
                p2 = P.ps()
                mean = P.sb([128, 512], F32)
                var = P.sb([128, 512], F32)
                rstd = P.sb([128, 512], F32)
                tmp = [P.sb([128, 512], F32) for _ in range(2)]
                co = [P.sb([128, 512], BF16) for _ in range(2)]
                ptc = P.ps()
                cnew = P.sb([32, CCH], F32)
                jobs = []
                for tb in range(max(1, NP * 128 // 512)):
                    ntk = min(512, NP * 128)
                    jobs.append(("p", tb, ntk))
                jobs += [("s", 0, 128), ("s", 1, 128)]
                for kind, tb, ntk in jobs:
                    if kind == "p":
                        c0 = 128 + tb * ntk
                        src = UTp[:, c0 - 30:c0 + ntk].rearrange("(j p) t -> p j t", p=128)
                        mcol = tb * ntk
                        sk = "UTp"
                    else:
                        src = UTs[tb][:, 2:160].rearrange("(j p) t -> p j t", p=128)
                        mcol = (NP + tb) * 128
                        sk = "UTs"
                    W = 30 + ntk
                    P.dma("sp", uin[:, :, 0:W], src, ikey=sk)
                    last_p = (kind == "p" and (tb + 1) * ntk == NP * 128)
                    if last_p or kind == "s":
                        a0 = (30 + ntk - 30) if kind == "p" else (30 + 64 - 30)
                        for j0 in range(0, CC, 4):
                            for j in range(j0, min(CC, j0 + 4)):
                                P.tr(ptc[0:30, (j - j0) * 128:(j - j0 + 1) * 128], uin[:, j, a0:a0 + 30], idf[:])
                            nj = min(CC, j0 + 4) - j0
                            P.copy("act", cnew[0:30, j0 * 128:(j0 + nj) * 128], ptc[0:30, 0:nj * 128])
                        P.dma("sp", o_conv if kind == "p" else o_convs[tb], cnew[0:30, :])
                    for j in range(CC):
                        acc = cc_t[:, j, 0:ntk]
                        P.ts("dve", acc, uin[:, j, 0:ntk], wt[:, j, 0:1], bt[:, j:j + 1], op0=ALU.mult, op1=ALU.add, okey=(cc_t, j))
                        for k in range(1, 31):
                            P.stt(acc, uin[:, j, k:k + ntk], wt[:, j, k:k + 1], acc, ALU.mult, ALU.add, okey=(cc_t, j),
                                  rkeys=[uin, (cc_t, j)])
                        s_t = sqt[j % 2]
                        P.act(s_t[:, 0:ntk], acc, AF.Square, ikey=(cc_t, j))
                        P.mm(p1[:, 0:ntk], onef[:], acc, start=(j == 0), stop=(j == CC - 1), rkeys=[onef, (cc_t, j)])
                        P.mm(p2[:, 0:ntk], onef[:], s_t[:, 0:ntk], start=(j == 0), stop=(j == CC - 1))
                    P.ts("dve", mean[:, 0:ntk], p1[:, 0:ntk], 1.0 / CCH, None, op0=ALU.mult)
                    P.tt("dve", var[:, 0:ntk], mean[:, 0:ntk], mean[:, 0:ntk], ALU.mult)
                    P.stt(var[:, 0:ntk], p2[:, 0:ntk], 1.0 / CCH, var[:, 0:ntk], ALU.mult, ALU.subtract)
                    P.ts("dve", var[:, 0:ntk], var[:, 0:ntk], EPS, None, op0=ALU.add)
                    P.act(var[:, 0:ntk], var[:, 0:ntk], AF.Sqrt)
                    P.recip(rstd[:, 0:ntk], var[:, 0:ntk])
                    for j in range(CC):
                        t_ = tmp[j % 2]
                        P.tt("dve", t_[:, 0:ntk], cc_t[:, j, 0:ntk], mean[:, 0:ntk], ALU.subtract, rkeys=[(cc_t, j), mean])
                        P.tt("dve", t_[:, 0:ntk], t_[:, 0:ntk], rstd[:, 0:ntk], ALU.mult)
                        c_o = co[j % 2]
                        P.act(c_o[:, 0:ntk], t_[:, 0:ntk], AF.Silu, scale=lg[:, j:j + 1], bias=lb[:, j:j + 1])
                        P.dma("sp", MIXT[j * 128:(j + 1) * 128, mcol:mcol + ntk], c_o[:, 0:ntk], okey=("MIXT", "conv"))
                P.flush()
            P.st = es

        if "C" in stages:
            with contextlib.ExitStack() as st:
                P.st = st
                SMAX = max(SEQ, SS)
                P.npool["kiT_c"] = 4
                P.npool["kT_c"] = 4
                P.npool["v_c"] = 4
                kiT_c = P.sb([128, SMAX], BF16, "kiT_c")
                kT_c = P.sb([128, NKV, SMAX], BF16, "kT_c")
                v_c = P.sb([128, SMAX // 128, NKV * 128], BF16, "v_c")
                kc_t = P.sb([128, SMAX], BF16)
                qch_t = P.sb([128, NT], F32)
                P.dma("sp", qch_t[:], qch)
                qiT = [P.sb([128, IH, 128], BF16) for _ in range(2)]
                qT = [P.sb([128, NH, 128], BF16) for _ in range(2)]
                wi_t = [P.sb([128, IH], F32) for _ in range(2)]
                acc2 = [P.sb([128, SMAX], F32) for _ in range(2)]
                madd = P.sb([128, SMAX], BF16)
                bs = P.sb([128, 8], F32)
                m8 = P.sb([128, 256], F32)
                thr = P.sb([128, 1], F32)
                mask = P.sb([128, SMAX], BF16)
                maskT = P.sb([128, SMAX // 128, 128], BF16)
                rl = [P.sb([128, 512], F32) for _ in range(4)]
                pe_ = [P.sb([128, GQ, 128], BF16) for _ in range(3)]
                pm = [P.sb([128, GQ, 128], BF16) for _ in range(4)]
                rz = P.sb([128, GQ * 128], F32)
                ob = [P.sb([128, GQ, 128], BF16) for _ in range(2)]
                ps_s = [P.ps() for _ in range(2)]
                ps_qk = [P.ps() for _ in range(3)]
                ps_o = P.ps()
                ps_z = P.ps()
                ptr = [P.ps([128, 1024], BF16) for _ in range(1)]

                def seglist(blocks):
                    nb = len(blocks)
                    segs = []
                    a = 0
                    while a < nb:
                        b_ = a + 1
                        while b_ < nb and b_ - a < 4 and blocks[b_] == blocks[b_ - 1] + 1:
                            b_ += 1
                        segs.append((a, b_))
                        a = b_
                    return segs

                cnt = dict(n=0, it=0)
                P.npool["UB16"] = 3
                P.npool["VB16"] = 3
                pre_ops = []
                for c in range(0, 128, 2):
                    pre_ops.append(("u", c))
                    pre_ops.append(("v", c))
                n_slots = (NP + 2) * NKV
                per_slot = -(-len(pre_ops) // n_slots)

                def precast_some():
                    dd = min(2048, KC * 128)
                    for _ in range(per_slot):
                        if not pre_ops:
                            return
                        kind, c = pre_ops.pop(0)
                        if kind == "u":
                            P.dma("pool", UB16[c:c + 2].rearrange("c p (x d) -> p c x d", d=dd),
                                  uT[c:c + 2].rearrange("c p (x d) -> p c x d", d=dd), okey=("UB16", None))
                        else:
                            dv = min(2048, D)
                            P.dma("pool", VB16[c * 128:(c + 2) * 128, :].rearrange("(c p) (x d) -> p c x d", p=128, d=dv),
                                  vtab[c * 128:(c + 2) * 128, :].rearrange("(c p) (x d) -> p c x d", p=128, d=dv), okey=("VB16", None))

                def idx_phase(job):
                    ti, blocks = job["ti"], job["blocks"]
                    if job.get("pre_idx"):
                        job["pre_idx"]()
                    k2 = ti % 2
                    acc = acc2[k2]
                    P.dma("sp", qiT[k2][:], QIT[ti], ikey="QIT")
                    P.dma("sp", wi_t[k2][:], WI[ti], ikey="WI")
                    segs = seglist(blocks)
                    N = len(blocks) * 128
                    for (a, b_) in segs:
                        P.ts("pool", madd[:, a * 128:b_ * 128], kc_t[:, blocks[a] * 128:(blocks[a] + b_ - a) * 128],
                             qch_t[:, ti:ti + 1], NEG, op0=ALU.is_gt, op1=ALU.mult)
                    for (a, b_) in segs:
                        w = (b_ - a) * 128
                        for h in range(IH):
                            n_ = cnt["n"]
                            cnt["n"] += 1
                            p_ = ps_s[n_ % 2]
                            r_ = rl[n_ % 4]
                            P.mm(p_[:, 0:w], qiT[k2][:, h, :], kiT_c[:, blocks[a] * 128:blocks[a] * 128 + w], rkeys=[qiT[k2], kiT_c])
                            P.act(r_[:, 0:w], p_[:, 0:w], AF.Relu)
                            if h == 0:
                                P.ts("dve", acc[:, a * 128:b_ * 128], r_[:, 0:w], wi_t[k2][:, 0:1], None, op0=ALU.mult)
                            else:
                                P.stt(acc[:, a * 128:b_ * 128], r_[:, 0:w], wi_t[k2][:, h:h + 1], acc[:, a * 128:b_ * 128], ALU.mult, ALU.add)
                    P.reduce(bs[:, 0:1], acc[:, 0:N], ALU.min)
                    P.tt("pool", acc[:, 0:N], acc[:, 0:N], madd[:, 0:N], ALU.add)
                    P.max8(m8[:, 0:8], acc[:, 0:N])
                    P.copy("dve", bs[:, 1:2], m8[:, 0:1])

                NITER = 22

                def topk_rounds(job, r0, r1):
                    ti, N, topk = job["ti"], len(job["blocks"]) * 128, job["topk"]
                    acc = acc2[ti % 2]
                    lo, hi, mid, tmp, cn, sel, dd = (bs[:, i:i + 1] for i in range(7))
                    for r in range(r0, min(r1, NITER)):
                        P.ts("dve", tmp, hi, 0.5, None, op0=ALU.mult)
                        P.stt(mid, lo, 0.5, tmp, ALU.mult, ALU.add)
                        P.ts("dve", mask[:, 0:N], acc[:, 0:N], mid, None, op0=ALU.is_ge, op1=ALU.add, accum_out=cn)
                        P.ts("dve", sel, cn, float(topk) - 0.5, None, op0=ALU.is_ge)
                        P.tt("dve", dd, mid, lo, ALU.subtract)
                        P.stt(lo, dd, sel, lo, ALU.mult, ALU.add)
                        P.tt("dve", dd, hi, mid, ALU.subtract)
                        P.stt(hi, dd, sel, mid, ALU.mult, ALU.add)

                def topk_final(job):
                    ti, blocks, topk = job["ti"], job["blocks"], job["topk"]
                    nb = len(blocks)
                    N = nb * 128
                    acc = acc2[ti % 2]
                    P.ts("dve", thr[:], bs[:, 0:1], 0.5 * NEG, None, op0=ALU.max)
                    P.ts("dve", mask[:, 0:N], acc[:, 0:N], thr[:, 0:1], None, op0=ALU.is_ge)
                    for b0 in range(0, nb, 8):
                        n8 = min(8, nb - b0)
                        pt = ptr[0]
                        for j in range(n8):
                            P.tr(pt[:, j * 128:(j + 1) * 128], mask[:, (b0 + j) * 128:(b0 + j + 1) * 128], idb[:])
                        P.copy("act", maskT[:, b0:b0 + n8, :], pt[:, 0:n8 * 128].rearrange("p (n t) -> p n t", t=128))

                def attn_group(job, g):
                    ti, blocks = job["ti"], job["blocks"]
                    k2 = ti % 2
                    nb = len(blocks)
                    if g == 0:
                        if job.get("pre_attn"):
                            job["pre_attn"]()
                        P.dma("sp", qT[k2][:], QT[ti], ikey="QT")
                    W = GQ * 128
                    LA = 2
                    bufs = {}

                    def front(ci):
                        blk = blocks[ci]
                        it = cnt["it"]
                        cnt["it"] += 1
                        pq_ = ps_qk[it % 3]
                        e_ = pe_[it % 3]
                        m_ = pm[it % 4]
                        bufs[ci] = m_
                        P.mm(pq_[:, 0:W], kT_c[:, g, blk * 128:(blk + 1) * 128],
                             qT[k2][:, g * GQ:(g + 1) * GQ, :].rearrange("p r t -> p (r t)"), rkeys=[kT_c, qT[k2]])
                        P.act(e_[:].rearrange("p r t -> p (r t)"), pq_[:, 0:W], AF.Exp, scale=ATT_SCALE)
                        P.tt("pool", m_[:], e_[:], maskT[:, ci, :].unsqueeze(1).to_broadcast([128, GQ, 128]), ALU.mult)

                    def back(ci):
                        blk = blocks[ci]
                        m_ = bufs[ci]
                        mf = m_[:].rearrange("p r t -> p (r t)")
                        P.mm(ps_o[:, 0:W], v_c[:, blk, g * 128:(g + 1) * 128], mf, start=(ci == 0), stop=(ci == nb - 1), rkeys=[v_c, m_])
                        P.mm(ps_z[:, 0:W], oneb[:], mf, start=(ci == 0), stop=(ci == nb - 1))

                    for ci in range(min(LA, nb)):
                        front(ci)
                    for ci in range(nb):
                        if ci + LA < nb:
                            front(ci + LA)
                        back(ci)
                    P.recip(rz[:, 0:W], ps_z[:, 0:W])
                    o_ = ob[g % 2]
                    P.tt("dve", o_[:].rearrange("p r t -> p (r t)"), ps_o[:, 0:W], rz[:, 0:W], ALU.mult)
                    P.dma("sp", MIXT[CCH + g * GQ * 128:CCH + (g + 1) * GQ * 128, ti * 128:(ti + 1) * 128].rearrange("(r d) t -> d r t", d=128),
                          o_[:], okey=("MIXT", "attn"))

                def load_prompt_ki():
                    P.cdma(kc_t[:, 0:SEQ], kc_p[0:1, :].to_broadcast([128, SEQ]))
                    P.dma("sp", kiT_c[:, 0:SEQ], KIT, ikey="KIT", okey=(kiT_c, 0))

                def load_prompt_kv():
                    P.dma("sp", kT_c[:, :, 0:SEQ], KT, ikey="KT", okey=(kT_c, 0))
                    P.dma("sp", v_c[:, 0:NCX, :], Vc.rearrange("(c p) n -> p c n", p=128), ikey="Vc", okey=(v_c, 0))

                def mk_sample_ki(s):
                    def f():
                        P.cdma(kc_t[:, 0:SS], kc_s[0:1, :].to_broadcast([128, SS]))
                        P.cdma(kiT_c[:, 0:PAST], ckiT[s], okey=(kiT_c, 0))
                        P.dma("sp", kiT_c[:, PAST:SS], KITs[s], ikey="KITs", okey=(kiT_c, 1))
                    return f

                def mk_sample_kv(s):
                    def f():
                        for g in range(NKV):
                            P.cdma(kT_c[:, g, 0:PAST], ckT[s][:, g, :], okey=(kT_c, 0))
                        P.dma("sp", kT_c[:, :, PAST:SS], KTs[s], ikey="KTs", okey=(kT_c, 1))
                        cvv = cv[s].rearrange("(c p) n -> p c n", p=128)
                        nq = 4 if (PAST // 128) % 4 == 0 else 1
                        stp = (PAST // 128) // nq
                        for q in range(nq):
                            P.dma("pool", v_c[:, q * stp:(q + 1) * stp, :], cvv[:, q * stp:(q + 1) * stp, :], okey=(v_c, 0))
                        P.dma("sp", v_c[:, PAST // 128, :], Vs[s], ikey="Vs", okey=(v_c, 1))
                    return f

                jobs = []
                for i in range(NP):
                    jobs.append(dict(ti=i, blocks=list(range(0, i + 1)) + list(range(NP, 2 * NP)), topk=cfg["TOPK_P"]))
                jobs[0]["pre_idx"] = load_prompt_ki
                jobs[0]["pre_attn"] = load_prompt_kv
                for s in range(2):
                    jobs.append(dict(ti=NP + s, blocks=list(range(SS // 128)), topk=cfg["TOPK_S"],
                                     pre_idx=mk_sample_ki(s), pre_attn=mk_sample_kv(s)))
                idx_phase(jobs[0])
                topk_rounds(jobs[0], 0, 10 ** 6)
                topk_final(jobs[0])
                for k, job in enumerate(jobs):
                    nxt = jobs[k + 1] if k + 1 < len(jobs) else None
                    if nxt is not None:
                        idx_phase(nxt)
                        per = -(-NITER // NKV)
                    for g in range(NKV):
                        if nxt is not None:
                            topk_rounds(nxt, g * per, (g + 1) * per)
                        precast_some()
                        attn_group(job, g)
                    if nxt is not None:
                        topk_final(nxt)
                while pre_ops:
                    precast_some()
                P.flush()
            P.st = es

        otiles = list(range(NT))
        ogroups = [otiles[i:i + 4] for i in range(0, NT, 4)]
        Hs = dscr("Hs", [NTOK, D])
        RC = dscr("RC", [NT, 128, 3, 128])

        def x_rows(ti):
            return xctx[ti * 128:(ti + 1) * 128, :] if ti < NP else xsp[ti - NP]

        if "D" in stages:
            with contextlib.ExitStack() as st:
                P.st = st
                mixT = P.sb([128, KC, 512], BF16)
                wbuf = [P.sb([128, KC, 512], BF16) for _ in range(2)]
                xb_ = [P.sb([128, 512], F32) for _ in range(3)]
                hb_ = [P.sb([128, 512], F32) for _ in range(3)]
                pp = [P.ps() for _ in range(4)]
                wcnt = 0
                k_ = 0
                for grp in ogroups:
                    N = len(grp) * 128
                    c0 = grp[0] * 128
                    P.dma("sp", mixT[:, :, 0:N], MIXT[:, c0:c0 + N].rearrange("(c p) n -> p c n", p=128), ikey="MIXT")
                    for b in range(D // 512):
                        wb = wbuf[wcnt % 2]
                        wcnt += 1
                        load_w(wb, w_out[:, b * 512:(b + 1) * 512], 512)
                        wk = wkeys(wb)
                        for tt, ti in enumerate(grp):
                            xb = xb_[k_ % 3]
                            hb = hb_[k_ % 3]
                            p_ = pp[k_ % 4]
                            k_ += 1
                            P.dma("sp", xb[:], x_rows(ti)[:, b * 512:(b + 1) * 512])
                            tok_mm(p_[:, 0:512], mixT, tt * 128, wb, 512, wk)
                            P.tt("dve", hb[:], p_[:, 0:512], xb[:], ALU.add)
                            P.dma("sp", Hs[ti * 128:(ti + 1) * 128, b * 512:(b + 1) * 512], hb[:], okey=("Hs", None))
                P.flush()
            P.st = es

            with contextlib.ExitStack() as st:
                P.st = st
                nt = NormT(g_memn)
                gbc2 = P.sb([128, D], F32)
                bcast_row(gbc2[:], g_ffn[0:1, :], D)
                gmq = P.sb([128, 128], F32)
                bcast_row(gmq[:], g_mq[0:1, :], 128)
                wqm_b = P.sb([128, KC, 512], BF16)
                load_w(wqm_b, w_qm, 512)
                wom_b = P.sb([128, 4, D], BF16)
                P.cdma(wom_b[:], w_om.rearrange("(h p) d -> p h d", p=128))
                mkT_c = P.sb([128, 4, MEMT], BF16)
                mv_c = P.sb([128, MC, 512], BF16)
                ht = [P.sb([128, D], F32) for _ in range(2)]
                hn = [P.sb([128, KC, 128], BF16) for _ in range(2)]
                pq_ = P.ps()
                pl_ = [P.ps() for _ in range(2)]
                po_ = P.ps()
                pz_ = pq_
                pw_ = [P.ps() for _ in range(1)]
                ptr = P.ps([128, 1024], BF16)
                sq_t = P.sb([128, 512], F32)
                sm_t = P.sb([128, 12], F32)
                qmf = P.sb([128, 4, 128], F32)
                qmb = P.sb([128, 512], BF16)
                qmT = P.sb([128, 4, 128], BF16)
                pmT = [P.sb([128, 4, 128], BF16) for _ in range(MC)]
                rz = P.sb([128, 512], F32)
                omT = P.sb([128, 4, 128], BF16)
                for ti in otiles:
                    if ti == 0:
                        P.dma("sp", mkT_c[:], MKT, ikey="MKT")
                        P.dma("sp", mv_c[:], MV.rearrange("(c p) n -> p c n", p=128), ikey="MV")
                    elif ti >= NP:
                        P.dma("pool", mkT_c[:], cmkT[ti - NP])
                        P.dma("pool", mv_c[:], cmv[ti - NP].rearrange("(c p) n -> p c n", p=128))
                    h_t = ht[ti % 2]
                    hn_t = hn[ti % 2]
                    P.dma("sp", h_t[:], Hs[ti * 128:(ti + 1) * 128, :], ikey="Hs")
                    nt.run(h_t[:], lambda c0, n, hn_t=hn_t: hn_t[:, c0:c0 + n, :])
                    tok_mm(pq_[:, 0:512], hn_t, 0, wqm_b, 512, wkeys(wqm_b))
                    head_norm(pq_[:, 0:512], 4, gmq, qmf[:], sq_t, sm_t)
                    P.copy("act", qmb[:], qmf[:].rearrange("p h d -> p (h d)"))
                    for h in range(4):
                        P.tr(ptr[:, h * 128:(h + 1) * 128], qmb[:, h * 128:(h + 1) * 128], idb[:])
                    P.copy("dve", qmT[:], ptr[:, 0:512].rearrange("p (h t) -> p h t", t=128))
                    for mc in range(MC):
                        for h in range(4):
                            P.mm(pl_[mc % 2][:, h * 128:(h + 1) * 128], mkT_c[:, h, mc * 128:(mc + 1) * 128], qmT[:, h, :])
                        P.act(pmT[mc][:].rearrange("p h t -> p (h t)"), pl_[mc % 2][:, 0:512], AF.Exp, scale=ATT_SCALE)
                    for h in range(4):
                        for mc in range(MC):
                            P.mm(po_[:, h * 128:(h + 1) * 128], mv_c[:, mc, h * 128:(h + 1) * 128], pmT[mc][:, h, :],
                                 start=(mc == 0), stop=(mc == MC - 1))
                    for mc in range(MC):
                        P.mm(pz_[:, 0:512], oneb[:], pmT[mc][:].rearrange("p h t -> p (h t)"), start=(mc == 0), stop=(mc == MC - 1))
                    P.recip(rz[:], pz_[:, 0:512])
                    P.tt("dve", omT[:].rearrange("p h t -> p (h t)"), po_[:, 0:512], rz[:], ALU.mult)
                    for b in range(D // 512):
                        p_ = pw_[0]
                        for h in range(4):
                            P.mm(p_[:, 0:512], omT[:, h, :], wom_b[:, h, b * 512:(b + 1) * 512], start=(h == 0), stop=(h == 3))
                        P.tt("dve", h_t[:, b * 512:(b + 1) * 512], p_[:, 0:512], h_t[:, b * 512:(b + 1) * 512], ALU.add)
                    P.dma("sp", H2[ti * 128:(ti + 1) * 128, :], h_t[:], okey=("H2", None))
                    nt.gbc, g_save = gbc2, nt.gbc
                    nt.run(h_t[:], lambda c0, n, hn_t=hn_t: hn_t[:, c0:c0 + n, :])
                    nt.gbc = g_save
                    P.dma("sp", HN2T[:, ti * 128:(ti + 1) * 128].rearrange("(c p) t -> p c t", p=128), hn_t[:], okey=("HN2T", None))
                P.flush()
            P.st = es

            with contextlib.ExitStack() as st:
                P.st = st
                hn2 = P.sb([128, KC, 512], BF16)
                wbuf = [P.sb([128, KC, 512], BF16) for _ in range(2)]
                qpT = P.sb([128, 16, 512], F32)
                sk_t = P.sb([128, 16, 128], F32)
                P.dma("sp", sk_t[:], subk)
                pq_ = [P.ps() for _ in range(2)]
                ps_ = [P.ps() for _ in range(2)]
                ptf = P.ps()
                s12 = [P.sb([128, 16, 128], F32) for _ in range(2)]
                v16 = P.sb([128, 16, 16], F32)
                tmp128 = P.sb([128, 128], F32)
                cand = P.sb([128, 8, 256], F32)
                tmpc = P.sb([128, 256], F32)
                t16 = P.sb([128, 8, 16], F32)
                e16 = P.sb([128, 8, 16], F32)
                zz = P.sb([128, 8], F32)
                mlz = P.sb([128, 8], F32)
                rc3 = P.sb([128, 3, 8, 16], F32)
                rcT = [P.sb([128, 3, 128], F32) for _ in range(2)]
                wcnt = 0
                for grp in ogroups:
                    N = len(grp) * 128
                    c0 = grp[0] * 128
                    P.dma("sp", hn2[:, :, 0:N], HN2T[:, c0:c0 + N].rearrange("(c p) n -> p c n", p=128), ikey="HN2T")
                    for b in range(4):
                        wb = wbuf[wcnt % 2]
                        wcnt += 1
                        load_w(wb, w_pq[:, b * 512:(b + 1) * 512], 512)
                        wk = wkeys(wb)
                        for jj in range(4):
                            j = b * 4 + jj
                            p_ = pq_[j % 2]
                            for c in range(KC):
                                P.mm(p_[:, 0:N], wb[:, c, jj * 128:(jj + 1) * 128], hn2[:, c, 0:N], start=(c == 0), stop=(c == KC - 1),
                                     rkeys=[hn2] + wk)
                            P.copy("act", qpT[:, j, 0:N], p_[:, 0:N], okey=(qpT, j))
                    for tt, ti in enumerate(grp):
                        s_t = s12[ti % 2]
                        for jb in range(4):
                            p_ = ps_[jb % 2]
                            for jj in range(4):
                                j = jb * 4 + jj
                                P.mm(p_[:, jj * 128:(jj + 1) * 128], qpT[:, j, tt * 128:(tt + 1) * 128], sk_t[:, j, :], rkeys=[(qpT, j), sk_t])
                            P.copy("act", s_t[:, jb * 4:(jb + 1) * 4, :].rearrange("p j k -> p (j k)"), p_[:, 0:512])
                        P.dma("sp", S12[:, ti * 128:(ti + 1) * 128, :].rearrange("j t i -> t j i"), s_t[:], okey=("S12", None))
                        for j in range(16):
                            P.max8(v16[:, j, 0:8], s_t[:, j, :])
                            P.mrep(tmp128[:], v16[:, j, 0:8], s_t[:, j, :], -3.0e38)
                            P.max8(v16[:, j, 8:16], tmp128[:])
                        v16v = v16[:].rearrange("p (h two) k -> p h two k", two=2)
                        for h in range(8):
                            P.tt("dve", cand[:, h, :].rearrange("p (a b) -> p a b", b=16),
                                 v16[:, 2 * h, :].unsqueeze(2).to_broadcast([128, 16, 16]),
                                 v16[:, 2 * h + 1, :].unsqueeze(1).to_broadcast([128, 16, 16]), ALU.add)
                        for h in range(8):
                            P.max8(t16[:, h, 0:8], cand[:, h, :])
                            P.mrep(tmpc[:], t16[:, h, 0:8], cand[:, h, :], -3.0e38)
                            P.max8(t16[:, h, 8:16], tmpc[:])
                        P.tt("dve", e16[:], t16[:], t16[:, :, 0:1].to_broadcast([128, 8, 16]), ALU.subtract)
                        P.act(e16[:], e16[:], AF.Exp)
                        P.reduce(zz[:], e16[:], ALU.add)
                        P.act(mlz[:], zz[:], AF.Ln)
                        P.tt("dve", mlz[:], mlz[:], t16[:, :, 0], ALU.add)
                        P.copy("dve", rc3[:, 0, :, :], v16v[:, :, 0, :])
                        P.tt("dve", rc3[:, 1, :, :], t16[:, :, 15:16].to_broadcast([128, 8, 16]), rc3[:, 0, :, :], ALU.subtract)
                        P.tt("dve", rc3[:, 2, :, :], rc3[:, 0, :, :], mlz[:].unsqueeze(2).to_broadcast([128, 8, 16]), ALU.subtract)
                        for q in range(3):
                            P.tr(ptf[:, q * 128:(q + 1) * 128], rc3[:, q, :, :].rearrange("p h a -> p (h a)"), idf[:])
                        r_T = rcT[ti % 2]
                        P.copy("act", r_T[:].rearrange("p q t -> p (q t)"), ptf[:, 0:384])
                        P.dma("sp", RC[ti], r_T[:], okey=("RC", None))
                P.flush()
            P.st = es

            with contextlib.ExitStack() as st:
                P.st = st
                TB = 32
                s1r = [P.sb([128, TB, 128], F32) for _ in range(2)]
                s2r = [P.sb([128, TB, 128], F32) for _ in range(2)]
                rct = [P.sb([128, 3, 128], F32) for _ in range(2)]
                o1 = [P.sb([128, 128], BF16) for _ in range(4)]
                ee = [P.sb([128, 128], F32) for _ in range(4)]
                rr = [P.sb([128, 128], BF16) for _ in range(4)]
                gst = [P.sb([128, 128, 128], BF16) for _ in range(2)]
                pg_ = [P.ps() for _ in range(2)]
                kk = 0
                for ti in otiles:
                    rc_ = rct[ti % 2]
                    g_s = gst[ti % 2]
                    P.dma("sp", rc_[:], RC[ti], ikey="RC")
                    for tb in range(128 // TB):
                        t0 = ti * 128 + tb * TB
                        a1 = s1r[tb % 2]
                        a2 = s2r[tb % 2]
                        for half, dst in ((0, a1), (1, a2)):
                            src = S12[:, t0:t0 + TB, :].rearrange("(h two) t i -> two h (t i)", two=2)[half]
                            P.dma("sp", dst[:].rearrange("p t i -> p (t i)"), src.unsqueeze(1).to_broadcast([8, 16, TB * 128]),
                                  ikey="S12", okey=(dst, None))
                        for tq in range(0, TB, 4):
                            p_ = pg_[(kk) % 2]
                            kk += 1
                            for u4 in range(4):
                                tl = tq + u4
                                t = tb * TB + tl
                                o_ = o1[u4]
                                e_ = ee[u4]
                                r_ = rr[u4]
                                P.ts("dve", o_[:], a1[:, tl, :], rc_[:, 0, t:t + 1], None, op0=ALU.is_equal)
                                P.act(e_[:], a2[:, tl, :], AF.Exp, bias=rc_[:, 2, t:t + 1])
                                P.stt(r_[:], a2[:, tl, :], rc_[:, 1, t:t + 1], e_[:], ALU.is_ge, ALU.mult)
                                P.mm(p_[:, u4 * 128:(u4 + 1) * 128], o_[:], r_[:])
                            tbase = tb * TB + tq
                            P.copy("act", g_s[:, :, tbase:tbase + 4].rearrange("p i t -> p t i"),
                                   p_[:, 0:512].rearrange("p (t i) -> p t i", i=128))
                    P.dma("sp", GALL[:, :, ti * 128:(ti + 1) * 128], g_s[:], okey=("GALL", None))
                P.flush()
            P.st = es

        if "E" in stages:
            with contextlib.ExitStack() as st:
                P.st = st
                NCH = PEER_KEYS
                EB = 4
                hn2 = P.sb([128, KC, 512], BF16)
                oacc = P.sb([128, 4, D], F32)
                ub = [P.sb([128, KC, 128], BF16) for _ in range(3)]
                vb = [P.sb([128, EB, D], BF16) for _ in range(2)]
                coef = [P.sb([128, EB, 512], BF16) for _ in range(2)]
                gl = [P.sb([128, 512], BF16) for _ in range(2)]
                gc = [P.sb([128, 512], BF16) for _ in range(4)]
                pa_ = [P.ps() for _ in range(2)]
                pv_ = [P.ps() for _ in range(4)]
                ucnt = 0
                vcnt = 0
                pcnt = 0
                DH = 2048 if D % 2048 == 0 else D
                for grp in ogroups:
                    ng = len(grp)
                    N = ng * 128
                    c0 = grp[0] * 128
                    P.dma("sp", hn2[:, :, 0:N], HN2T[:, c0:c0 + N].rearrange("(c p) n -> p c n", p=128), ikey="HN2T")
                    for tt, ti in enumerate(grp):
                        P.dma("sp", oacc[:, tt, :], H2[ti * 128:(ti + 1) * 128, :], ikey="H2", okey=(oacc, tt))
                    def v_load(eb):
                        v_b = vb[eb % 2]
                        vsrc = VB16[eb * EB * 128:(eb + 1) * EB * 128, :].rearrange("(cc p) d -> p cc d", p=128)
                        P.dma("sp", v_b[:], vsrc, ikey="VB16")

                    loaded = set()

                    def u_load(gidx):
                        if gidx >= NCH or gidx in loaded:
                            return
                        loaded.add(gidx)
                        k_ = ucnt + gidx
                        P.dma("sp", ub[k_ % 3][:].rearrange("p c e -> p (c e)"), UB16[gidx], ikey="UB16")
                        P.dma("sp", gc[k_ % 4][:, 0:N], GALL[gidx][:, c0:c0 + N], ikey="GALL")

                    def u_phase(eb):
                        cf = coef[eb % 2]
                        for cc in range(EB):
                            c = eb * EB + cc
                            u_load(c)
                            u_load(c + 1)
                            u_load(c + 2)
                            k_ = ucnt + c
                            u_b = ub[k_ % 3]
                            g_l = gl[k_ % 2]
                            g_c = gc[k_ % 4]
                            p_ = pa_[k_ % 2]
                            for dc in range(KC):
                                P.mm(p_[:, 0:N], u_b[:, dc, :], hn2[:, dc, 0:N], start=(dc == 0), stop=(dc == KC - 1))
                            P.act(g_l[:, 0:N], p_[:, 0:N], AF.Gelu)
                            P.tt("dve", cf[:, cc, 0:N], g_l[:, 0:N], g_c[:, 0:N], ALU.mult, okey=(cf, cc))

                    def v_phase(eb):
                        nonlocal pcnt
                        cf = coef[eb % 2]
                        v_b = vb[eb % 2]
                        for tt in range(ng):
                            for db in range(D // 512):
                                pv = pv_[pcnt % 4]
                                pcnt += 1
                                for cc in range(EB):
                                    P.mm(pv[:, 0:512], cf[:, cc, tt * 128:(tt + 1) * 128], v_b[:, cc, db * 512:(db + 1) * 512],
                                         start=(cc == 0), stop=(cc == EB - 1), rkeys=[(cf, cc), v_b])
                                P.tt("dve", oacc[:, tt, db * 512:(db + 1) * 512], pv[:, 0:512], oacc[:, tt, db * 512:(db + 1) * 512], ALU.add,
                                     okey=(oacc, tt), rkeys=[pv, (oacc, tt)])

                    nE = NCH // EB
                    u_load(0)
                    u_load(1)
                    v_load(0)
                    u_phase(0)
                    for eb in range(nE):
                        if eb + 1 < nE:
                            v_load(eb + 1)
                            u_phase(eb + 1)
                        v_phase(eb)
                    ucnt += NCH
                    for tt, ti in enumerate(grp):
                        P.dma("sp", y[ti * 128:(ti + 1) * 128, :], oacc[:, tt, :], ikey=(oacc, tt), okey=("y", None))
                P.flush()
            P.st = es

        if dbg:
            for nm, ap_ in (("MIXT", MIXT), ("UTp", UTp), ("UTs", UTs), ("QT", QT), ("QIT", QIT), ("WI", WI), ("KT", KT), ("KIT", KIT),
                            ("Vc", Vc), ("H2", H2), ("HN2T", HN2T), ("S12", S12), ("GALL", GALL), ("MKT", MKT), ("MV", MV)):
                if nm in dbg:
                    o_ = dout("dbg_" + nm, list(ap_.shape), ap_.dtype)
                    P.dma("sp", o_, ap_)
        P.flush()
    return nc


def _rope_table(pos):
    half = 16
    inv_freq = np.power(np.float32(ROPE_THETA), -np.arange(half, dtype=np.float32) / np.float32(half)).astype(np.float32)
    ang = pos.astype(np.float32)[:, None] * inv_freq[None, :]
    return np.concatenate([np.cos(ang), np.sin(ang)], axis=1).astype(np.float32)


def host_prep(inp, cfg):
    D, KC, CCH, CC, NH, NKV, NP, NT, IH, SEQ, PAST, SS, MEMT = (cfg[k] for k in (
        "D", "KC", "CCH", "CC", "NH", "NKV", "NP", "NT", "IH", "SEQ", "PAST", "SS", "MEMT"))
    DS = cfg["DS"]
    f = lambda a: np.ascontiguousarray(a, dtype=np.float32)
    half = SEQ // 2
    w_in = inp["w_in"][0]
    OFF_Q = 2 * CCH
    OFF_K = OFF_Q + NH * 128
    OFF_V = OFF_K + NKV * 128
    OFF_QI = OFF_V + NKV * 128
    OFF_KI = OFF_QI + IH * 128
    OFF_WI = OFF_KI + 128
    a_ = w_in[:, :CCH].reshape(D, CC, 128)
    g_ = w_in[:, CCH:2 * CCH].reshape(D, CC, 128)
    w_glu = f(np.stack([a_, g_], axis=2).reshape(D, 2 * CCH))
    shared = dict(
        w_glu=w_glu,
        w_q=f(w_in[:, OFF_Q:OFF_K]),
        w_qi=f(w_in[:, OFF_QI:OFF_KI]),
        w_wi=f(w_in[:, OFF_WI:OFF_WI + IH]),
        w_kv=f(np.concatenate([w_in[:, OFF_K:OFF_V], w_in[:, OFF_V:OFF_QI], w_in[:, OFF_KI:OFF_WI]], axis=1)),
        w_out=f(inp["w_out"][0]),
        w_qm=f(inp["w_q_mem"][0]), w_km=f(inp["w_k_mem"][0]), w_vm=f(inp["w_v_mem"][0]), w_om=f(inp["w_o_mem"][0]),
        w_pq=f(inp["peer_wq"][0]),
        g_mix=f(inp["norm_mix_g"]), g_memn=f(inp["norm_mem_g"]), g_ffn=f(inp["norm_ffn_g"]), g_mem=f(inp["mem_norm_g"]),
        g_q=f(inp["q_norm_g"]), g_k=f(inp["k_norm_g"]), g_mq=f(inp["mem_q_norm_g"]), g_mk=f(inp["mem_k_norm_g"]),
        dww=f(inp["dw_w"][0].reshape(31, CC, 128).transpose(2, 1, 0)),
        dwb=f(inp["dw_b"][0].reshape(CC, 128).T), lng=f(inp["conv_ln_g"][0].reshape(CC, 128).T),
        lnb=f(inp["conv_ln_b"][0].reshape(CC, 128).T),
        vtab=f(inp["peer_v"][0]),
        c_idb=np.eye(128).astype(ml_dtypes.bfloat16), c_idf=np.eye(128, dtype=np.float32),
        c_oneb=np.ones((128, 128)).astype(ml_dtypes.bfloat16), c_onef=np.ones((128, 128), dtype=np.float32),
    )
    sk = np.stack([inp["peer_sub_k1"][0], inp["peer_sub_k2"][0]], axis=1)
    shared["subk"] = f(sk.reshape(16, 128, 128).transpose(2, 0, 1))
    u = inp["peer_u"][0]
    shared["uT"] = f(u.reshape(128, 128, KC, 128).transpose(0, 3, 2, 1).reshape(128, 128, KC * 128))
    kcs = (np.arange(SS) // 64).astype(np.float32)
    kcs[PAST + DS:] = 1.0e9
    shared["kc_s"] = kcs[None, :]
    shared["rope_s"] = _rope_table(PAST + np.arange(128))
    maps = []
    for c in range(8):
        b, hf = c // 2, c % 2
        xb = inp["x_prompt"][b]
        own = xb[hf * half:(hf + 1) * half]
        oth = xb[(1 - hf) * half:(2 - hf) * half]
        pos = np.concatenate([hf * half + np.arange(half), (1 - hf) * half + np.arange(half)])
        m = dict(shared)
        m["xctx"] = f(np.concatenate([own, oth], axis=0))
        m["xhalo"] = f(xb[half - 128:half]) if hf == 1 else np.zeros((128, D), np.float32)
        xsp = np.zeros((2, 128, D), np.float32)
        for s in range(2):
            xsp[s, :DS] = inp["x_sample"][2 * c + s]
        m["xsp"] = xsp
        m["mem"] = f(inp["mem_prompt"][b])
        m["ckT"] = f(np.stack([inp["cache_k"][0, 2 * c + s].transpose(2, 1, 0) for s in range(2)]))
        m["cv"] = f(np.stack([inp["cache_v"][0, 2 * c + s].reshape(PAST, NKV * 128) for s in range(2)]))
        m["ckiT"] = f(np.stack([inp["cache_k_idx"][0, 2 * c + s].T for s in range(2)]))
        m["stT"] = f(np.stack([inp["state_conv"][0, 2 * c + s].T for s in range(2)]))
        m["cmkT"] = f(np.stack([inp["cache_mem_k"][0, 2 * c + s].transpose(2, 1, 0) for s in range(2)]))
        m["cmv"] = f(np.stack([inp["cache_mem_v"][0, 2 * c + s].reshape(MEMT, 512) for s in range(2)]))
        m["rope_c"] = _rope_table(pos)
        m["kc_p"] = (pos // 64).astype(np.float32)[None, :]
        q = np.zeros((128, NT), np.float32)
        for i in range(NP):
            q[:, i] = (hf * half + i * 128 + np.arange(128)) // 64
        q[:, NP:] = PAST // 64
        m["qch"] = q
        maps.append(m)
    return maps


def assemble(res, cfg):
    D, CCH, NKV, NP, SEQ, DS, MEMT, B, DB = (cfg[k] for k in ("D", "CCH", "NKV", "NP", "SEQ", "DS", "MEMT", "B", "DB"))
    half = SEQ // 2
    y_p = np.zeros((B, SEQ, D), np.float32)
    y_s = np.zeros((DB, DS, D), np.float32)
    k_p = np.zeros((1, B, SEQ, NKV, 128), np.float32)
    v_p = np.zeros_like(k_p)
    ki_p = np.zeros((1, B, SEQ, 128), np.float32)
    conv_p = np.zeros((1, B, 30, CCH), np.float32)
    mk_p = np.zeros((1, B, MEMT, 4, 128), np.float32)
    mv_p = np.zeros_like(mk_p)
    k_s = np.zeros((1, DB, DS, NKV, 128), np.float32)
    v_s = np.zeros_like(k_s)
    ki_s = np.zeros((1, DB, DS, 128), np.float32)
    conv_s = np.zeros((1, DB, 30, CCH), np.float32)
    for c in range(8):
        r = res[c]
        b, hf = c // 2, c % 2
        y_p[b, hf * half:(hf + 1) * half] = r["y"][:NP * 128]
        if hf == 0:
            k_p[0, b] = r["o_k"].reshape(SEQ, NKV, 128)
            v_p[0, b] = r["o_v"].reshape(SEQ, NKV, 128)
            ki_p[0, b] = r["o_ki"]
            mk_p[0, b] = r["o_mk"].reshape(MEMT, 4, 128)
            mv_p[0, b] = r["o_mv"].reshape(MEMT, 4, 128)
        else:
            conv_p[0, b] = r["o_conv"]
        for s in range(2):
            q = 2 * c + s
            y_s[q] = r["y"][(NP + s) * 128:(NP + s) * 128 + DS]
            k_s[0, q] = r["o_ks"][s, :DS].reshape(DS, NKV, 128)
            v_s[0, q] = r["o_vs"][s, :DS].reshape(DS, NKV, 128)
            ki_s[0, q] = r["o_kis"][s, :DS]
            conv_s[0, q] = r["o_convs"][s]
    return (y_p, y_s, k_p, v_p, ki_p, conv_p, mk_p, mv_p, k_s, v_s, ki_s, conv_s)


def kernel(**inputs):
    cfg = mkcfg()
    inp = {k: np.asarray(v) for k, v in inputs.items()}
    maps = host_prep(inp, cfg)
    nc = build(cfg)
    res = run_bass_kernel_spmd(nc, maps, core_ids=list(range(8)))
    return assemble(res.results, cfg)
```

```python
import contextlib
import math
import numpy as np
import ml_dtypes
import concourse.bass as bass
import concourse.mybir as mybir
from concourse.bass_utils import run_bass_kernel_spmd

F32 = mybir.dt.float32
BF16 = mybir.dt.bfloat16
ALU = mybir.AluOpType
AF = mybir.ActivationFunctionType
AX = mybir.AxisListType

EPS = 1e-6
ROPE_THETA = 500000.0
NEG = -1.0e30


class Prog:
    def __init__(self, nc, es):
        self.nc = nc
        self.es = es
        self.st = es
        self.ops = []
        self.engs = {"pe": nc.tensor, "act": nc.scalar, "dve": nc.vector, "pool": nc.gpsimd, "sp": nc.sync}
        self.n_t = 0
        self.eng_sem = {}
        self.eng_cnt = {}
        self.pool = {}
        self.npool = {}
        self.key_sem = {}
        self.fence_sem = None
        self.fence_cnt = 0
        self.tot_ops = 0
        self.tot_wait = 0
        self.free_sems = []
        self.n_dsem = 0

    def sb(self, shape, dt=F32, name=None):
        self.n_t += 1
        return self.st.enter_context(self.nc.sbuf_tensor(name or f"sb{self.n_t}", list(shape), dt))

    def ps(self, shape=(128, 512), dt=F32, name=None):
        self.n_t += 1
        return self.st.enter_context(self.nc.psum_tensor(name or f"ps{self.n_t}", list(shape), dt))

    @staticmethod
    def key(x):
        def nm(a):
            if isinstance(a, str):
                return a
            t = getattr(a, "tensor", None)
            return t.name if t is not None else a.name
        if isinstance(x, tuple):
            return (nm(x[0]), x[1])
        return (nm(x), None)

    def op(self, eng, fn, reads=(), writes=(), dma=False):
        rk = []
        for r in reads:
            if r is None or isinstance(r, (int, float)):
                continue
            k = self.key(r)
            if k not in rk:
                rk.append(k)
        wk = []
        for w in writes:
            k = self.key(w)
            if k not in wk:
                wk.append(k)
        self.ops.append(dict(eng=eng, fn=fn, reads=rk, writes=wk, dma=dma))

    def _esem(self, e):
        if e not in self.eng_sem:
            self.eng_sem[e] = self.es.enter_context(self.nc.semaphore(f"s_{e}"))
            self.eng_cnt[e] = 0
        return self.eng_sem[e]

    def flush(self):
        nc = self.nc
        ops = self.ops
        state = {}
        deps = [None] * len(ops)

        def confl(k):
            ent = state.get(k[0])
            if not ent:
                return []
            if k[1] is None:
                return list(ent.values())
            return [ent[s_] for s_ in (k[1], None) if s_ in ent]

        joined = [False] * len(ops)
        for i, o in enumerate(ops):
            d = set()
            for k in o["reads"]:
                for st in confl(k):
                    d.update(st[0])
            joins = {}
            for k in o["writes"]:
                own = state.get(k[0], {}).get(k[1])
                joinable = bool(o["dma"] and own and own[0] and all(ops[j]["dma"] for j in own[0]) and not own[1])
                joins[k] = joinable
                for st in confl(k):
                    d.update(st[1])
                    if not (joinable and st is own):
                        d.update(st[0])
            if o["dma"]:
                joined[i] = joins[o["writes"][0]]
            for k in o["reads"]:
                st = state.setdefault(k[0], {}).setdefault(k[1], [[], []])
                st[1].append(i)
            for k in o["writes"]:
                ent = state.setdefault(k[0], {})
                if joins[k]:
                    ent[k[1]][0].append(i)
                else:
                    if k[1] is None:
                        ent.clear()
                    ent[k[1]] = [[i], []]
            d.discard(i)
            if o["eng"] == "pe":
                d = {j for j in d if not (ops[j]["eng"] == "pe" and not ops[j]["dma"])}
            deps[i] = d
        need = [False] * len(ops)
        for d in deps:
            for j in d:
                need[j] = True
        last_on = {}
        for i, o in enumerate(ops):
            if not o["dma"]:
                last_on[o["eng"]] = i
        for i in last_on.values():
            need[i] = True

        sig = [None] * len(ops)
        waited = {}
        for i, o in enumerate(ops):
            e = o["eng"]
            eo = self.engs[e]
            wl = {}
            for j in deps[i]:
                s, v = sig[j]
                kk = id(s)
                if kk not in wl or wl[kk][1] < v:
                    wl[kk] = (s, v)
            pre = None
            if o["dma"]:
                k = o["writes"][0]
                name = k[0]
                pl = self.pool.get(name)
                if pl is None:
                    n = self.npool.get(name, 2)
                    sems_, cnt_ = [], []
                    for q in range(n):
                        if self.free_sems:
                            s_, c_ = self.free_sems.pop()
                        else:
                            self.n_dsem += 1
                            s_, c_ = self.es.enter_context(nc.semaphore(f"dma{self.n_dsem}")), 0
                        sems_.append(s_)
                        cnt_.append(c_)
                    pl = dict(sems=sems_, cnt=cnt_, last=[None] * n, rr=0)
                    self.pool[name] = pl
                idx = None
                if joined[i] and k in self.key_sem and pl["last"][self.key_sem[k]] == k:
                    idx = self.key_sem[k]
                else:
                    idx = pl["rr"]
                    pl["rr"] = (pl["rr"] + 1) % len(pl["sems"])
                    if pl["cnt"][idx] > 0:
                        s = pl["sems"][idx]
                        kk = id(s)
                        if kk not in wl or wl[kk][1] < pl["cnt"][idx]:
                            wl[kk] = (s, pl["cnt"][idx])
                self.key_sem[k] = idx
                pl["last"][idx] = k
                pre = (pl, idx)
            for kk, (s, v) in wl.items():
                if waited.get((e, kk), -1) >= v:
                    continue
                waited[(e, kk)] = v
                eo.wait_ge(s, v)
                self.tot_wait += 1
            ins = o["fn"](eo)
            if o["dma"]:
                pl, idx = pre
                pl["cnt"][idx] += 16
                ins.then_inc(pl["sems"][idx], 16)
                sig[i] = (pl["sems"][idx], pl["cnt"][idx])
            elif need[i]:
                s = self._esem(e)
                self.eng_cnt[e] += 1
                ins.then_inc(s, 1)
                sig[i] = (s, self.eng_cnt[e])
        self.tot_ops += len(ops)
        self.ops = []
        if self.fence_sem is None:
            self.fence_sem = self.es.enter_context(nc.semaphore("fence"))
        for e, s in self.eng_sem.items():
            if self.eng_cnt[e] > 0:
                nc.sync.wait_ge(s, self.eng_cnt[e])
        for pl in self.pool.values():
            for s, c in zip(pl["sems"], pl["cnt"]):
                if c > 0:
                    nc.sync.wait_ge(s, c)
        for pl in self.pool.values():
            for s, c in zip(pl["sems"], pl["cnt"]):
                self.free_sems.append((s, c))
        self.pool = {}
        self.key_sem = {}
        self.fence_cnt += 1
        nc.sync.drain().then_inc(self.fence_sem, 1)
        for e in ("pe", "act", "dve", "pool"):
            self.engs[e].wait_ge(self.fence_sem, self.fence_cnt)

    def dma(self, q, out, in_, okey=None, ikey=None, **kw):
        self.op(q, lambda e: e.dma_start(out=out, in_=in_, **kw), reads=[ikey or in_], writes=[okey or out], dma=True)

    def cdma(self, out, in_, okey=None, ikey=None):
        n = out.shape[-1]
        if n > 2048:
            d = 2048
            while n % d:
                d //= 2
            names = " ".join(f"a{i}" for i in range(len(out.shape) - 1))
            pat = f"{names} (x d) -> {names} x d"
            self.dma("pool", out.rearrange(pat, d=d), in_.rearrange(pat, d=d), okey=okey or out, ikey=ikey or in_)
        else:
            self.dma("pool", out, in_, okey=okey, ikey=ikey)

    def mm(self, out, lhsT, rhs, start=True, stop=True, okey=None, rkeys=None):
        self.op("pe", lambda e: e.matmul(out, lhsT, rhs, start=start, stop=stop), reads=rkeys or [lhsT, rhs], writes=[okey or out])

    def tr(self, out, in_, ident, okey=None, ikey=None):
        self.op("pe", lambda e: e.transpose(out, in_, ident), reads=[ikey or in_, ident], writes=[okey or out])

    def act(self, out, in_, func, scale=1.0, bias=0.0, accum_out=None, okey=None, ikey=None):
        rd = [ikey or in_] + [x for x in (scale, bias) if not isinstance(x, (int, float))]
        wr = [okey or out] + ([accum_out] if accum_out is not None else [])
        if accum_out is not None:
            self.op("act", lambda e: e.activation(out, in_, func, bias=bias, scale=scale, accum_out=accum_out), reads=rd, writes=wr)
        else:
            self.op("act", lambda e: e.activation(out, in_, func, bias=bias, scale=scale), reads=rd, writes=wr)

    def ts(self, eng, out, in0, s1, s2=None, op0=ALU.mult, op1=None, accum_out=None, okey=None, ikey=None):
        rd = [ikey or in0] + [x for x in (s1, s2) if x is not None and not isinstance(x, (int, float))]
        wr = [okey or out] + ([accum_out] if accum_out is not None else [])
        kw = {}
        if op1 is not None:
            kw["op1"] = op1
        if accum_out is not None:
            kw["accum_out"] = accum_out
        self.op(eng, lambda e: e.tensor_scalar(out, in0, s1, s2, op0, **kw), reads=rd, writes=wr)

    def tt(self, eng, out, in0, in1, op, okey=None, rkeys=None):
        self.op(eng, lambda e: e.tensor_tensor(out, in0, in1, op), reads=rkeys or [in0, in1], writes=[okey or out])

    def stt(self, out, in0, scalar, in1, op0, op1, okey=None, rkeys=None):
        rd = list(rkeys or [in0, in1]) + ([scalar] if not isinstance(scalar, (int, float)) else [])
        self.op("dve", lambda e: e.scalar_tensor_tensor(out, in0, scalar, in1, op0, op1), reads=rd, writes=[okey or out])

    def copy(self, eng, out, in_, okey=None, ikey=None):
        if eng == "act":
            self.op("act", lambda e: e.copy(out, in_), reads=[ikey or in_], writes=[okey or out])
        else:
            self.op(eng, lambda e: e.tensor_copy(out, in_), reads=[ikey or in_], writes=[okey or out])

    def max8(self, out, in_, okey=None):
        self.op("dve", lambda e: e.max(out, in_), reads=[in_], writes=[okey or out])

    def mrep(self, out, in_to_replace, in_values, imm, rkeys=None):
        self.op("dve", lambda e: e.match_replace(out, in_to_replace, in_values, imm), reads=rkeys or [in_to_replace, in_values], writes=[out])

    def memset(self, eng, ap, val):
        self.op(eng, lambda e: e.memset(ap, val), reads=[], writes=[ap])

    def recip(self, out, in_, okey=None):
        self.op("dve", lambda e: e.reciprocal(out, in_), reads=[in_], writes=[okey or out])

    def reduce(self, out, in_, op, axis=AX.X):
        self.op("dve", lambda e: e.tensor_reduce(out, in_, axis, op), reads=[in_], writes=[out])


def mkcfg(D=4096, SEQ=4096, B=4, DB=16, DS=64, PAST=4096, IH=32, TOPK_MAX=256, MEMT=256):
    c = dict(D=D, SEQ=SEQ, B=B, DB=DB, DS=DS, PAST=PAST, IH=IH, MEMT=MEMT)
    c["KC"] = D // 128
    c["CCH"] = D // 2
    c["CC"] = c["CCH"] // 128
    c["NH"] = (D // 2) // 128
    c["NKV"] = 4
    c["GQ"] = c["NH"] // 4
    c["NP"] = SEQ // 2 // 128
    c["NCX"] = SEQ // 128
    c["NT"] = c["NP"] + 2
    c["TOPK_P"] = min(TOPK_MAX, SEQ // 4)
    c["TOPK_S"] = min(TOPK_MAX, (PAST + DS) // 4)
    c["SS"] = PAST + 128
    c["MH"] = 4
    c["MC"] = MEMT // 128
    return c


PEER_KEYS = 128
PEER_HEADS = 8
PEER_TOPK = 16


def build(cfg, stages=("M", "KV", "MAIN", "B", "C", "D", "E"), dbg=False):
    D, KC, CCH, CC, NH, NKV, GQ, NP, NCX, NT, IH, SEQ, PAST, SS, MEMT, MC = (cfg[k] for k in (
        "D", "KC", "CCH", "CC", "NH", "NKV", "GQ", "NP", "NCX", "NT", "IH", "SEQ", "PAST", "SS", "MEMT", "MC"))
    NTOK = NT * 128
    IDX_SCALE = (IH ** -0.5) * (128 ** -0.5)
    ATT_SCALE = 128 ** -0.5
    nc = bass.Bass("TRN2", target_bir_lowering=False)

    def din(name, shape, dt=F32):
        return nc.dram_tensor(name, list(shape), dt, kind="ExternalInput").ap()

    def dout(name, shape, dt=F32):
        return nc.dram_tensor(name, list(shape), dt, kind="ExternalOutput").ap()

    def dscr(name, shape, dt=F32):
        return nc.dram_tensor(name, list(shape), dt, kind="Internal").ap()

    xctx = din("xctx", [SEQ, D])
    xsp = din("xsp", [2, 128, D])
    xhalo = din("xhalo", [128, D])
    mem = din("mem", [MEMT, D])
    ckT = din("ckT", [2, 128, NKV, PAST])
    cv = din("cv", [2, PAST, NKV * 128])
    ckiT = din("ckiT", [2, 128, PAST])
    stT = din("stT", [2, CCH, 30])
    cmkT = din("cmkT", [2, 128, 4, MEMT])
    cmv = din("cmv", [2, MEMT, 512])
    w_glu = din("w_glu", [D, 2 * CCH])
    w_q = din("w_q", [D, NH * 128])
    w_qi = din("w_qi", [D, IH * 128])
    w_wi = din("w_wi", [D, IH])
    w_kv = din("w_kv", [D, 1152])
    w_out = din("w_out", [D, D])
    w_qm = din("w_qm", [D, 512])
    w_km = din("w_km", [D, 512])
    w_vm = din("w_vm", [D, 512])
    w_om = din("w_om", [512, D])
    w_pq = din("w_pq", [D, 2048])
    subk = din("subk", [128, 16, 128])
    uT = din("uT", [128, 128, KC * 128])
    vtab = din("vtab", [PEER_KEYS * PEER_KEYS, D])
    g_mix = din("g_mix", [1, D])
    g_memn = din("g_memn", [1, D])
    g_ffn = din("g_ffn", [1, D])
    g_mem = din("g_mem", [1, D])
    g_q = din("g_q", [1, 128])
    g_k = din("g_k", [1, 128])
    g_mq = din("g_mq", [1, 128])
    g_mk = din("g_mk", [1, 128])
    dww = din("dww", [128, CC, 31])
    dwb = din("dwb", [128, CC])
    lng = din("lng", [128, CC])
    lnb = din("lnb", [128, CC])
    rope_c = din("rope_c", [SEQ, 32])
    rope_s = din("rope_s", [128, 32])
    kc_p = din("kc_p", [1, SEQ])
    kc_s = din("kc_s", [1, SS])
    qch = din("qch", [128, NT])
    c_idb = din("c_idb", [128, 128], BF16)
    c_idf = din("c_idf", [128, 128])
    c_oneb = din("c_oneb", [128, 128], BF16)
    c_onef = din("c_onef", [128, 128])
    c_iota = din("c_iota", [128, 128])

    y = dout("y", [NTOK, D])
    o_k = dout("o_k", [SEQ, NKV * 128])
    o_v = dout("o_v", [SEQ, NKV * 128])
    o_ki = dout("o_ki", [SEQ, 128])
    o_conv = dout("o_conv", [30, CCH])
    o_mk = dout("o_mk", [MEMT, 512])
    o_mv = dout("o_mv", [MEMT, 512])
    o_ks = dout("o_ks", [2, 128, NKV * 128])
    o_vs = dout("o_vs", [2, 128, NKV * 128])
    o_kis = dout("o_kis", [2, 128, 128])
    o_convs = dout("o_convs", [2, 30, CCH])

    UTp = dscr("UTp", [CCH, 128 + NP * 128])
    UTs = dscr("UTs", [2, CCH, 160])
    KT = dscr("KT", [128, NKV, SEQ], BF16)
    Vc = dscr("Vc", [SEQ, NKV * 128], BF16)
    KIT = dscr("KIT", [128, SEQ], BF16)
    KTs = dscr("KTs", [2, 128, NKV, 128], BF16)
    Vs = dscr("Vs", [2, 128, NKV * 128], BF16)
    KITs = dscr("KITs", [2, 128, 128], BF16)
    MKT = dscr("MKT", [128, 4, MEMT], BF16)
    MV = dscr("MV", [MEMT, 512], BF16)
    QT = dscr("QT", [NT, 128, NH, 128], BF16)
    QIT = dscr("QIT", [NT, 128, IH, 128], BF16)
    WI = dscr("WI", [NT, 128, IH])
    MIXT = dscr("MIXT", [D, NTOK], BF16)
    H2 = dscr("H2", [NTOK, D])
    HN2T = dscr("HN2T", [D, NTOK], BF16)
    S12 = dscr("S12", [16, NTOK, 128])
    GALL = dscr("GALL", [128, 128, NTOK], BF16)

    UB16 = dscr("UB16", [128, 128, KC * 128], BF16)
    VB16 = dscr("VB16", [PEER_KEYS * PEER_KEYS, D], BF16)
    dbg_out = {}

    with contextlib.ExitStack() as es:
        P = Prog(nc, es)
        idb = P.sb([128, 128], BF16, "idb")
        idf = P.sb([128, 128], F32, "idf")
        oneb = P.sb([128, 128], BF16, "oneb")
        onef = P.sb([128, 128], F32, "onef")
        P.dma("sp", idb[:], c_idb)
        P.dma("sp", idf[:], c_idf)
        P.dma("sp", oneb[:], c_oneb)
        P.dma("sp", onef[:], c_onef)
        P.flush()

        def bcast_row(dst, src_row, n):
            P.dma("sp", dst, src_row.to_broadcast([128, n]))

        def rstd_from_ss(ss, n, out, tmp):
            P.ts("dve", tmp, ss, 1.0 / n, EPS, op0=ALU.mult, op1=ALU.add)
            P.act(tmp, tmp, AF.Sqrt)
            P.recip(out, tmp)

        def load_w(dst, src, ncols):
            sv = src.rearrange("(c p) n -> p c n", p=128)
            nq = 4 if KC % 4 == 0 else 1
            step = KC // nq
            for q in range(nq):
                P.dma("pool", dst[:, q * step:(q + 1) * step, 0:ncols], sv[:, q * step:(q + 1) * step, :], okey=(dst, q))

        def wkeys(dst):
            return [(dst, q) for q in range(4 if KC % 4 == 0 else 1)]

        class NormT:
            def __init__(self, gsrc, npt=2):
                self.npt = npt
                self.gbc = P.sb([128, D], F32)
                bcast_row(self.gbc[:], gsrc[0:1, :], D)
                self.sq = P.sb([128, D], BF16)
                self.xs = [P.sb([128, D], BF16) for _ in range(2)]
                self.sm = [P.sb([128, 4], F32) for _ in range(2)]
                self.pt = [P.ps([128, 1024], BF16) for _ in range(npt)]
                self.k = 0

            def run(self, x_t, dst_fn):
                k = self.k
                self.k += 1
                sm = self.sm[k % 2]
                xs = self.xs[k % 2]
                P.act(self.sq[:], x_t, AF.Square, accum_out=sm[:, 0:1])
                rstd_from_ss(sm[:, 0:1], D, sm[:, 1:2], sm[:, 2:3])
                P.stt(xs[:], x_t, sm[:, 1:2], self.gbc[:], ALU.mult, ALU.mult)
                nb = 8 if KC % 8 == 0 else KC
                for b0 in range(0, KC, nb):
                    pt = self.pt[(b0 // nb) % self.npt]
                    for j in range(nb):
                        P.tr(pt[:, j * 128:(j + 1) * 128], xs[:, (b0 + j) * 128:(b0 + j + 1) * 128], idb[:])
                    eng = "act" if (b0 // nb) % 2 == 0 else "dve"
                    P.copy(eng, dst_fn(b0, nb), pt[:, 0:nb * 128].rearrange("p (n t) -> p n t", t=128))

        def head_norm(ps_ap, nh, gain_bc, out_f, sq_t, sm_t):
            P.act(sq_t[:, 0:nh * 128], ps_ap, AF.Square)
            P.reduce(sm_t[:, 0:nh], sq_t[:, 0:nh * 128].rearrange("p (h d) -> p h d", d=128), ALU.add)
            rstd_from_ss(sm_t[:, 0:nh], 128, sm_t[:, 4:4 + nh], sm_t[:, 8:8 + nh])
            P.tt("dve", out_f, ps_ap.rearrange("p (h d) -> p h d", d=128),
                 sm_t[:, 4:4 + nh].unsqueeze(2).to_broadcast([128, nh, 128]), ALU.mult)
            P.tt("dve", out_f, out_f, gain_bc[:, 0:128].unsqueeze(1).to_broadcast([128, nh, 128]), ALU.mult)

        def rope(f, nh, cs, tmp):
            x1 = f[:, :, 0:16]
            x2 = f[:, :, 16:32]
            cosb = cs[:, 0:16].unsqueeze(1).to_broadcast([128, nh, 16])
            sinb = cs[:, 16:32].unsqueeze(1).to_broadcast([128, nh, 16])
            P.tt("dve", tmp[:, 0, 0:nh, :], x1, cosb, ALU.mult)
            P.tt("dve", tmp[:, 1, 0:nh, :], x2, sinb, ALU.mult)
            P.tt("dve", tmp[:, 2, 0:nh, :], x2, cosb, ALU.mult)
            P.tt("dve", tmp[:, 3, 0:nh, :], x1, sinb, ALU.mult)
            P.tt("dve", x1, tmp[:, 0, 0:nh, :], tmp[:, 1, 0:nh, :], ALU.subtract)
            P.tt("dve", x2, tmp[:, 2, 0:nh, :], tmp[:, 3, 0:nh, :], ALU.add)

        def tok_mm(ps_ap, hnT, tcol, wb, ncols, wk):
            for c in range(KC):
                P.mm(ps_ap, hnT[:, c, tcol:tcol + 128], wb[:, c, 0:ncols], start=(c == 0), stop=(c == KC - 1),
                     rkeys=[hnT] + wk)

        if "M" in stages:
            with contextlib.ExitStack() as st:
                P.st = st
                nt = NormT(g_mem)
                wk_b = P.sb([128, KC, 512], BF16)
                wv_b = P.sb([128, KC, 512], BF16)
                load_w(wk_b, w_km, 512)
                load_w(wv_b, w_vm, 512)
                gk = P.sb([128, 128], F32)
                bcast_row(gk[:], g_mk[0:1, :], 128)
                xt = [P.sb([128, D], F32) for _ in range(2)]
                hn = [P.sb([128, KC, 128], BF16) for _ in range(2)]
                pk = P.ps()
                pv = P.ps()
                ptr = P.ps([128, 1024], BF16)
                sq_t = P.sb([128, 512], F32)
                sm_t = P.sb([128, 12], F32)
                kf = P.sb([128, 4, 128], F32)
                kb = P.sb([128, 512], BF16)
                kTt = P.sb([128, 4, 128], BF16)
                vf = P.sb([128, 512], F32)
                vb = P.sb([128, 512], BF16)
                for m in range(MC):
                    x_t = xt[m % 2]
                    h_t = hn[m % 2]
                    P.dma("sp", x_t[:], mem[m * 128:(m + 1) * 128, :])
                    nt.run(x_t[:], lambda c0, n, h_t=h_t: h_t[:, c0:c0 + n, :])
                    tok_mm(pk[:, 0:512], h_t, 0, wk_b, 512, wkeys(wk_b))
                    tok_mm(pv[:, 0:512], h_t, 0, wv_b, 512, wkeys(wv_b))
                    head_norm(pk[:, 0:512], 4, gk, kf[:], sq_t, sm_t)
                    P.dma("sp", o_mk[m * 128:(m + 1) * 128, :], kf[:].rearrange("p h d -> p (h d)"))
                    P.copy("act", kb[:], kf[:].rearrange("p h d -> p (h d)"))
                    for h in range(4):
                        P.tr(ptr[:, h * 128:(h + 1) * 128], kb[:, h * 128:(h + 1) * 128], idb[:])
                    P.copy("dve", kTt[:], ptr[:, 0:512].rearrange("p (h t) -> p h t", t=128))
                    P.dma("sp", MKT[:, :, m * 128:(m + 1) * 128], kTt[:])
                    P.copy("act", vf[:], pv[:, 0:512])
                    P.dma("sp", o_mv[m * 128:(m + 1) * 128, :], vf[:])
                    P.copy("dve", vb[:], pv[:, 0:512])
                    P.dma("sp", MV[m * 128:(m + 1) * 128, :], vb[:])
                P.flush()
            P.st = es

        if "KV" in stages:
            with contextlib.ExitStack() as st:
                P.st = st
                nt = NormT(g_mix, npt=1)
                wb = P.sb([128, KC, 1152], BF16)
                load_w(wb, w_kv, 1152)
                wk = wkeys(wb)
                gk = P.sb([128, 128], F32)
                bcast_row(gk[:], g_k[0:1, :], 128)
                xt = [P.sb([128, D], F32) for _ in range(2)]
                hn = [P.sb([128, KC, 128], BF16) for _ in range(2)]
                cs = [P.sb([128, 32], F32) for _ in range(2)]
                pk2 = [P.ps() for _ in range(2)]
                pv2 = [P.ps() for _ in range(2)]
                pki2 = [P.ps() for _ in range(2)]
                ptr = P.ps([128, 1024], BF16)
                sq_t = P.sb([128, 512], F32)
                sm_t = P.sb([128, 12], F32)
                rtmp = P.sb([128, 4, 4, 16], F32)
                kf = [P.sb([128, 4, 128], F32) for _ in range(2)]
                kb = P.sb([128, 512], BF16)
                kTt = [P.sb([128, 4, 128], BF16) for _ in range(2)]
                vf = [P.sb([128, 512], F32) for _ in range(2)]
                vb = [P.sb([128, 512], BF16) for _ in range(2)]
                kif = [P.sb([128, 1, 128], F32) for _ in range(2)]
                kib = P.sb([128, 128], BF16)
                kiTt = [P.sb([128, 128], BF16) for _ in range(2)]
                tiles = [("p", i) for i in range(NCX)] + [("s", 0), ("s", 1)]
                def kv_front(n_):
                    kind, i = tiles[n_]
                    pk, pv, pki = pk2[n_ % 2], pv2[n_ % 2], pki2[n_ % 2]
                    x_t = xt[n_ % 2]
                    h_t = hn[n_ % 2]
                    c_t = cs[n_ % 2]
                    if kind == "p":
                        P.dma("sp", x_t[:], xctx[i * 128:(i + 1) * 128, :])
                        P.dma("sp", c_t[:], rope_c[i * 128:(i + 1) * 128, :])
                    else:
                        P.dma("sp", x_t[:], xsp[i])
                        P.dma("sp", c_t[:], rope_s)
                    nt.run(x_t[:], lambda c0, n, h_t=h_t: h_t[:, c0:c0 + n, :])
                    for c in range(KC):
                        P.mm(pk[:, 0:512], h_t[:, c, :], wb[:, c, 0:512], start=(c == 0), stop=(c == KC - 1), rkeys=[h_t] + wk)
                    for c in range(KC):
                        P.mm(pv[:, 0:512], h_t[:, c, :], wb[:, c, 512:1024], start=(c == 0), stop=(c == KC - 1), rkeys=[h_t] + wk)
                    for c in range(KC):
                        P.mm(pki[:, 0:128], h_t[:, c, :], wb[:, c, 1024:1152], start=(c == 0), stop=(c == KC - 1), rkeys=[h_t] + wk)

                def kv_back(n_):
                    kind, i = tiles[n_]
                    pk, pv, pki = pk2[n_ % 2], pv2[n_ % 2], pki2[n_ % 2]
                    c_t = cs[n_ % 2]
                    kf_t = kf[n_ % 2]
                    head_norm(pk[:, 0:512], 4, gk, kf_t[:], sq_t, sm_t)
                    rope(kf_t, 4, c_t, rtmp)
                    kflat = kf_t[:].rearrange("p h d -> p (h d)")
                    if kind == "p":
                        P.dma("sp", o_k[i * 128:(i + 1) * 128, :], kflat)
                    else:
                        P.dma("sp", o_ks[i], kflat)
                    P.copy("act", kb[:], kflat)
                    for h in range(4):
                        P.tr(ptr[:, h * 128:(h + 1) * 128], kb[:, h * 128:(h + 1) * 128], idb[:])
                    kT_t = kTt[n_ % 2]
                    P.copy("dve", kT_t[:], ptr[:, 0:512].rearrange("p (h t) -> p h t", t=128))
                    if kind == "p":
                        P.dma("sp", KT[:, :, i * 128:(i + 1) * 128], kT_t[:])
                    else:
                        P.dma("sp", KTs[i], kT_t[:])
                    vf_t = vf[n_ % 2]
                    vb_t = vb[n_ % 2]
                    P.copy("act", vf_t[:], pv[:, 0:512])
                    P.copy("dve", vb_t[:], pv[:, 0:512])
                    if kind == "p":
                        P.dma("sp", o_v[i * 128:(i + 1) * 128, :], vf_t[:])
                        P.dma("sp", Vc[i * 128:(i + 1) * 128, :], vb_t[:])
                    else:
                        P.dma("sp", o_vs[i], vf_t[:])
                        P.dma("sp", Vs[i], vb_t[:])
                    ki_t = kif[n_ % 2]
                    P.copy("act", ki_t[:, 0, :], pki[:, 0:128])
                    rope(ki_t, 1, c_t, rtmp)
                    if kind == "p":
                        P.dma("sp", o_ki[i * 128:(i + 1) * 128, :], ki_t[:, 0, :])
                    else:
                        P.dma("sp", o_kis[i], ki_t[:, 0, :])
                    P.copy("act", kib[:], ki_t[:, 0, :])
                    P.tr(ptr[:, 512:640], kib[:], idb[:])
                    kiT_t = kiTt[n_ % 2]
                    P.copy("dve", kiT_t[:], ptr[:, 512:640])
                    if kind == "p":
                        P.dma("sp", KIT[:, i * 128:(i + 1) * 128], kiT_t[:])
                    else:
                        P.dma("sp", KITs[i], kiT_t[:])
                kv_front(0)
                for n_ in range(len(tiles)):
                    if n_ + 1 < len(tiles):
                        kv_front(n_ + 1)
                    kv_back(n_)
                P.flush()
            P.st = es

        own = [("h", -1)] + [("p", i) for i in range(NP)] + [("s", 0), ("s", 1)]
        groups = [own[i:i + 4] for i in range(0, len(own), 4)]

        def tile_index(kind, i):
            return i if kind == "p" else NP + i

        if "MAIN" in stages:
            with contextlib.ExitStack() as st:
                P.st = st
                nt = NormT(g_mix)
                gq = P.sb([128, 128], F32)
                bcast_row(gq[:], g_q[0:1, :], 128)
                for s in range(2):
                    P.dma("sp", UTs[s][:, 2:32], stT[s], okey=("UTs", "st"))
                xt = [P.sb([128, D], F32) for _ in range(2)]
                hnT = P.sb([128, KC, 512], BF16)
                wbuf = [P.sb([128, KC, 512], BF16) for _ in range(2)]
                cst = P.sb([128, 4, 32], F32)
                pa = P.ps()
                pg = P.ps()
                pq = [P.ps() for _ in range(2)]
                ptr = P.ps([128, 1024], BF16)
                sg = P.sb([128, 512], F32)
                ut = [P.sb([128, 512], F32) for _ in range(2)]
                sq_t = P.sb([128, 512], F32)
                sm_t = P.sb([128, 12], F32)
                rtmp = P.sb([128, 4, 4, 16], F32)
                qf = P.sb([128, 4, 128], F32)
                qb = P.sb([128, 512], BF16)
                qTt = [P.sb([128, 4, 128], BF16) for _ in range(2)]
                wis = [P.sb([128, IH], F32) for _ in range(2)]
                wcnt = 0
                xcnt = 0
                for grp in groups:
                    ng = len(grp)
                    N = ng * 128
                    for tt, (kind, i) in enumerate(grp):
                        x_t = xt[xcnt % 2]
                        xcnt += 1
                        if kind == "h":
                            P.dma("sp", x_t[:], xhalo)
                        elif kind == "p":
                            P.dma("sp", x_t[:], xctx[i * 128:(i + 1) * 128, :])
                            P.dma("sp", cst[:, tt, :], rope_c[i * 128:(i + 1) * 128, :], okey=(cst, tt))
                        else:
                            P.dma("sp", x_t[:], xsp[i])
                            P.dma("sp", cst[:, tt, :], rope_s, okey=(cst, tt))
                        nt.run(x_t[:], lambda c0, n, tt=tt: hnT[:, c0:c0 + n, tt * 128:(tt + 1) * 128])
                    for b in range(CC // 2):
                        wb = wbuf[wcnt % 2]
                        wcnt += 1
                        load_w(wb, w_glu[:, b * 512:(b + 1) * 512], 512)
                        wk = wkeys(wb)
                        for s in range(2):
                            j = 2 * b + s
                            for c in range(KC):
                                P.mm(pa[:, 0:N], wb[:, c, (2 * s) * 128:(2 * s + 1) * 128], hnT[:, c, 0:N],
                                     start=(c == 0), stop=(c == KC - 1), rkeys=[hnT] + wk)
                            for c in range(KC):
                                P.mm(pg[:, 0:N], wb[:, c, (2 * s + 1) * 128:(2 * s + 2) * 128], hnT[:, c, 0:N],
                                     start=(c == 0), stop=(c == KC - 1), rkeys=[hnT] + wk)
                            P.act(sg[:, 0:N], pg[:, 0:N], AF.Sigmoid)
                            u_t = ut[j % 2]
                            P.tt("dve", u_t[:, 0:N], pa[:, 0:N], sg[:, 0:N], ALU.mult)
                            for tt, (kind, i) in enumerate(grp):
                                src = u_t[:, tt * 128:(tt + 1) * 128]
                                if kind == "h":
                                    P.dma("sp", UTp[j * 128:(j + 1) * 128, 0:128], src, okey=("UTp", None))
                                elif kind == "p":
                                    P.dma("sp", UTp[j * 128:(j + 1) * 128, 128 + i * 128:128 + (i + 1) * 128], src, okey=("UTp", None))
                                else:
                                    P.dma("sp", UTs[i][j * 128:(j + 1) * 128, 32:160], src, okey=("UTs", "tok"))
                    for which, nblk, wsrc, dst in (("q", NH // 4, w_q, QT), ("qi", IH // 4, w_qi, QIT)):
                        for b in range(nblk):
                            wb = wbuf[wcnt % 2]
                            wcnt += 1
                            load_w(wb, wsrc[:, b * 512:(b + 1) * 512], 512)
                            wk = wkeys(wb)
                            real = [(tt, kind, i) for tt, (kind, i) in enumerate(grp) if kind != "h"]
                            for n2, (tt, kind, i) in enumerate(real):
                                if n2 == 0:
                                    tok_mm(pq[n2 % 2][:, 0:512], hnT, tt * 128, wb, 512, wk)
                                if n2 + 1 < len(real):
                                    tok_mm(pq[(n2 + 1) % 2][:, 0:512], hnT, real[n2 + 1][0] * 128, wb, 512, wk)
                                ti = tile_index(kind, i)
                                pq_t = pq[n2 % 2]
                                if which == "q":
                                    head_norm(pq_t[:, 0:512], 4, gq, qf[:], sq_t, sm_t)
                                else:
                                    P.copy("act", qf[:].rearrange("p h d -> p (h d)"), pq_t[:, 0:512])
                                rope(qf, 4, cst[:, tt, :], rtmp)
                                P.copy("act", qb[:], qf[:].rearrange("p h d -> p (h d)"))
                                for h in range(4):
                                    P.tr(ptr[:, h * 128:(h + 1) * 128], qb[:, h * 128:(h + 1) * 128], idb[:])
                                q_T = qTt[n2 % 2]
                                P.copy("dve", q_T[:], ptr[:, 0:512].rearrange("p (h t) -> p h t", t=128))
                                P.dma("sp", dst[ti][:, b * 4:(b + 1) * 4, :], q_T[:], okey=(dst.tensor.name, None))
                    wb = wbuf[wcnt % 2]
                    wcnt += 1
                    load_w(wb, w_wi, IH)
                    wk = wkeys(wb)
                    for tt, (kind, i) in enumerate(grp):
                        if kind == "h":
                            continue
                        ti = tile_index(kind, i)
                        pq_t = pq[tt % 2]
                        tok_mm(pq_t[:, 0:IH], hnT, tt * 128, wb, IH, wk)
                        w_s = wis[tt % 2]
                        P.act(w_s[:], pq_t[:, 0:IH], AF.Copy, scale=IDX_SCALE)
                        P.dma("sp", WI[ti], w_s[:], okey=("WI", None))
                P.flush()
            P.st = es

        if "B" in stages:
            with contextlib.ExitStack() as st:
                P.st = st
                P.npool["uin"] = 4
                wt = P.sb([128, CC, 31], F32)
                bt = P.sb([128, CC], F32)
                lg = P.sb([128, CC], F32)
                lb = P.sb([128, CC], F32)
                P.dma("sp", wt[:], dww)
                P.dma("sp", bt[:], dwb)
                P.dma("sp", lg[:], lng)
                P.dma("sp", lb[:], lnb)
                uin = P.sb([128, CC, 544], F32, "uin")
                cc_t = P.sb([128, CC, 512], F32)
                sqt = [P.sb([128, 512], F32) for _ in range(2)]
                p1 = P.ps()
                p2 = P.ps()
                mean = P.sb([128, 512], F32)
                var = P.sb([128, 512], F32)
                rstd = P.sb([128, 512], F32)
                tmp = [P.sb([128, 512], F32) for _ in range(2)]
                co = [P.sb([128, 512], BF16) for _ in range(2)]
                ptc = P.ps()
                cnew = P.sb([32, CCH], F32)
                jobs = []
                for tb in range(max(1, NP * 128 // 512)):
                    ntk = min(512, NP * 128)
                    jobs.append(("p", tb, ntk))
                jobs += [("s", 0, 128), ("s", 1, 128)]
                for kind, tb, ntk in jobs:
                    if kind == "p":
                        c0 = 128 + tb * ntk
                        src = UTp[:, c0 - 30:c0 + ntk].rearrange("(j p) t -> p j t", p=128)
                        mcol = tb * ntk
                        sk = "UTp"
                    else:
                        src = UTs[tb][:, 2:160].rearrange("(j p) t -> p j t", p=128)
                        mcol = (NP + tb) * 128
                        sk = "UTs"
                    W = 30 + ntk
                    P.dma("sp", uin[:, :, 0:W], src, ikey=sk)
                    last_p = (kind == "p" and (tb + 1) * ntk == NP * 128)
                    if last_p or kind == "s":
                        a0 = (30 + ntk - 30) if kind == "p" else (30 + 64 - 30)
                        for j0 in range(0, CC, 4):
                            for j in range(j0, min(CC, j0 + 4)):
                                P.tr(ptc[0:30, (j - j0) * 128:(j - j0 + 1) * 128], uin[:, j, a0:a0 + 30], idf[:])
                            nj = min(CC, j0 + 4) - j0
                            P.copy("act", cnew[0:30, j0 * 128:(j0 + nj) * 128], ptc[0:30, 0:nj * 128])
                        P.dma("sp", o_conv if kind == "p" else o_convs[tb], cnew[0:30, :])
                    for j in range(CC):
                        acc = cc_t[:, j, 0:ntk]
                        P.ts("dve", acc, uin[:, j, 0:ntk], wt[:, j, 0:1], bt[:, j:j + 1], op0=ALU.mult, op1=ALU.add, okey=(cc_t, j))
                        for k in range(1, 31):
                            P.stt(acc, uin[:, j, k:k + ntk], wt[:, j, k:k + 1], acc, ALU.mult, ALU.add, okey=(cc_t, j),
                                  rkeys=[uin, (cc_t, j)])
                        s_t = sqt[j % 2]
                        P.act(s_t[:, 0:ntk], acc, AF.Square, ikey=(cc_t, j))
                        P.mm(p1[:, 0:ntk], onef[:], acc, start=(j == 0), stop=(j == CC - 1), rkeys=[onef, (cc_t, j)])
                        P.mm(p2[:, 0:ntk], onef[:], s_t[:, 0:ntk], start=(j == 0), stop=(j == CC - 1))
                    P.ts("dve", mean[:, 0:ntk], p1[:, 0:ntk], 1.0 / CCH, None, op0=ALU.mult)
                    P.tt("dve", var[:, 0:ntk], mean[:, 0:ntk], mean[:, 0:ntk], ALU.mult)
                    P.stt(var[:, 0:ntk], p2[:, 0:ntk], 1.0 / CCH, var[:, 0:ntk], ALU.mult, ALU.subtract)
                    P.ts("dve", var[:, 0:ntk], var[:, 0:ntk], EPS, None, op0=ALU.add)
                    P.act(var[:, 0:ntk], var[:, 0:ntk], AF.Sqrt)
                    P.recip(rstd[:, 0:ntk], var[:, 0:ntk])
                    for j in range(CC):
                        t_ = tmp[j % 2]
                        P.tt("dve", t_[:, 0:ntk], cc_t[:, j, 0:ntk], mean[:, 0:ntk], ALU.subtract, rkeys=[(cc_t, j), mean])
                        P.tt("dve", t_[:, 0:ntk], t_[:, 0:ntk], rstd[:, 0:ntk], ALU.mult)
                        c_o = co[j % 2]
                        P.act(c_o[:, 0:ntk], t_[:, 0:ntk], AF.Silu, scale=lg[:, j:j + 1], bias=lb[:, j:j + 1])
                        P.dma("sp", MIXT[j * 128:(j + 1) * 128, mcol:mcol + ntk], c_o[:, 0:ntk], okey=("MIXT", "conv"))
                P.flush()
            P.st = es

        if "C" in stages:
            with contextlib.ExitStack() as st:
                P.st = st
                SMAX = max(SEQ, SS)
                P.npool["kiT_c"] = 4
                P.npool["kT_c"] = 4
                P.npool["v_c"] = 4
                kiT_c = P.sb([128, SMAX], BF16, "kiT_c")
                kT_c = P.sb([128, NKV, SMAX], BF16, "kT_c")
                v_c = P.sb([128, SMAX // 128, NKV * 128], BF16, "v_c")
                kc_t = P.sb([128, SMAX], BF16)
                qch_t = P.sb([128, NT], F32)
                P.dma("sp", qch_t[:], qch)
                qiT = [P.sb([128, IH, 128], BF16) for _ in range(2)]
                qT = [P.sb([128, NH, 128], BF16) for _ in range(2)]
                wi_t = [P.sb([128, IH], F32) for _ in range(2)]
                acc2 = [P.sb([128, SMAX], F32) for _ in range(2)]
                madd = P.sb([128, SMAX], BF16)
                bs = P.sb([128, 8], F32)
                m8 = P.sb([128, 256], F32)
                thr = P.sb([128, 1], F32)
                mask = P.sb([128, SMAX], BF16)
                maskT = P.sb([128, SMAX // 128, 128], BF16)
                rl = [P.sb([128, 512], F32) for _ in range(4)]
                pe_ = [P.sb([128, GQ, 128], BF16) for _ in range(3)]
                pm = [P.sb([128, GQ, 128], BF16) for _ in range(4)]
                rz = P.sb([128, GQ * 128], F32)
                ob = [P.sb([128, GQ, 128], BF16) for _ in range(2)]
                ps_s = [P.ps() for _ in range(2)]
                ps_qk = [P.ps() for _ in range(3)]
                ps_o = P.ps()
                ps_z = P.ps()
                ptr = [P.ps([128, 1024], BF16) for _ in range(1)]

                def seglist(blocks):
                    nb = len(blocks)
                    segs = []
                    a = 0
                    while a < nb:
                        b_ = a + 1
                        while b_ < nb and b_ - a < 4 and blocks[b_] == blocks[b_ - 1] + 1:
                            b_ += 1
                        segs.append((a, b_))
                        a = b_
                    return segs

                cnt = dict(n=0, it=0)
                P.npool["UB16"] = 3
                P.npool["VB16"] = 3
                pre_ops = []
                for c in range(0, 128, 2):
                    pre_ops.append(("u", c))
                    pre_ops.append(("v", c))
                n_slots = (NP + 2) * NKV
                per_slot = -(-len(pre_ops) // n_slots)

                def precast_some():
                    dd = min(2048, KC * 128)
                    for _ in range(per_slot):
                        if not pre_ops:
                            return
                        kind, c = pre_ops.pop(0)
                        if kind == "u":
                            P.dma("pool", UB16[c:c + 2].rearrange("c p (x d) -> p c x d", d=dd),
                                  uT[c:c + 2].rearrange("c p (x d) -> p c x d", d=dd), okey=("UB16", None))
                        else:
                            dv = min(2048, D)
                            P.dma("pool", VB16[c * 128:(c + 2) * 128, :].rearrange("(c p) (x d) -> p c x d", p=128, d=dv),
                                  vtab[c * 128:(c + 2) * 128, :].rearrange("(c p) (x d) -> p c x d", p=128, d=dv), okey=("VB16", None))

                def idx_phase(job):
                    ti, blocks = job["ti"], job["blocks"]
                    if job.get("pre_idx"):
                        job["pre_idx"]()
                    k2 = ti % 2
                    acc = acc2[k2]
                    P.dma("sp", qiT[k2][:], QIT[ti], ikey="QIT")
                    P.dma("sp", wi_t[k2][:], WI[ti], ikey="WI")
                    segs = seglist(blocks)
                    N = len(blocks) * 128
                    for (a, b_) in segs:
                        P.ts("pool", madd[:, a * 128:b_ * 128], kc_t[:, blocks[a] * 128:(blocks[a] + b_ - a) * 128],
                             qch_t[:, ti:ti + 1], NEG, op0=ALU.is_gt, op1=ALU.mult)
                    for (a, b_) in segs:
                        w = (b_ - a) * 128
                        for h in range(IH):
                            n_ = cnt["n"]
                            cnt["n"] += 1
                            p_ = ps_s[n_ % 2]
                            r_ = rl[n_ % 4]
                            P.mm(p_[:, 0:w], qiT[k2][:, h, :], kiT_c[:, blocks[a] * 128:blocks[a] * 128 + w], rkeys=[qiT[k2], kiT_c])
                            P.act(r_[:, 0:w], p_[:, 0:w], AF.Relu)
                            if h == 0:
                                P.ts("dve", acc[:, a * 128:b_ * 128], r_[:, 0:w], wi_t[k2][:, 0:1], None, op0=ALU.mult)
                            else:
                                P.stt(acc[:, a * 128:b_ * 128], r_[:, 0:w], wi_t[k2][:, h:h + 1], acc[:, a * 128:b_ * 128], ALU.mult, ALU.add)
                    P.reduce(bs[:, 0:1], acc[:, 0:N], ALU.min)
                    P.tt("pool", acc[:, 0:N], acc[:, 0:N], madd[:, 0:N], ALU.add)
                    P.max8(m8[:, 0:8], acc[:, 0:N])
                    P.copy("dve", bs[:, 1:2], m8[:, 0:1])

                NITER = 22

                def topk_rounds(job, r0, r1):
                    ti, N, topk = job["ti"], len(job["blocks"]) * 128, job["topk"]
                    acc = acc2[ti % 2]
                    lo, hi, mid, tmp, cn, sel, dd = (bs[:, i:i + 1] for i in range(7))
                    for r in range(r0, min(r1, NITER)):
                        P.ts("dve", tmp, hi, 0.5, None, op0=ALU.mult)
                        P.stt(mid, lo, 0.5, tmp, ALU.mult, ALU.add)
                        P.ts("dve", mask[:, 0:N], acc[:, 0:N], mid, None, op0=ALU.is_ge, op1=ALU.add, accum_out=cn)
                        P.ts("dve", sel, cn, float(topk) - 0.5, None, op0=ALU.is_ge)
                        P.tt("dve", dd, mid, lo, ALU.subtract)
                        P.stt(lo, dd, sel, lo, ALU.mult, ALU.add)
                        P.tt("dve", dd, hi, mid, ALU.subtract)
                        P.stt(hi, dd, sel, mid, ALU.mult, ALU.add)

                def topk_final(job):
                    ti, blocks, topk = job["ti"], job["blocks"], job["topk"]
                    nb = len(blocks)
                    N = nb * 128
                    acc = acc2[ti % 2]
                    P.ts("dve", thr[:], bs[:, 0:1], 0.5 * NEG, None, op0=ALU.max)
                    P.ts("dve", mask[:, 0:N], acc[:, 0:N], thr[:, 0:1], None, op0=ALU.is_ge)
                    for b0 in range(0, nb, 8):
                        n8 = min(8, nb - b0)
                        pt = ptr[0]
                        for j in range(n8):
                            P.tr(pt[:, j * 128:(j + 1) * 128], mask[:, (b0 + j) * 128:(b0 + j + 1) * 128], idb[:])
                        P.copy("act", maskT[:, b0:b0 + n8, :], pt[:, 0:n8 * 128].rearrange("p (n t) -> p n t", t=128))

                def attn_group(job, g):
                    ti, blocks = job["ti"], job["blocks"]
                    k2 = ti % 2
                    nb = len(blocks)
                    if g == 0:
                        if job.get("pre_attn"):
                            job["pre_attn"]()
                        P.dma("sp", qT[k2][:], QT[ti], ikey="QT")
                    W = GQ * 128
                    LA = 2
                    bufs = {}

                    def front(ci):
                        blk = blocks[ci]
                        it = cnt["it"]
                        cnt["it"] += 1
                        pq_ = ps_qk[it % 3]
                        e_ = pe_[it % 3]
                        m_ = pm[it % 4]
                        bufs[ci] = m_
                        P.mm(pq_[:, 0:W], kT_c[:, g, blk * 128:(blk + 1) * 128],
                             qT[k2][:, g * GQ:(g + 1) * GQ, :].rearrange("p r t -> p (r t)"), rkeys=[kT_c, qT[k2]])
                        P.act(e_[:].rearrange("p r t -> p (r t)"), pq_[:, 0:W], AF.Exp, scale=ATT_SCALE)
                        P.tt("pool", m_[:], e_[:], maskT[:, ci, :].unsqueeze(1).to_broadcast([128, GQ, 128]), ALU.mult)

                    def back(ci):
                        blk = blocks[ci]
                        m_ = bufs[ci]
                        mf = m_[:].rearrange("p r t -> p (r t)")
                        P.mm(ps_o[:, 0:W], v_c[:, blk, g * 128:(g + 1) * 128], mf, start=(ci == 0), stop=(ci == nb - 1), rkeys=[v_c, m_])
                        P.mm(ps_z[:, 0:W], oneb[:], mf, start=(ci == 0), stop=(ci == nb - 1))

                    for ci in range(min(LA, nb)):
                        front(ci)
                    for ci in range(nb):
                        if ci + LA < nb:
                            front(ci + LA)
                        back(ci)
                    P.recip(rz[:, 0:W], ps_z[:, 0:W])
                    o_ = ob[g % 2]
                    P.tt("dve", o_[:].rearrange("p r t -> p (r t)"), ps_o[:, 0:W], rz[:, 0:W], ALU.mult)
                    P.dma("sp", MIXT[CCH + g * GQ * 128:CCH + (g + 1) * GQ * 128, ti * 128:(ti + 1) * 128].rearrange("(r d) t -> d r t", d=128),
                          o_[:], okey=("MIXT", "attn"))

                def load_prompt_ki():
                    P.cdma(kc_t[:, 0:SEQ], kc_p[0:1, :].to_broadcast([128, SEQ]))
                    P.dma("sp", kiT_c[:, 0:SEQ], KIT, ikey="KIT", okey=(kiT_c, 0))

                def load_prompt_kv():
                    P.dma("sp", kT_c[:, :, 0:SEQ], KT, ikey="KT", okey=(kT_c, 0))
                    P.dma("sp", v_c[:, 0:NCX, :], Vc.rearrange("(c p) n -> p c n", p=128), ikey="Vc", okey=(v_c, 0))

                def mk_sample_ki(s):
                    def f():
                        P.cdma(kc_t[:, 0:SS], kc_s[0:1, :].to_broadcast([128, SS]))
                        P.cdma(kiT_c[:, 0:PAST], ckiT[s], okey=(kiT_c, 0))
                        P.dma("sp", kiT_c[:, PAST:SS], KITs[s], ikey="KITs", okey=(kiT_c, 1))
                    return f

                def mk_sample_kv(s):
                    def f():
                        for g in range(NKV):
                            P.cdma(kT_c[:, g, 0:PAST], ckT[s][:, g, :], okey=(kT_c, 0))
                        P.dma("sp", kT_c[:, :, PAST:SS], KTs[s], ikey="KTs", okey=(kT_c, 1))
                        cvv = cv[s].rearrange("(c p) n -> p c n", p=128)
                        nq = 4 if (PAST // 128) % 4 == 0 else 1
                        stp = (PAST // 128) // nq
                        for q in range(nq):
                            P.dma("pool", v_c[:, q * stp:(q + 1) * stp, :], cvv[:, q * stp:(q + 1) * stp, :], okey=(v_c, 0))
                        P.dma("sp", v_c[:, PAST // 128, :], Vs[s], ikey="Vs", okey=(v_c, 1))
                    return f

                jobs = []
                for i in range(NP):
                    jobs.append(dict(ti=i, blocks=list(range(0, i + 1)) + list(range(NP, 2 * NP)), topk=cfg["TOPK_P"]))
                jobs[0]["pre_idx"] = load_prompt_ki
                jobs[0]["pre_attn"] = load_prompt_kv
                for s in range(2):
                    jobs.append(dict(ti=NP + s, blocks=list(range(SS // 128)), topk=cfg["TOPK_S"],
                                     pre_idx=mk_sample_ki(s), pre_attn=mk_sample_kv(s)))
                idx_phase(jobs[0])
                topk_rounds(jobs[0], 0, 10 ** 6)
                topk_final(jobs[0])
                for k, job in enumerate(jobs):
                    nxt = jobs[k + 1] if k + 1 < len(jobs) else None
                    if nxt is not None:
                        idx_phase(nxt)
                        per = -(-NITER // NKV)
                    for g in range(NKV):
                        if nxt is not None:
                            topk_rounds(nxt, g * per, (g + 1) * per)
                        precast_some()
                        attn_group(job, g)
                    if nxt is not None:
                        topk_final(nxt)
                while pre_ops:
                    precast_some()
                P.flush()
            P.st = es

        otiles = list(range(NT))
        ogroups = [otiles[i:i + 4] for i in range(0, NT, 4)]
        Hs = dscr("Hs", [NTOK, D])
        RC = dscr("RC", [NT, 128, 4, 128])

        def x_rows(ti):
            return xctx[ti * 128:(ti + 1) * 128, :] if ti < NP else xsp[ti - NP]

        if "D" in stages:
            with contextlib.ExitStack() as st:
                P.st = st
                mixT = P.sb([128, KC, 512], BF16)
                wbuf = [P.sb([128, KC, 512], BF16) for _ in range(2)]
                xb_ = [P.sb([128, 512], F32) for _ in range(3)]
                hb_ = [P.sb([128, 512], F32) for _ in range(3)]
                pp = [P.ps() for _ in range(4)]
                wcnt = 0
                k_ = 0
                for grp in ogroups:
                    N = len(grp) * 128
                    c0 = grp[0] * 128
                    P.dma("sp", mixT[:, :, 0:N], MIXT[:, c0:c0 + N].rearrange("(c p) n -> p c n", p=128), ikey="MIXT")
                    for b in range(D // 512):
                        wb = wbuf[wcnt % 2]
                        wcnt += 1
                        load_w(wb, w_out[:, b * 512:(b + 1) * 512], 512)
                        wk = wkeys(wb)
                        for tt, ti in enumerate(grp):
                            xb = xb_[k_ % 3]
                            hb = hb_[k_ % 3]
                            p_ = pp[k_ % 4]
                            k_ += 1
                            P.dma("sp", xb[:], x_rows(ti)[:, b * 512:(b + 1) * 512])
                            tok_mm(p_[:, 0:512], mixT, tt * 128, wb, 512, wk)
                            P.tt("dve", hb[:], p_[:, 0:512], xb[:], ALU.add)
                            P.dma("sp", Hs[ti * 128:(ti + 1) * 128, b * 512:(b + 1) * 512], hb[:], okey=("Hs", None))
                P.flush()
            P.st = es

            with contextlib.ExitStack() as st:
                P.st = st
                nt = NormT(g_memn)
                gbc2 = P.sb([128, D], F32)
                bcast_row(gbc2[:], g_ffn[0:1, :], D)
                gmq = P.sb([128, 128], F32)
                bcast_row(gmq[:], g_mq[0:1, :], 128)
                wqm_b = P.sb([128, KC, 512], BF16)
                load_w(wqm_b, w_qm, 512)
                wom_b = P.sb([128, 4, D], BF16)
                P.cdma(wom_b[:], w_om.rearrange("(h p) d -> p h d", p=128))
                mkT_c = P.sb([128, 4, MEMT], BF16)
                mv_c = P.sb([128, MC, 512], BF16)
                ht = [P.sb([128, D], F32) for _ in range(2)]
                hn = [P.sb([128, KC, 128], BF16) for _ in range(2)]
                pq_ = P.ps()
                pl_ = [P.ps() for _ in range(2)]
                po_ = P.ps()
                pz_ = pq_
                pw_ = [P.ps() for _ in range(1)]
                ptr = P.ps([128, 1024], BF16)
                sq_t = P.sb([128, 512], F32)
                sm_t = P.sb([128, 12], F32)
                qmf = P.sb([128, 4, 128], F32)
                qmb = P.sb([128, 512], BF16)
                qmT = P.sb([128, 4, 128], BF16)
                pmT = [P.sb([128, 4, 128], BF16) for _ in range(MC)]
                rz = P.sb([128, 512], F32)
                omT = P.sb([128, 4, 128], BF16)
                for ti in otiles:
                    if ti == 0:
                        P.dma("sp", mkT_c[:], MKT, ikey="MKT")
                        P.dma("sp", mv_c[:], MV.rearrange("(c p) n -> p c n", p=128), ikey="MV")
                    elif ti >= NP:
                        P.dma("pool", mkT_c[:], cmkT[ti - NP])
                        P.dma("pool", mv_c[:], cmv[ti - NP].rearrange("(c p) n -> p c n", p=128))
                    h_t = ht[ti % 2]
                    hn_t = hn[ti % 2]
                    P.dma("sp", h_t[:], Hs[ti * 128:(ti + 1) * 128, :], ikey="Hs")
                    nt.run(h_t[:], lambda c0, n, hn_t=hn_t: hn_t[:, c0:c0 + n, :])
                    tok_mm(pq_[:, 0:512], hn_t, 0, wqm_b, 512, wkeys(wqm_b))
                    head_norm(pq_[:, 0:512], 4, gmq, qmf[:], sq_t, sm_t)
                    P.copy("act", qmb[:], qmf[:].rearrange("p h d -> p (h d)"))
                    for h in range(4):
                        P.tr(ptr[:, h * 128:(h + 1) * 128], qmb[:, h * 128:(h + 1) * 128], idb[:])
                    P.copy("dve", qmT[:], ptr[:, 0:512].rearrange("p (h t) -> p h t", t=128))
                    for mc in range(MC):
                        for h in range(4):
                            P.mm(pl_[mc % 2][:, h * 128:(h + 1) * 128], mkT_c[:, h, mc * 128:(mc + 1) * 128], qmT[:, h, :])
                        P.act(pmT[mc][:].rearrange("p h t -> p (h t)"), pl_[mc % 2][:, 0:512], AF.Exp, scale=ATT_SCALE)
                    for h in range(4):
                        for mc in range(MC):
                            P.mm(po_[:, h * 128:(h + 1) * 128], mv_c[:, mc, h * 128:(h + 1) * 128], pmT[mc][:, h, :],
                                 start=(mc == 0), stop=(mc == MC - 1))
                    for mc in range(MC):
                        P.mm(pz_[:, 0:512], oneb[:], pmT[mc][:].rearrange("p h t -> p (h t)"), start=(mc == 0), stop=(mc == MC - 1))
                    P.recip(rz[:], pz_[:, 0:512])
                    P.tt("dve", omT[:].rearrange("p h t -> p (h t)"), po_[:, 0:512], rz[:], ALU.mult)
                    for b in range(D // 512):
                        p_ = pw_[0]
                        for h in range(4):
                            P.mm(p_[:, 0:512], omT[:, h, :], wom_b[:, h, b * 512:(b + 1) * 512], start=(h == 0), stop=(h == 3))
                        P.tt("dve", h_t[:, b * 512:(b + 1) * 512], p_[:, 0:512], h_t[:, b * 512:(b + 1) * 512], ALU.add)
                    P.dma("sp", H2[ti * 128:(ti + 1) * 128, :], h_t[:], okey=("H2", None))
                    nt.gbc, g_save = gbc2, nt.gbc
                    nt.run(h_t[:], lambda c0, n, hn_t=hn_t: hn_t[:, c0:c0 + n, :])
                    nt.gbc = g_save
                    P.dma("sp", HN2T[:, ti * 128:(ti + 1) * 128].rearrange("(c p) t -> p c t", p=128), hn_t[:], okey=("HN2T", None))
                P.flush()
            P.st = es

            with contextlib.ExitStack() as st:
                P.st = st
                hn2 = P.sb([128, KC, 512], BF16)
                wbuf = [P.sb([128, KC, 512], BF16) for _ in range(2)]
                qpT = P.sb([128, 16, 512], F32)
                sk_t = P.sb([128, 16, 128], F32)
                P.dma("sp", sk_t[:], subk)
                pq_ = [P.ps() for _ in range(2)]
                ps_ = [P.ps() for _ in range(2)]
                ptf = P.ps()
                s12 = [P.sb([128, 16, 128], F32) for _ in range(2)]
                v16 = P.sb([128, 16, 16], F32)
                tmp128 = P.sb([128, 128], F32)
                cand = P.sb([128, 8, 256], F32)
                tmpc = P.sb([128, 256], F32)
                t16 = P.sb([128, 8, 16], F32)
                e16 = P.sb([128, 8, 16], F32)
                zz = P.sb([128, 8], F32)
                mlz = P.sb([128, 8], F32)
                rc3 = P.sb([128, 4, 8, 16], F32)
                rcT = [P.sb([128, 4, 128], F32) for _ in range(2)]
                idxu = P.sb([128, 8, 16], mybir.dt.uint32)
                wcnt = 0
                for grp in ogroups:
                    N = len(grp) * 128
                    c0 = grp[0] * 128
                    P.dma("sp", hn2[:, :, 0:N], HN2T[:, c0:c0 + N].rearrange("(c p) n -> p c n", p=128), ikey="HN2T")
                    for b in range(4):
                        wb = wbuf[wcnt % 2]
                        wcnt += 1
                        load_w(wb, w_pq[:, b * 512:(b + 1) * 512], 512)
                        wk = wkeys(wb)
                        for jj in range(4):
                            j = b * 4 + jj
                            p_ = pq_[j % 2]
                            for c in range(KC):
                                P.mm(p_[:, 0:N], wb[:, c, jj * 128:(jj + 1) * 128], hn2[:, c, 0:N], start=(c == 0), stop=(c == KC - 1),
                                     rkeys=[hn2] + wk)
                            P.copy("act", qpT[:, j, 0:N], p_[:, 0:N], okey=(qpT, j))
                    for tt, ti in enumerate(grp):
                        s_t = s12[ti % 2]
                        for jb in range(4):
                            p_ = ps_[jb % 2]
                            for jj in range(4):
                                j = jb * 4 + jj
                                P.mm(p_[:, jj * 128:(jj + 1) * 128], qpT[:, j, tt * 128:(tt + 1) * 128], sk_t[:, j, :], rkeys=[(qpT, j), sk_t])
                            P.copy("act", s_t[:, jb * 4:(jb + 1) * 4, :].rearrange("p j k -> p (j k)"), p_[:, 0:512])
                        P.dma("sp", S12[:, ti * 128:(ti + 1) * 128, :].rearrange("j t i -> t j i"), s_t[:], okey=("S12", None))
                        for j in range(16):
                            P.max8(v16[:, j, 0:8], s_t[:, j, :])
                            P.mrep(tmp128[:], v16[:, j, 0:8], s_t[:, j, :], -3.0e38)
                            P.max8(v16[:, j, 8:16], tmp128[:])
                            if j % 2 == 0:
                                hh = j // 2
                                P.op("dve", lambda e, hh=hh, j=j, s_t=s_t: e.max_index(idxu[:, hh, 0:8], v16[:, j, 0:8], s_t[:, j, :]),
                                     reads=[v16, s_t], writes=[idxu])
                                P.op("dve", lambda e, hh=hh, j=j: e.max_index(idxu[:, hh, 8:16], v16[:, j, 8:16], tmp128[:]),
                                     reads=[v16, tmp128], writes=[idxu])
                        v16v = v16[:].rearrange("p (h two) k -> p h two k", two=2)
                        for h in range(8):
                            P.tt("dve", cand[:, h, :].rearrange("p (a b) -> p a b", b=16),
                                 v16[:, 2 * h, :].unsqueeze(2).to_broadcast([128, 16, 16]),
                                 v16[:, 2 * h + 1, :].unsqueeze(1).to_broadcast([128, 16, 16]), ALU.add)
                        for h in range(8):
                            P.max8(t16[:, h, 0:8], cand[:, h, :])
                            P.mrep(tmpc[:], t16[:, h, 0:8], cand[:, h, :], -3.0e38)
                            P.max8(t16[:, h, 8:16], tmpc[:])
                        P.tt("dve", e16[:], t16[:], t16[:, :, 0:1].to_broadcast([128, 8, 16]), ALU.subtract)
                        P.act(e16[:], e16[:], AF.Exp)
                        P.reduce(zz[:], e16[:], ALU.add)
                        P.act(mlz[:], zz[:], AF.Ln)
                        P.tt("dve", mlz[:], mlz[:], t16[:, :, 0], ALU.add)
                        P.copy("dve", rc3[:, 0, :, :], v16v[:, :, 0, :])
                        P.tt("dve", rc3[:, 1, :, :], t16[:, :, 15:16].to_broadcast([128, 8, 16]), rc3[:, 0, :, :], ALU.subtract)
                        P.tt("dve", rc3[:, 2, :, :], rc3[:, 0, :, :], mlz[:].unsqueeze(2).to_broadcast([128, 8, 16]), ALU.subtract)
                        P.copy("dve", rc3[:, 3, :, :], idxu[:])
                        for q in range(4):
                            P.tr(ptf[:, q * 128:(q + 1) * 128], rc3[:, q, :, :].rearrange("p h a -> p (h a)"), idf[:])
                        r_T = rcT[ti % 2]
                        P.copy("act", r_T[:].rearrange("p q t -> p (q t)"), ptf[:, 0:512])
                        P.dma("sp", RC[ti], r_T[:], okey=("RC", None))
                P.flush()
            P.st = es

            with contextlib.ExitStack() as st:
                P.st = st
                TB = 32
                iota_t = P.sb([128, 128], F32)
                P.dma("sp", iota_t[:], c_iota)
                s2r = [P.sb([128, TB, 128], F32) for _ in range(2)]
                rct = [P.sb([128, 4, 128], F32) for _ in range(2)]
                o1 = [P.sb([128, 128], BF16) for _ in range(4)]
                ee = [P.sb([128, 128], F32) for _ in range(4)]
                rr = [P.sb([128, 128], BF16) for _ in range(4)]
                gst = [P.sb([128, 128, 128], BF16) for _ in range(2)]
                pg_ = [P.ps() for _ in range(2)]
                kk = 0
                for ti in otiles:
                    rc_ = rct[ti % 2]
                    g_s = gst[ti % 2]
                    P.dma("sp", rc_[:], RC[ti], ikey="RC")
                    for tb in range(128 // TB):
                        t0 = ti * 128 + tb * TB
                        a2 = s2r[tb % 2]
                        src = S12[:, t0:t0 + TB, :].rearrange("(h two) t i -> two h (t i)", two=2)[1]
                        P.dma("sp", a2[:].rearrange("p t i -> p (t i)"), src.unsqueeze(1).to_broadcast([8, 16, TB * 128]),
                              ikey="S12", okey=(a2, None))
                        for tq in range(0, TB, 4):
                            p_ = pg_[(kk) % 2]
                            kk += 1
                            for u4 in range(4):
                                tl = tq + u4
                                t = tb * TB + tl
                                o_ = o1[u4]
                                e_ = ee[u4]
                                r_ = rr[u4]
                                P.ts("dve", o_[:], iota_t[:], rc_[:, 3, t:t + 1], None, op0=ALU.is_equal)
                                P.act(e_[:], a2[:, tl, :], AF.Exp, bias=rc_[:, 2, t:t + 1])
                                P.stt(r_[:], a2[:, tl, :], rc_[:, 1, t:t + 1], e_[:], ALU.is_ge, ALU.mult)
                                P.mm(p_[:, u4 * 128:(u4 + 1) * 128], o_[:], r_[:])
                            tbase = tb * TB + tq
                            P.copy("act", g_s[:, :, tbase:tbase + 4].rearrange("p i t -> p t i"),
                                   p_[:, 0:512].rearrange("p (t i) -> p t i", i=128))
                    P.dma("sp", GALL[:, :, ti * 128:(ti + 1) * 128], g_s[:], okey=("GALL", None))
                P.flush()
            P.st = es

        if "E" in stages:
            with contextlib.ExitStack() as st:
                P.st = st
                NCH = PEER_KEYS
                EB = 4
                hn2 = P.sb([128, KC, 512], BF16)
                oacc = P.sb([128, 4, D], F32)
                ub = [P.sb([128, KC, 128], BF16) for _ in range(3)]
                vb = [P.sb([128, EB, D], BF16) for _ in range(2)]
                coef = [P.sb([128, EB, 512], BF16) for _ in range(2)]
                gl = [P.sb([128, 512], BF16) for _ in range(2)]
                gc = [P.sb([128, 512], BF16) for _ in range(4)]
                pa_ = [P.ps() for _ in range(2)]
                pv_ = [P.ps() for _ in range(4)]
                ucnt = 0
                vcnt = 0
                pcnt = 0
                DH = 2048 if D % 2048 == 0 else D
                for grp in ogroups:
                    ng = len(grp)
                    N = ng * 128
                    c0 = grp[0] * 128
                    P.dma("sp", hn2[:, :, 0:N], HN2T[:, c0:c0 + N].rearrange("(c p) n -> p c n", p=128), ikey="HN2T")
                    for tt, ti in enumerate(grp):
                        P.dma("sp", oacc[:, tt, :], H2[ti * 128:(ti + 1) * 128, :], ikey="H2", okey=(oacc, tt))
                    def v_load(eb):
                        v_b = vb[eb % 2]
                        vsrc = VB16[eb * EB * 128:(eb + 1) * EB * 128, :].rearrange("(cc p) d -> p cc d", p=128)
                        P.dma("sp", v_b[:], vsrc, ikey="VB16")

                    loaded = set()

                    def u_load(gidx):
                        if gidx >= NCH or gidx in loaded:
                            return
                        loaded.add(gidx)
                        k_ = ucnt + gidx
                        P.dma("sp", ub[k_ % 3][:].rearrange("p c e -> p (c e)"), UB16[gidx], ikey="UB16")
                        P.dma("sp", gc[k_ % 4][:, 0:N], GALL[gidx][:, c0:c0 + N], ikey="GALL")

                    def u_phase(eb):
                        cf = coef[eb % 2]
                        for cc in range(EB):
                            c = eb * EB + cc
                            u_load(c)
                            u_load(c + 1)
                            u_load(c + 2)
                            k_ = ucnt + c
                            u_b = ub[k_ % 3]
                            g_l = gl[k_ % 2]
                            g_c = gc[k_ % 4]
                            p_ = pa_[k_ % 2]
                            for dc in range(KC):
                                P.mm(p_[:, 0:N], u_b[:, dc, :], hn2[:, dc, 0:N], start=(dc == 0), stop=(dc == KC - 1))
                            P.act(g_l[:, 0:N], p_[:, 0:N], AF.Gelu)
                            P.tt("dve", cf[:, cc, 0:N], g_l[:, 0:N], g_c[:, 0:N], ALU.mult, okey=(cf, cc))

                    def v_phase(eb):
                        nonlocal pcnt
                        cf = coef[eb % 2]
                        v_b = vb[eb % 2]
                        for tt in range(ng):
                            for db in range(D // 512):
                                pv = pv_[pcnt % 4]
                                pcnt += 1
                                for cc in range(EB):
                                    P.mm(pv[:, 0:512], cf[:, cc, tt * 128:(tt + 1) * 128], v_b[:, cc, db * 512:(db + 1) * 512],
                                         start=(cc == 0), stop=(cc == EB - 1), rkeys=[(cf, cc), v_b])
                                P.tt("dve", oacc[:, tt, db * 512:(db + 1) * 512], pv[:, 0:512], oacc[:, tt, db * 512:(db + 1) * 512], ALU.add,
                                     okey=(oacc, tt), rkeys=[pv, (oacc, tt)])

                    nE = NCH // EB
                    u_load(0)
                    u_load(1)
                    v_load(0)
                    u_phase(0)
                    for eb in range(nE):
                        if eb + 1 < nE:
                            v_load(eb + 1)
                            u_phase(eb + 1)
                        v_phase(eb)
                    ucnt += NCH
                    for tt, ti in enumerate(grp):
                        P.dma("sp", y[ti * 128:(ti + 1) * 128, :], oacc[:, tt, :], ikey=(oacc, tt), okey=("y", None))
                P.flush()
            P.st = es

        if dbg:
            for nm, ap_ in (("MIXT", MIXT), ("UTp", UTp), ("UTs", UTs), ("QT", QT), ("QIT", QIT), ("WI", WI), ("KT", KT), ("KIT", KIT),
                            ("Vc", Vc), ("H2", H2), ("HN2T", HN2T), ("S12", S12), ("GALL", GALL), ("MKT", MKT), ("MV", MV)):
                if nm in dbg:
                    o_ = dout("dbg_" + nm, list(ap_.shape), ap_.dtype)
                    P.dma("sp", o_, ap_)
        P.flush()
    return nc


def _rope_table(pos):
    half = 16
    inv_freq = np.power(np.float32(ROPE_THETA), -np.arange(half, dtype=np.float32) / np.float32(half)).astype(np.float32)
    ang = pos.astype(np.float32)[:, None] * inv_freq[None, :]
    return np.concatenate([np.cos(ang), np.sin(ang)], axis=1).astype(np.float32)


def host_prep(inp, cfg):
    D, KC, CCH, CC, NH, NKV, NP, NT, IH, SEQ, PAST, SS, MEMT = (cfg[k] for k in (
        "D", "KC", "CCH", "CC", "NH", "NKV", "NP", "NT", "IH", "SEQ", "PAST", "SS", "MEMT"))
    DS = cfg["DS"]
    f = lambda a: np.ascontiguousarray(a, dtype=np.float32)
    half = SEQ // 2
    w_in = inp["w_in"][0]
    OFF_Q = 2 * CCH
    OFF_K = OFF_Q + NH * 128
    OFF_V = OFF_K + NKV * 128
    OFF_QI = OFF_V + NKV * 128
    OFF_KI = OFF_QI + IH * 128
    OFF_WI = OFF_KI + 128
    a_ = w_in[:, :CCH].reshape(D, CC, 128)
    g_ = w_in[:, CCH:2 * CCH].reshape(D, CC, 128)
    w_glu = f(np.stack([a_, g_], axis=2).reshape(D, 2 * CCH))
    shared = dict(
        w_glu=w_glu,
        w_q=f(w_in[:, OFF_Q:OFF_K]),
        w_qi=f(w_in[:, OFF_QI:OFF_KI]),
        w_wi=f(w_in[:, OFF_WI:OFF_WI + IH]),
        w_kv=f(np.concatenate([w_in[:, OFF_K:OFF_V], w_in[:, OFF_V:OFF_QI], w_in[:, OFF_KI:OFF_WI]], axis=1)),
        w_out=f(inp["w_out"][0]),
        w_qm=f(inp["w_q_mem"][0]), w_km=f(inp["w_k_mem"][0]), w_vm=f(inp["w_v_mem"][0]), w_om=f(inp["w_o_mem"][0]),
        w_pq=f(inp["peer_wq"][0]),
        g_mix=f(inp["norm_mix_g"]), g_memn=f(inp["norm_mem_g"]), g_ffn=f(inp["norm_ffn_g"]), g_mem=f(inp["mem_norm_g"]),
        g_q=f(inp["q_norm_g"]), g_k=f(inp["k_norm_g"]), g_mq=f(inp["mem_q_norm_g"]), g_mk=f(inp["mem_k_norm_g"]),
        dww=f(inp["dw_w"][0].reshape(31, CC, 128).transpose(2, 1, 0)),
        dwb=f(inp["dw_b"][0].reshape(CC, 128).T), lng=f(inp["conv_ln_g"][0].reshape(CC, 128).T),
        lnb=f(inp["conv_ln_b"][0].reshape(CC, 128).T),
        vtab=f(inp["peer_v"][0]),
        c_idb=np.eye(128).astype(ml_dtypes.bfloat16), c_idf=np.eye(128, dtype=np.float32),
        c_oneb=np.ones((128, 128)).astype(ml_dtypes.bfloat16), c_onef=np.ones((128, 128), dtype=np.float32),
        c_iota=np.ascontiguousarray(np.tile(np.arange(128, dtype=np.float32)[None, :], (128, 1))),
    )
    sk = np.stack([inp["peer_sub_k1"][0], inp["peer_sub_k2"][0]], axis=1)
    shared["subk"] = f(sk.reshape(16, 128, 128).transpose(2, 0, 1))
    u = inp["peer_u"][0]
    shared["uT"] = f(u.reshape(128, 128, KC, 128).transpose(0, 3, 2, 1).reshape(128, 128, KC * 128))
    kcs = (np.arange(SS) // 64).astype(np.float32)
    kcs[PAST + DS:] = 1.0e9
    shared["kc_s"] = kcs[None, :]
    shared["rope_s"] = _rope_table(PAST + np.arange(128))
    maps = []
    for c in range(8):
        b, hf = c // 2, c % 2
        xb = inp["x_prompt"][b]
        own = xb[hf * half:(hf + 1) * half]
        oth = xb[(1 - hf) * half:(2 - hf) * half]
        pos = np.concatenate([hf * half + np.arange(half), (1 - hf) * half + np.arange(half)])
        m = dict(shared)
        m["xctx"] = f(np.concatenate([own, oth], axis=0))
        m["xhalo"] = f(xb[half - 128:half]) if hf == 1 else np.zeros((128, D), np.float32)
        xsp = np.zeros((2, 128, D), np.float32)
        for s in range(2):
            xsp[s, :DS] = inp["x_sample"][2 * c + s]
        m["xsp"] = xsp
        m["mem"] = f(inp["mem_prompt"][b])
        m["ckT"] = f(np.stack([inp["cache_k"][0, 2 * c + s].transpose(2, 1, 0) for s in range(2)]))
        m["cv"] = f(np.stack([inp["cache_v"][0, 2 * c + s].reshape(PAST, NKV * 128) for s in range(2)]))
        m["ckiT"] = f(np.stack([inp["cache_k_idx"][0, 2 * c + s].T for s in range(2)]))
        m["stT"] = f(np.stack([inp["state_conv"][0, 2 * c + s].T for s in range(2)]))
        m["cmkT"] = f(np.stack([inp["cache_mem_k"][0, 2 * c + s].transpose(2, 1, 0) for s in range(2)]))
        m["cmv"] = f(np.stack([inp["cache_mem_v"][0, 2 * c + s].reshape(MEMT, 512) for s in range(2)]))
        m["rope_c"] = _rope_table(pos)
        m["kc_p"] = (pos // 64).astype(np.float32)[None, :]
        q = np.zeros((128, NT), np.float32)
        for i in range(NP):
            q[:, i] = (hf * half + i * 128 + np.arange(128)) // 64
        q[:, NP:] = PAST // 64
        m["qch"] = q
        maps.append(m)
    return maps


def assemble(res, cfg):
    D, CCH, NKV, NP, SEQ, DS, MEMT, B, DB = (cfg[k] for k in ("D", "CCH", "NKV", "NP", "SEQ", "DS", "MEMT", "B", "DB"))
    half = SEQ // 2
    y_p = np.zeros((B, SEQ, D), np.float32)
    y_s = np.zeros((DB, DS, D), np.float32)
    k_p = np.zeros((1, B, SEQ, NKV, 128), np.float32)
    v_p = np.zeros_like(k_p)
    ki_p = np.zeros((1, B, SEQ, 128), np.float32)
    conv_p = np.zeros((1, B, 30, CCH), np.float32)
    mk_p = np.zeros((1, B, MEMT, 4, 128), np.float32)
    mv_p = np.zeros_like(mk_p)
    k_s = np.zeros((1, DB, DS, NKV, 128), np.float32)
    v_s = np.zeros_like(k_s)
    ki_s = np.zeros((1, DB, DS, 128), np.float32)
    conv_s = np.zeros((1, DB, 30, CCH), np.float32)
    for c in range(8):
        r = res[c]
        b, hf = c // 2, c % 2
        y_p[b, hf * half:(hf + 1) * half] = r["y"][:NP * 128]
        if hf == 0:
            k_p[0, b] = r["o_k"].reshape(SEQ, NKV, 128)
            v_p[0, b] = r["o_v"].reshape(SEQ, NKV, 128)
            ki_p[0, b] = r["o_ki"]
            mk_p[0, b] = r["o_mk"].reshape(MEMT, 4, 128)
            mv_p[0, b] = r["o_mv"].reshape(MEMT, 4, 128)
        else:
            conv_p[0, b] = r["o_conv"]
        for s in range(2):
            q = 2 * c + s
            y_s[q] = r["y"][(NP + s) * 128:(NP + s) * 128 + DS]
            k_s[0, q] = r["o_ks"][s, :DS].reshape(DS, NKV, 128)
            v_s[0, q] = r["o_vs"][s, :DS].reshape(DS, NKV, 128)
            ki_s[0, q] = r["o_kis"][s, :DS]
            conv_s[0, q] = r["o_convs"][s]
    return (y_p, y_s, k_p, v_p, ki_p, conv_p, mk_p, mv_p, k_s, v_s, ki_s, conv_s)


def kernel(**inputs):
    cfg = mkcfg()
    inp = {k: np.asarray(v) for k, v in inputs.items()}
    maps = host_prep(inp, cfg)
    nc = build(cfg)
    res = run_bass_kernel_spmd(nc, maps, core_ids=list(range(8)))
    return assemble(res.results, cfg)
```

```python
import contextlib
import math
import numpy as np
import ml_dtypes
import concourse.bass as bass
import concourse.mybir as mybir
from concourse.bass_utils import run_bass_kernel_spmd

F32 = mybir.dt.float32
BF16 = mybir.dt.bfloat16
ALU = mybir.AluOpType
AF = mybir.ActivationFunctionType
AX = mybir.AxisListType

EPS = 1e-6
ROPE_THETA = 500000.0
NEG = -1.0e30


class Prog:
    def __init__(self, nc, es):
        self.nc = nc
        self.es = es
        self.st = es
        self.ops = []
        self.engs = {"pe": nc.tensor, "act": nc.scalar, "dve": nc.vector, "pool": nc.gpsimd, "sp": nc.sync}
        self.n_t = 0
        self.eng_sem = {}
        self.eng_cnt = {}
        self.pool = {}
        self.npool = {}
        self.key_sem = {}
        self.fence_sem = None
        self.fence_cnt = 0
        self.tot_ops = 0
        self.tot_wait = 0
        self.free_sems = []
        self.n_dsem = 0

    def sb(self, shape, dt=F32, name=None):
        self.n_t += 1
        return self.st.enter_context(self.nc.sbuf_tensor(name or f"sb{self.n_t}", list(shape), dt))

    def ps(self, shape=(128, 512), dt=F32, name=None):
        self.n_t += 1
        return self.st.enter_context(self.nc.psum_tensor(name or f"ps{self.n_t}", list(shape), dt))

    @staticmethod
    def key(x):
        def nm(a):
            if isinstance(a, str):
                return a
            t = getattr(a, "tensor", None)
            return t.name if t is not None else a.name
        if isinstance(x, tuple):
            return (nm(x[0]), x[1])
        return (nm(x), None)

    def op(self, eng, fn, reads=(), writes=(), dma=False):
        rk = []
        for r in reads:
            if r is None or isinstance(r, (int, float)):
                continue
            k = self.key(r)
            if k not in rk:
                rk.append(k)
        wk = []
        for w in writes:
            k = self.key(w)
            if k not in wk:
                wk.append(k)
        self.ops.append(dict(eng=eng, fn=fn, reads=rk, writes=wk, dma=dma))

    def _esem(self, e):
        if e not in self.eng_sem:
            self.eng_sem[e] = self.es.enter_context(self.nc.semaphore(f"s_{e}"))
            self.eng_cnt[e] = 0
        return self.eng_sem[e]

    def flush(self):
        nc = self.nc
        ops = self.ops
        state = {}
        deps = [None] * len(ops)

        def confl(k):
            ent = state.get(k[0])
            if not ent:
                return []
            if k[1] is None:
                return list(ent.values())
            return [ent[s_] for s_ in (k[1], None) if s_ in ent]

        joined = [False] * len(ops)
        for i, o in enumerate(ops):
            d = set()
            for k in o["reads"]:
                for st in confl(k):
                    d.update(st[0])
            joins = {}
            for k in o["writes"]:
                own = state.get(k[0], {}).get(k[1])
                joinable = bool(o["dma"] and own and own[0] and all(ops[j]["dma"] for j in own[0]) and not own[1])
                joins[k] = joinable
                for st in confl(k):
                    d.update(st[1])
                    if not (joinable and st is own):
                        d.update(st[0])
            if o["dma"]:
                joined[i] = joins[o["writes"][0]]
            for k in o["reads"]:
                st = state.setdefault(k[0], {}).setdefault(k[1], [[], []])
                st[1].append(i)
            for k in o["writes"]:
                ent = state.setdefault(k[0], {})
                if joins[k]:
                    ent[k[1]][0].append(i)
                else:
                    if k[1] is None:
                        ent.clear()
                    ent[k[1]] = [[i], []]
            d.discard(i)
            if o["eng"] == "pe":
                d = {j for j in d if not (ops[j]["eng"] == "pe" and not ops[j]["dma"])}
            deps[i] = d
        need = [False] * len(ops)
        for d in deps:
            for j in d:
                need[j] = True
        last_on = {}
        for i, o in enumerate(ops):
            if not o["dma"]:
                last_on[o["eng"]] = i
        for i in last_on.values():
            need[i] = True

        sig = [None] * len(ops)
        waited = {}
        for i, o in enumerate(ops):
            e = o["eng"]
            eo = self.engs[e]
            wl = {}
            for j in deps[i]:
                s, v = sig[j]
                kk = id(s)
                if kk not in wl or wl[kk][1] < v:
                    wl[kk] = (s, v)
            pre = None
            if o["dma"]:
                k = o["writes"][0]
                name = k[0]
                pl = self.pool.get(name)
                if pl is None:
                    n = self.npool.get(name, 2)
                    sems_, cnt_ = [], []
                    for q in range(n):
                        if self.free_sems:
                            s_, c_ = self.free_sems.pop()
                        else:
                            self.n_dsem += 1
                            s_, c_ = self.es.enter_context(nc.semaphore(f"dma{self.n_dsem}")), 0
                        sems_.append(s_)
                        cnt_.append(c_)
                    pl = dict(sems=sems_, cnt=cnt_, last=[None] * n, rr=0)
                    self.pool[name] = pl
                idx = None
                if joined[i] and k in self.key_sem and pl["last"][self.key_sem[k]] == k:
                    idx = self.key_sem[k]
                else:
                    idx = pl["rr"]
                    pl["rr"] = (pl["rr"] + 1) % len(pl["sems"])
                    if pl["cnt"][idx] > 0:
                        s = pl["sems"][idx]
                        kk = id(s)
                        if kk not in wl or wl[kk][1] < pl["cnt"][idx]:
                            wl[kk] = (s, pl["cnt"][idx])
                self.key_sem[k] = idx
                pl["last"][idx] = k
                pre = (pl, idx)
            for kk, (s, v) in wl.items():
                if waited.get((e, kk), -1) >= v:
                    continue
                waited[(e, kk)] = v
                eo.wait_ge(s, v)
                self.tot_wait += 1
            ins = o["fn"](eo)
            if o["dma"]:
                pl, idx = pre
                pl["cnt"][idx] += 16
                ins.then_inc(pl["sems"][idx], 16)
                sig[i] = (pl["sems"][idx], pl["cnt"][idx])
            elif need[i]:
                s = self._esem(e)
                self.eng_cnt[e] += 1
                ins.then_inc(s, 1)
                sig[i] = (s, self.eng_cnt[e])
        self.tot_ops += len(ops)
        self.ops = []
        if self.fence_sem is None:
            self.fence_sem = self.es.enter_context(nc.semaphore("fence"))
        for e, s in self.eng_sem.items():
            if self.eng_cnt[e] > 0:
                nc.sync.wait_ge(s, self.eng_cnt[e])
        for pl in self.pool.values():
            for s, c in zip(pl["sems"], pl["cnt"]):
                if c > 0:
                    nc.sync.wait_ge(s, c)
        for pl in self.pool.values():
            for s, c in zip(pl["sems"], pl["cnt"]):
                self.free_sems.append((s, c))
        self.pool = {}
        self.key_sem = {}
        self.fence_cnt += 1
        nc.sync.drain().then_inc(self.fence_sem, 1)
        for e in ("pe", "act", "dve", "pool"):
            self.engs[e].wait_ge(self.fence_sem, self.fence_cnt)

    def dma(self, q, out, in_, okey=None, ikey=None, **kw):
        self.op(q, lambda e: e.dma_start(out=out, in_=in_, **kw), reads=[ikey or in_], writes=[okey or out], dma=True)

    def cdma(self, out, in_, okey=None, ikey=None):
        n = out.shape[-1]
        if n > 2048:
            d = 2048
            while n % d:
                d //= 2
            names = " ".join(f"a{i}" for i in range(len(out.shape) - 1))
            pat = f"{names} (x d) -> {names} x d"
            self.dma("pool", out.rearrange(pat, d=d), in_.rearrange(pat, d=d), okey=okey or out, ikey=ikey or in_)
        else:
            self.dma("pool", out, in_, okey=okey, ikey=ikey)

    def mm(self, out, lhsT, rhs, start=True, stop=True, okey=None, rkeys=None):
        self.op("pe", lambda e: e.matmul(out, lhsT, rhs, start=start, stop=stop), reads=rkeys or [lhsT, rhs], writes=[okey or out])

    def tr(self, out, in_, ident, okey=None, ikey=None):
        self.op("pe", lambda e: e.transpose(out, in_, ident), reads=[ikey or in_, ident], writes=[okey or out])

    def act(self, out, in_, func, scale=1.0, bias=0.0, accum_out=None, okey=None, ikey=None):
        rd = [ikey or in_] + [x for x in (scale, bias) if not isinstance(x, (int, float))]
        wr = [okey or out] + ([accum_out] if accum_out is not None else [])
        if accum_out is not None:
            self.op("act", lambda e: e.activation(out, in_, func, bias=bias, scale=scale, accum_out=accum_out), reads=rd, writes=wr)
        else:
            self.op("act", lambda e: e.activation(out, in_, func, bias=bias, scale=scale), reads=rd, writes=wr)

    def ts(self, eng, out, in0, s1, s2=None, op0=ALU.mult, op1=None, accum_out=None, okey=None, ikey=None):
        rd = [ikey or in0] + [x for x in (s1, s2) if x is not None and not isinstance(x, (int, float))]
        wr = [okey or out] + ([accum_out] if accum_out is not None else [])
        kw = {}
        if op1 is not None:
            kw["op1"] = op1
        if accum_out is not None:
            kw["accum_out"] = accum_out
        self.op(eng, lambda e: e.tensor_scalar(out, in0, s1, s2, op0, **kw), reads=rd, writes=wr)

    def tt(self, eng, out, in0, in1, op, okey=None, rkeys=None):
        self.op(eng, lambda e: e.tensor_tensor(out, in0, in1, op), reads=rkeys or [in0, in1], writes=[okey or out])

    def stt(self, out, in0, scalar, in1, op0, op1, okey=None, rkeys=None):
        rd = list(rkeys or [in0, in1]) + ([scalar] if not isinstance(scalar, (int, float)) else [])
        self.op("dve", lambda e: e.scalar_tensor_tensor(out, in0, scalar, in1, op0, op1), reads=rd, writes=[okey or out])

    def copy(self, eng, out, in_, okey=None, ikey=None):
        if eng == "act":
            self.op("act", lambda e: e.copy(out, in_), reads=[ikey or in_], writes=[okey or out])
        else:
            self.op(eng, lambda e: e.tensor_copy(out, in_), reads=[ikey or in_], writes=[okey or out])

    def max8(self, out, in_, okey=None):
        self.op("dve", lambda e: e.max(out, in_), reads=[in_], writes=[okey or out])

    def mrep(self, out, in_to_replace, in_values, imm, rkeys=None):
        self.op("dve", lambda e: e.match_replace(out, in_to_replace, in_values, imm), reads=rkeys or [in_to_replace, in_values], writes=[out])

    def memset(self, eng, ap, val):
        self.op(eng, lambda e: e.memset(ap, val), reads=[], writes=[ap])

    def recip(self, out, in_, okey=None):
        self.op("dve", lambda e: e.reciprocal(out, in_), reads=[in_], writes=[okey or out])

    def reduce(self, out, in_, op, axis=AX.X):
        self.op("dve", lambda e: e.tensor_reduce(out, in_, axis, op), reads=[in_], writes=[out])


def mkcfg(D=4096, SEQ=4096, B=4, DB=16, DS=64, PAST=4096, IH=32, TOPK_MAX=256, MEMT=256):
    c = dict(D=D, SEQ=SEQ, B=B, DB=DB, DS=DS, PAST=PAST, IH=IH, MEMT=MEMT)
    c["KC"] = D // 128
    c["CCH"] = D // 2
    c["CC"] = c["CCH"] // 128
    c["NH"] = (D // 2) // 128
    c["NKV"] = 4
    c["GQ"] = c["NH"] // 4
    c["NP"] = SEQ // 2 // 128
    c["NCX"] = SEQ // 128
    c["NT"] = c["NP"] + 2
    c["TOPK_P"] = min(TOPK_MAX, SEQ // 4)
    c["TOPK_S"] = min(TOPK_MAX, (PAST + DS) // 4)
    c["SS"] = PAST + 128
    c["MH"] = 4
    c["MC"] = MEMT // 128
    return c


PEER_KEYS = 128
PEER_HEADS = 8
PEER_TOPK = 16


def build(cfg, stages=("M", "KV", "MAIN", "B", "C", "D", "E"), dbg=False):
    D, KC, CCH, CC, NH, NKV, GQ, NP, NCX, NT, IH, SEQ, PAST, SS, MEMT, MC = (cfg[k] for k in (
        "D", "KC", "CCH", "CC", "NH", "NKV", "GQ", "NP", "NCX", "NT", "IH", "SEQ", "PAST", "SS", "MEMT", "MC"))
    NTOK = NT * 128
    IDX_SCALE = (IH ** -0.5) * (128 ** -0.5)
    ATT_SCALE = 128 ** -0.5
    nc = bass.Bass("TRN2", target_bir_lowering=False)

    def din(name, shape, dt=F32):
        return nc.dram_tensor(name, list(shape), dt, kind="ExternalInput").ap()

    def dout(name, shape, dt=F32):
        return nc.dram_tensor(name, list(shape), dt, kind="ExternalOutput").ap()

    def dscr(name, shape, dt=F32):
        return nc.dram_tensor(name, list(shape), dt, kind="Internal").ap()

    xctx = din("xctx", [SEQ, D])
    xsp = din("xsp", [2, 128, D])
    xhalo = din("xhalo", [128, D])
    mem = din("mem", [MEMT, D])
    ckT = din("ckT", [2, 128, NKV, PAST])
    cv = din("cv", [2, PAST, NKV * 128])
    ckiT = din("ckiT", [2, 128, PAST])
    stT = din("stT", [2, CCH, 30])
    cmkT = din("cmkT", [2, 128, 4, MEMT])
    cmv = din("cmv", [2, MEMT, 512])
    w_glu = din("w_glu", [D, 2 * CCH])
    w_q = din("w_q", [D, NH * 128])
    w_qi = din("w_qi", [D, IH * 128])
    w_wi = din("w_wi", [D, IH])
    w_kv = din("w_kv", [D, 1152])
    w_out = din("w_out", [D, D])
    w_qm = din("w_qm", [D, 512])
    w_km = din("w_km", [D, 512])
    w_vm = din("w_vm", [D, 512])
    w_om = din("w_om", [512, D])
    w_pq = din("w_pq", [D, 2048])
    subk = din("subk", [128, 16, 128])
    uT = din("uT", [128, 128, KC * 128])
    vtab = din("vtab", [PEER_KEYS * PEER_KEYS, D])
    g_mix = din("g_mix", [1, D])
    g_memn = din("g_memn", [1, D])
    g_ffn = din("g_ffn", [1, D])
    g_mem = din("g_mem", [1, D])
    g_q = din("g_q", [1, 128])
    g_k = din("g_k", [1, 128])
    g_mq = din("g_mq", [1, 128])
    g_mk = din("g_mk", [1, 128])
    dww = din("dww", [128, CC, 31])
    dwb = din("dwb", [128, CC])
    lng = din("lng", [128, CC])
    lnb = din("lnb", [128, CC])
    rope_c = din("rope_c", [SEQ, 32])
    rope_s = din("rope_s", [128, 32])
    kc_p = din("kc_p", [1, SEQ])
    kc_s = din("kc_s", [1, SS])
    qch = din("qch", [128, NT])
    c_idb = din("c_idb", [128, 128], BF16)
    c_idf = din("c_idf", [128, 128])
    c_oneb = din("c_oneb", [128, 128], BF16)
    c_onef = din("c_onef", [128, 128])
    c_iota = din("c_iota", [128, 128])

    y = dout("y", [NTOK, D])
    o_k = dout("o_k", [SEQ, NKV * 128])
    o_v = dout("o_v", [SEQ, NKV * 128])
    o_ki = dout("o_ki", [SEQ, 128])
    o_conv = dout("o_conv", [30, CCH])
    o_mk = dout("o_mk", [MEMT, 512])
    o_mv = dout("o_mv", [MEMT, 512])
    o_ks = dout("o_ks", [2, 128, NKV * 128])
    o_vs = dout("o_vs", [2, 128, NKV * 128])
    o_kis = dout("o_kis", [2, 128, 128])
    o_convs = dout("o_convs", [2, 30, CCH])

    UTp = dscr("UTp", [CCH, 128 + NP * 128])
    UTs = dscr("UTs", [2, CCH, 160])
    KT = dscr("KT", [128, NKV, SEQ], BF16)
    Vc = dscr("Vc", [SEQ, NKV * 128], BF16)
    KIT = dscr("KIT", [128, SEQ], BF16)
    KTs = dscr("KTs", [2, 128, NKV, 128], BF16)
    Vs = dscr("Vs", [2, 128, NKV * 128], BF16)
    KITs = dscr("KITs", [2, 128, 128], BF16)
    MKT = dscr("MKT", [128, 4, MEMT], BF16)
    MV = dscr("MV", [MEMT, 512], BF16)
    QT = dscr("QT", [NT, 128, NH, 128], BF16)
    QIT = dscr("QIT", [NT, 128, IH, 128], BF16)
    WI = dscr("WI", [NT, 128, IH])
    MIXT = dscr("MIXT", [D, NTOK], BF16)
    H2 = dscr("H2", [NTOK, D])
    HN2T = dscr("HN2T", [D, NTOK], BF16)
    S12 = dscr("S12", [16, NTOK, 128])
    GALL = dscr("GALL", [128, 128, NTOK], BF16)

    UB16 = dscr("UB16", [128, 128, KC * 128], BF16)
    VB16 = dscr("VB16", [PEER_KEYS * PEER_KEYS, D], BF16)
    dbg_out = {}

    with contextlib.ExitStack() as es:
        P = Prog(nc, es)
        idb = P.sb([128, 128], BF16, "idb")
        idf = P.sb([128, 128], F32, "idf")
        oneb = P.sb([128, 128], BF16, "oneb")
        onef = P.sb([128, 128], F32, "onef")
        P.dma("sp", idb[:], c_idb)
        P.dma("sp", idf[:], c_idf)
        P.dma("sp", oneb[:], c_oneb)
        P.dma("sp", onef[:], c_onef)
        P.flush()

        def bcast_row(dst, src_row, n):
            P.dma("sp", dst, src_row.to_broadcast([128, n]))

        def rstd_from_ss(ss, n, out, tmp):
            P.ts("dve", tmp, ss, 1.0 / n, EPS, op0=ALU.mult, op1=ALU.add)
            P.act(tmp, tmp, AF.Sqrt)
            P.recip(out, tmp)

        def load_w(dst, src, ncols):
            sv = src.rearrange("(c p) n -> p c n", p=128)
            nq = 4 if KC % 4 == 0 else 1
            step = KC // nq
            for q in range(nq):
                P.dma("pool", dst[:, q * step:(q + 1) * step, 0:ncols], sv[:, q * step:(q + 1) * step, :], okey=(dst, q))

        def wkeys(dst):
            return [(dst, q) for q in range(4 if KC % 4 == 0 else 1)]

        class NormT:
            def __init__(self, gsrc, npt=2):
                self.npt = npt
                self.gbc = P.sb([128, D], F32)
                bcast_row(self.gbc[:], gsrc[0:1, :], D)
                self.sq = P.sb([128, D], BF16)
                self.xs = [P.sb([128, D], BF16) for _ in range(2)]
                self.sm = [P.sb([128, 4], F32) for _ in range(2)]
                self.pt = [P.ps([128, 1024], BF16) for _ in range(npt)]
                self.k = 0

            def run(self, x_t, dst_fn):
                k = self.k
                self.k += 1
                sm = self.sm[k % 2]
                xs = self.xs[k % 2]
                P.act(self.sq[:], x_t, AF.Square, accum_out=sm[:, 0:1])
                rstd_from_ss(sm[:, 0:1], D, sm[:, 1:2], sm[:, 2:3])
                P.stt(xs[:], x_t, sm[:, 1:2], self.gbc[:], ALU.mult, ALU.mult)
                nb = 8 if KC % 8 == 0 else KC
                for b0 in range(0, KC, nb):
                    pt = self.pt[(b0 // nb) % self.npt]
                    for j in range(nb):
                        P.tr(pt[:, j * 128:(j + 1) * 128], xs[:, (b0 + j) * 128:(b0 + j + 1) * 128], idb[:])
                    eng = "act" if (b0 // nb) % 2 == 0 else "dve"
                    P.copy(eng, dst_fn(b0, nb), pt[:, 0:nb * 128].rearrange("p (n t) -> p n t", t=128))

        def head_norm(ps_ap, nh, gain_bc, out_f, sq_t, sm_t):
            P.act(sq_t[:, 0:nh * 128], ps_ap, AF.Square)
            P.reduce(sm_t[:, 0:nh], sq_t[:, 0:nh * 128].rearrange("p (h d) -> p h d", d=128), ALU.add)
            rstd_from_ss(sm_t[:, 0:nh], 128, sm_t[:, 4:4 + nh], sm_t[:, 8:8 + nh])
            P.tt("dve", out_f, ps_ap.rearrange("p (h d) -> p h d", d=128),
                 sm_t[:, 4:4 + nh].unsqueeze(2).to_broadcast([128, nh, 128]), ALU.mult)
            P.tt("dve", out_f, out_f, gain_bc[:, 0:128].unsqueeze(1).to_broadcast([128, nh, 128]), ALU.mult)

        def rope(f, nh, cs, tmp):
            x1 = f[:, :, 0:16]
            x2 = f[:, :, 16:32]
            cosb = cs[:, 0:16].unsqueeze(1).to_broadcast([128, nh, 16])
            sinb = cs[:, 16:32].unsqueeze(1).to_broadcast([128, nh, 16])
            P.tt("dve", tmp[:, 0, 0:nh, :], x1, cosb, ALU.mult)
            P.tt("dve", tmp[:, 1, 0:nh, :], x2, sinb, ALU.mult)
            P.tt("dve", tmp[:, 2, 0:nh, :], x2, cosb, ALU.mult)
            P.tt("dve", tmp[:, 3, 0:nh, :], x1, sinb, ALU.mult)
            P.tt("dve", x1, tmp[:, 0, 0:nh, :], tmp[:, 1, 0:nh, :], ALU.subtract)
            P.tt("dve", x2, tmp[:, 2, 0:nh, :], tmp[:, 3, 0:nh, :], ALU.add)

        def tok_mm(ps_ap, hnT, tcol, wb, ncols, wk):
            for c in range(KC):
                P.mm(ps_ap, hnT[:, c, tcol:tcol + 128], wb[:, c, 0:ncols], start=(c == 0), stop=(c == KC - 1),
                     rkeys=[hnT] + wk)

        if "M" in stages:
            with contextlib.ExitStack() as st:
                P.st = st
                nt = NormT(g_mem)
                wk_b = P.sb([128, KC, 512], BF16)
                wv_b = P.sb([128, KC, 512], BF16)
                load_w(wk_b, w_km, 512)
                load_w(wv_b, w_vm, 512)
                gk = P.sb([128, 128], F32)
                bcast_row(gk[:], g_mk[0:1, :], 128)
                xt = [P.sb([128, D], F32) for _ in range(2)]
                hn = [P.sb([128, KC, 128], BF16) for _ in range(2)]
                pk = P.ps()
                pv = P.ps()
                ptr = P.ps([128, 1024], BF16)
                sq_t = P.sb([128, 512], F32)
                sm_t = P.sb([128, 12], F32)
                kf = P.sb([128, 4, 128], F32)
                kb = P.sb([128, 512], BF16)
                kTt = P.sb([128, 4, 128], BF16)
                vf = P.sb([128, 512], F32)
                vb = P.sb([128, 512], BF16)
                for m in range(MC):
                    x_t = xt[m % 2]
                    h_t = hn[m % 2]
                    P.dma("sp", x_t[:], mem[m * 128:(m + 1) * 128, :])
                    nt.run(x_t[:], lambda c0, n, h_t=h_t: h_t[:, c0:c0 + n, :])
                    tok_mm(pk[:, 0:512], h_t, 0, wk_b, 512, wkeys(wk_b))
                    tok_mm(pv[:, 0:512], h_t, 0, wv_b, 512, wkeys(wv_b))
                    head_norm(pk[:, 0:512], 4, gk, kf[:], sq_t, sm_t)
                    P.dma("sp", o_mk[m * 128:(m + 1) * 128, :], kf[:].rearrange("p h d -> p (h d)"))
                    P.copy("act", kb[:], kf[:].rearrange("p h d -> p (h d)"))
                    for h in range(4):
                        P.tr(ptr[:, h * 128:(h + 1) * 128], kb[:, h * 128:(h + 1) * 128], idb[:])
                    P.copy("dve", kTt[:], ptr[:, 0:512].rearrange("p (h t) -> p h t", t=128))
                    P.dma("sp", MKT[:, :, m * 128:(m + 1) * 128], kTt[:])
                    P.copy("act", vf[:], pv[:, 0:512])
                    P.dma("sp", o_mv[m * 128:(m + 1) * 128, :], vf[:])
                    P.copy("dve", vb[:], pv[:, 0:512])
                    P.dma("sp", MV[m * 128:(m + 1) * 128, :], vb[:])
                P.flush()
            P.st = es

        if "KV" in stages:
            with contextlib.ExitStack() as st:
                P.st = st
                nt = NormT(g_mix)
                wb = P.sb([128, KC, 1152], BF16)
                load_w(wb, w_kv, 1152)
                wk = wkeys(wb)
                gk = P.sb([128, 128], F32)
                bcast_row(gk[:], g_k[0:1, :], 128)
                xt = [P.sb([128, D], F32) for _ in range(2)]
                hn = [P.sb([128, KC, 128], BF16) for _ in range(2)]
                cs = [P.sb([128, 32], F32) for _ in range(2)]
                pk = P.ps()
                pv = P.ps()
                pki = P.ps()
                ptr = P.ps([128, 1024], BF16)
                sq_t = P.sb([128, 512], F32)
                sm_t = P.sb([128, 12], F32)
                rtmp = P.sb([128, 4, 4, 16], F32)
                kf = [P.sb([128, 4, 128], F32) for _ in range(2)]
                kb = P.sb([128, 512], BF16)
                kTt = [P.sb([128, 4, 128], BF16) for _ in range(2)]
                vf = [P.sb([128, 512], F32) for _ in range(2)]
                vb = [P.sb([128, 512], BF16) for _ in range(2)]
                kif = [P.sb([128, 1, 128], F32) for _ in range(2)]
                kib = P.sb([128, 128], BF16)
                kiTt = [P.sb([128, 128], BF16) for _ in range(2)]
                tiles = [("p", i) for i in range(NCX)] + [("s", 0), ("s", 1)]
                for n_, (kind, i) in enumerate(tiles):
                    x_t = xt[n_ % 2]
                    h_t = hn[n_ % 2]
                    c_t = cs[n_ % 2]
                    if kind == "p":
                        P.dma("sp", x_t[:], xctx[i * 128:(i + 1) * 128, :])
                        P.dma("sp", c_t[:], rope_c[i * 128:(i + 1) * 128, :])
                    else:
                        P.dma("sp", x_t[:], xsp[i])
                        P.dma("sp", c_t[:], rope_s)
                    nt.run(x_t[:], lambda c0, n, h_t=h_t: h_t[:, c0:c0 + n, :])
                    for c in range(KC):
                        P.mm(pk[:, 0:512], h_t[:, c, :], wb[:, c, 0:512], start=(c == 0), stop=(c == KC - 1), rkeys=[h_t] + wk)
                    for c in range(KC):
                        P.mm(pv[:, 0:512], h_t[:, c, :], wb[:, c, 512:1024], start=(c == 0), stop=(c == KC - 1), rkeys=[h_t] + wk)
                    for c in range(KC):
                        P.mm(pki[:, 0:128], h_t[:, c, :], wb[:, c, 1024:1152], start=(c == 0), stop=(c == KC - 1), rkeys=[h_t] + wk)
                    kf_t = kf[n_ % 2]
                    head_norm(pk[:, 0:512], 4, gk, kf_t[:], sq_t, sm_t)
                    rope(kf_t, 4, c_t, rtmp)
                    kflat = kf_t[:].rearrange("p h d -> p (h d)")
                    if kind == "p":
                        P.dma("sp", o_k[i * 128:(i + 1) * 128, :], kflat)
                    else:
                        P.dma("sp", o_ks[i], kflat)
                    P.copy("act", kb[:], kflat)
                    for h in range(4):
                        P.tr(ptr[:, h * 128:(h + 1) * 128], kb[:, h * 128:(h + 1) * 128], idb[:])
                    kT_t = kTt[n_ % 2]
                    P.copy("dve", kT_t[:], ptr[:, 0:512].rearrange("p (h t) -> p h t", t=128))
                    if kind == "p":
                        P.dma("sp", KT[:, :, i * 128:(i + 1) * 128], kT_t[:])
                    else:
                        P.dma("sp", KTs[i], kT_t[:])
                    vf_t = vf[n_ % 2]
                    vb_t = vb[n_ % 2]
                    P.copy("act", vf_t[:], pv[:, 0:512])
                    P.copy("dve", vb_t[:], pv[:, 0:512])
                    if kind == "p":
                        P.dma("sp", o_v[i * 128:(i + 1) * 128, :], vf_t[:])
                        P.dma("sp", Vc[i * 128:(i + 1) * 128, :], vb_t[:])
                    else:
                        P.dma("sp", o_vs[i], vf_t[:])
                        P.dma("sp", Vs[i], vb_t[:])
                    ki_t = kif[n_ % 2]
                    P.copy("act", ki_t[:, 0, :], pki[:, 0:128])
                    rope(ki_t, 1, c_t, rtmp)
                    if kind == "p":
                        P.dma("sp", o_ki[i * 128:(i + 1) * 128, :], ki_t[:, 0, :])
                    else:
                        P.dma("sp", o_kis[i], ki_t[:, 0, :])
                    P.copy("act", kib[:], ki_t[:, 0, :])
                    P.tr(ptr[:, 512:640], kib[:], idb[:])
                    kiT_t = kiTt[n_ % 2]
                    P.copy("dve", kiT_t[:], ptr[:, 512:640])
                    if kind == "p":
                        P.dma("sp", KIT[:, i * 128:(i + 1) * 128], kiT_t[:])
                    else:
                        P.dma("sp", KITs[i], kiT_t[:])
                P.flush()
            P.st = es

        own = [("h", -1)] + [("p", i) for i in range(NP)] + [("s", 0), ("s", 1)]
        groups = [own[i:i + 4] for i in range(0, len(own), 4)]

        def tile_index(kind, i):
            return i if kind == "p" else NP + i

        if "MAIN" in stages:
            with contextlib.ExitStack() as st:
                P.st = st
                nt = NormT(g_mix)
                gq = P.sb([128, 128], F32)
                bcast_row(gq[:], g_q[0:1, :], 128)
                for s in range(2):
                    P.dma("sp", UTs[s][:, 2:32], stT[s], okey=("UTs", "st"))
                xt = [P.sb([128, D], F32) for _ in range(2)]
                hnT = P.sb([128, KC, 512], BF16)
                wbuf = [P.sb([128, KC, 512], BF16) for _ in range(2)]
                cst = P.sb([128, 4, 32], F32)
                pa = P.ps()
                pg = P.ps()
                pq = [P.ps() for _ in range(2)]
                ptr = P.ps([128, 1024], BF16)
                sg = P.sb([128, 512], F32)
                ut = [P.sb([128, 512], F32) for _ in range(2)]
                sq_t = P.sb([128, 512], F32)
                sm_t = P.sb([128, 12], F32)
                rtmp = P.sb([128, 4, 4, 16], F32)
                qf = P.sb([128, 4, 128], F32)
                qb = P.sb([128, 512], BF16)
                qTt = [P.sb([128, 4, 128], BF16) for _ in range(2)]
                wis = [P.sb([128, IH], F32) for _ in range(2)]
                wcnt = 0
                xcnt = 0
                for grp in groups:
                    ng = len(grp)
                    N = ng * 128
                    for tt, (kind, i) in enumerate(grp):
                        x_t = xt[xcnt % 2]
                        xcnt += 1
                        if kind == "h":
                            P.dma("sp", x_t[:], xhalo)
                        elif kind == "p":
                            P.dma("sp", x_t[:], xctx[i * 128:(i + 1) * 128, :])
                            P.dma("sp", cst[:, tt, :], rope_c[i * 128:(i + 1) * 128, :], okey=(cst, tt))
                        else:
                            P.dma("sp", x_t[:], xsp[i])
                            P.dma("sp", cst[:, tt, :], rope_s, okey=(cst, tt))
                        nt.run(x_t[:], lambda c0, n, tt=tt: hnT[:, c0:c0 + n, tt * 128:(tt + 1) * 128])
                    for b in range(CC // 2):
                        wb = wbuf[wcnt % 2]
                        wcnt += 1
                        load_w(wb, w_glu[:, b * 512:(b + 1) * 512], 512)
                        wk = wkeys(wb)
                        for s in range(2):
                            j = 2 * b + s
                            for c in range(KC):
                                P.mm(pa[:, 0:N], wb[:, c, (2 * s) * 128:(2 * s + 1) * 128], hnT[:, c, 0:N],
                                     start=(c == 0), stop=(c == KC - 1), rkeys=[hnT] + wk)
                            for c in range(KC):
                                P.mm(pg[:, 0:N], wb[:, c, (2 * s + 1) * 128:(2 * s + 2) * 128], hnT[:, c, 0:N],
                                     start=(c == 0), stop=(c == KC - 1), rkeys=[hnT] + wk)
                            P.act(sg[:, 0:N], pg[:, 0:N], AF.Sigmoid)
                            u_t = ut[j % 2]
                            P.tt("dve", u_t[:, 0:N], pa[:, 0:N], sg[:, 0:N], ALU.mult)
                            for tt, (kind, i) in enumerate(grp):
                                src = u_t[:, tt * 128:(tt + 1) * 128]
                                if kind == "h":
                                    P.dma("sp", UTp[j * 128:(j + 1) * 128, 0:128], src, okey=("UTp", None))
                                elif kind == "p":
                                    P.dma("sp", UTp[j * 128:(j + 1) * 128, 128 + i * 128:128 + (i + 1) * 128], src, okey=("UTp", None))
                                else:
                                    P.dma("sp", UTs[i][j * 128:(j + 1) * 128, 32:160], src, okey=("UTs", "tok"))
                    for which, nblk, wsrc, dst in (("q", NH // 4, w_q, QT), ("qi", IH // 4, w_qi, QIT)):
                        for b in range(nblk):
                            wb = wbuf[wcnt % 2]
                            wcnt += 1
                            load_w(wb, wsrc[:, b * 512:(b + 1) * 512], 512)
                            wk = wkeys(wb)
                            real = [(tt, kind, i) for tt, (kind, i) in enumerate(grp) if kind != "h"]
                            for n2, (tt, kind, i) in enumerate(real):
                                if n2 == 0:
                                    tok_mm(pq[n2 % 2][:, 0:512], hnT, tt * 128, wb, 512, wk)
                                if n2 + 1 < len(real):
                                    tok_mm(pq[(n2 + 1) % 2][:, 0:512], hnT, real[n2 + 1][0] * 128, wb, 512, wk)
                                ti = tile_index(kind, i)
                                pq_t = pq[n2 % 2]
                                if which == "q":
                                    head_norm(pq_t[:, 0:512], 4, gq, qf[:], sq_t, sm_t)
                                else:
                                    P.copy("act", qf[:].rearrange("p h d -> p (h d)"), pq_t[:, 0:512])
                                rope(qf, 4, cst[:, tt, :], rtmp)
                                P.copy("act", qb[:], qf[:].rearrange("p h d -> p (h d)"))
                                for h in range(4):
                                    P.tr(ptr[:, h * 128:(h + 1) * 128], qb[:, h * 128:(h + 1) * 128], idb[:])
                                q_T = qTt[n2 % 2]
                                P.copy("dve", q_T[:], ptr[:, 0:512].rearrange("p (h t) -> p h t", t=128))
                                P.dma("sp", dst[ti][:, b * 4:(b + 1) * 4, :], q_T[:], okey=(dst.tensor.name, None))
                    wb = wbuf[wcnt % 2]
                    wcnt += 1
                    load_w(wb, w_wi, IH)
                    wk = wkeys(wb)
                    for tt, (kind, i) in enumerate(grp):
                        if kind == "h":
                            continue
                        ti = tile_index(kind, i)
                        pq_t = pq[tt % 2]
                        tok_mm(pq_t[:, 0:IH], hnT, tt * 128, wb, IH, wk)
                        w_s = wis[tt % 2]
                        P.act(w_s[:], pq_t[:, 0:IH], AF.Copy, scale=IDX_SCALE)
                        P.dma("sp", WI[ti], w_s[:], okey=("WI", None))
                P.flush()
            P.st = es

        if "B" in stages:
            with contextlib.ExitStack() as st:
                P.st = st
                P.npool["uin"] = 4
                wt = P.sb([128, CC, 31], F32)
                bt = P.sb([128, CC], F32)
                lg = P.sb([128, CC], F32)
                lb = P.sb([128, CC], F32)
                P.dma("sp", wt[:], dww)
                P.dma("sp", bt[:], dwb)
                P.dma("sp", lg[:], lng)
                P.dma("sp", lb[:], lnb)
                uin = P.sb([128, CC, 544], F32, "uin")
                cc_t = P.sb([128, CC, 512], F32)
                sqt = [P.sb([128, 512], F32) for _ in range(2)]
                p1 = P.ps()
                p2 = P.ps()
                mean = P.sb([128, 512], F32)
                var = P.sb([128, 512], F32)
                rstd = P.sb([128, 512], F32)
                tmp = [P.sb([128, 512], F32) for _ in range(2)]
                co = [P.sb([128, 512], BF16) for _ in range(2)]
                ptc = P.ps()
                cnew = P.sb([32, CCH], F32)
                jobs = []
                for tb in range(max(1, NP * 128 // 512)):
                    ntk = min(512, NP * 128)
                    jobs.append(("p", tb, ntk))
                jobs += [("s", 0, 128), ("s", 1, 128)]
                for kind, tb, ntk in jobs:
                    if kind == "p":
                        c0 = 128 + tb * ntk
                        src = UTp[:, c0 - 30:c0 + ntk].rearrange("(j p) t -> p j t", p=128)
                        mcol = tb * ntk
                        sk = "UTp"
                    else:
                        src = UTs[tb][:, 2:160].rearrange("(j p) t -> p j t", p=128)
                        mcol = (NP + tb) * 128
                        sk = "UTs"
                    W = 30 + ntk
                    P.dma("sp", uin[:, :, 0:W], src, ikey=sk)
                    last_p = (kind == "p" and (tb + 1) * ntk == NP * 128)
                    if last_p or kind == "s":
                        a0 = (30 + ntk - 30) if kind == "p" else (30 + 64 - 30)
                        for j0 in range(0, CC, 4):
                            for j in range(j0, min(CC, j0 + 4)):
                                P.tr(ptc[0:30, (j - j0) * 128:(j - j0 + 1) * 128], uin[:, j, a0:a0 + 30], idf[:])
                            nj = min(CC, j0 + 4) - j0
                            P.copy("act", cnew[0:30, j0 * 128:(j0 + nj) * 128], ptc[0:30, 0:nj * 128])
                        P.dma("sp", o_conv if kind == "p" else o_convs[tb], cnew[0:30, :])
                    for j in range(CC):
                        acc = cc_t[:, j, 0:ntk]
                        P.ts("dve", acc, uin[:, j, 0:ntk], wt[:, j, 0:1], bt[:, j:j + 1], op0=ALU.mult, op1=ALU.add, okey=(cc_t, j))
                        for k in range(1, 31):
                            P.stt(acc, uin[:, j, k:k + ntk], wt[:, j, k:k + 1], acc, ALU.mult, ALU.add, okey=(cc_t, j),
                                  rkeys=[uin, (cc_t, j)])
                        s_t = sqt[j % 2]
                        P.act(s_t[:, 0:ntk], acc, AF.Square, ikey=(cc_t, j))
                        P.mm(p1[:, 0:ntk], onef[:], acc, start=(j == 0), stop=(j == CC - 1), rkeys=[onef, (cc_t, j)])
                        P.mm(p2[:, 0:ntk], onef[:], s_t[:, 0:ntk], start=(j == 0), stop=(j == CC - 1))
                    P.ts("dve", mean[:, 0:ntk], p1[:, 0:ntk], 1.0 / CCH, None, op0=ALU.mult)
                    P.tt("dve", var[:, 0:ntk], mean[:, 0:ntk], mean[:, 0:ntk], ALU.mult)
                    P.stt(var[:, 0:ntk], p2[:, 0:ntk], 1.0 / CCH, var[:, 0:ntk], ALU.mult, ALU.subtract)
                    P.ts("dve", var[:, 0:ntk], var[:, 0:ntk], EPS, None, op0=ALU.add)
                    P.act(var[:, 0:ntk], var[:, 0:ntk], AF.Sqrt)
                    P.recip(rstd[:, 0:ntk], var[:, 0:ntk])
                    for j in range(CC):
                        t_ = tmp[j % 2]
                        P.tt("dve", t_[:, 0:ntk], cc_t[:, j, 0:ntk], mean[:, 0:ntk], ALU.subtract, rkeys=[(cc_t, j), mean])
                        P.tt("dve", t_[:, 0:ntk], t_[:, 0:ntk], rstd[:, 0:ntk], ALU.mult)
                        c_o = co[j % 2]
                        P.act(c_o[:, 0:ntk], t_[:, 0:ntk], AF.Silu, scale=lg[:, j:j + 1], bias=lb[:, j:j + 1])
                        P.dma("sp", MIXT[j * 128:(j + 1) * 128, mcol:mcol + ntk], c_o[:, 0:ntk], okey=("MIXT", "conv"))
                P.flush()
            P.st = es

        if "C" in stages:
            with contextlib.ExitStack() as st:
                P.st = st
                SMAX = max(SEQ, SS)
                P.npool["kiT_c"] = 4
                P.npool["kT_c"] = 4
                P.npool["v_c"] = 4
                kiT_c = P.sb([128, SMAX], BF16, "kiT_c")
                kT_c = P.sb([128, NKV, SMAX], BF16, "kT_c")
                v_c = P.sb([128, SMAX // 128, NKV * 128], BF16, "v_c")
                kc_t = P.sb([128, SMAX], BF16)
                qch_t = P.sb([128, NT], F32)
                P.dma("sp", qch_t[:], qch)
                qiT = [P.sb([128, IH, 128], BF16) for _ in range(2)]
                qT = [P.sb([128, NH, 128], BF16) for _ in range(2)]
                wi_t = [P.sb([128, IH], F32) for _ in range(2)]
                acc2 = [P.sb([128, SMAX], F32) for _ in range(2)]
                madd = P.sb([128, SMAX], BF16)
                bs = P.sb([128, 8], F32)
                m8 = P.sb([128, 256], F32)
                thr = P.sb([128, 1], F32)
                mask = P.sb([128, SMAX], BF16)
                maskT = P.sb([128, SMAX // 128, 128], BF16)
                rl = [P.sb([128, 512], F32) for _ in range(4)]
                pe_ = [P.sb([128, GQ, 128], BF16) for _ in range(3)]
                pm = [P.sb([128, GQ, 128], BF16) for _ in range(4)]
                rz = P.sb([128, GQ * 128], F32)
                ob = [P.sb([128, GQ, 128], BF16) for _ in range(2)]
                ps_s = [P.ps() for _ in range(2)]
                ps_qk = [P.ps() for _ in range(3)]
                ps_o = P.ps()
                ps_z = P.ps()
                ptr = [P.ps([128, 1024], BF16) for _ in range(1)]

                def seglist(blocks):
                    nb = len(blocks)
                    segs = []
                    a = 0
                    while a < nb:
                        b_ = a + 1
                        while b_ < nb and b_ - a < 4 and blocks[b_] == blocks[b_ - 1] + 1:
                            b_ += 1
                        segs.append((a, b_))
                        a = b_
                    return segs

                cnt = dict(n=0, it=0)
                P.npool["UB16"] = 3
                P.npool["VB16"] = 3
                pre_ops = []
                for c in range(0, 128, 2):
                    pre_ops.append(("u", c))
                    pre_ops.append(("v", c))
                n_slots = (NP + 2) * NKV
                per_slot = -(-len(pre_ops) // n_slots)

                def precast_some():
                    dd = min(2048, KC * 128)
                    for _ in range(per_slot):
                        if not pre_ops:
                            return
                        kind, c = pre_ops.pop(0)
                        if kind == "u":
                            P.dma("pool", UB16[c:c + 2].rearrange("c p (x d) -> p c x d", d=dd),
                                  uT[c:c + 2].rearrange("c p (x d) -> p c x d", d=dd), okey=("UB16", None))
                        else:
                            dv = min(2048, D)
                            P.dma("pool", VB16[c * 128:(c + 2) * 128, :].rearrange("(c p) (x d) -> p c x d", p=128, d=dv),
                                  vtab[c * 128:(c + 2) * 128, :].rearrange("(c p) (x d) -> p c x d", p=128, d=dv), okey=("VB16", None))

                def idx_units(job):
                    ti, blocks = job["ti"], job["blocks"]
                    if job.get("pre_idx"):
                        job["pre_idx"]()
                    k2 = ti % 2
                    acc = acc2[k2]
                    P.dma("sp", qiT[k2][:], QIT[ti], ikey="QIT")
                    P.dma("sp", wi_t[k2][:], WI[ti], ikey="WI")
                    segs = seglist(blocks)
                    N = len(blocks) * 128
                    for (a, b_) in segs:
                        w = (b_ - a) * 128
                        P.ts("pool", madd[:, a * 128:b_ * 128], kc_t[:, blocks[a] * 128:(blocks[a] + b_ - a) * 128],
                             qch_t[:, ti:ti + 1], NEG, op0=ALU.is_gt, op1=ALU.mult)
                        for h in range(IH):
                            n_ = cnt["n"]
                            cnt["n"] += 1
                            p_ = ps_s[n_ % 2]
                            r_ = rl[n_ % 4]
                            P.mm(p_[:, 0:w], qiT[k2][:, h, :], kiT_c[:, blocks[a] * 128:blocks[a] * 128 + w], rkeys=[qiT[k2], kiT_c])
                            P.act(r_[:, 0:w], p_[:, 0:w], AF.Relu)
                            if h == 0:
                                P.ts("dve", acc[:, a * 128:b_ * 128], r_[:, 0:w], wi_t[k2][:, 0:1], None, op0=ALU.mult)
                            else:
                                P.stt(acc[:, a * 128:b_ * 128], r_[:, 0:w], wi_t[k2][:, h:h + 1], acc[:, a * 128:b_ * 128], ALU.mult, ALU.add)
                            yield 1
                    P.reduce(bs[:, 0:1], acc[:, 0:N], ALU.min)
                    P.tt("pool", acc[:, 0:N], acc[:, 0:N], madd[:, 0:N], ALU.add)
                    P.max8(m8[:, 0:8], acc[:, 0:N])
                    P.copy("dve", bs[:, 1:2], m8[:, 0:1])

                def idx_phase(job):
                    for _ in idx_units(job):
                        pass

                def n_idx_units(job):
                    return len(seglist(job["blocks"])) * IH

                NITER = 22

                def topk_rounds(job, r0, r1):
                    ti, N, topk = job["ti"], len(job["blocks"]) * 128, job["topk"]
                    acc = acc2[ti % 2]
                    lo, hi, mid, tmp, cn, sel, dd = (bs[:, i:i + 1] for i in range(7))
                    for r in range(r0, min(r1, NITER)):
                        P.ts("dve", tmp, hi, 0.5, None, op0=ALU.mult)
                        P.stt(mid, lo, 0.5, tmp, ALU.mult, ALU.add)
                        P.ts("dve", mask[:, 0:N], acc[:, 0:N], mid, None, op0=ALU.is_ge, op1=ALU.add, accum_out=cn)
                        P.ts("dve", sel, cn, float(topk) - 0.5, None, op0=ALU.is_ge)
                        P.tt("dve", dd, mid, lo, ALU.subtract)
                        P.stt(lo, dd, sel, lo, ALU.mult, ALU.add)
                        P.tt("dve", dd, hi, mid, ALU.subtract)
                        P.stt(hi, dd, sel, mid, ALU.mult, ALU.add)

                def topk_final(job):
                    ti, blocks, topk = job["ti"], job["blocks"], job["topk"]
                    nb = len(blocks)
                    N = nb * 128
                    acc = acc2[ti % 2]
                    P.ts("dve", thr[:], bs[:, 0:1], 0.5 * NEG, None, op0=ALU.max)
                    P.ts("dve", mask[:, 0:N], acc[:, 0:N], thr[:, 0:1], None, op0=ALU.is_ge)
                    for b0 in range(0, nb, 8):
                        n8 = min(8, nb - b0)
                        pt = ptr[0]
                        for j in range(n8):
                            P.tr(pt[:, j * 128:(j + 1) * 128], mask[:, (b0 + j) * 128:(b0 + j + 1) * 128], idb[:])
                        P.copy("act", maskT[:, b0:b0 + n8, :], pt[:, 0:n8 * 128].rearrange("p (n t) -> p n t", t=128))

                def attn_steps(job, g):
                    ti, blocks = job["ti"], job["blocks"]
                    k2 = ti % 2
                    nb = len(blocks)
                    if g == 0:
                        if job.get("pre_attn"):
                            job["pre_attn"]()
                        P.dma("sp", qT[k2][:], QT[ti], ikey="QT")
                    W = GQ * 128
                    LA = 2
                    bufs = {}

                    def front(ci):
                        blk = blocks[ci]
                        it = cnt["it"]
                        cnt["it"] += 1
                        pq_ = ps_qk[it % 3]
                        e_ = pe_[it % 3]
                        m_ = pm[it % 4]
                        bufs[ci] = m_
                        P.mm(pq_[:, 0:W], kT_c[:, g, blk * 128:(blk + 1) * 128],
                             qT[k2][:, g * GQ:(g + 1) * GQ, :].rearrange("p r t -> p (r t)"), rkeys=[kT_c, qT[k2]])
                        P.act(e_[:].rearrange("p r t -> p (r t)"), pq_[:, 0:W], AF.Exp, scale=ATT_SCALE)
                        P.tt("pool", m_[:], e_[:], maskT[:, ci, :].unsqueeze(1).to_broadcast([128, GQ, 128]), ALU.mult)

                    def back(ci):
                        blk = blocks[ci]
                        m_ = bufs[ci]
                        mf = m_[:].rearrange("p r t -> p (r t)")
                        P.mm(ps_o[:, 0:W], v_c[:, blk, g * 128:(g + 1) * 128], mf, start=(ci == 0), stop=(ci == nb - 1), rkeys=[v_c, m_])
                        P.mm(ps_z[:, 0:W], oneb[:], mf, start=(ci == 0), stop=(ci == nb - 1))

                    for ci in range(min(LA, nb)):
                        front(ci)
                    for ci in range(nb):
                        if ci + LA < nb:
                            front(ci + LA)
                        back(ci)
                        yield 1
                    P.recip(rz[:, 0:W], ps_z[:, 0:W])
                    o_ = ob[g % 2]
                    P.tt("dve", o_[:].rearrange("p r t -> p (r t)"), ps_o[:, 0:W], rz[:, 0:W], ALU.mult)
                    P.dma("sp", MIXT[CCH + g * GQ * 128:CCH + (g + 1) * GQ * 128, ti * 128:(ti + 1) * 128].rearrange("(r d) t -> d r t", d=128),
                          o_[:], okey=("MIXT", "attn"))

                def attn_group(job, g):
                    for _ in attn_steps(job, g):
                        pass

                def load_prompt_ki():
                    P.cdma(kc_t[:, 0:SEQ], kc_p[0:1, :].to_broadcast([128, SEQ]))
                    P.dma("sp", kiT_c[:, 0:SEQ], KIT, ikey="KIT", okey=(kiT_c, 0))

                def load_prompt_kv():
                    P.dma("sp", kT_c[:, :, 0:SEQ], KT, ikey="KT", okey=(kT_c, 0))
                    P.dma("sp", v_c[:, 0:NCX, :], Vc.rearrange("(c p) n -> p c n", p=128), ikey="Vc", okey=(v_c, 0))

                def mk_sample_ki(s):
                    def f():
                        P.cdma(kc_t[:, 0:SS], kc_s[0:1, :].to_broadcast([128, SS]))
                        P.cdma(kiT_c[:, 0:PAST], ckiT[s], okey=(kiT_c, 0))
                        P.dma("sp", kiT_c[:, PAST:SS], KITs[s], ikey="KITs", okey=(kiT_c, 1))
                    return f

                def mk_sample_kv(s):
                    def f():
                        for g in range(NKV):
                            P.cdma(kT_c[:, g, 0:PAST], ckT[s][:, g, :], okey=(kT_c, 0))
                        P.dma("sp", kT_c[:, :, PAST:SS], KTs[s], ikey="KTs", okey=(kT_c, 1))
                        cvv = cv[s].rearrange("(c p) n -> p c n", p=128)
                        nq = 4 if (PAST // 128) % 4 == 0 else 1
                        stp = (PAST // 128) // nq
                        for q in range(nq):
                            P.dma("pool", v_c[:, q * stp:(q + 1) * stp, :], cvv[:, q * stp:(q + 1) * stp, :], okey=(v_c, 0))
                        P.dma("sp", v_c[:, PAST // 128, :], Vs[s], ikey="Vs", okey=(v_c, 1))
                    return f

                jobs = []
                for i in range(NP):
                    jobs.append(dict(ti=i, blocks=list(range(0, i + 1)) + list(range(NP, 2 * NP)), topk=cfg["TOPK_P"]))
                jobs[0]["pre_idx"] = load_prompt_ki
                jobs[0]["pre_attn"] = load_prompt_kv
                for s in range(2):
                    jobs.append(dict(ti=NP + s, blocks=list(range(SS // 128)), topk=cfg["TOPK_S"],
                                     pre_idx=mk_sample_ki(s), pre_attn=mk_sample_kv(s)))
                idx_phase(jobs[0])
                topk_rounds(jobs[0], 0, NITER)
                topk_final(jobs[0])
                for k, job in enumerate(jobs):
                    nxt = jobs[k + 1] if k + 1 < len(jobs) else None
                    nbk = len(job["blocks"])
                    gen_i = idx_units(nxt) if nxt is not None else None
                    half = NKV // 2
                    if nxt is not None:
                        per_step = -(-n_idx_units(nxt) // (half * nbk))
                    for g in range(half):
                        precast_some()
                        for _ in attn_steps(job, g):
                            if gen_i is not None:
                                for _u in range(per_step):
                                    if next(gen_i, None) is None:
                                        gen_i = None
                                        break
                    if gen_i is not None:
                        for _ in gen_i:
                            pass
                    it_done = 0
                    for g in range(half, NKV):
                        precast_some()
                        for bi, _ in enumerate(attn_steps(job, g)):
                            if nxt is not None and bi % 3 == 0 and it_done < NITER:
                                topk_rounds(nxt, it_done, it_done + 1)
                                it_done += 1
                    if nxt is not None:
                        topk_rounds(nxt, it_done, NITER)
                        topk_final(nxt)
                while pre_ops:
                    precast_some()
                P.flush()
            P.st = es

        otiles = list(range(NT))
        ogroups = [otiles[i:i + 4] for i in range(0, NT, 4)]
        Hs = dscr("Hs", [NTOK, D])
        RC = dscr("RC", [NT, 128, 4, 128])

        def x_rows(ti):
            return xctx[ti * 128:(ti + 1) * 128, :] if ti < NP else xsp[ti - NP]

        if "D" in stages:
            with contextlib.ExitStack() as st:
                P.st = st
                mixT = P.sb([128, KC, 512], BF16)
                wbuf = [P.sb([128, KC, 512], BF16) for _ in range(2)]
                xb_ = [P.sb([128, 512], F32) for _ in range(3)]
                hb_ = [P.sb([128, 512], F32) for _ in range(3)]
                pp = [P.ps() for _ in range(4)]
                wcnt = 0
                k_ = 0
                for grp in ogroups:
                    N = len(grp) * 128
                    c0 = grp[0] * 128
                    P.dma("sp", mixT[:, :, 0:N], MIXT[:, c0:c0 + N].rearrange("(c p) n -> p c n", p=128), ikey="MIXT")
                    for b in range(D // 512):
                        wb = wbuf[wcnt % 2]
                        wcnt += 1
                        load_w(wb, w_out[:, b * 512:(b + 1) * 512], 512)
                        wk = wkeys(wb)
                        for tt, ti in enumerate(grp):
                            xb = xb_[k_ % 3]
                            hb = hb_[k_ % 3]
                            p_ = pp[k_ % 4]
                            k_ += 1
                            P.dma("sp", xb[:], x_rows(ti)[:, b * 512:(b + 1) * 512])
                            tok_mm(p_[:, 0:512], mixT, tt * 128, wb, 512, wk)
                            P.tt("dve", hb[:], p_[:, 0:512], xb[:], ALU.add)
                            P.dma("sp", Hs[ti * 128:(ti + 1) * 128, b * 512:(b + 1) * 512], hb[:], okey=("Hs", None))
                P.flush()
            P.st = es

            with contextlib.ExitStack() as st:
                P.st = st
                nt = NormT(g_memn)
                gbc2 = P.sb([128, D], F32)
                bcast_row(gbc2[:], g_ffn[0:1, :], D)
                gmq = P.sb([128, 128], F32)
                bcast_row(gmq[:], g_mq[0:1, :], 128)
                wqm_b = P.sb([128, KC, 512], BF16)
                load_w(wqm_b, w_qm, 512)
                wom_b = P.sb([128, 4, D], BF16)
                P.cdma(wom_b[:], w_om.rearrange("(h p) d -> p h d", p=128))
                mkT_c = P.sb([128, 4, MEMT], BF16)
                mv_c = P.sb([128, MC, 512], BF16)
                ht = [P.sb([128, D], F32) for _ in range(2)]
                hn = [P.sb([128, KC, 128], BF16) for _ in range(2)]
                pq_ = P.ps()
                pl_ = [P.ps() for _ in range(2)]
                po_ = P.ps()
                pz_ = pq_
                pw_ = [P.ps() for _ in range(1)]
                ptr = P.ps([128, 1024], BF16)
                sq_t = P.sb([128, 512], F32)
                sm_t = P.sb([128, 12], F32)
                qmf = P.sb([128, 4, 128], F32)
                qmb = P.sb([128, 512], BF16)
                qmT = P.sb([128, 4, 128], BF16)
                pmT = [P.sb([128, 4, 128], BF16) for _ in range(MC)]
                rz = P.sb([128, 512], F32)
                omT = P.sb([128, 4, 128], BF16)
                for ti in otiles:
                    if ti == 0:
                        P.dma("sp", mkT_c[:], MKT, ikey="MKT")
                        P.dma("sp", mv_c[:], MV.rearrange("(c p) n -> p c n", p=128), ikey="MV")
                    elif ti >= NP:
                        P.dma("pool", mkT_c[:], cmkT[ti - NP])
                        P.dma("pool", mv_c[:], cmv[ti - NP].rearrange("(c p) n -> p c n", p=128))
                    h_t = ht[ti % 2]
                    hn_t = hn[ti % 2]
                    P.dma("sp", h_t[:], Hs[ti * 128:(ti + 1) * 128, :], ikey="Hs")
                    nt.run(h_t[:], lambda c0, n, hn_t=hn_t: hn_t[:, c0:c0 + n, :])
                    tok_mm(pq_[:, 0:512], hn_t, 0, wqm_b, 512, wkeys(wqm_b))
                    head_norm(pq_[:, 0:512], 4, gmq, qmf[:], sq_t, sm_t)
                    P.copy("act", qmb[:], qmf[:].rearrange("p h d -> p (h d)"))
                    for h in range(4):
                        P.tr(ptr[:, h * 128:(h + 1) * 128], qmb[:, h * 128:(h + 1) * 128], idb[:])
                    P.copy("dve", qmT[:], ptr[:, 0:512].rearrange("p (h t) -> p h t", t=128))
                    for mc in range(MC):
                        for h in range(4):
                            P.mm(pl_[mc % 2][:, h * 128:(h + 1) * 128], mkT_c[:, h, mc * 128:(mc + 1) * 128], qmT[:, h, :])
                        P.act(pmT[mc][:].rearrange("p h t -> p (h t)"), pl_[mc % 2][:, 0:512], AF.Exp, scale=ATT_SCALE)
                    for h in range(4):
                        for mc in range(MC):
                            P.mm(po_[:, h * 128:(h + 1) * 128], mv_c[:, mc, h * 128:(h + 1) * 128], pmT[mc][:, h, :],
                                 start=(mc == 0), stop=(mc == MC - 1))
                    for mc in range(MC):
                        P.mm(pz_[:, 0:512], oneb[:], pmT[mc][:].rearrange("p h t -> p (h t)"), start=(mc == 0), stop=(mc == MC - 1))
                    P.recip(rz[:], pz_[:, 0:512])
                    P.tt("dve", omT[:].rearrange("p h t -> p (h t)"), po_[:, 0:512], rz[:], ALU.mult)
                    for b in range(D // 512):
                        p_ = pw_[0]
                        for h in range(4):
                            P.mm(p_[:, 0:512], omT[:, h, :], wom_b[:, h, b * 512:(b + 1) * 512], start=(h == 0), stop=(h == 3))
                        P.tt("dve", h_t[:, b * 512:(b + 1) * 512], p_[:, 0:512], h_t[:, b * 512:(b + 1) * 512], ALU.add)
                    P.dma("sp", H2[ti * 128:(ti + 1) * 128, :], h_t[:], okey=("H2", None))
                    nt.gbc, g_save = gbc2, nt.gbc
                    nt.run(h_t[:], lambda c0, n, hn_t=hn_t: hn_t[:, c0:c0 + n, :])
                    nt.gbc = g_save
                    P.dma("sp", HN2T[:, ti * 128:(ti + 1) * 128].rearrange("(c p) t -> p c t", p=128), hn_t[:], okey=("HN2T", None))
                P.flush()
            P.st = es

            with contextlib.ExitStack() as st:
                P.st = st
                hn2 = P.sb([128, KC, 512], BF16)
                wbuf = [P.sb([128, KC, 512], BF16) for _ in range(2)]
                qpT = P.sb([128, 16, 512], F32)
                sk_t = P.sb([128, 16, 128], F32)
                P.dma("sp", sk_t[:], subk)
                pq_ = [P.ps() for _ in range(2)]
                ps_ = [P.ps() for _ in range(2)]
                ptf = P.ps()
                s12 = [P.sb([128, 16, 128], F32) for _ in range(2)]
                v16 = P.sb([128, 16, 16], F32)
                tmp128 = P.sb([128, 128], F32)
                cand = P.sb([128, 8, 256], F32)
                tmpc = P.sb([128, 256], F32)
                t16 = P.sb([128, 8, 16], F32)
                e16 = P.sb([128, 8, 16], F32)
                zz = P.sb([128, 8], F32)
                mlz = P.sb([128, 8], F32)
                rc3 = P.sb([128, 4, 8, 16], F32)
                rcT = [P.sb([128, 4, 128], F32) for _ in range(2)]
                idxu = P.sb([128, 8, 16], mybir.dt.uint32)
                wcnt = 0
                for grp in ogroups:
                    N = len(grp) * 128
                    c0 = grp[0] * 128
                    P.dma("sp", hn2[:, :, 0:N], HN2T[:, c0:c0 + N].rearrange("(c p) n -> p c n", p=128), ikey="HN2T")
                    for b in range(4):
                        wb = wbuf[wcnt % 2]
                        wcnt += 1
                        load_w(wb, w_pq[:, b * 512:(b + 1) * 512], 512)
                        wk = wkeys(wb)
                        for jj in range(4):
                            j = b * 4 + jj
                            p_ = pq_[j % 2]
                            for c in range(KC):
                                P.mm(p_[:, 0:N], wb[:, c, jj * 128:(jj + 1) * 128], hn2[:, c, 0:N], start=(c == 0), stop=(c == KC - 1),
                                     rkeys=[hn2] + wk)
                            P.copy("act", qpT[:, j, 0:N], p_[:, 0:N], okey=(qpT, j))
                    for tt, ti in enumerate(grp):
                        s_t = s12[ti % 2]
                        for jb in range(4):
                            p_ = ps_[jb % 2]
                            for jj in range(4):
                                j = jb * 4 + jj
                                P.mm(p_[:, jj * 128:(jj + 1) * 128], qpT[:, j, tt * 128:(tt + 1) * 128], sk_t[:, j, :], rkeys=[(qpT, j), sk_t])
                            P.copy("act", s_t[:, jb * 4:(jb + 1) * 4, :].rearrange("p j k -> p (j k)"), p_[:, 0:512])
                        P.dma("sp", S12[:, ti * 128:(ti + 1) * 128, :].rearrange("j t i -> t j i"), s_t[:], okey=("S12", None))
                        for j in range(16):
                            P.max8(v16[:, j, 0:8], s_t[:, j, :])
                            P.mrep(tmp128[:], v16[:, j, 0:8], s_t[:, j, :], -3.0e38)
                            P.max8(v16[:, j, 8:16], tmp128[:])
                            if j % 2 == 0:
                                hh = j // 2
                                P.op("dve", lambda e, hh=hh, j=j, s_t=s_t: e.max_index(idxu[:, hh, 0:8], v16[:, j, 0:8], s_t[:, j, :]),
                                     reads=[v16, s_t], writes=[idxu])
                                P.op("dve", lambda e, hh=hh, j=j: e.max_index(idxu[:, hh, 8:16], v16[:, j, 8:16], tmp128[:]),
                                     reads=[v16, tmp128], writes=[idxu])
                        v16v = v16[:].rearrange("p (h two) k -> p h two k", two=2)
                        for h in range(8):
                            P.tt("dve", cand[:, h, :].rearrange("p (a b) -> p a b", b=16),
                                 v16[:, 2 * h, :].unsqueeze(2).to_broadcast([128, 16, 16]),
                                 v16[:, 2 * h + 1, :].unsqueeze(1).to_broadcast([128, 16, 16]), ALU.add)
                        for h in range(8):
                            P.max8(t16[:, h, 0:8], cand[:, h, :])
                            P.mrep(tmpc[:], t16[:, h, 0:8], cand[:, h, :], -3.0e38)
                            P.max8(t16[:, h, 8:16], tmpc[:])
                        P.tt("dve", e16[:], t16[:], t16[:, :, 0:1].to_broadcast([128, 8, 16]), ALU.subtract)
                        P.act(e16[:], e16[:], AF.Exp)
                        P.reduce(zz[:], e16[:], ALU.add)
                        P.act(mlz[:], zz[:], AF.Ln)
                        P.tt("dve", mlz[:], mlz[:], t16[:, :, 0], ALU.add)
                        P.copy("dve", rc3[:, 0, :, :], v16v[:, :, 0, :])
                        P.tt("dve", rc3[:, 1, :, :], t16[:, :, 15:16].to_broadcast([128, 8, 16]), rc3[:, 0, :, :], ALU.subtract)
                        P.tt("dve", rc3[:, 2, :, :], rc3[:, 0, :, :], mlz[:].unsqueeze(2).to_broadcast([128, 8, 16]), ALU.subtract)
                        P.copy("dve", rc3[:, 3, :, :], idxu[:])
                        for q in range(4):
                            P.tr(ptf[:, q * 128:(q + 1) * 128], rc3[:, q, :, :].rearrange("p h a -> p (h a)"), idf[:])
                        r_T = rcT[ti % 2]
                        P.copy("act", r_T[:].rearrange("p q t -> p (q t)"), ptf[:, 0:512])
                        P.dma("sp", RC[ti], r_T[:], okey=("RC", None))
                P.flush()
            P.st = es

            with contextlib.ExitStack() as st:
                P.st = st
                TB = 32
                iota_t = P.sb([128, 128], F32)
                P.dma("sp", iota_t[:], c_iota)
                s2r = [P.sb([128, TB, 128], F32) for _ in range(2)]
                rct = [P.sb([128, 4, 128], F32) for _ in range(2)]
                o1 = [P.sb([128, 128], BF16) for _ in range(8)]
                ee = [P.sb([128, 128], F32) for _ in range(8)]
                rr = [P.sb([128, 128], BF16) for _ in range(8)]
                gst = [P.sb([128, 128, 128], BF16) for _ in range(2)]
                pg_ = [P.ps() for _ in range(3)]
                kk = 0
                pend = [None]
                for ti in otiles:
                    rc_ = rct[ti % 2]
                    g_s = gst[ti % 2]
                    P.dma("sp", rc_[:], RC[ti], ikey="RC")
                    for tb in range(128 // TB):
                        t0 = ti * 128 + tb * TB
                        a2 = s2r[tb % 2]
                        src = S12[:, t0:t0 + TB, :].rearrange("(h two) t i -> two h (t i)", two=2)[1]
                        P.dma("sp", a2[:].rearrange("p t i -> p (t i)"), src.unsqueeze(1).to_broadcast([8, 16, TB * 128]),
                              ikey="S12", okey=(a2, None))
                        for tq in range(0, TB, 4):
                            p_ = pg_[(kk) % 3]
                            kk += 1
                            for u4 in range(4):
                                tl = tq + u4
                                t = tb * TB + tl
                                o_ = o1[(kk % 2) * 4 + u4]
                                e_ = ee[(kk % 2) * 4 + u4]
                                r_ = rr[(kk % 2) * 4 + u4]
                                P.ts("dve", o_[:], iota_t[:], rc_[:, 3, t:t + 1], None, op0=ALU.is_equal)
                                P.act(e_[:], a2[:, tl, :], AF.Exp, bias=rc_[:, 2, t:t + 1])
                                P.stt(r_[:], a2[:, tl, :], rc_[:, 1, t:t + 1], e_[:], ALU.is_ge, ALU.mult)
                                P.mm(p_[:, u4 * 128:(u4 + 1) * 128], o_[:], r_[:])
                            if pend[0] is not None:
                                pend[0]()
                            tbase = tb * TB + tq

                            def evac(p_=p_, tbase=tbase, g_s=g_s):
                                P.copy("act", g_s[:, :, tbase:tbase + 4].rearrange("p i t -> p t i"),
                                       p_[:, 0:512].rearrange("p (t i) -> p t i", i=128))
                            pend[0] = evac
                    pend[0]()
                    pend[0] = None
                    P.dma("sp", GALL[:, :, ti * 128:(ti + 1) * 128], g_s[:], okey=("GALL", None))
                P.flush()
            P.st = es

        if "E" in stages:
            with contextlib.ExitStack() as st:
                P.st = st
                NCH = PEER_KEYS
                EB = 4
                hn2 = P.sb([128, KC, 512], BF16)
                oacc = P.sb([128, 4, D], F32)
                ub = [P.sb([128, KC, 128], BF16) for _ in range(3)]
                vb = [P.sb([128, EB, D], BF16) for _ in range(2)]
                coef = [P.sb([128, EB, 512], BF16) for _ in range(2)]
                gl = [P.sb([128, 512], BF16) for _ in range(2)]
                gc = [P.sb([128, 512], BF16) for _ in range(4)]
                pa_ = [P.ps() for _ in range(2)]
                pv_ = [P.ps() for _ in range(4)]
                ucnt = 0
                vcnt = 0
                pcnt = 0
                DH = 2048 if D % 2048 == 0 else D
                for grp in ogroups:
                    ng = len(grp)
                    N = ng * 128
                    c0 = grp[0] * 128
                    P.dma("sp", hn2[:, :, 0:N], HN2T[:, c0:c0 + N].rearrange("(c p) n -> p c n", p=128), ikey="HN2T")
                    for tt, ti in enumerate(grp):
                        P.dma("sp", oacc[:, tt, :], H2[ti * 128:(ti + 1) * 128, :], ikey="H2", okey=(oacc, tt))
                    def v_load(eb):
                        v_b = vb[eb % 2]
                        vsrc = VB16[eb * EB * 128:(eb + 1) * EB * 128, :].rearrange("(cc p) d -> p cc d", p=128)
                        P.dma("sp", v_b[:], vsrc, ikey="VB16")

                    loaded = set()

                    def u_load(gidx):
                        if gidx >= NCH or gidx in loaded:
                            return
                        loaded.add(gidx)
                        k_ = ucnt + gidx
                        P.dma("sp", ub[k_ % 3][:].rearrange("p c e -> p (c e)"), UB16[gidx], ikey="UB16")
                        P.dma("sp", gc[k_ % 4][:, 0:N], GALL[gidx][:, c0:c0 + N], ikey="GALL")

                    def u_phase(eb):
                        cf = coef[eb % 2]
                        for cc in range(EB):
                            c = eb * EB + cc
                            u_load(c)
                            u_load(c + 1)
                            u_load(c + 2)
                            k_ = ucnt + c
                            u_b = ub[k_ % 3]
                            g_l = gl[k_ % 2]
                            g_c = gc[k_ % 4]
                            p_ = pa_[k_ % 2]
                            for dc in range(KC):
                                P.mm(p_[:, 0:N], u_b[:, dc, :], hn2[:, dc, 0:N], start=(dc == 0), stop=(dc == KC - 1))
                            P.act(g_l[:, 0:N], p_[:, 0:N], AF.Gelu)
                            P.tt("dve", cf[:, cc, 0:N], g_l[:, 0:N], g_c[:, 0:N], ALU.mult, okey=(cf, cc))

                    def v_phase(eb):
                        nonlocal pcnt
                        cf = coef[eb % 2]
                        v_b = vb[eb % 2]
                        for tt in range(ng):
                            for db in range(D // 512):
                                pv = pv_[pcnt % 4]
                                pcnt += 1
                                for cc in range(EB):
                                    P.mm(pv[:, 0:512], cf[:, cc, tt * 128:(tt + 1) * 128], v_b[:, cc, db * 512:(db + 1) * 512],
                                         start=(cc == 0), stop=(cc == EB - 1), rkeys=[(cf, cc), v_b])
                                P.tt("dve", oacc[:, tt, db * 512:(db + 1) * 512], pv[:, 0:512], oacc[:, tt, db * 512:(db + 1) * 512], ALU.add,
                                     okey=(oacc, tt), rkeys=[pv, (oacc, tt)])

                    nE = NCH // EB
                    u_load(0)
                    u_load(1)
                    v_load(0)
                    u_phase(0)
                    for eb in range(nE):
                        if eb + 1 < nE:
                            v_load(eb + 1)
                            u_phase(eb + 1)
                        v_phase(eb)
                    ucnt += NCH
                    for tt, ti in enumerate(grp):
                        P.dma("sp", y[ti * 128:(ti + 1) * 128, :], oacc[:, tt, :], ikey=(oacc, tt), okey=("y", None))
                P.flush()
            P.st = es

        if dbg:
            for nm, ap_ in (("MIXT", MIXT), ("UTp", UTp), ("UTs", UTs), ("QT", QT), ("QIT", QIT), ("WI", WI), ("KT", KT), ("KIT", KIT),
                            ("Vc", Vc), ("H2", H2), ("HN2T", HN2T), ("S12", S12), ("GALL", GALL), ("MKT", MKT), ("MV", MV)):
                if nm in dbg:
                    o_ = dout("dbg_" + nm, list(ap_.shape), ap_.dtype)
                    P.dma("sp", o_, ap_)
        P.flush()
    return nc


def _rope_table(pos):
    half = 16
    inv_freq = np.power(np.float32(ROPE_THETA), -np.arange(half, dtype=np.float32) / np.float32(half)).astype(np.float32)
    ang = pos.astype(np.float32)[:, None] * inv_freq[None, :]
    return np.concatenate([np.cos(ang), np.sin(ang)], axis=1).astype(np.float32)


def host_prep(inp, cfg):
    D, KC, CCH, CC, NH, NKV, NP, NT, IH, SEQ, PAST, SS, MEMT = (cfg[k] for k in (
        "D", "KC", "CCH", "CC", "NH", "NKV", "NP", "NT", "IH", "SEQ", "PAST", "SS", "MEMT"))
    DS = cfg["DS"]
    f = lambda a: np.ascontiguousarray(a, dtype=np.float32)
    half = SEQ // 2
    w_in = inp["w_in"][0]
    OFF_Q = 2 * CCH
    OFF_K = OFF_Q + NH * 128
    OFF_V = OFF_K + NKV * 128
    OFF_QI = OFF_V + NKV * 128
    OFF_KI = OFF_QI + IH * 128
    OFF_WI = OFF_KI + 128
    a_ = w_in[:, :CCH].reshape(D, CC, 128)
    g_ = w_in[:, CCH:2 * CCH].reshape(D, CC, 128)
    w_glu = f(np.stack([a_, g_], axis=2).reshape(D, 2 * CCH))
    shared = dict(
        w_glu=w_glu,
        w_q=f(w_in[:, OFF_Q:OFF_K]),
        w_qi=f(w_in[:, OFF_QI:OFF_KI]),
        w_wi=f(w_in[:, OFF_WI:OFF_WI + IH]),
        w_kv=f(np.concatenate([w_in[:, OFF_K:OFF_V], w_in[:, OFF_V:OFF_QI], w_in[:, OFF_KI:OFF_WI]], axis=1)),
        w_out=f(inp["w_out"][0]),
        w_qm=f(inp["w_q_mem"][0]), w_km=f(inp["w_k_mem"][0]), w_vm=f(inp["w_v_mem"][0]), w_om=f(inp["w_o_mem"][0]),
        w_pq=f(inp["peer_wq"][0]),
        g_mix=f(inp["norm_mix_g"]), g_memn=f(inp["norm_mem_g"]), g_ffn=f(inp["norm_ffn_g"]), g_mem=f(inp["mem_norm_g"]),
        g_q=f(inp["q_norm_g"]), g_k=f(inp["k_norm_g"]), g_mq=f(inp["mem_q_norm_g"]), g_mk=f(inp["mem_k_norm_g"]),
        dww=f(inp["dw_w"][0].reshape(31, CC, 128).transpose(2, 1, 0)),
        dwb=f(inp["dw_b"][0].reshape(CC, 128).T), lng=f(inp["conv_ln_g"][0].reshape(CC, 128).T),
        lnb=f(inp["conv_ln_b"][0].reshape(CC, 128).T),
        vtab=f(inp["peer_v"][0]),
        c_idb=np.eye(128).astype(ml_dtypes.bfloat16), c_idf=np.eye(128, dtype=np.float32),
        c_oneb=np.ones((128, 128)).astype(ml_dtypes.bfloat16), c_onef=np.ones((128, 128), dtype=np.float32),
        c_iota=np.ascontiguousarray(np.tile(np.arange(128, dtype=np.float32)[None, :], (128, 1))),
    )
    sk = np.stack([inp["peer_sub_k1"][0], inp["peer_sub_k2"][0]], axis=1)
    shared["subk"] = f(sk.reshape(16, 128, 128).transpose(2, 0, 1))
    u = inp["peer_u"][0]
    shared["uT"] = f(u.reshape(128, 128, KC, 128).transpose(0, 3, 2, 1).reshape(128, 128, KC * 128))
    kcs = (np.arange(SS) // 64).astype(np.float32)
    kcs[PAST + DS:] = 1.0e9
    shared["kc_s"] = kcs[None, :]
    shared["rope_s"] = _rope_table(PAST + np.arange(128))
    maps = []
    for c in range(8):
        b, hf = c // 2, c % 2
        xb = inp["x_prompt"][b]
        own = xb[hf * half:(hf + 1) * half]
        oth = xb[(1 - hf) * half:(2 - hf) * half]
        pos = np.concatenate([hf * half + np.arange(half), (1 - hf) * half + np.arange(half)])
        m = dict(shared)
        m["xctx"] = f(np.concatenate([own, oth], axis=0))
        m["xhalo"] = f(xb[half - 128:half]) if hf == 1 else np.zeros((128, D), np.float32)
        xsp = np.zeros((2, 128, D), np.float32)
        for s in range(2):
            xsp[s, :DS] = inp["x_sample"][2 * c + s]
        m["xsp"] = xsp
        m["mem"] = f(inp["mem_prompt"][b])
        m["ckT"] = f(np.stack([inp["cache_k"][0, 2 * c + s].transpose(2, 1, 0) for s in range(2)]))
        m["cv"] = f(np.stack([inp["cache_v"][0, 2 * c + s].reshape(PAST, NKV * 128) for s in range(2)]))
        m["ckiT"] = f(np.stack([inp["cache_k_idx"][0, 2 * c + s].T for s in range(2)]))
        m["stT"] = f(np.stack([inp["state_conv"][0, 2 * c + s].T for s in range(2)]))
        m["cmkT"] = f(np.stack([inp["cache_mem_k"][0, 2 * c + s].transpose(2, 1, 0) for s in range(2)]))
        m["cmv"] = f(np.stack([inp["cache_mem_v"][0, 2 * c + s].reshape(MEMT, 512) for s in range(2)]))
        m["rope_c"] = _rope_table(pos)
        m["kc_p"] = (pos // 64).astype(np.float32)[None, :]
        q = np.zeros((128, NT), np.float32)
        for i in range(NP):
            q[:, i] = (hf * half + i * 128 + np.arange(128)) // 64
        q[:, NP:] = PAST // 64
        m["qch"] = q
        maps.append(m)
    return maps


def assemble(res, cfg):
    D, CCH, NKV, NP, SEQ, DS, MEMT, B, DB = (cfg[k] for k in ("D", "CCH", "NKV", "NP", "SEQ", "DS", "MEMT", "B", "DB"))
    half = SEQ // 2
    y_p = np.zeros((B, SEQ, D), np.float32)
    y_s = np.zeros((DB, DS, D), np.float32)
    k_p = np.zeros((1, B, SEQ, NKV, 128), np.float32)
    v_p = np.zeros_like(k_p)
    ki_p = np.zeros((1, B, SEQ, 128), np.float32)
    conv_p = np.zeros((1, B, 30, CCH), np.float32)
    mk_p = np.zeros((1, B, MEMT, 4, 128), np.float32)
    mv_p = np.zeros_like(mk_p)
    k_s = np.zeros((1, DB, DS, NKV, 128), np.float32)
    v_s = np.zeros_like(k_s)
    ki_s = np.zeros((1, DB, DS, 128), np.float32)
    conv_s = np.zeros((1, DB, 30, CCH), np.float32)
    for c in range(8):
        r = res[c]
        b, hf = c // 2, c % 2
        y_p[b, hf * half:(hf + 1) * half] = r["y"][:NP * 128]
        if hf == 0:
            k_p[0, b] = r["o_k"].reshape(SEQ, NKV, 128)
            v_p[0, b] = r["o_v"].reshape(SEQ, NKV, 128)
            ki_p[0, b] = r["o_ki"]
            mk_p[0, b] = r["o_mk"].reshape(MEMT, 4, 128)
            mv_p[0, b] = r["o_mv"].reshape(MEMT, 4, 128)
        else:
            conv_p[0, b] = r["o_conv"]
        for s in range(2):
            q = 2 * c + s
            y_s[q] = r["y"][(NP + s) * 128:(NP + s) * 128 + DS]
            k_s[0, q] = r["o_ks"][s, :DS].reshape(DS, NKV, 128)
            v_s[0, q] = r["o_vs"][s, :DS].reshape(DS, NKV, 128)
            ki_s[0, q] = r["o_kis"][s, :DS]
            conv_s[0, q] = r["o_convs"][s]
    return (y_p, y_s, k_p, v_p, ki_p, conv_p, mk_p, mv_p, k_s, v_s, ki_s, conv_s)


def kernel(**inputs):
    cfg = mkcfg()
    inp = {k: np.asarray(v) for k, v in inputs.items()}
    maps = host_prep(inp, cfg)
    nc = build(cfg)
    res = run_bass_kernel_spmd(nc, maps, core_ids=list(range(8)))
    return assemble(res.results, cfg)
```

```python
import contextlib
import math
import numpy as np
import ml_dtypes
import concourse.bass as bass
import concourse.mybir as mybir
from concourse.bass_utils import run_bass_kernel_spmd

F32 = mybir.dt.float32
BF16 = mybir.dt.bfloat16
ALU = mybir.AluOpType
AF = mybir.ActivationFunctionType
AX = mybir.AxisListType

EPS = 1e-6
ROPE_THETA = 500000.0
NEG = -1.0e30


class Prog:
    def __init__(self, nc, es):
        self.nc = nc
        self.es = es
        self.st = es
        self.ops = []
        self.engs = {"pe": nc.tensor, "act": nc.scalar, "dve": nc.vector, "pool": nc.gpsimd, "sp": nc.sync}
        self.n_t = 0
        self.eng_sem = {}
        self.eng_cnt = {}
        self.pool = {}
        self.npool = {}
        self.key_sem = {}
        self.fence_sem = None
        self.fence_cnt = 0
        self.tot_ops = 0
        self.tot_wait = 0
        self.free_sems = []
        self.n_dsem = 0

    def sb(self, shape, dt=F32, name=None):
        self.n_t += 1
        return self.st.enter_context(self.nc.sbuf_tensor(name or f"sb{self.n_t}", list(shape), dt))

    def ps(self, shape=(128, 512), dt=F32, name=None):
        self.n_t += 1
        return self.st.enter_context(self.nc.psum_tensor(name or f"ps{self.n_t}", list(shape), dt))

    @staticmethod
    def key(x):
        def nm(a):
            if isinstance(a, str):
                return a
            t = getattr(a, "tensor", None)
            return t.name if t is not None else a.name
        if isinstance(x, tuple):
            return (nm(x[0]), x[1])
        return (nm(x), None)

    def op(self, eng, fn, reads=(), writes=(), dma=False):
        rk = []
        for r in reads:
            if r is None or isinstance(r, (int, float)):
                continue
            k = self.key(r)
            if k not in rk:
                rk.append(k)
        wk = []
        for w in writes:
            k = self.key(w)
            if k not in wk:
                wk.append(k)
        self.ops.append(dict(eng=eng, fn=fn, reads=rk, writes=wk, dma=dma))

    def _esem(self, e):
        if e not in self.eng_sem:
            self.eng_sem[e] = self.es.enter_context(self.nc.semaphore(f"s_{e}"))
            self.eng_cnt[e] = 0
        return self.eng_sem[e]

    def flush(self):
        nc = self.nc
        ops = self.ops
        state = {}
        deps = [None] * len(ops)

        def confl(k):
            ent = state.get(k[0])
            if not ent:
                return []
            if k[1] is None:
                return list(ent.values())
            return [ent[s_] for s_ in (k[1], None) if s_ in ent]

        joined = [False] * len(ops)
        for i, o in enumerate(ops):
            d = set()
            for k in o["reads"]:
                for st in confl(k):
                    d.update(st[0])
            joins = {}
            for k in o["writes"]:
                own = state.get(k[0], {}).get(k[1])
                joinable = bool(o["dma"] and own and own[0] and all(ops[j]["dma"] for j in own[0]) and not own[1])
                joins[k] = joinable
                for st in confl(k):
                    d.update(st[1])
                    if not (joinable and st is own):
                        d.update(st[0])
            if o["dma"]:
                joined[i] = joins[o["writes"][0]]
            for k in o["reads"]:
                st = state.setdefault(k[0], {}).setdefault(k[1], [[], []])
                st[1].append(i)
            for k in o["writes"]:
                ent = state.setdefault(k[0], {})
                if joins[k]:
                    ent[k[1]][0].append(i)
                else:
                    if k[1] is None:
                        ent.clear()
                    ent[k[1]] = [[i], []]
            d.discard(i)
            if o["eng"] == "pe":
                d = {j for j in d if not (ops[j]["eng"] == "pe" and not ops[j]["dma"])}
            deps[i] = d
        need = [False] * len(ops)
        for d in deps:
            for j in d:
                need[j] = True
        last_on = {}
        for i, o in enumerate(ops):
            if not o["dma"]:
                last_on[o["eng"]] = i
        for i in last_on.values():
            need[i] = True

        sig = [None] * len(ops)
        waited = {}
        for i, o in enumerate(ops):
            e = o["eng"]
            eo = self.engs[e]
            wl = {}
            for j in deps[i]:
                s, v = sig[j]
                kk = id(s)
                if kk not in wl or wl[kk][1] < v:
                    wl[kk] = (s, v)
            pre = None
            if o["dma"]:
                k = o["writes"][0]
                name = k[0]
                pl = self.pool.get(name)
                if pl is None:
                    n = self.npool.get(name, 2)
                    sems_, cnt_ = [], []
                    for q in range(n):
                        if self.free_sems:
                            s_, c_ = self.free_sems.pop()
                        else:
                            self.n_dsem += 1
                            s_, c_ = self.es.enter_context(nc.semaphore(f"dma{self.n_dsem}")), 0
                        sems_.append(s_)
                        cnt_.append(c_)
                    pl = dict(sems=sems_, cnt=cnt_, last=[None] * n, rr=0)
                    self.pool[name] = pl
                idx = None
                if joined[i] and k in self.key_sem and pl["last"][self.key_sem[k]] == k:
                    idx = self.key_sem[k]
                else:
                    idx = pl["rr"]
                    pl["rr"] = (pl["rr"] + 1) % len(pl["sems"])
                    if pl["cnt"][idx] > 0:
                        s = pl["sems"][idx]
                        kk = id(s)
                        if kk not in wl or wl[kk][1] < pl["cnt"][idx]:
                            wl[kk] = (s, pl["cnt"][idx])
                self.key_sem[k] = idx
                pl["last"][idx] = k
                pre = (pl, idx)
            for kk, (s, v) in wl.items():
                if waited.get((e, kk), -1) >= v:
                    continue
                waited[(e, kk)] = v
                eo.wait_ge(s, v)
                self.tot_wait += 1
            ins = o["fn"](eo)
            if o["dma"]:
                pl, idx = pre
                pl["cnt"][idx] += 16
                ins.then_inc(pl["sems"][idx], 16)
                sig[i] = (pl["sems"][idx], pl["cnt"][idx])
            elif need[i]:
                s = self._esem(e)
                self.eng_cnt[e] += 1
                ins.then_inc(s, 1)
                sig[i] = (s, self.eng_cnt[e])
        self.tot_ops += len(ops)
        self.ops = []
        if self.fence_sem is None:
            self.fence_sem = self.es.enter_context(nc.semaphore("fence"))
        for e, s in self.eng_sem.items():
            if self.eng_cnt[e] > 0:
                nc.sync.wait_ge(s, self.eng_cnt[e])
        for pl in self.pool.values():
            for s, c in zip(pl["sems"], pl["cnt"]):
                if c > 0:
                    nc.sync.wait_ge(s, c)
        for pl in self.pool.values():
            for s, c in zip(pl["sems"], pl["cnt"]):
                self.free_sems.append((s, c))
        self.pool = {}
        self.key_sem = {}
        self.fence_cnt += 1
        nc.sync.drain().then_inc(self.fence_sem, 1)
        for e in ("pe", "act", "dve", "pool"):
            self.engs[e].wait_ge(self.fence_sem, self.fence_cnt)

    def dma(self, q, out, in_, okey=None, ikey=None, **kw):
        self.op(q, lambda e: e.dma_start(out=out, in_=in_, **kw), reads=[ikey or in_], writes=[okey or out], dma=True)

    def cdma(self, out, in_, okey=None, ikey=None):
        n = out.shape[-1]
        if n > 2048:
            d = 2048
            while n % d:
                d //= 2
            names = " ".join(f"a{i}" for i in range(len(out.shape) - 1))
            pat = f"{names} (x d) -> {names} x d"
            self.dma("pool", out.rearrange(pat, d=d), in_.rearrange(pat, d=d), okey=okey or out, ikey=ikey or in_)
        else:
            self.dma("pool", out, in_, okey=okey, ikey=ikey)

    def mm(self, out, lhsT, rhs, start=True, stop=True, okey=None, rkeys=None):
        self.op("pe", lambda e: e.matmul(out, lhsT, rhs, start=start, stop=stop), reads=rkeys or [lhsT, rhs], writes=[okey or out])

    def tr(self, out, in_, ident, okey=None, ikey=None):
        self.op("pe", lambda e: e.transpose(out, in_, ident), reads=[ikey or in_, ident], writes=[okey or out])

    def act(self, out, in_, func, scale=1.0, bias=0.0, accum_out=None, okey=None, ikey=None):
        rd = [ikey or in_] + [x for x in (scale, bias) if not isinstance(x, (int, float))]
        wr = [okey or out] + ([accum_out] if accum_out is not None else [])
        if accum_out is not None:
            self.op("act", lambda e: e.activation(out, in_, func, bias=bias, scale=scale, accum_out=accum_out), reads=rd, writes=wr)
        else:
            self.op("act", lambda e: e.activation(out, in_, func, bias=bias, scale=scale), reads=rd, writes=wr)

    def ts(self, eng, out, in0, s1, s2=None, op0=ALU.mult, op1=None, accum_out=None, okey=None, ikey=None):
        rd = [ikey or in0] + [x for x in (s1, s2) if x is not None and not isinstance(x, (int, float))]
        wr = [okey or out] + ([accum_out] if accum_out is not None else [])
        kw = {}
        if op1 is not None:
            kw["op1"] = op1
        if accum_out is not None:
            kw["accum_out"] = accum_out
        self.op(eng, lambda e: e.tensor_scalar(out, in0, s1, s2, op0, **kw), reads=rd, writes=wr)

    def tt(self, eng, out, in0, in1, op, okey=None, rkeys=None):
        self.op(eng, lambda e: e.tensor_tensor(out, in0, in1, op), reads=rkeys or [in0, in1], writes=[okey or out])

    def stt(self, out, in0, scalar, in1, op0, op1, okey=None, rkeys=None):
        rd = list(rkeys or [in0, in1]) + ([scalar] if not isinstance(scalar, (int, float)) else [])
        self.op("dve", lambda e: e.scalar_tensor_tensor(out, in0, scalar, in1, op0, op1), reads=rd, writes=[okey or out])

    def copy(self, eng, out, in_, okey=None, ikey=None):
        if eng == "act":
            self.op("act", lambda e: e.copy(out, in_), reads=[ikey or in_], writes=[okey or out])
        else:
            self.op(eng, lambda e: e.tensor_copy(out, in_), reads=[ikey or in_], writes=[okey or out])

    def max8(self, out, in_, okey=None):
        self.op("dve", lambda e: e.max(out, in_), reads=[in_], writes=[okey or out])

    def mrep(self, out, in_to_replace, in_values, imm, rkeys=None):
        self.op("dve", lambda e: e.match_replace(out, in_to_replace, in_values, imm), reads=rkeys or [in_to_replace, in_values], writes=[out])

    def memset(self, eng, ap, val):
        self.op(eng, lambda e: e.memset(ap, val), reads=[], writes=[ap])

    def recip(self, out, in_, okey=None):
        self.op("dve", lambda e: e.reciprocal(out, in_), reads=[in_], writes=[okey or out])

    def reduce(self, out, in_, op, axis=AX.X):
        self.op("dve", lambda e: e.tensor_reduce(out, in_, axis, op), reads=[in_], writes=[out])


def mkcfg(D=4096, SEQ=4096, B=4, DB=16, DS=64, PAST=4096, IH=32, TOPK_MAX=256, MEMT=256):
    c = dict(D=D, SEQ=SEQ, B=B, DB=DB, DS=DS, PAST=PAST, IH=IH, MEMT=MEMT)
    c["KC"] = D // 128
    c["CCH"] = D // 2
    c["CC"] = c["CCH"] // 128
    c["NH"] = (D // 2) // 128
    c["NKV"] = 4
    c["GQ"] = c["NH"] // 4
    c["NP"] = SEQ // 2 // 128
    c["NCX"] = SEQ // 128
    c["NT"] = c["NP"] + 2
    c["TOPK_P"] = min(TOPK_MAX, SEQ // 4)
    c["TOPK_S"] = min(TOPK_MAX, (PAST + DS) // 4)
    c["SS"] = PAST + 128
    c["MH"] = 4
    c["MC"] = MEMT // 128
    return c


PEER_KEYS = 128
PEER_HEADS = 8
PEER_TOPK = 16


def build(cfg, stages=("M", "KV", "MAIN", "B", "C", "D", "E"), dbg=False):
    D, KC, CCH, CC, NH, NKV, GQ, NP, NCX, NT, IH, SEQ, PAST, SS, MEMT, MC = (cfg[k] for k in (
        "D", "KC", "CCH", "CC", "NH", "NKV", "GQ", "NP", "NCX", "NT", "IH", "SEQ", "PAST", "SS", "MEMT", "MC"))
    NTOK = NT * 128
    IDX_SCALE = (IH ** -0.5) * (128 ** -0.5)
    ATT_SCALE = 128 ** -0.5
    nc = bass.Bass("TRN2", target_bir_lowering=False)

    def din(name, shape, dt=F32):
        return nc.dram_tensor(name, list(shape), dt, kind="ExternalInput").ap()

    def dout(name, shape, dt=F32):
        return nc.dram_tensor(name, list(shape), dt, kind="ExternalOutput").ap()

    def dscr(name, shape, dt=F32):
        return nc.dram_tensor(name, list(shape), dt, kind="Internal").ap()

    xctx = din("xctx", [SEQ, D])
    xsp = din("xsp", [2, 128, D])
    xhalo = din("xhalo", [128, D])
    mem = din("mem", [MEMT, D])
    ckT = din("ckT", [2, 128, NKV, PAST])
    cv = din("cv", [2, PAST, NKV * 128])
    ckiT = din("ckiT", [2, 128, PAST])
    stT = din("stT", [2, CCH, 30])
    cmkT = din("cmkT", [2, 128, 4, MEMT])
    cmv = din("cmv", [2, MEMT, 512])
    w_glu = din("w_glu", [D, 2 * CCH])
    w_q = din("w_q", [D, NH * 128])
    w_qi = din("w_qi", [D, IH * 128])
    w_wi = din("w_wi", [D, IH])
    w_kv = din("w_kv", [D, 1152])
    w_out = din("w_out", [D, D])
    w_qm = din("w_qm", [D, 512])
    w_km = din("w_km", [D, 512])
    w_vm = din("w_vm", [D, 512])
    w_om = din("w_om", [512, D])
    w_pq = din("w_pq", [D, 2048])
    subk = din("subk", [128, 16, 128])
    uT = din("uT", [128, 128, KC * 128])
    vtab = din("vtab", [PEER_KEYS * PEER_KEYS, D])
    g_mix = din("g_mix", [1, D])
    g_memn = din("g_memn", [1, D])
    g_ffn = din("g_ffn", [1, D])
    g_mem = din("g_mem", [1, D])
    g_q = din("g_q", [1, 128])
    g_k = din("g_k", [1, 128])
    g_mq = din("g_mq", [1, 128])
    g_mk = din("g_mk", [1, 128])
    dww = din("dww", [128, CC, 31])
    dwb = din("dwb", [128, CC])
    lng = din("lng", [128, CC])
    lnb = din("lnb", [128, CC])
    rope_c = din("rope_c", [SEQ, 32])
    rope_s = din("rope_s", [128, 32])
    kc_p = din("kc_p", [1, SEQ])
    kc_s = din("kc_s", [1, SS])
    qch = din("qch", [128, NT])
    c_idb = din("c_idb", [128, 128], BF16)
    c_idf = din("c_idf", [128, 128])
    c_oneb = din("c_oneb", [128, 128], BF16)
    c_onef = din("c_onef", [128, 128])
    c_iota = din("c_iota", [128, 128])

    y = dout("y", [NTOK, D])
    o_k = dout("o_k", [SEQ, NKV * 128])
    o_v = dout("o_v", [SEQ, NKV * 128])
    o_ki = dout("o_ki", [SEQ, 128])
    o_conv = dout("o_conv", [30, CCH])
    o_mk = dout("o_mk", [MEMT, 512])
    o_mv = dout("o_mv", [MEMT, 512])
    o_ks = dout("o_ks", [2, 128, NKV * 128])
    o_vs = dout("o_vs", [2, 128, NKV * 128])
    o_kis = dout("o_kis", [2, 128, 128])
    o_convs = dout("o_convs", [2, 30, CCH])

    UTp = dscr("UTp", [CCH, 128 + NP * 128])
    UTs = dscr("UTs", [2, CCH, 160])
    KT = dscr("KT", [128, NKV, SEQ], BF16)
    Vc = dscr("Vc", [SEQ, NKV * 128], BF16)
    KIT = dscr("KIT", [128, SEQ], BF16)
    KTs = dscr("KTs", [2, 128, NKV, 128], BF16)
    Vs = dscr("Vs", [2, 128, NKV * 128], BF16)
    KITs = dscr("KITs", [2, 128, 128], BF16)
    MKT = dscr("MKT", [128, 4, MEMT], BF16)
    MV = dscr("MV", [MEMT, 512], BF16)
    QT = dscr("QT", [NT, 128, NH, 128], BF16)
    QIT = dscr("QIT", [NT, 128, IH, 128], BF16)
    WI = dscr("WI", [NT, 128, IH])
    MIXT = dscr("MIXT", [D, NTOK], BF16)
    H2 = dscr("H2", [NTOK, D])
    HN2T = dscr("HN2T", [D, NTOK], BF16)
    S12 = dscr("S12", [16, NTOK, 128])
    GALL = dscr("GALL", [128, 128, NTOK], BF16)

    UB16 = dscr("UB16", [128, 128, KC * 128], BF16)
    VB16 = dscr("VB16", [PEER_KEYS * PEER_KEYS, D], BF16)
    dbg_out = {}

    with contextlib.ExitStack() as es:
        P = Prog(nc, es)
        idb = P.sb([128, 128], BF16, "idb")
        idf = P.sb([128, 128], F32, "idf")
        oneb = P.sb([128, 128], BF16, "oneb")
        onef = P.sb([128, 128], F32, "onef")
        P.dma("sp", idb[:], c_idb)
        P.dma("sp", idf[:], c_idf)
        P.dma("sp", oneb[:], c_oneb)
        P.dma("sp", onef[:], c_onef)
        P.flush()

        def bcast_row(dst, src_row, n):
            P.dma("sp", dst, src_row.to_broadcast([128, n]))

        def rstd_from_ss(ss, n, out, tmp):
            P.ts("dve", tmp, ss, 1.0 / n, EPS, op0=ALU.mult, op1=ALU.add)
            P.act(tmp, tmp, AF.Sqrt)
            P.recip(out, tmp)

        def load_w(dst, src, ncols):
            sv = src.rearrange("(c p) n -> p c n", p=128)
            nq = 4 if KC % 4 == 0 else 1
            step = KC // nq
            for q in range(nq):
                P.dma("pool", dst[:, q * step:(q + 1) * step, 0:ncols], sv[:, q * step:(q + 1) * step, :], okey=(dst, q))

        def wkeys(dst):
            return [(dst, q) for q in range(4 if KC % 4 == 0 else 1)]

        class NormT:
            def __init__(self, gsrc, npt=2):
                self.npt = npt
                self.gbc = P.sb([128, D], F32)
                bcast_row(self.gbc[:], gsrc[0:1, :], D)
                self.sq = P.sb([128, D], BF16)
                self.xs = [P.sb([128, D], BF16) for _ in range(2)]
                self.sm = [P.sb([128, 4], F32) for _ in range(2)]
                self.pt = [P.ps([128, 1024], BF16) for _ in range(npt)]
                self.k = 0

            def run(self, x_t, dst_fn):
                k = self.k
                self.k += 1
                sm = self.sm[k % 2]
                xs = self.xs[k % 2]
                P.act(self.sq[:], x_t, AF.Square, accum_out=sm[:, 0:1])
                rstd_from_ss(sm[:, 0:1], D, sm[:, 1:2], sm[:, 2:3])
                P.stt(xs[:], x_t, sm[:, 1:2], self.gbc[:], ALU.mult, ALU.mult)
                nb = 8 if KC % 8 == 0 else KC
                for b0 in range(0, KC, nb):
                    pt = self.pt[(b0 // nb) % self.npt]
                    for j in range(nb):
                        P.tr(pt[:, j * 128:(j + 1) * 128], xs[:, (b0 + j) * 128:(b0 + j + 1) * 128], idb[:])
                    eng = "act" if (b0 // nb) % 2 == 0 else "dve"
                    P.copy(eng, dst_fn(b0, nb), pt[:, 0:nb * 128].rearrange("p (n t) -> p n t", t=128))

        def head_norm(ps_ap, nh, gain_bc, out_f, sq_t, sm_t):
            P.act(sq_t[:, 0:nh * 128], ps_ap, AF.Square)
            P.reduce(sm_t[:, 0:nh], sq_t[:, 0:nh * 128].rearrange("p (h d) -> p h d", d=128), ALU.add)
            rstd_from_ss(sm_t[:, 0:nh], 128, sm_t[:, 4:4 + nh], sm_t[:, 8:8 + nh])
            P.tt("dve", out_f, ps_ap.rearrange("p (h d) -> p h d", d=128),
                 sm_t[:, 4:4 + nh].unsqueeze(2).to_broadcast([128, nh, 128]), ALU.mult)
            P.tt("dve", out_f, out_f, gain_bc[:, 0:128].unsqueeze(1).to_broadcast([128, nh, 128]), ALU.mult)

        def rope(f, nh, cs, tmp):
            x1 = f[:, :, 0:16]
            x2 = f[:, :, 16:32]
            cosb = cs[:, 0:16].unsqueeze(1).to_broadcast([128, nh, 16])
            sinb = cs[:, 16:32].unsqueeze(1).to_broadcast([128, nh, 16])
            P.tt("dve", tmp[:, 0, 0:nh, :], x1, cosb, ALU.mult)
            P.tt("dve", tmp[:, 1, 0:nh, :], x2, sinb, ALU.mult)
            P.tt("dve", tmp[:, 2, 0:nh, :], x2, cosb, ALU.mult)
            P.tt("dve", tmp[:, 3, 0:nh, :], x1, sinb, ALU.mult)
            P.tt("dve", x1, tmp[:, 0, 0:nh, :], tmp[:, 1, 0:nh, :], ALU.subtract)
            P.tt("dve", x2, tmp[:, 2, 0:nh, :], tmp[:, 3, 0:nh, :], ALU.add)

        def tok_mm(ps_ap, hnT, tcol, wb, ncols, wk):
            for c in range(KC):
                P.mm(ps_ap, hnT[:, c, tcol:tcol + 128], wb[:, c, 0:ncols], start=(c == 0), stop=(c == KC - 1),
                     rkeys=[hnT] + wk)

        if "M" in stages:
            with contextlib.ExitStack() as st:
                P.st = st
                nt = NormT(g_mem)
                wk_b = P.sb([128, KC, 512], BF16)
                wv_b = P.sb([128, KC, 512], BF16)
                load_w(wk_b, w_km, 512)
                load_w(wv_b, w_vm, 512)
                gk = P.sb([128, 128], F32)
                bcast_row(gk[:], g_mk[0:1, :], 128)
                xt = [P.sb([128, D], F32) for _ in range(2)]
                hn = [P.sb([128, KC, 128], BF16) for _ in range(2)]
                pk = P.ps()
                pv = P.ps()
                ptr = P.ps([128, 1024], BF16)
                sq_t = P.sb([128, 512], F32)
                sm_t = P.sb([128, 12], F32)
                kf = P.sb([128, 4, 128], F32)
                kb = P.sb([128, 512], BF16)
                kTt = P.sb([128, 4, 128], BF16)
                vf = P.sb([128, 512], F32)
                vb = P.sb([128, 512], BF16)
                for m in range(MC):
                    x_t = xt[m % 2]
                    h_t = hn[m % 2]
                    P.dma("sp", x_t[:], mem[m * 128:(m + 1) * 128, :])
                    nt.run(x_t[:], lambda c0, n, h_t=h_t: h_t[:, c0:c0 + n, :])
                    tok_mm(pk[:, 0:512], h_t, 0, wk_b, 512, wkeys(wk_b))
                    tok_mm(pv[:, 0:512], h_t, 0, wv_b, 512, wkeys(wv_b))
                    head_norm(pk[:, 0:512], 4, gk, kf[:], sq_t, sm_t)
                    P.dma("sp", o_mk[m * 128:(m + 1) * 128, :], kf[:].rearrange("p h d -> p (h d)"))
                    P.copy("act", kb[:], kf[:].rearrange("p h d -> p (h d)"))
                    for h in range(4):
                        P.tr(ptr[:, h * 128:(h + 1) * 128], kb[:, h * 128:(h + 1) * 128], idb[:])
                    P.copy("dve", kTt[:], ptr[:, 0:512].rearrange("p (h t) -> p h t", t=128))
                    P.dma("sp", MKT[:, :, m * 128:(m + 1) * 128], kTt[:])
                    P.copy("act", vf[:], pv[:, 0:512])
                    P.dma("sp", o_mv[m * 128:(m + 1) * 128, :], vf[:])
                    P.copy("dve", vb[:], pv[:, 0:512])
                    P.dma("sp", MV[m * 128:(m + 1) * 128, :], vb[:])
                P.flush()
            P.st = es

        if "KV" in stages:
            with contextlib.ExitStack() as st:
                P.st = st
                nt = NormT(g_mix)
                wb = P.sb([128, KC, 1152], BF16)
                load_w(wb, w_kv, 1152)
                wk = wkeys(wb)
                gk = P.sb([128, 128], F32)
                bcast_row(gk[:], g_k[0:1, :], 128)
                xt = [P.sb([128, D], F32) for _ in range(2)]
                hn = [P.sb([128, KC, 128], BF16) for _ in range(2)]
                cs = [P.sb([128, 32], F32) for _ in range(2)]
                pk = P.ps()
                pv = P.ps()
                pki = P.ps()
                ptr = P.ps([128, 1024], BF16)
                sq_t = P.sb([128, 512], F32)
                sm_t = P.sb([128, 12], F32)
                rtmp = P.sb([128, 4, 4, 16], F32)
                kf = [P.sb([128, 4, 128], F32) for _ in range(2)]
                kb = P.sb([128, 512], BF16)
                kTt = [P.sb([128, 4, 128], BF16) for _ in range(2)]
                vf = [P.sb([128, 512], F32) for _ in range(2)]
                vb = [P.sb([128, 512], BF16) for _ in range(2)]
                kif = [P.sb([128, 1, 128], F32) for _ in range(2)]
                kib = P.sb([128, 128], BF16)
                kiTt = [P.sb([128, 128], BF16) for _ in range(2)]
                tiles = [("p", i) for i in range(NCX)] + [("s", 0), ("s", 1)]
                for n_, (kind, i) in enumerate(tiles):
                    x_t = xt[n_ % 2]
                    h_t = hn[n_ % 2]
                    c_t = cs[n_ % 2]
                    if kind == "p":
                        P.dma("sp", x_t[:], xctx[i * 128:(i + 1) * 128, :])
                        P.dma("sp", c_t[:], rope_c[i * 128:(i + 1) * 128, :])
                    else:
                        P.dma("sp", x_t[:], xsp[i])
                        P.dma("sp", c_t[:], rope_s)
                    nt.run(x_t[:], lambda c0, n, h_t=h_t: h_t[:, c0:c0 + n, :])
                    for c in range(KC):
                        P.mm(pk[:, 0:512], h_t[:, c, :], wb[:, c, 0:512], start=(c == 0), stop=(c == KC - 1), rkeys=[h_t] + wk)
                    for c in range(KC):
                        P.mm(pv[:, 0:512], h_t[:, c, :], wb[:, c, 512:1024], start=(c == 0), stop=(c == KC - 1), rkeys=[h_t] + wk)
                    for c in range(KC):
                        P.mm(pki[:, 0:128], h_t[:, c, :], wb[:, c, 1024:1152], start=(c == 0), stop=(c == KC - 1), rkeys=[h_t] + wk)
                    kf_t = kf[n_ % 2]
                    head_norm(pk[:, 0:512], 4, gk, kf_t[:], sq_t, sm_t)
                    rope(kf_t, 4, c_t, rtmp)
                    kflat = kf_t[:].rearrange("p h d -> p (h d)")
                    if kind == "p":
                        P.dma("sp", o_k[i * 128:(i + 1) * 128, :], kflat)
                    else:
                        P.dma("sp", o_ks[i], kflat)
                    P.copy("act", kb[:], kflat)
                    for h in range(4):
                        P.tr(ptr[:, h * 128:(h + 1) * 128], kb[:, h * 128:(h + 1) * 128], idb[:])
                    kT_t = kTt[n_ % 2]
                    P.copy("dve", kT_t[:], ptr[:, 0:512].rearrange("p (h t) -> p h t", t=128))
                    if kind == "p":
                        P.dma("sp", KT[:, :, i * 128:(i + 1) * 128], kT_t[:])
                    else:
                        P.dma("sp", KTs[i], kT_t[:])
                    vf_t = vf[n_ % 2]
                    vb_t = vb[n_ % 2]
                    P.copy("act", vf_t[:], pv[:, 0:512])
                    P.copy("dve", vb_t[:], pv[:, 0:512])
                    if kind == "p":
                        P.dma("sp", o_v[i * 128:(i + 1) * 128, :], vf_t[:])
                        P.dma("sp", Vc[i * 128:(i + 1) * 128, :], vb_t[:])
                    else:
                        P.dma("sp", o_vs[i], vf_t[:])
                        P.dma("sp", Vs[i], vb_t[:])
                    ki_t = kif[n_ % 2]
                    P.copy("act", ki_t[:, 0, :], pki[:, 0:128])
                    rope(ki_t, 1, c_t, rtmp)
                    if kind == "p":
                        P.dma("sp", o_ki[i * 128:(i + 1) * 128, :], ki_t[:, 0, :])
                    else:
                        P.dma("sp", o_kis[i], ki_t[:, 0, :])
                    P.copy("act", kib[:], ki_t[:, 0, :])
                    P.tr(ptr[:, 512:640], kib[:], idb[:])
                    kiT_t = kiTt[n_ % 2]
                    P.copy("dve", kiT_t[:], ptr[:, 512:640])
                    if kind == "p":
                        P.dma("sp", KIT[:, i * 128:(i + 1) * 128], kiT_t[:])
                    else:
                        P.dma("sp", KITs[i], kiT_t[:])
                P.flush()
            P.st = es

        own = [("h", -1)] + [("p", i) for i in range(NP)] + [("s", 0), ("s", 1)]
        groups = [own[i:i + 4] for i in range(0, len(own), 4)]

        def tile_index(kind, i):
            return i if kind == "p" else NP + i

        if "MAIN" in stages:
            with contextlib.ExitStack() as st:
                P.st = st
                nt = NormT(g_mix)
                gq = P.sb([128, 128], F32)
                bcast_row(gq[:], g_q[0:1, :], 128)
                for s in range(2):
                    P.dma("sp", UTs[s][:, 2:32], stT[s], okey=("UTs", "st"))
                xt = [P.sb([128, D], F32) for _ in range(2)]
                hnT = P.sb([128, KC, 512], BF16)
                wbuf = [P.sb([128, KC, 512], BF16) for _ in range(2)]
                cst = P.sb([128, 4, 32], F32)
                pa = P.ps()
                pg = P.ps()
                pq = [P.ps() for _ in range(2)]
                ptr = P.ps([128, 1024], BF16)
                sg = P.sb([128, 512], F32)
                ut = [P.sb([128, 512], F32) for _ in range(2)]
                sq_t = P.sb([128, 512], F32)
                sm_t = P.sb([128, 12], F32)
                rtmp = P.sb([128, 4, 4, 16], F32)
                qf = P.sb([128, 4, 128], F32)
                qb = P.sb([128, 512], BF16)
                qTt = [P.sb([128, 4, 128], BF16) for _ in range(2)]
                wis = [P.sb([128, IH], F32) for _ in range(2)]
                wcnt = 0
                xcnt = 0
                for grp in groups:
                    ng = len(grp)
                    N = ng * 128
                    for tt, (kind, i) in enumerate(grp):
                        x_t = xt[xcnt % 2]
                        xcnt += 1
                        if kind == "h":
                            P.dma("sp", x_t[:], xhalo)
                        elif kind == "p":
                            P.dma("sp", x_t[:], xctx[i * 128:(i + 1) * 128, :])
                            P.dma("sp", cst[:, tt, :], rope_c[i * 128:(i + 1) * 128, :], okey=(cst, tt))
                        else:
                            P.dma("sp", x_t[:], xsp[i])
                            P.dma("sp", cst[:, tt, :], rope_s, okey=(cst, tt))
                        nt.run(x_t[:], lambda c0, n, tt=tt: hnT[:, c0:c0 + n, tt * 128:(tt + 1) * 128])
                    for b in range(CC // 2):
                        wb = wbuf[wcnt % 2]
                        wcnt += 1
                        load_w(wb, w_glu[:, b * 512:(b + 1) * 512], 512)
                        wk = wkeys(wb)
                        for s in range(2):
                            j = 2 * b + s
                            for c in range(KC):
                                P.mm(pa[:, 0:N], wb[:, c, (2 * s) * 128:(2 * s + 1) * 128], hnT[:, c, 0:N],
                                     start=(c == 0), stop=(c == KC - 1), rkeys=[hnT] + wk)
                            for c in range(KC):
                                P.mm(pg[:, 0:N], wb[:, c, (2 * s + 1) * 128:(2 * s + 2) * 128], hnT[:, c, 0:N],
                                     start=(c == 0), stop=(c == KC - 1), rkeys=[hnT] + wk)
                            P.act(sg[:, 0:N], pg[:, 0:N], AF.Sigmoid)
                            u_t = ut[j % 2]
                            P.tt("dve", u_t[:, 0:N], pa[:, 0:N], sg[:, 0:N], ALU.mult)
                            for tt, (kind, i) in enumerate(grp):
                                src = u_t[:, tt * 128:(tt + 1) * 128]
                                if kind == "h":
                                    P.dma("sp", UTp[j * 128:(j + 1) * 128, 0:128], src, okey=("UTp", None))
                                elif kind == "p":
                                    P.dma("sp", UTp[j * 128:(j + 1) * 128, 128 + i * 128:128 + (i + 1) * 128], src, okey=("UTp", None))
                                else:
                                    P.dma("sp", UTs[i][j * 128:(j + 1) * 128, 32:160], src, okey=("UTs", "tok"))
                    for which, nblk, wsrc, dst in (("q", NH // 4, w_q, QT), ("qi", IH // 4, w_qi, QIT)):
                        for b in range(nblk):
                            wb = wbuf[wcnt % 2]
                            wcnt += 1
                            load_w(wb, wsrc[:, b * 512:(b + 1) * 512], 512)
                            wk = wkeys(wb)
                            real = [(tt, kind, i) for tt, (kind, i) in enumerate(grp) if kind != "h"]
                            for n2, (tt, kind, i) in enumerate(real):
                                if n2 == 0:
                                    tok_mm(pq[n2 % 2][:, 0:512], hnT, tt * 128, wb, 512, wk)
                                if n2 + 1 < len(real):
                                    tok_mm(pq[(n2 + 1) % 2][:, 0:512], hnT, real[n2 + 1][0] * 128, wb, 512, wk)
                                ti = tile_index(kind, i)
                                pq_t = pq[n2 % 2]
                                if which == "q":
                                    head_norm(pq_t[:, 0:512], 4, gq, qf[:], sq_t, sm_t)
                                else:
                                    P.copy("act", qf[:].rearrange("p h d -> p (h d)"), pq_t[:, 0:512])
                                rope(qf, 4, cst[:, tt, :], rtmp)
                                P.copy("act", qb[:], qf[:].rearrange("p h d -> p (h d)"))
                                for h in range(4):
                                    P.tr(ptr[:, h * 128:(h + 1) * 128], qb[:, h * 128:(h + 1) * 128], idb[:])
                                q_T = qTt[n2 % 2]
                                P.copy("dve", q_T[:], ptr[:, 0:512].rearrange("p (h t) -> p h t", t=128))
                                P.dma("sp", dst[ti][:, b * 4:(b + 1) * 4, :], q_T[:], okey=(dst.tensor.name, None))
                    wb = wbuf[wcnt % 2]
                    wcnt += 1
                    load_w(wb, w_wi, IH)
                    wk = wkeys(wb)
                    for tt, (kind, i) in enumerate(grp):
                        if kind == "h":
                            continue
                        ti = tile_index(kind, i)
                        pq_t = pq[tt % 2]
                        tok_mm(pq_t[:, 0:IH], hnT, tt * 128, wb, IH, wk)
                        w_s = wis[tt % 2]
                        P.act(w_s[:], pq_t[:, 0:IH], AF.Copy, scale=IDX_SCALE)
                        P.dma("sp", WI[ti], w_s[:], okey=("WI", None))
                P.flush()
            P.st = es

        if "B" in stages:
            with contextlib.ExitStack() as st:
                P.st = st
                P.npool["uin"] = 4
                wt = P.sb([128, CC, 31], F32)
                bt = P.sb([128, CC], F32)
                lg = P.sb([128, CC], F32)
                lb = P.sb([128, CC], F32)
                P.dma("sp", wt[:], dww)
                P.dma("sp", bt[:], dwb)
                P.dma("sp", lg[:], lng)
                P.dma("sp", lb[:], lnb)
                uin = P.sb([128, CC, 544], F32, "uin")
                cc_t = P.sb([128, CC, 512], F32)
                sqt = [P.sb([128, 512], F32) for _ in range(2)]
                p1 = P.ps()
                p2 = P.ps()
                mean = P.sb([128, 512], F32)
                var = P.sb([128, 512], F32)
                rstd = P.sb([128, 512], F32)
                tmp = [P.sb([128, 512], F32) for _ in range(2)]
                co = [P.sb([128, 512], BF16) for _ in range(2)]
                ptc = P.ps()
                cnew = P.sb([32, CCH], F32)
                jobs = []
                for tb in range(max(1, NP * 128 // 512)):
                    ntk = min(512, NP * 128)
                    jobs.append(("p", tb, ntk))
                jobs += [("s", 0, 128), ("s", 1, 128)]
                for kind, tb, ntk in jobs:
                    if kind == "p":
                        c0 = 128 + tb * ntk
                        src = UTp[:, c0 - 30:c0 + ntk].rearrange("(j p) t -> p j t", p=128)
                        mcol = tb * ntk
                        sk = "UTp"
                    else:
                        src = UTs[tb][:, 2:160].rearrange("(j p) t -> p j t", p=128)
                        mcol = (NP + tb) * 128
                        sk = "UTs"
                    W = 30 + ntk
                    P.dma("sp", uin[:, :, 0:W], src, ikey=sk)
                    last_p = (kind == "p" and (tb + 1) * ntk == NP * 128)
                    if last_p or kind == "s":
                        a0 = (30 + ntk - 30) if kind == "p" else (30 + 64 - 30)
                        for j0 in range(0, CC, 4):
                            for j in range(j0, min(CC, j0 + 4)):
                                P.tr(ptc[0:30, (j - j0) * 128:(j - j0 + 1) * 128], uin[:, j, a0:a0 + 30], idf[:])
                            nj = min(CC, j0 + 4) - j0
                            P.copy("act", cnew[0:30, j0 * 128:(j0 + nj) * 128], ptc[0:30, 0:nj * 128])
                        P.dma("sp", o_conv if kind == "p" else o_convs[tb], cnew[0:30, :])
                    for j in range(CC):
                        acc = cc_t[:, j, 0:ntk]
                        P.ts("dve", acc, uin[:, j, 0:ntk], wt[:, j, 0:1], bt[:, j:j + 1], op0=ALU.mult, op1=ALU.add, okey=(cc_t, j))
                        for k in range(1, 31):
                            P.stt(acc, uin[:, j, k:k + ntk], wt[:, j, k:k + 1], acc, ALU.mult, ALU.add, okey=(cc_t, j),
                                  rkeys=[uin, (cc_t, j)])
                        s_t = sqt[j % 2]
                        P.act(s_t[:, 0:ntk], acc, AF.Square, ikey=(cc_t, j))
                        P.mm(p1[:, 0:ntk], onef[:], acc, start=(j == 0), stop=(j == CC - 1), rkeys=[onef, (cc_t, j)])
                        P.mm(p2[:, 0:ntk], onef[:], s_t[:, 0:ntk], start=(j == 0), stop=(j == CC - 1))
                    P.ts("dve", mean[:, 0:ntk], p1[:, 0:ntk], 1.0 / CCH, None, op0=ALU.mult)
                    P.tt("dve", var[:, 0:ntk], mean[:, 0:ntk], mean[:, 0:ntk], ALU.mult)
                    P.stt(var[:, 0:ntk], p2[:, 0:ntk], 1.0 / CCH, var[:, 0:ntk], ALU.mult, ALU.subtract)
                    P.ts("dve", var[:, 0:ntk], var[:, 0:ntk], EPS, None, op0=ALU.add)
                    P.act(var[:, 0:ntk], var[:, 0:ntk], AF.Sqrt)
                    P.recip(rstd[:, 0:ntk], var[:, 0:ntk])
                    for j in range(CC):
                        t_ = tmp[j % 2]
                        P.tt("dve", t_[:, 0:ntk], cc_t[:, j, 0:ntk], mean[:, 0:ntk], ALU.subtract, rkeys=[(cc_t, j), mean])
                        P.tt("dve", t_[:, 0:ntk], t_[:, 0:ntk], rstd[:, 0:ntk], ALU.mult)
                        c_o = co[j % 2]
                        P.act(c_o[:, 0:ntk], t_[:, 0:ntk], AF.Silu, scale=lg[:, j:j + 1], bias=lb[:, j:j + 1])
                        P.dma("sp", MIXT[j * 128:(j + 1) * 128, mcol:mcol + ntk], c_o[:, 0:ntk], okey=("MIXT", "conv"))
                P.flush()
            P.st = es

        if "C" in stages:
            with contextlib.ExitStack() as st:
                P.st = st
                SMAX = max(SEQ, SS)
                P.npool["kiT_c"] = 4
                P.npool["kT_c"] = 4
                P.npool["v_c"] = 4
                kiT_c = P.sb([128, SMAX], BF16, "kiT_c")
                kT_c = P.sb([128, NKV, SMAX], BF16, "kT_c")
                v_c = P.sb([128, SMAX // 128, NKV * 128], BF16, "v_c")
                kc_t = P.sb([128, SMAX], BF16)
                qch_t = P.sb([128, NT], F32)
                P.dma("sp", qch_t[:], qch)
                qiT = [P.sb([128, IH, 128], BF16) for _ in range(2)]
                qT = [P.sb([128, NH, 128], BF16) for _ in range(2)]
                wi_t = [P.sb([128, IH], F32) for _ in range(2)]
                acc2 = [P.sb([128, SMAX], F32) for _ in range(2)]
                madd = P.sb([128, SMAX], BF16)
                bs = P.sb([128, 8], F32)
                m8 = P.sb([128, 256], F32)
                thr = P.sb([128, 1], F32)
                mask = P.sb([128, SMAX], BF16)
                maskT = P.sb([128, SMAX // 128, 128], BF16)
                rl = [P.sb([128, 512], F32) for _ in range(4)]
                pe_ = [P.sb([128, GQ, 128], BF16) for _ in range(3)]
                pm = [P.sb([128, GQ, 128], BF16) for _ in range(4)]
                rz = P.sb([128, GQ * 128], F32)
                ob = [P.sb([128, GQ, 128], BF16) for _ in range(2)]
                ps_s = [P.ps() for _ in range(2)]
                ps_qk = [P.ps() for _ in range(3)]
                ps_o = P.ps()
                ps_z = P.ps()
                ptr = [P.ps([128, 1024], BF16) for _ in range(1)]

                def seglist(blocks):
                    nb = len(blocks)
                    segs = []
                    a = 0
                    while a < nb:
                        b_ = a + 1
                        while b_ < nb and b_ - a < 4 and blocks[b_] == blocks[b_ - 1] + 1:
                            b_ += 1
                        segs.append((a, b_))
                        a = b_
                    return segs

                cnt = dict(n=0, it=0)
                P.npool["UB16"] = 3
                P.npool["VB16"] = 3
                pre_ops = []
                for c in range(0, 128, 2):
                    pre_ops.append(("u", c))
                    pre_ops.append(("v", c))
                n_slots = (NP + 2) * NKV
                per_slot = -(-len(pre_ops) // n_slots)

                def precast_some():
                    dd = min(2048, KC * 128)
                    for _ in range(per_slot):
                        if not pre_ops:
                            return
                        kind, c = pre_ops.pop(0)
                        if kind == "u":
                            P.dma("pool", UB16[c:c + 2].rearrange("c p (x d) -> p c x d", d=dd),
                                  uT[c:c + 2].rearrange("c p (x d) -> p c x d", d=dd), okey=("UB16", None))
                        else:
                            dv = min(2048, D)
                            P.dma("pool", VB16[c * 128:(c + 2) * 128, :].rearrange("(c p) (x d) -> p c x d", p=128, d=dv),
                                  vtab[c * 128:(c + 2) * 128, :].rearrange("(c p) (x d) -> p c x d", p=128, d=dv), okey=("VB16", None))

                def idx_units(job):
                    ti, blocks = job["ti"], job["blocks"]
                    if job.get("pre_idx"):
                        job["pre_idx"]()
                    k2 = ti % 2
                    acc = acc2[k2]
                    P.dma("sp", qiT[k2][:], QIT[ti], ikey="QIT")
                    P.dma("sp", wi_t[k2][:], WI[ti], ikey="WI")
                    segs = seglist(blocks)
                    N = len(blocks) * 128
                    for (a, b_) in segs:
                        w = (b_ - a) * 128
                        P.ts("pool", madd[:, a * 128:b_ * 128], kc_t[:, blocks[a] * 128:(blocks[a] + b_ - a) * 128],
                             qch_t[:, ti:ti + 1], NEG, op0=ALU.is_gt, op1=ALU.mult)
                        for h in range(IH):
                            n_ = cnt["n"]
                            cnt["n"] += 1
                            p_ = ps_s[n_ % 2]
                            r_ = rl[n_ % 4]
                            P.mm(p_[:, 0:w], qiT[k2][:, h, :], kiT_c[:, blocks[a] * 128:blocks[a] * 128 + w], rkeys=[qiT[k2], kiT_c])
                            P.act(r_[:, 0:w], p_[:, 0:w], AF.Relu)
                            if h == 0:
                                P.ts("dve", acc[:, a * 128:b_ * 128], r_[:, 0:w], wi_t[k2][:, 0:1], None, op0=ALU.mult)
                            else:
                                P.stt(acc[:, a * 128:b_ * 128], r_[:, 0:w], wi_t[k2][:, h:h + 1], acc[:, a * 128:b_ * 128], ALU.mult, ALU.add)
                            yield 1
                    P.reduce(bs[:, 0:1], acc[:, 0:N], ALU.min)
                    P.tt("pool", acc[:, 0:N], acc[:, 0:N], madd[:, 0:N], ALU.add)
                    P.max8(m8[:, 0:8], acc[:, 0:N])
                    P.copy("dve", bs[:, 1:2], m8[:, 0:1])

                def idx_phase(job):
                    for _ in idx_units(job):
                        pass

                def n_idx_units(job):
                    return len(seglist(job["blocks"])) * IH

                NITER = 22

                def topk_rounds(job, r0, r1):
                    ti, N, topk = job["ti"], len(job["blocks"]) * 128, job["topk"]
                    acc = acc2[ti % 2]
                    lo, hi, mid, w0, cn, sel, dd = (bs[:, i:i + 1] for i in range(7))
                    for r in range(r0, min(r1, NITER)):
                        if r == 0:
                            P.tt("dve", w0, hi, lo, ALU.subtract)
                        c_ = 2.0 ** -(r + 1)
                        P.stt(mid, w0, c_, lo, ALU.mult, ALU.add)
                        P.ts("dve", mask[:, 0:N], acc[:, 0:N], mid, None, op0=ALU.is_ge, op1=ALU.add, accum_out=cn)
                        P.stt(dd, cn, float(topk) - 0.5, w0, ALU.is_ge, ALU.mult)
                        P.stt(lo, dd, c_, lo, ALU.mult, ALU.add)

                def topk_final(job):
                    ti, blocks, topk = job["ti"], job["blocks"], job["topk"]
                    nb = len(blocks)
                    N = nb * 128
                    acc = acc2[ti % 2]
                    P.ts("dve", thr[:], bs[:, 0:1], 0.5 * NEG, None, op0=ALU.max)
                    P.ts("dve", mask[:, 0:N], acc[:, 0:N], thr[:, 0:1], None, op0=ALU.is_ge)
                    for b0 in range(0, nb, 8):
                        n8 = min(8, nb - b0)
                        pt = ptr[0]
                        for j in range(n8):
                            P.tr(pt[:, j * 128:(j + 1) * 128], mask[:, (b0 + j) * 128:(b0 + j + 1) * 128], idb[:])
                        P.copy("act", maskT[:, b0:b0 + n8, :], pt[:, 0:n8 * 128].rearrange("p (n t) -> p n t", t=128))

                def attn_steps(job, g):
                    ti, blocks = job["ti"], job["blocks"]
                    k2 = ti % 2
                    nb = len(blocks)
                    if g == 0:
                        if job.get("pre_attn"):
                            job["pre_attn"]()
                        P.dma("sp", qT[k2][:], QT[ti], ikey="QT")
                    W = GQ * 128
                    LA = 2
                    bufs = {}

                    def front(ci):
                        blk = blocks[ci]
                        it = cnt["it"]
                        cnt["it"] += 1
                        pq_ = ps_qk[it % 3]
                        e_ = pe_[it % 3]
                        m_ = pm[it % 4]
                        bufs[ci] = m_
                        P.mm(pq_[:, 0:W], kT_c[:, g, blk * 128:(blk + 1) * 128],
                             qT[k2][:, g * GQ:(g + 1) * GQ, :].rearrange("p r t -> p (r t)"), rkeys=[kT_c, qT[k2]])
                        P.act(e_[:].rearrange("p r t -> p (r t)"), pq_[:, 0:W], AF.Exp, scale=ATT_SCALE)
                        P.tt("pool", m_[:], e_[:], maskT[:, ci, :].unsqueeze(1).to_broadcast([128, GQ, 128]), ALU.mult)

                    def back(ci):
                        blk = blocks[ci]
                        m_ = bufs[ci]
                        mf = m_[:].rearrange("p r t -> p (r t)")
                        P.mm(ps_o[:, 0:W], v_c[:, blk, g * 128:(g + 1) * 128], mf, start=(ci == 0), stop=(ci == nb - 1), rkeys=[v_c, m_])
                        P.mm(ps_z[:, 0:W], oneb[:], mf, start=(ci == 0), stop=(ci == nb - 1))

                    for ci in range(min(LA, nb)):
                        front(ci)
                    for ci in range(nb):
                        if ci + LA < nb:
                            front(ci + LA)
                        back(ci)
                        yield 1
                    P.recip(rz[:, 0:W], ps_z[:, 0:W])
                    o_ = ob[g % 2]
                    P.tt("dve", o_[:].rearrange("p r t -> p (r t)"), ps_o[:, 0:W], rz[:, 0:W], ALU.mult)
                    P.dma("sp", MIXT[CCH + g * GQ * 128:CCH + (g + 1) * GQ * 128, ti * 128:(ti + 1) * 128].rearrange("(r d) t -> d r t", d=128),
                          o_[:], okey=("MIXT", "attn"))

                def attn_group(job, g):
                    for _ in attn_steps(job, g):
                        pass

                def load_prompt_ki():
                    P.cdma(kc_t[:, 0:SEQ], kc_p[0:1, :].to_broadcast([128, SEQ]))
                    P.dma("sp", kiT_c[:, 0:SEQ], KIT, ikey="KIT", okey=(kiT_c, 0))

                def load_prompt_kv():
                    P.dma("sp", kT_c[:, :, 0:SEQ], KT, ikey="KT", okey=(kT_c, 0))
                    P.dma("sp", v_c[:, 0:NCX, :], Vc.rearrange("(c p) n -> p c n", p=128), ikey="Vc", okey=(v_c, 0))

                def mk_sample_ki(s):
                    def f():
                        P.cdma(kc_t[:, 0:SS], kc_s[0:1, :].to_broadcast([128, SS]))
                        P.cdma(kiT_c[:, 0:PAST], ckiT[s], okey=(kiT_c, 0))
                        P.dma("sp", kiT_c[:, PAST:SS], KITs[s], ikey="KITs", okey=(kiT_c, 1))
                    return f

                def mk_sample_kv(s):
                    def f():
                        for g in range(NKV):
                            P.cdma(kT_c[:, g, 0:PAST], ckT[s][:, g, :], okey=(kT_c, 0))
                        P.dma("sp", kT_c[:, :, PAST:SS], KTs[s], ikey="KTs", okey=(kT_c, 1))
                        cvv = cv[s].rearrange("(c p) n -> p c n", p=128)
                        nq = 4 if (PAST // 128) % 4 == 0 else 1
                        stp = (PAST // 128) // nq
                        for q in range(nq):
                            P.dma("pool", v_c[:, q * stp:(q + 1) * stp, :], cvv[:, q * stp:(q + 1) * stp, :], okey=(v_c, 0))
                        P.dma("sp", v_c[:, PAST // 128, :], Vs[s], ikey="Vs", okey=(v_c, 1))
                    return f

                jobs = []
                for i in range(NP):
                    jobs.append(dict(ti=i, blocks=list(range(0, i + 1)) + list(range(NP, 2 * NP)), topk=cfg["TOPK_P"]))
                jobs[0]["pre_idx"] = load_prompt_ki
                jobs[0]["pre_attn"] = load_prompt_kv
                for s in range(2):
                    jobs.append(dict(ti=NP + s, blocks=list(range(SS // 128)), topk=cfg["TOPK_S"],
                                     pre_idx=mk_sample_ki(s), pre_attn=mk_sample_kv(s)))
                idx_phase(jobs[0])
                topk_rounds(jobs[0], 0, 10 ** 6)
                topk_final(jobs[0])
                for k, job in enumerate(jobs):
                    nxt = jobs[k + 1] if k + 1 < len(jobs) else None
                    if nxt is not None:
                        idx_phase(nxt)
                        per = -(-NITER // NKV)
                    for g in range(NKV):
                        if nxt is not None:
                            topk_rounds(nxt, g * per, (g + 1) * per)
                        precast_some()
                        attn_group(job, g)
                    if nxt is not None:
                        topk_final(nxt)
                while pre_ops:
                    precast_some()
                P.flush()
            P.st = es

        otiles = list(range(NT))
        ogroups = [otiles[i:i + 4] for i in range(0, NT, 4)]
        Hs = dscr("Hs", [NTOK, D])
        RC = dscr("RC", [NT, 128, 4, 128])

        def x_rows(ti):
            return xctx[ti * 128:(ti + 1) * 128, :] if ti < NP else xsp[ti - NP]

        if "D" in stages:
            with contextlib.ExitStack() as st:
                P.st = st
                mixT = P.sb([128, KC, 512], BF16)
                wbuf = [P.sb([128, KC, 512], BF16) for _ in range(2)]
                xb_ = [P.sb([128, 512], F32) for _ in range(3)]
                hb_ = [P.sb([128, 512], F32) for _ in range(3)]
                pp = [P.ps() for _ in range(4)]
                wcnt = 0
                k_ = 0
                for grp in ogroups:
                    N = len(grp) * 128
                    c0 = grp[0] * 128
                    P.dma("sp", mixT[:, :, 0:N], MIXT[:, c0:c0 + N].rearrange("(c p) n -> p c n", p=128), ikey="MIXT")
                    for b in range(D // 512):
                        wb = wbuf[wcnt % 2]
                        wcnt += 1
                        load_w(wb, w_out[:, b * 512:(b + 1) * 512], 512)
                        wk = wkeys(wb)
                        for tt, ti in enumerate(grp):
                            xb = xb_[k_ % 3]
                            hb = hb_[k_ % 3]
                            p_ = pp[k_ % 4]
                            k_ += 1
                            P.dma("sp", xb[:], x_rows(ti)[:, b * 512:(b + 1) * 512])
                            tok_mm(p_[:, 0:512], mixT, tt * 128, wb, 512, wk)
                            P.tt("dve", hb[:], p_[:, 0:512], xb[:], ALU.add)
                            P.dma("sp", Hs[ti * 128:(ti + 1) * 128, b * 512:(b + 1) * 512], hb[:], okey=("Hs", None))
                P.flush()
            P.st = es

            with contextlib.ExitStack() as st:
                P.st = st
                nt = NormT(g_memn)
                gbc2 = P.sb([128, D], F32)
                bcast_row(gbc2[:], g_ffn[0:1, :], D)
                gmq = P.sb([128, 128], F32)
                bcast_row(gmq[:], g_mq[0:1, :], 128)
                wqm_b = P.sb([128, KC, 512], BF16)
                load_w(wqm_b, w_qm, 512)
                wom_b = P.sb([128, 4, D], BF16)
                P.cdma(wom_b[:], w_om.rearrange("(h p) d -> p h d", p=128))
                mkT_c = P.sb([128, 4, MEMT], BF16)
                mv_c = P.sb([128, MC, 512], BF16)
                ht = [P.sb([128, D], F32) for _ in range(2)]
                hn = [P.sb([128, KC, 128], BF16) for _ in range(2)]
                pq_ = P.ps()
                pl_ = [P.ps() for _ in range(2)]
                po_ = P.ps()
                pz_ = pq_
                pw_ = [P.ps() for _ in range(1)]
                ptr = P.ps([128, 1024], BF16)
                sq_t = P.sb([128, 512], F32)
                sm_t = P.sb([128, 12], F32)
                qmf = P.sb([128, 4, 128], F32)
                qmb = P.sb([128, 512], BF16)
                qmT = P.sb([128, 4, 128], BF16)
                pmT = [P.sb([128, 4, 128], BF16) for _ in range(MC)]
                rz = P.sb([128, 512], F32)
                omT = P.sb([128, 4, 128], BF16)
                for ti in otiles:
                    if ti == 0:
                        P.dma("sp", mkT_c[:], MKT, ikey="MKT")
                        P.dma("sp", mv_c[:], MV.rearrange("(c p) n -> p c n", p=128), ikey="MV")
                    elif ti >= NP:
                        P.dma("pool", mkT_c[:], cmkT[ti - NP])
                        P.dma("pool", mv_c[:], cmv[ti - NP].rearrange("(c p) n -> p c n", p=128))
                    h_t = ht[ti % 2]
                    hn_t = hn[ti % 2]
                    P.dma("sp", h_t[:], Hs[ti * 128:(ti + 1) * 128, :], ikey="Hs")
                    nt.run(h_t[:], lambda c0, n, hn_t=hn_t: hn_t[:, c0:c0 + n, :])
                    tok_mm(pq_[:, 0:512], hn_t, 0, wqm_b, 512, wkeys(wqm_b))
                    head_norm(pq_[:, 0:512], 4, gmq, qmf[:], sq_t, sm_t)
                    P.copy("act", qmb[:], qmf[:].rearrange("p h d -> p (h d)"))
                    for h in range(4):
                        P.tr(ptr[:, h * 128:(h + 1) * 128], qmb[:, h * 128:(h + 1) * 128], idb[:])
                    P.copy("dve", qmT[:], ptr[:, 0:512].rearrange("p (h t) -> p h t", t=128))
                    for mc in range(MC):
                        for h in range(4):
                            P.mm(pl_[mc % 2][:, h * 128:(h + 1) * 128], mkT_c[:, h, mc * 128:(mc + 1) * 128], qmT[:, h, :])
                        P.act(pmT[mc][:].rearrange("p h t -> p (h t)"), pl_[mc % 2][:, 0:512], AF.Exp, scale=ATT_SCALE)
                    for h in range(4):
                        for mc in range(MC):
                            P.mm(po_[:, h * 128:(h + 1) * 128], mv_c[:, mc, h * 128:(h + 1) * 128], pmT[mc][:, h, :],
                                 start=(mc == 0), stop=(mc == MC - 1))
                    for mc in range(MC):
                        P.mm(pz_[:, 0:512], oneb[:], pmT[mc][:].rearrange("p h t -> p (h t)"), start=(mc == 0), stop=(mc == MC - 1))
                    P.recip(rz[:], pz_[:, 0:512])
                    P.tt("dve", omT[:].rearrange("p h t -> p (h t)"), po_[:, 0:512], rz[:], ALU.mult)
                    for b in range(D // 512):
                        p_ = pw_[0]
                        for h in range(4):
                            P.mm(p_[:, 0:512], omT[:, h, :], wom_b[:, h, b * 512:(b + 1) * 512], start=(h == 0), stop=(h == 3))
                        P.tt("dve", h_t[:, b * 512:(b + 1) * 512], p_[:, 0:512], h_t[:, b * 512:(b + 1) * 512], ALU.add)
                    P.dma("sp", H2[ti * 128:(ti + 1) * 128, :], h_t[:], okey=("H2", None))
                    nt.gbc, g_save = gbc2, nt.gbc
                    nt.run(h_t[:], lambda c0, n, hn_t=hn_t: hn_t[:, c0:c0 + n, :])
                    nt.gbc = g_save
                    P.dma("sp", HN2T[:, ti * 128:(ti + 1) * 128].rearrange("(c p) t -> p c t", p=128), hn_t[:], okey=("HN2T", None))
                P.flush()
            P.st = es

            with contextlib.ExitStack() as st:
                P.st = st
                hn2 = P.sb([128, KC, 512], BF16)
                wbuf = [P.sb([128, KC, 512], BF16) for _ in range(2)]
                qpT = P.sb([128, 16, 512], F32)
                sk_t = P.sb([128, 16, 128], F32)
                P.dma("sp", sk_t[:], subk)
                pq_ = [P.ps() for _ in range(2)]
                ps_ = [P.ps() for _ in range(2)]
                ptf = P.ps()
                s12 = [P.sb([128, 16, 128], F32) for _ in range(2)]
                v16 = P.sb([128, 16, 16], F32)
                tmp128 = P.sb([128, 128], F32)
                cand = P.sb([128, 8, 256], F32)
                tmpc = P.sb([128, 256], F32)
                t16 = P.sb([128, 8, 16], F32)
                e16 = P.sb([128, 8, 16], F32)
                zz = P.sb([128, 8], F32)
                mlz = P.sb([128, 8], F32)
                rc3 = P.sb([128, 4, 8, 16], F32)
                rcT = [P.sb([128, 4, 128], F32) for _ in range(2)]
                idxu = P.sb([128, 8, 16], mybir.dt.uint32)
                wcnt = 0
                for grp in ogroups:
                    N = len(grp) * 128
                    c0 = grp[0] * 128
                    P.dma("sp", hn2[:, :, 0:N], HN2T[:, c0:c0 + N].rearrange("(c p) n -> p c n", p=128), ikey="HN2T")
                    for b in range(4):
                        wb = wbuf[wcnt % 2]
                        wcnt += 1
                        load_w(wb, w_pq[:, b * 512:(b + 1) * 512], 512)
                        wk = wkeys(wb)
                        for jj in range(4):
                            j = b * 4 + jj
                            p_ = pq_[j % 2]
                            for c in range(KC):
                                P.mm(p_[:, 0:N], wb[:, c, jj * 128:(jj + 1) * 128], hn2[:, c, 0:N], start=(c == 0), stop=(c == KC - 1),
                                     rkeys=[hn2] + wk)
                            P.copy("act", qpT[:, j, 0:N], p_[:, 0:N], okey=(qpT, j))
                    for tt, ti in enumerate(grp):
                        s_t = s12[ti % 2]
                        for jb in range(4):
                            p_ = ps_[jb % 2]
                            for jj in range(4):
                                j = jb * 4 + jj
                                P.mm(p_[:, jj * 128:(jj + 1) * 128], qpT[:, j, tt * 128:(tt + 1) * 128], sk_t[:, j, :], rkeys=[(qpT, j), sk_t])
                            P.copy("act", s_t[:, jb * 4:(jb + 1) * 4, :].rearrange("p j k -> p (j k)"), p_[:, 0:512])
                        P.dma("sp", S12[:, ti * 128:(ti + 1) * 128, :].rearrange("j t i -> t j i"), s_t[:], okey=("S12", None))
                        for j in range(16):
                            P.max8(v16[:, j, 0:8], s_t[:, j, :])
                            P.mrep(tmp128[:], v16[:, j, 0:8], s_t[:, j, :], -3.0e38)
                            P.max8(v16[:, j, 8:16], tmp128[:])
                            if j % 2 == 0:
                                hh = j // 2
                                P.op("dve", lambda e, hh=hh, j=j, s_t=s_t: e.max_index(idxu[:, hh, 0:8], v16[:, j, 0:8], s_t[:, j, :]),
                                     reads=[v16, s_t], writes=[idxu])
                                P.op("dve", lambda e, hh=hh, j=j: e.max_index(idxu[:, hh, 8:16], v16[:, j, 8:16], tmp128[:]),
                                     reads=[v16, tmp128], writes=[idxu])
                        v16v = v16[:].rearrange("p (h two) k -> p h two k", two=2)
                        for h in range(8):
                            P.tt("dve", cand[:, h, :].rearrange("p (a b) -> p a b", b=16),
                                 v16[:, 2 * h, :].unsqueeze(2).to_broadcast([128, 16, 16]),
                                 v16[:, 2 * h + 1, :].unsqueeze(1).to_broadcast([128, 16, 16]), ALU.add)
                        for h in range(8):
                            P.max8(t16[:, h, 0:8], cand[:, h, :])
                            P.mrep(tmpc[:], t16[:, h, 0:8], cand[:, h, :], -3.0e38)
                            P.max8(t16[:, h, 8:16], tmpc[:])
                        P.tt("dve", e16[:], t16[:], t16[:, :, 0:1].to_broadcast([128, 8, 16]), ALU.subtract)
                        P.act(e16[:], e16[:], AF.Exp)
                        P.reduce(zz[:], e16[:], ALU.add)
                        P.act(mlz[:], zz[:], AF.Ln)
                        P.tt("dve", mlz[:], mlz[:], t16[:, :, 0], ALU.add)
                        P.copy("dve", rc3[:, 0, :, :], v16v[:, :, 0, :])
                        P.tt("dve", rc3[:, 1, :, :], t16[:, :, 15:16].to_broadcast([128, 8, 16]), rc3[:, 0, :, :], ALU.subtract)
                        P.tt("dve", rc3[:, 2, :, :], rc3[:, 0, :, :], mlz[:].unsqueeze(2).to_broadcast([128, 8, 16]), ALU.subtract)
                        P.copy("dve", rc3[:, 3, :, :], idxu[:])
                        for q in range(4):
                            P.tr(ptf[:, q * 128:(q + 1) * 128], rc3[:, q, :, :].rearrange("p h a -> p (h a)"), idf[:])
                        r_T = rcT[ti % 2]
                        P.copy("act", r_T[:].rearrange("p q t -> p (q t)"), ptf[:, 0:512])
                        P.dma("sp", RC[ti], r_T[:], okey=("RC", None))
                P.flush()
            P.st = es

            with contextlib.ExitStack() as st:
                P.st = st
                TB = 32
                iota_t = P.sb([128, 128], F32)
                P.dma("sp", iota_t[:], c_iota)
                s2r = [P.sb([128, TB, 128], F32) for _ in range(2)]
                rct = [P.sb([128, 4, 128], F32) for _ in range(2)]
                o1 = [P.sb([128, 128], BF16) for _ in range(8)]
                ee = [P.sb([128, 128], F32) for _ in range(8)]
                rr = [P.sb([128, 128], BF16) for _ in range(8)]
                gst = [P.sb([128, 128, 128], BF16) for _ in range(2)]
                pg_ = [P.ps() for _ in range(3)]
                kk = 0
                pend = [None]
                for ti in otiles:
                    rc_ = rct[ti % 2]
                    g_s = gst[ti % 2]
                    P.dma("sp", rc_[:], RC[ti], ikey="RC")
                    for tb in range(128 // TB):
                        t0 = ti * 128 + tb * TB
                        a2 = s2r[tb % 2]
                        src = S12[:, t0:t0 + TB, :].rearrange("(h two) t i -> two h (t i)", two=2)[1]
                        P.dma("sp", a2[:].rearrange("p t i -> p (t i)"), src.unsqueeze(1).to_broadcast([8, 16, TB * 128]),
                              ikey="S12", okey=(a2, None))
                        for tq in range(0, TB, 4):
                            p_ = pg_[(kk) % 3]
                            kk += 1
                            for u4 in range(4):
                                tl = tq + u4
                                t = tb * TB + tl
                                o_ = o1[(kk % 2) * 4 + u4]
                                e_ = ee[(kk % 2) * 4 + u4]
                                r_ = rr[(kk % 2) * 4 + u4]
                                P.ts("dve", o_[:], iota_t[:], rc_[:, 3, t:t + 1], None, op0=ALU.is_equal)
                                P.act(e_[:], a2[:, tl, :], AF.Exp, bias=rc_[:, 2, t:t + 1])
                                P.stt(r_[:], a2[:, tl, :], rc_[:, 1, t:t + 1], e_[:], ALU.is_ge, ALU.mult)
                                P.mm(p_[:, u4 * 128:(u4 + 1) * 128], o_[:], r_[:])
                            if pend[0] is not None:
                                pend[0]()
                            tbase = tb * TB + tq

                            def evac(p_=p_, tbase=tbase, g_s=g_s):
                                P.copy("act", g_s[:, :, tbase:tbase + 4].rearrange("p i t -> p t i"),
                                       p_[:, 0:512].rearrange("p (t i) -> p t i", i=128))
                            pend[0] = evac
                    pend[0]()
                    pend[0] = None
                    P.dma("sp", GALL[:, :, ti * 128:(ti + 1) * 128], g_s[:], okey=("GALL", None))
                P.flush()
            P.st = es

        if "E" in stages:
            with contextlib.ExitStack() as st:
                P.st = st
                NCH = PEER_KEYS
                EB = 4
                hn2 = P.sb([128, KC, 512], BF16)
                oacc = P.sb([128, 4, D], F32)
                ub = [P.sb([128, KC, 128], BF16) for _ in range(3)]
                vb = [P.sb([128, EB, D], BF16) for _ in range(2)]
                coef = [P.sb([128, EB, 512], BF16) for _ in range(2)]
                gl = [P.sb([128, 512], BF16) for _ in range(2)]
                gc = [P.sb([128, 512], BF16) for _ in range(4)]
                pa_ = [P.ps() for _ in range(2)]
                pv_ = [P.ps() for _ in range(4)]
                ucnt = 0
                vcnt = 0
                pcnt = 0
                DH = 2048 if D % 2048 == 0 else D
                for grp in ogroups:
                    ng = len(grp)
                    N = ng * 128
                    c0 = grp[0] * 128
                    P.dma("sp", hn2[:, :, 0:N], HN2T[:, c0:c0 + N].rearrange("(c p) n -> p c n", p=128), ikey="HN2T")
                    for tt, ti in enumerate(grp):
                        P.dma("sp", oacc[:, tt, :], H2[ti * 128:(ti + 1) * 128, :], ikey="H2", okey=(oacc, tt))
                    def v_load(eb):
                        v_b = vb[eb % 2]
                        vsrc = VB16[eb * EB * 128:(eb + 1) * EB * 128, :].rearrange("(cc p) d -> p cc d", p=128)
                        P.dma("sp", v_b[:], vsrc, ikey="VB16")

                    loaded = set()

                    def u_load(gidx):
                        if gidx >= NCH or gidx in loaded:
                            return
                        loaded.add(gidx)
                        k_ = ucnt + gidx
                        P.dma("sp", ub[k_ % 3][:].rearrange("p c e -> p (c e)"), UB16[gidx], ikey="UB16")
                        P.dma("sp", gc[k_ % 4][:, 0:N], GALL[gidx][:, c0:c0 + N], ikey="GALL")

                    def u_phase(eb):
                        cf = coef[eb % 2]
                        for cc in range(EB):
                            c = eb * EB + cc
                            u_load(c)
                            u_load(c + 1)
                            u_load(c + 2)
                            k_ = ucnt + c
                            u_b = ub[k_ % 3]
                            g_l = gl[k_ % 2]
                            g_c = gc[k_ % 4]
                            p_ = pa_[k_ % 2]
                            for dc in range(KC):
                                P.mm(p_[:, 0:N], u_b[:, dc, :], hn2[:, dc, 0:N], start=(dc == 0), stop=(dc == KC - 1))
                            P.act(g_l[:, 0:N], p_[:, 0:N], AF.Gelu)
                            P.tt("dve", cf[:, cc, 0:N], g_l[:, 0:N], g_c[:, 0:N], ALU.mult, okey=(cf, cc))

                    def v_phase(eb):
                        nonlocal pcnt
                        cf = coef[eb % 2]
                        v_b = vb[eb % 2]
                        for tt in range(ng):
                            for db in range(D // 512):
                                pv = pv_[pcnt % 4]
                                pcnt += 1
                                for cc in range(EB):
                                    P.mm(pv[:, 0:512], cf[:, cc, tt * 128:(tt + 1) * 128], v_b[:, cc, db * 512:(db + 1) * 512],
                                         start=(cc == 0), stop=(cc == EB - 1), rkeys=[(cf, cc), v_b])
                                P.tt("dve", oacc[:, tt, db * 512:(db + 1) * 512], pv[:, 0:512], oacc[:, tt, db * 512:(db + 1) * 512], ALU.add,
                                     okey=(oacc, tt), rkeys=[pv, (oacc, tt)])

                    nE = NCH // EB
                    u_load(0)
                    u_load(1)
                    v_load(0)
                    u_phase(0)
                    for eb in range(nE):
                        if eb + 1 < nE:
                            v_load(eb + 1)
                            u_phase(eb + 1)
                        v_phase(eb)
                    ucnt += NCH
                    for tt, ti in enumerate(grp):
                        P.dma("sp", y[ti * 128:(ti + 1) * 128, :], oacc[:, tt, :], ikey=(oacc, tt), okey=("y", None))
                P.flush()
            P.st = es

        if dbg:
            for nm, ap_ in (("MIXT", MIXT), ("UTp", UTp), ("UTs", UTs), ("QT", QT), ("QIT", QIT), ("WI", WI), ("KT", KT), ("KIT", KIT),
                            ("Vc", Vc), ("H2", H2), ("HN2T", HN2T), ("S12", S12), ("GALL", GALL), ("MKT", MKT), ("MV", MV)):
                if nm in dbg:
                    o_ = dout("dbg_" + nm, list(ap_.shape), ap_.dtype)
                    P.dma("sp", o_, ap_)
        P.flush()
    return nc


def _rope_table(pos):
    half = 16
    inv_freq = np.power(np.float32(ROPE_THETA), -np.arange(half, dtype=np.float32) / np.float32(half)).astype(np.float32)
    ang = pos.astype(np.float32)[:, None] * inv_freq[None, :]
    return np.concatenate([np.cos(ang), np.sin(ang)], axis=1).astype(np.float32)


def host_prep(inp, cfg):
    D, KC, CCH, CC, NH, NKV, NP, NT, IH, SEQ, PAST, SS, MEMT = (cfg[k] for k in (
        "D", "KC", "CCH", "CC", "NH", "NKV", "NP", "NT", "IH", "SEQ", "PAST", "SS", "MEMT"))
    DS = cfg["DS"]
    f = lambda a: np.ascontiguousarray(a, dtype=np.float32)
    half = SEQ // 2
    w_in = inp["w_in"][0]
    OFF_Q = 2 * CCH
    OFF_K = OFF_Q + NH * 128
    OFF_V = OFF_K + NKV * 128
    OFF_QI = OFF_V + NKV * 128
    OFF_KI = OFF_QI + IH * 128
    OFF_WI = OFF_KI + 128
    a_ = w_in[:, :CCH].reshape(D, CC, 128)
    g_ = w_in[:, CCH:2 * CCH].reshape(D, CC, 128)
    w_glu = f(np.stack([a_, g_], axis=2).reshape(D, 2 * CCH))
    shared = dict(
        w_glu=w_glu,
        w_q=f(w_in[:, OFF_Q:OFF_K]),
        w_qi=f(w_in[:, OFF_QI:OFF_KI]),
        w_wi=f(w_in[:, OFF_WI:OFF_WI + IH]),
        w_kv=f(np.concatenate([w_in[:, OFF_K:OFF_V], w_in[:, OFF_V:OFF_QI], w_in[:, OFF_KI:OFF_WI]], axis=1)),
        w_out=f(inp["w_out"][0]),
        w_qm=f(inp["w_q_mem"][0]), w_km=f(inp["w_k_mem"][0]), w_vm=f(inp["w_v_mem"][0]), w_om=f(inp["w_o_mem"][0]),
        w_pq=f(inp["peer_wq"][0]),
        g_mix=f(inp["norm_mix_g"]), g_memn=f(inp["norm_mem_g"]), g_ffn=f(inp["norm_ffn_g"]), g_mem=f(inp["mem_norm_g"]),
        g_q=f(inp["q_norm_g"]), g_k=f(inp["k_norm_g"]), g_mq=f(inp["mem_q_norm_g"]), g_mk=f(inp["mem_k_norm_g"]),
        dww=f(inp["dw_w"][0].reshape(31, CC, 128).transpose(2, 1, 0)),
        dwb=f(inp["dw_b"][0].reshape(CC, 128).T), lng=f(inp["conv_ln_g"][0].reshape(CC, 128).T),
        lnb=f(inp["conv_ln_b"][0].reshape(CC, 128).T),
        vtab=f(inp["peer_v"][0]),
        c_idb=np.eye(128).astype(ml_dtypes.bfloat16), c_idf=np.eye(128, dtype=np.float32),
        c_oneb=np.ones((128, 128)).astype(ml_dtypes.bfloat16), c_onef=np.ones((128, 128), dtype=np.float32),
        c_iota=np.ascontiguousarray(np.tile(np.arange(128, dtype=np.float32)[None, :], (128, 1))),
    )
    sk = np.stack([inp["peer_sub_k1"][0], inp["peer_sub_k2"][0]], axis=1)
    shared["subk"] = f(sk.reshape(16, 128, 128).transpose(2, 0, 1))
    u = inp["peer_u"][0]
    shared["uT"] = f(u.reshape(128, 128, KC, 128).transpose(0, 3, 2, 1).reshape(128, 128, KC * 128))
    kcs = (np.arange(SS) // 64).astype(np.float32)
    kcs[PAST + DS:] = 1.0e9
    shared["kc_s"] = kcs[None, :]
    shared["rope_s"] = _rope_table(PAST + np.arange(128))
    maps = []
    for c in range(8):
        b, hf = c // 2, c % 2
        xb = inp["x_prompt"][b]
        own = xb[hf * half:(hf + 1) * half]
        oth = xb[(1 - hf) * half:(2 - hf) * half]
        pos = np.concatenate([hf * half + np.arange(half), (1 - hf) * half + np.arange(half)])
        m = dict(shared)
        m["xctx"] = f(np.concatenate([own, oth], axis=0))
        m["xhalo"] = f(xb[half - 128:half]) if hf == 1 else np.zeros((128, D), np.float32)
        xsp = np.zeros((2, 128, D), np.float32)
        for s in range(2):
            xsp[s, :DS] = inp["x_sample"][2 * c + s]
        m["xsp"] = xsp
        m["mem"] = f(inp["mem_prompt"][b])
        m["ckT"] = f(np.stack([inp["cache_k"][0, 2 * c + s].transpose(2, 1, 0) for s in range(2)]))
        m["cv"] = f(np.stack([inp["cache_v"][0, 2 * c + s].reshape(PAST, NKV * 128) for s in range(2)]))
        m["ckiT"] = f(np.stack([inp["cache_k_idx"][0, 2 * c + s].T for s in range(2)]))
        m["stT"] = f(np.stack([inp["state_conv"][0, 2 * c + s].T for s in range(2)]))
        m["cmkT"] = f(np.stack([inp["cache_mem_k"][0, 2 * c + s].transpose(2, 1, 0) for s in range(2)]))
        m["cmv"] = f(np.stack([inp["cache_mem_v"][0, 2 * c + s].reshape(MEMT, 512) for s in range(2)]))
        m["rope_c"] = _rope_table(pos)
        m["kc_p"] = (pos // 64).astype(np.float32)[None, :]
        q = np.zeros((128, NT), np.float32)
        for i in range(NP):
            q[:, i] = (hf * half + i * 128 + np.arange(128)) // 64
        q[:, NP:] = PAST // 64
        m["qch"] = q
        maps.append(m)
    return maps


def assemble(res, cfg):
    D, CCH, NKV, NP, SEQ, DS, MEMT, B, DB = (cfg[k] for k in ("D", "CCH", "NKV", "NP", "SEQ", "DS", "MEMT", "B", "DB"))
    half = SEQ // 2
    y_p = np.zeros((B, SEQ, D), np.float32)
    y_s = np.zeros((DB, DS, D), np.float32)
    k_p = np.zeros((1, B, SEQ, NKV, 128), np.float32)
    v_p = np.zeros_like(k_p)
    ki_p = np.zeros((1, B, SEQ, 128), np.float32)
    conv_p = np.zeros((1, B, 30, CCH), np.float32)
    mk_p = np.zeros((1, B, MEMT, 4, 128), np.float32)
    mv_p = np.zeros_like(mk_p)
    k_s = np.zeros((1, DB, DS, NKV, 128), np.float32)
    v_s = np.zeros_like(k_s)
    ki_s = np.zeros((1, DB, DS, 128), np.float32)
    conv_s = np.zeros((1, DB, 30, CCH), np.float32)
    for c in range(8):
        r = res[c]
        b, hf = c // 2, c % 2
        y_p[b, hf * half:(hf + 1) * half] = r["y"][:NP * 128]
        if hf == 0:
            k_p[0, b] = r["o_k"].reshape(SEQ, NKV, 128)
            v_p[0, b] = r["o_v"].reshape(SEQ, NKV, 128)
            ki_p[0, b] = r["o_ki"]
            mk_p[0, b] = r["o_mk"].reshape(MEMT, 4, 128)
            mv_p[0, b] = r["o_mv"].reshape(MEMT, 4, 128)
        else:
            conv_p[0, b] = r["o_conv"]
        for s in range(2):
            q = 2 * c + s
            y_s[q] = r["y"][(NP + s) * 128:(NP + s) * 128 + DS]
            k_s[0, q] = r["o_ks"][s, :DS].reshape(DS, NKV, 128)
            v_s[0, q] = r["o_vs"][s, :DS].reshape(DS, NKV, 128)
            ki_s[0, q] = r["o_kis"][s, :DS]
            conv_s[0, q] = r["o_convs"][s]
    return (y_p, y_s, k_p, v_p, ki_p, conv_p, mk_p, mv_p, k_s, v_s, ki_s, conv_s)


def kernel(**inputs):
    cfg = mkcfg()
    inp = {k: np.asarray(v) for k, v in inputs.items()}
    maps = host_prep(inp, cfg)
    nc = build(cfg)
    res = run_bass_kernel_spmd(nc, maps, core_ids=list(range(8)))
    return assemble(res.results, cfg)
```
